# Optimizing a Trainium2 kernel written in Bass

```python
import jax
import jax.numpy as jnp
from jax import lax
import numpy as np

D_MODEL = 1024
BATCH = 8
SEQ = 2048
DEPTH = 2
DEC_BATCH = 32
DEC_SEQ = 8
PAST_LEN = 16384
PAGE_SIZE = 128

N_MIXERS = 2
N_LAYERS_A = (DEPTH + 1) // 2
N_LAYERS_B = DEPTH // 2
EPS = 1e-6
L2_EPS = 1e-6

H_A = 8
DK_A = 128
DV_A = 128
WIDTH_A = H_A * DV_A
CONV_W = 4
CONV_DIM = H_A * (2 * DK_A + DV_A)
CHUNK = 64
W_IN_A = CONV_DIM + WIDTH_A + 2 * H_A

H_B = 8
DH_B = 128
DIL_GROUPS = ((128, 1), (512, 4), (2048, 16))
N_GROUPS = len(DIL_GROUPS)
WIDTH_B = H_B * DH_B
W_IN_B = 3 * N_GROUPS * WIDTH_B + WIDTH_B
ATTN_SCALE = DH_B ** -0.5

kernel_name = 'hybrid_gdn_dilated_window_step'


def rms_norm(x, g):
    xf = x.astype(jnp.float32)
    y = xf * lax.rsqrt(jnp.mean(xf * xf, axis=-1, keepdims=True) + EPS)
    return (y * g.astype(jnp.float32)).astype(x.dtype)


def l2_normalize(x):
    xf = x.astype(jnp.float32)
    return xf * lax.rsqrt(jnp.sum(xf * xf, axis=-1, keepdims=True) + L2_EPS)


def ada_modulate(x, c, norm_g, ada_w, ada_b):
    mod = jnp.einsum('bd,de->be', jax.nn.silu(c), ada_w) + ada_b
    shift, scale, gate = jnp.split(mod[:, None, :], 3, axis=-1)
    h = rms_norm(x, norm_g) * (1 + scale) + shift
    return h, gate


def causal_conv_silu(u, buf, w):
    L = u.shape[1]
    ext = jnp.concatenate([buf.astype(u.dtype), u], axis=1)
    y = ext[:, 0:L] * w[0]
    for i in range(1, CONV_W):
        y = y + ext[:, i:i + L] * w[i]
    return jax.nn.silu(y), ext[:, L:]


def gated_delta_rule(q, k, v, g, beta, S0):
    B, L, H, DK = q.shape
    DV = v.shape[-1]
    C = min(CHUNK, L)
    pad = (-L) % C
    if pad:
        pw = ((0, 0), (0, pad), (0, 0), (0, 0))
        q, k, v = jnp.pad(q, pw), jnp.pad(k, pw), jnp.pad(v, pw)
        g, beta = jnp.pad(g, pw[:3]), jnp.pad(beta, pw[:3])
    n = (L + pad) // C

    def blocks(t):
        t = t.reshape((B, n, C, H) + t.shape[3:])
        return jnp.moveaxis(t, (1, 3), (0, 2))

    qc, kc, vc, bc = blocks(q), blocks(k), blocks(v), blocks(beta)
    gc = jnp.cumsum(blocks(g), axis=-1)
    pos = jnp.arange(C)
    tril = pos[:, None] >= pos[None, :]
    strict = pos[:, None] > pos[None, :]
    diff = gc[..., :, None] - gc[..., None, :]
    decay = jnp.where(tril, jnp.exp(jnp.where(tril, diff, 0.0)), 0.0)
    kb = kc * bc[..., None]
    n_mat = jnp.where(strict, jnp.einsum('nbhik,nbhjk->nbhij', kb, kc) * decay, 0.0)
    a_mat = n_mat + jnp.eye(C, dtype=n_mat.dtype)
    rhs = jnp.concatenate([vc * bc[..., None], kb * jnp.exp(gc)[..., None]], axis=-1)
    sol = lax.linalg.triangular_solve(a_mat, rhs, left_side=True, lower=True, unit_diagonal=True)
    w_val, k_cum = sol[..., :DV], sol[..., DV:]
    qk = jnp.einsum('nbhik,nbhjk->nbhij', qc, kc) * decay
    q_dec = qc * jnp.exp(gc)[..., None]
    k_dec = kc * jnp.exp(gc[..., -1:] - gc)[..., None]
    g_tot = jnp.exp(gc[..., -1])

    def step(S, xs):
        w_c, kcum_c, qk_c, qdec_c, kdec_c, gtot_c = xs
        u = w_c - jnp.einsum('bhck,bhkv->bhcv', kcum_c, S)
        o = jnp.einsum('bhck,bhkv->bhcv', qdec_c, S) + jnp.einsum('bhij,bhjv->bhiv', qk_c, u)
        S = S * gtot_c[..., None, None] + jnp.einsum('bhck,bhcv->bhkv', kdec_c, u)
        return S, o

    S, o = lax.scan(step, S0, (w_val, k_cum, qk, q_dec, k_dec, g_tot))
    o = jnp.moveaxis(o, (0, 2), (1, 3)).reshape(B, n * C, H, DV)[:, :L]
    return o, S


def mixer_a(h, conv_buf, S0, w_in, conv_w, A_log, dt_bias, out_g, w_out):
    B, L, _ = h.shape
    proj = jnp.einsum('bld,de->ble', h, w_in)
    qkv = proj[..., :CONV_DIM]
    z = proj[..., CONV_DIM:CONV_DIM + WIDTH_A]
    a = proj[..., CONV_DIM + WIDTH_A:CONV_DIM + WIDTH_A + H_A]
    b = proj[..., CONV_DIM + WIDTH_A + H_A:]
    qkv, new_buf = causal_conv_silu(qkv, conv_buf, conv_w)
    q = l2_normalize(qkv[..., :H_A * DK_A].reshape(B, L, H_A, DK_A)) * (DK_A ** -0.5)
    k = l2_normalize(qkv[..., H_A * DK_A:2 * H_A * DK_A].reshape(B, L, H_A, DK_A))
    v = qkv[..., 2 * H_A * DK_A:].reshape(B, L, H_A, DV_A).astype(jnp.float32)
    g = -jnp.exp(A_log.astype(jnp.float32)) * jax.nn.softplus(a.astype(jnp.float32) + dt_bias.astype(jnp.float32))
    beta = jax.nn.sigmoid(b.astype(jnp.float32))
    o, S = gated_delta_rule(q, k, v, g, beta, S0.astype(jnp.float32))
    o = rms_norm(o, out_g).reshape(B, L, WIDTH_A) * jax.nn.silu(z.astype(jnp.float32))
    y = jnp.einsum('ble,ed->bld', o.astype(h.dtype), w_out)
    return y, new_buf, S


def dilated_attn_prompt(q, k, v, window, dil):
    B, S, H, Dh = q.shape
    n = window // dil
    Sd = S // dil
    blk = n
    nb = -(-Sd // blk)
    Sp = nb * blk

    def sub(t):
        t = t.reshape(B, Sd, dil, H, Dh).transpose(0, 2, 1, 3, 4)
        t = jnp.pad(t, ((0, 0), (0, 0), (0, Sp - Sd), (0, 0), (0, 0)))
        return t.reshape(B, dil, nb, blk, H, Dh)

    def with_prev(t):
        prev = jnp.pad(t, ((0, 0), (0, 0), (1, 0), (0, 0), (0, 0), (0, 0)))[:, :, :-1]
        return jnp.concatenate([prev, t], axis=3)

    qs = sub(q).astype(jnp.float32)
    kk = with_prev(sub(k)).astype(jnp.float32)
    vv = with_prev(sub(v)).astype(jnp.float32)
    s = jnp.einsum('brnqhe,brnkhe->brnhqk', qs, kk) * ATTN_SCALE
    qi = jnp.arange(nb)[:, None, None] * blk + jnp.arange(blk)[None, :, None]
    ki = jnp.arange(nb)[:, None, None] * blk - blk + jnp.arange(2 * blk)[None, None, :]
    rel = qi - ki
    valid = (rel >= 0) & (rel <= n) & (ki >= 0)
    s = jnp.where(valid[:, None], s, -jnp.inf)
    m = jnp.max(s, axis=-1, keepdims=True)
    p = jnp.exp(s - m)
    l = jnp.sum(p, axis=-1, keepdims=True)
    o = jnp.einsum('brnhqk,brnkhe->brnqhe', p / l, vv)
    lse = (m + jnp.log(l))[..., 0].transpose(0, 1, 2, 4, 3)
    o = o.reshape(B, dil, Sp, H, Dh)[:, :, :Sd].transpose(0, 2, 1, 3, 4).reshape(B, S, H, Dh)
    lse = lse.reshape(B, dil, Sp, H)[:, :, :Sd].transpose(0, 2, 1, 3).reshape(B, S, H)
    return o, lse


def dilated_attn_step(q, k, v, kv_buf, window, dil):
    Bd, L, H, Dh = q.shape
    n = window // dil
    Lbuf = kv_buf.shape[1]
    k_all = jnp.concatenate([kv_buf[:, :, 0].astype(k.dtype), k], axis=1)
    v_all = jnp.concatenate([kv_buf[:, :, 1].astype(v.dtype), v], axis=1)
    idx = Lbuf + jnp.arange(L)[:, None] - dil * jnp.arange(n + 1)[None, :]
    valid = idx >= 0
    idx = jnp.maximum(idx, 0)
    kg = k_all[:, idx].astype(jnp.float32)
    vg = v_all[:, idx].astype(jnp.float32)
    s = jnp.einsum('blhe,blmhe->blhm', q.astype(jnp.float32), kg) * ATTN_SCALE
    s = jnp.where(valid[:, None, :], s, -jnp.inf)
    m = jnp.max(s, axis=-1, keepdims=True)
    p = jnp.exp(s - m)
    l = jnp.sum(p, axis=-1, keepdims=True)
    o = jnp.einsum('blhm,blmhe->blhe', p / l, vg)
    lse = (m + jnp.log(l))[..., 0]
    return o, lse


def mixer_b(h, kv_bufs, w_in, w_out):
    B, L, _ = h.shape
    proj = jnp.einsum('bld,de->ble', h, w_in)
    qkv = proj[..., :3 * N_GROUPS * WIDTH_B].reshape(B, L, 3, N_GROUPS, H_B, DH_B)
    z = proj[..., 3 * N_GROUPS * WIDTH_B:]
    outs, lses, new_kv = [], [], []
    for gi, (window, dil) in enumerate(DIL_GROUPS):
        q, k, v = qkv[:, :, 0, gi], qkv[:, :, 1, gi], qkv[:, :, 2, gi]
        if kv_bufs is None:
            o, lse = dilated_attn_prompt(q, k, v, window, dil)
            keep = min(window, L)
            new_kv.append(jnp.stack([k[:, L - keep:], v[:, L - keep:]], axis=2))
        else:
            o, lse = dilated_attn_step(q, k, v, kv_bufs[gi], window, dil)
            new_kv.append(jnp.stack([k, v], axis=2))
        outs.append(o)
        lses.append(lse)
    wts = jax.nn.softmax(jnp.stack(lses, axis=0), axis=0)
    o = jnp.sum(wts[..., None] * jnp.stack(outs, axis=0), axis=0)
    o = o.reshape(B, L, WIDTH_B) * jax.nn.silu(z.astype(jnp.float32))
    y = jnp.einsum('ble,ed->bld', o.astype(h.dtype), w_out)
    return y, new_kv


def setup_inputs(seed: int = 0) -> dict:
    key = jax.random.key(seed)
    ks = jax.random.split(key, 24)
    f32 = jnp.float32
    n_buf = [min(w, PAST_LEN) for (w, _) in DIL_GROUPS]
    dt = jnp.exp(jax.random.uniform(ks[14], (N_LAYERS_A, H_A), f32) * (np.log(0.1) - np.log(0.001)) + np.log(0.001))
    return {
        'x_prompt': jax.random.normal(ks[0], (BATCH, SEQ, D_MODEL), f32),
        'x_sample': jax.random.normal(ks[1], (DEC_BATCH, DEC_SEQ, D_MODEL), f32),
        'state_delta': 0.1 * jax.random.normal(ks[2], (N_LAYERS_A, DEC_BATCH, H_A, DK_A, DV_A), f32),
        'state_conv': jax.random.normal(ks[3], (N_LAYERS_A, DEC_BATCH, CONV_W - 1, CONV_DIM), f32),
        'cache_kv_w128': jax.random.normal(ks[4], (N_LAYERS_B, DEC_BATCH, n_buf[0], 2, H_B, DH_B), f32),
        'cache_kv_w512': jax.random.normal(ks[5], (N_LAYERS_B, DEC_BATCH, n_buf[1], 2, H_B, DH_B), f32),
        'cache_kv_w2048': jax.random.normal(ks[6], (N_LAYERS_B, DEC_BATCH, n_buf[2], 2, H_B, DH_B), f32),
        'c_prompt': jax.random.normal(ks[7], (BATCH, D_MODEL), f32),
        'c_sample': jax.random.normal(ks[8], (DEC_BATCH, D_MODEL), f32),
        'norm_g': 1.0 + 0.02 * jax.random.normal(ks[9], (DEPTH, D_MODEL), f32),
        'ada_w': 0.5 * D_MODEL ** -0.5 * jax.random.normal(ks[10], (DEPTH, D_MODEL, 3 * D_MODEL), f32),
        'ada_b': 0.01 * jax.random.normal(ks[11], (DEPTH, 3 * D_MODEL), f32),
        'a_w_in': D_MODEL ** -0.5 * jax.random.normal(ks[12], (N_LAYERS_A, D_MODEL, W_IN_A), f32),
        'a_conv_w': CONV_W ** -0.5 * jax.random.normal(ks[13], (N_LAYERS_A, CONV_W, CONV_DIM), f32),
        'a_A_log': jnp.log(jax.random.uniform(ks[15], (N_LAYERS_A, H_A), f32, 1.0, 16.0)),
        'a_dt_bias': dt + jnp.log(-jnp.expm1(-dt)),
        'a_out_norm_g': 1.0 + 0.02 * jax.random.normal(ks[16], (N_LAYERS_A, DV_A), f32),
        'a_w_out': WIDTH_A ** -0.5 * jax.random.normal(ks[17], (N_LAYERS_A, WIDTH_A, D_MODEL), f32),
        'b_w_in': D_MODEL ** -0.5 * jax.random.normal(ks[18], (N_LAYERS_B, D_MODEL, W_IN_B), f32),
        'b_w_out': WIDTH_B ** -0.5 * jax.random.normal(ks[19], (N_LAYERS_B, WIDTH_B, D_MODEL), f32),
        'final_norm_g': 1.0 + 0.02 * jax.random.normal(ks[20], (D_MODEL,), f32),
    }


def reference(x_prompt, x_sample, state_delta, state_conv, cache_kv_w128, cache_kv_w512, cache_kv_w2048,
              c_prompt, c_sample, norm_g, ada_w, ada_b, a_w_in, a_conv_w, a_A_log, a_dt_bias, a_out_norm_g,
              a_w_out, b_w_in, b_w_out, final_norm_g):
    kv_caches = (cache_kv_w128, cache_kv_w512, cache_kv_w2048)
    xp, xs = x_prompt, x_sample
    bp = xp.shape[0]
    delta_p, delta_s, conv_p, conv_s = [], [], [], []
    kv_p = [[] for _ in DIL_GROUPS]
    kv_s = [[] for _ in DIL_GROUPS]
    for layer in range(DEPTH):
        hp, gate_p = ada_modulate(xp, c_prompt, norm_g[layer], ada_w[layer], ada_b[layer])
        hs, gate_s = ada_modulate(xs, c_sample, norm_g[layer], ada_w[layer], ada_b[layer])
        i = layer // N_MIXERS
        if layer % N_MIXERS == 0:
            params = (a_w_in[i], a_conv_w[i], a_A_log[i], a_dt_bias[i], a_out_norm_g[i], a_w_out[i])
            zero_buf = jnp.zeros((bp, CONV_W - 1, CONV_DIM), xp.dtype)
            zero_S = jnp.zeros((bp, H_A, DK_A, DV_A), jnp.float32)
            yp, buf_p, S_p = mixer_a(hp, zero_buf, zero_S, *params)
            ys, buf_s, S_s = mixer_a(hs, state_conv[i], state_delta[i], *params)
            delta_p.append(S_p.astype(state_delta.dtype))
            delta_s.append(S_s.astype(state_delta.dtype))
            conv_p.append(buf_p)
            conv_s.append(buf_s)
        else:
            yp, new_p = mixer_b(hp, None, b_w_in[i], b_w_out[i])
            ys, new_s = mixer_b(hs, [cache[i] for cache in kv_caches], b_w_in[i], b_w_out[i])
            for gi in range(N_GROUPS):
                kv_p[gi].append(new_p[gi])
                kv_s[gi].append(new_s[gi])
        xp = xp + gate_p * yp
        xs = xs + gate_s * ys
    y_prompt = rms_norm(xp, final_norm_g)
    y_sample = rms_norm(xs, final_norm_g)
    return (y_prompt, y_sample,
            jnp.stack(delta_p), jnp.stack(delta_s), jnp.stack(conv_p), jnp.stack(conv_s),
            jnp.stack(kv_p[0]), jnp.stack(kv_s[0]), jnp.stack(kv_p[1]), jnp.stack(kv_s[1]),
            jnp.stack(kv_p[2]), jnp.stack(kv_s[2]))
```

```python
import numpy as np
import ml_dtypes
from contextlib import ExitStack
import concourse.bass as bass
import concourse.mybir as mybir
from concourse.bass_utils import run_bass_kernel_spmd

F32 = mybir.dt.float32
BF16 = mybir.dt.bfloat16
AF = mybir.ActivationFunctionType
ALU = mybir.AluOpType

NEG = -30000.0
NCORES = 8
L = 2048
NS = 4
LS = 8
TT = L + NS * LS
D = 1024
EPS = 1e-6


class Sched:
    ENG = ("pe", "act", "dve", "pool", "sp")

    def __init__(self, nc, stack, sempool):
        self.nc = nc
        self.stack = stack
        self.sempool = sempool
        self.eng = {"pe": nc.tensor, "act": nc.scalar, "dve": nc.vector, "pool": nc.gpsimd, "sp": nc.sync}
        self.gen = {e: 0 for e in self.ENG}
        self.cnt = {e: 0 for e in self.ENG}
        self.esem = {}
        self.seen = {e: {} for e in self.ENG}
        self.res = {}
        self.dsem = {}
        self.nsem = 0
        self.ninst = {e: 0 for e in self.ENG}

    def _newsem(self, name):
        self.nsem += 1
        return self.sempool.pop()

    def _R(self, k):
        r = self.res.get(k)
        if r is None:
            r = [None, {}]
            self.res[k] = r
        return r

    def _wait(self, en, events):
        need = {}
        for ev in events:
            if ev is None:
                continue
            if ev[0] == "E":
                if ev[1] == en and en == "pe":
                    continue
                k = ("E", ev[1])
                v = (ev[2], ev[3])
            else:
                k = ("D", ev[1])
                v = (0, ev[2])
            if need.get(k, (-1, -1)) < v:
                need[k] = v
        for k, v in need.items():
            if self.seen[en].get(k, (-1, -1)) >= v:
                continue
            self.seen[en][k] = v
            sem = self.esem[(k[1], v[0])] if k[0] == "E" else self.dsem[k[1]][0]
            self.eng[en].wait_ge(sem, v[1])

    def _deps(self, r, w):
        evs = []
        for k in r:
            R = self.res.get(k)
            if R is not None:
                evs.append(R[0])
                if isinstance(k, tuple) and k[0] == "ps":
                    evs.extend(R[1].values())
        for k in w:
            R = self.res.get(k)
            if R is not None:
                evs.append(R[0])
                evs.extend(R[1].values())
        return evs

    def _record(self, ev, semid, r, w):
        for k in r:
            self._R(k)[1][semid] = ev
        for k in w:
            R = self._R(k)
            R[0] = ev
            R[1] = {}

    def op(self, en, fn, r=(), w=()):
        self._wait(en, self._deps(r, w))
        inst = fn(self.eng[en])
        self.ninst[en] += 1
        if self.cnt[en] >= 12000:
            self.gen[en] += 1
            self.cnt[en] = 0
        self.cnt[en] += 1
        g = self.gen[en]
        if (en, g) not in self.esem:
            self.esem[(en, g)] = self._newsem(f"e_{en}_{g}")
        inst.then_inc(self.esem[(en, g)], 1)
        ev = ("E", en, g, self.cnt[en])
        self._record(ev, ("E", en), r, w)
        return inst

    def dma(self, q, out, in_, r=(), w=(), key=None, **kw):
        self._wait(q, self._deps(r, w))
        inst = self.eng[q].dma_start(out=out, in_=in_, **kw)
        if key is None:
            key = ("w", w[0]) if w else ("r", r[0])
        ds = self.dsem.get(key)
        if ds is None:
            ds = [self._newsem(f"d{len(self.dsem)}"), 0]
            self.dsem[key] = ds
        ds[1] += 16
        inst.then_inc(ds[0], 16)
        ev = ("D", key, ds[1])
        self._record(ev, ("D", key), r, w)
        return inst

    def _all_events(self):
        evs = []
        for e in self.ENG:
            if self.cnt[e] > 0 or self.gen[e] > 0:
                evs.append(("E", e, self.gen[e], self.cnt[e]))
        for k, ds in self.dsem.items():
            evs.append(("D", k, ds[1]))
        return evs

    def barrier(self):
        evs = self._all_events()
        for e in self.ENG:
            self._wait(e, evs)

    def finish(self):
        self._wait("sp", self._all_events())


WAIT = "WAIT"


def run_tasks(gens):
    act = list(gens)
    idle = 0
    i = 0
    while act:
        i %= len(act)
        g = act[i]
        try:
            v = next(g)
        except StopIteration:
            act.pop(i)
            idle = 0
            continue
        if v is WAIT:
            idle += 1
            assert idle <= 4 * len(act) + 4, "all tasks waiting"
        else:
            idle = 0
        i += 1


class Arena:
    def __init__(self, ap, nbytes):
        self.ap = ap
        self.nbytes = nbytes

    def view(self, off, parts, shape, dt):
        esz = 4 if dt == F32 else 2
        n = int(np.prod(shape))
        nb = n * esz
        assert off % 4 == 0 and off + nb <= self.nbytes, (off, nb, self.nbytes)
        nw = (nb + 3) // 4
        v = self.ap[0:parts, off // 4: off // 4 + nw]
        if dt != F32:
            v = v.bitcast(dt)
            if v.shape[1] != n:
                v = v[:, 0:n]
        if len(shape) == 2:
            return v.rearrange("p (a b) -> p a b", a=shape[0])
        if len(shape) == 3:
            return v.rearrange("p (a b c) -> p a b c", a=shape[0], b=shape[1])
        return v


class Region:
    def __init__(self, arena, lo, hi):
        self.arena, self.lo, self.hi = arena, lo, hi
        self.cur = lo

    def reset(self):
        self.cur = self.lo

    def alloc(self, shape, dt, parts=128):
        esz = 4 if dt == F32 else 2
        nb = (int(np.prod(shape)) * esz + 3) // 4 * 4
        off = self.cur
        assert off + nb <= self.hi, ("region overflow", off, nb, self.hi)
        self.cur += nb
        return self.arena.view(off, parts, shape, dt)


CF_ID, CF_U, CF_MASK, CF_EPS, CF_ONE, CF_NHALF, CF_LNSC, CF_ZERO, CF_N = 0, 128, 256, 512, 513, 514, 515, 516, 520
CB_ID, CB_ONES, CB_BD, CB_ML = 0, 128, 256, 384
CB_MC, CB_MP, CB_MN, CB_MK, CB_N = 1152, 1280, 1408, 1504, 1528


def make_consts():
    cf = np.zeros((128, CF_N), np.float32)
    cf[:, CF_ID:CF_ID + 128] = np.eye(128)
    k = np.arange(128)[:, None]
    i = np.arange(128)[None, :]
    cf[:, CF_U:CF_U + 128] = (k <= i)
    cf[:, CF_MASK:CF_MASK + 128] = np.where(i >= k, 0.0, NEG)
    cf[:, CF_MASK + 128:CF_MASK + 256] = np.where(i > k, 0.0, NEG)
    cf[:, CF_EPS] = EPS
    cf[:, CF_ONE] = 1.0
    cf[:, CF_NHALF] = -0.5
    cf[:, CF_LNSC] = np.log(128.0 ** -0.5)
    cb = np.zeros((128, CB_N), np.float32)
    cb[:, CB_ID:CB_ID + 128] = np.eye(128)
    cb[:, CB_ONES:CB_ONES + 128] = 1.0
    ii = np.arange(128)[:, None]
    jj = np.arange(128)[None, :]
    cb[:, CB_BD:CB_BD + 128] = (ii // 16 == jj // 16)
    for lv, b in enumerate((16, 32, 64)):
        off = (ii // (2 * b) == jj // (2 * b)) & (ii % (2 * b) >= b) & (jj % (2 * b) < b)
        cb[:, CB_ML + lv * 256:CB_ML + lv * 256 + 128] = off.T
        cb[:, CB_ML + lv * 256 + 128:CB_ML + lv * 256 + 256] = off
    cb[:, CB_MC:CB_MC + 128] = np.where(jj >= ii, 0.0, NEG)
    cb[:, CB_MP:CB_MP + 128] = np.where(jj <= ii, 0.0, NEG)
    for gi, d in enumerate((1, 4, 16)):
        a = np.arange(32)
        sk, jk = a[:, None] // 8, a[:, None] % 8
        sq, lq = a[None, :] // 8, a[None, :] % 8
        ok = (sk == sq) & (jk <= lq) & ((lq - jk) % d == 0)
        cb[0:32, CB_MN + gi * 32:CB_MN + (gi + 1) * 32] = np.where(ok, 0.0, NEG)
        cb[:, CB_MK + gi * 8:CB_MK + (gi + 1) * 8] = np.where(ii >= (np.arange(8)[None, :] // d), 0.0, NEG)
    return cf, cb.astype(ml_dtypes.bfloat16)


def build(dbg=False, nheads=8, do_l1=True, stop=None):
    nc = bass.Bass("TRN2", target_bir_lowering=False)

    def din(name, shape, dt=F32):
        return nc.dram_tensor(name, list(shape), dt, kind="ExternalInput").ap()

    def dout(name, shape, dt=F32):
        return nc.dram_tensor(name, list(shape), dt, kind="ExternalOutput").ap()

    I = dict(
        xp=din("xp", [L, D]), xs=din("xs", [NS * LS, D]),
        cT=din("cT", [128, 8, 5]), ngc=din("ngc", [2, 128, 8]),
        adaw=din("adaw", [2, D, 3 * D]), adabc=din("adabc", [2, 128, 16]), adab=din("adab", [2, 3 * D]),
        awin=din("awin", [D, 4112]), cw=din("cw", [128, 24, 4]), hp=din("hp", [128, 17]),
        sconv=din("sconv", [128, 24, NS, 3]), sdelta=din("sdelta", [NS, 8, 128, 128]),
        awout=din("awout", [D, D]),
        cf=din("cf", [128, CF_N]), cb=din("cb", [128, CB_N], BF16),
        fng=din("fng", [D]),
        bwin=din("bwin", [D, 10240]), bwout=din("bwout", [D, D]),
        kc0=din("kc0", [NS, 128, 2, 8, 128]), kc1=din("kc1", [NS, 512, 2, 8, 128]), kc2=din("kc2", [NS, 2048, 2, 8, 128]),
    )
    O = dict(
        yp=dout("yp", [L, D]), ys=dout("ys", [NS * LS, D]),
        dp=dout("dp", [8, 128, 128]), ds=dout("ds", [NS, 8, 128, 128]),
        cp=dout("cp", [3, 3072]), cs=dout("cs", [NS * 3, 3072]),
        kv0p=dout("kv0p", [128, 2, 8, 128]), kv1p=dout("kv1p", [512, 2, 8, 128]), kv2p=dout("kv2p", [2048, 2, 8, 128]),
        kv0s=dout("kv0s", [NS, LS, 2, 8, 128]), kv1s=dout("kv1s", [NS, LS, 2, 8, 128]), kv2s=dout("kv2s", [NS, LS, 2, 8, 128]),
    )
    if dbg:
        O["dbg_x1p"] = dout("dbg_x1p", [L, D])
        O["dbg_x1s"] = dout("dbg_x1s", [NS * LS, D])
        O["dbg_og"] = dout("dbg_og", [128, 8, TT], BF16)
        O["dbg_hT"] = dout("dbg_hT", [128, 8, TT], BF16)

    stack = ExitStack()
    with stack:
        NB = 212000
        arena_t = stack.enter_context(nc.sbuf_tensor("arena", [128, NB // 4], F32))
        banks = [stack.enter_context(nc.psum_tensor(f"bank{i}", [128, 512], F32)) for i in range(8)]
        sempool = [stack.enter_context(nc.semaphore(f"s{i}")) for i in range(96)]
        stack.enter_context(nc.Block())
        S = Sched(nc, stack, sempool)
        A = Arena(arena_t, NB)

        def PS(b):
            return banks[b][:, :]

        def PSB(b):
            return banks[b][:, :].bitcast(BF16)

        RC = Region(A, 0, 8192)
        R1 = Region(A, 8192, 73728)
        R2 = Region(A, 73728, 107008)
        R3 = Region(A, 107008, 140288)
        R4 = Region(A, 140288, NB)

        cf = RC.alloc([CF_N], F32)
        cbf = RC.alloc([CB_N], BF16)
        hpar = RC.alloc([17], F32)
        S.dma("sp", cf, I["cf"], w=["cf"])
        S.dma("sp", cbf, I["cb"], w=["cb"])
        S.dma("sp", hpar, I["hp"], w=["hpar"])
        ident_f = cf[:, CF_ID:CF_ID + 128]
        U_f = cf[:, CF_U:CF_U + 128]
        mask2 = cf[:, CF_MASK:CF_MASK + 256]
        eps_c = cf[:, CF_EPS:CF_EPS + 1]
        one_c = cf[:, CF_ONE:CF_ONE + 1]
        nhalf_c = cf[:, CF_NHALF:CF_NHALF + 1]
        lnsc_c = cf[:, CF_LNSC:CF_LNSC + 1]
        ident_b = cbf[:, CB_ID:CB_ID + 128]
        ones_b = cbf[:, CB_ONES:CB_ONES + 128]
        bd_b = cbf[:, CB_BD:CB_BD + 128]
        ml_b = [cbf[:, CB_ML + lv * 256:CB_ML + (lv + 1) * 256] for lv in range(3)]
        mT_cur = cbf[:, CB_MC:CB_MC + 128]
        mT_prev = cbf[:, CB_MP:CB_MP + 128]
        CK = ["cf", "cb", "hpar"]

        modc = RC.alloc([16, 5], F32)
        gmod = RC.alloc([8, 5], F32)
        ngc = RC.alloc([8], F32)
        adabc = RC.alloc([16], F32)
        cT = RC.alloc([8, 5], F32)
        scb = RC.alloc([8, 5], BF16)

        def ada_phase(l, gate_p, gate_s, reg, parts=("col", "gate")):
            S.dma("sp", ngc, I["ngc"][l], w=["ngc"])
            S.dma("sp", adabc, I["adabc"][l], w=["adabc"])
            if l == 0:
                S.dma("sp", cT, I["cT"], w=["cT"])
                S.op("act", lambda e: e.activation(out=scb, in_=cT, func=AF.Silu), r=["cT"], w=["scb"])
            scp = reg.alloc([8, 128], BF16)
            scs = reg.alloc([8, 32], BF16)
            S.op("act", lambda e: e.activation(out=scp, in_=cT[:, :, 0:1].broadcast_to([128, 8, 128]), func=AF.Silu),
                 r=["cT"], w=["scp"])
            for s in range(NS):
                S.op("act", lambda e: e.activation(out=scs[:, :, 8 * s:8 * s + 8],
                                                   in_=cT[:, :, 1 + s:2 + s].broadcast_to([128, 8, 8]), func=AF.Silu),
                     r=["cT"], w=[("scs", s)])
            gb = reg.alloc([D], F32)
            S.dma("sp", gb, I["adab"][l, 2 * D:3 * D].partition_broadcast(128), w=["gb"])
            wb = [reg.alloc([8, 512], BF16) for _ in range(2)]
            for blk in range(6):
                if (blk < 4 and "col" not in parts) or (blk >= 4 and "gate" not in parts):
                    continue
                buf = wb[blk % 2]
                key = ("adaw", blk % 2)
                S.dma("pool", buf, I["adaw"][l][:, blk * 512:(blk + 1) * 512].rearrange("(c p) n -> p c n", p=128),
                      w=[key])
                if blk < 4:
                    def f(e, blk=blk, buf=buf):
                        last = None
                        for ecl in range(4):
                            ec = blk * 4 + ecl
                            for kc in range(8):
                                last = e.matmul(PS(0)[:, ec * 5:ec * 5 + 5], buf[:, kc, ecl * 128:(ecl + 1) * 128],
                                                scb[:, kc, :], start=(kc == 0), stop=(kc == 7))
                        return last
                    S.op("pe", f, r=[key, "scb"], w=[("ps", 0)])
                else:
                    hb = blk - 4
                    bk = 1 + (hb % 2)
                    def f(e, buf=buf, bk=bk):
                        last = None
                        for kc in range(8):
                            last = e.matmul(PS(bk), scp[:, kc, :], buf[:, kc, :], start=(kc == 0), stop=(kc == 7))
                        return last
                    S.op("pe", f, r=[key, "scp"], w=[("ps", bk)])
                    S.op("dve", lambda e, bk=bk, hb=hb: e.tensor_tensor(out=gate_p[:, hb * 512:(hb + 1) * 512], in0=PS(bk),
                                                                         in1=gb[:, hb * 512:(hb + 1) * 512], op=ALU.add),
                         r=[("ps", bk), "gb"], w=[("gate_p", l)])
                    def f2(e, buf=buf, bk=bk):
                        last = None
                        for kc in range(8):
                            last = e.matmul(PS(bk)[0:32, :], scs[:, kc, :], buf[:, kc, :], start=(kc == 0), stop=(kc == 7))
                        return last
                    S.op("pe", f2, r=[key] + [("scs", s) for s in range(NS)], w=[("ps", bk)])
                    S.op("dve", lambda e, bk=bk, hb=hb: e.tensor_tensor(out=gate_s[:, hb * 512:(hb + 1) * 512], in0=PS(bk)[0:32, :],
                                                                         in1=gb[0:32, hb * 512:(hb + 1) * 512], op=ALU.add),
                         r=[("ps", bk), "gb"], w=[("gate_s", l)])
            if "col" not in parts:
                return
            S.op("dve", lambda e: e.tensor_tensor(out=modc, in0=PS(0)[:, 0:80].rearrange("p (a b) -> p a b", b=5),
                                                  in1=adabc.unsqueeze(2).broadcast_to([128, 16, 5]), op=ALU.add),
                 r=[("ps", 0), "adabc"], w=["modc"])
            S.op("dve", lambda e: e.scalar_tensor_tensor(out=gmod, in0=modc[:, 8:16, :], scalar=1.0,
                                                         in1=ngc.unsqueeze(2).broadcast_to([128, 8, 5]),
                                                         op0=ALU.add, op1=ALU.mult),
                 r=["modc", "ngc"], w=["gmod"])

        def norm_phase(l, hT, x_src, reg):
            ssq = reg.alloc([17], F32)
            rstd = reg.alloc([17], F32)
            junk = reg.alloc([D], BF16)
            xn = [reg.alloc([D], BF16) for _ in range(2)]
            ntile = 17
            tiles = []
            if l == 0:
                xst = [reg.alloc([D], F32) for _ in range(3)]

            def xt(t):
                if l == 0:
                    return xst[t % 3], ("xst", t % 3)
                return x_src(t)

            def load(t):
                if l != 0:
                    return
                buf, key = xt(t)
                if t < 16:
                    S.dma("sp", buf, I["xp"][t * 128:(t + 1) * 128, :], w=[key])
                else:
                    S.dma("sp", buf[0:32], I["xs"], w=[key])

            def sq(t):
                buf, key = xt(t)
                p = 128 if t < 16 else 32
                keys = key if isinstance(key, list) else [key]
                S.op("act", lambda e: e.activation(out=junk[0:p], in_=buf[0:p], func=AF.Square, accum_out=ssq[0:p, t:t + 1]),
                     r=keys, w=["junk", ("ssq", t)])
                S.op("pool", lambda e: e.tensor_scalar(out=rstd[0:p, t:t + 1], in0=ssq[0:p, t:t + 1], scalar1=1.0 / D, scalar2=EPS,
                                                       op0=ALU.mult, op1=ALU.add), r=[("ssq", t)], w=[("rstd", t)])
                S.op("pool", lambda e: e.tensor_tensor(out=rstd[0:p, t:t + 1], in0=rstd[0:p, t:t + 1], in1=nhalf_c[0:p], op=ALU.pow),
                     r=[("rstd", t), "cf"], w=[("rstd", t)])

            def scale_T(t):
                buf, key = xt(t)
                p = 128 if t < 16 else 32
                xb = xn[t % 2]
                keys = key if isinstance(key, list) else [key]
                S.op("act", lambda e: e.activation(out=xb[0:p], in_=buf[0:p], func=AF.Identity, scale=rstd[0:p, t:t + 1]),
                     r=keys + [("rstd", t)], w=[("xn", t % 2)])
                bk = 2 + (t % 2)
                def f(e):
                    last = None
                    for kc in range(8):
                        last = e.transpose(PSB(bk)[:, kc * 128:kc * 128 + p], xb[0:p, kc * 128:(kc + 1) * 128], ident_b[0:p, 0:p])
                    return last
                S.op("pe", f, r=[("xn", t % 2), "cb"], w=[("ps", bk)])
                for kc in range(8):
                    if t < 16:
                        dst = hT[:, kc, t * 128:(t + 1) * 128]
                        src = PSB(bk)[:, kc * 128:(kc + 1) * 128]
                        if kc % 2 == 0:
                            S.op("dve", lambda e, dst=dst, src=src, kc=kc: e.tensor_scalar(
                                out=dst, in0=src, scalar1=gmod[:, kc, 0:1], scalar2=modc[:, kc, 0:1], op0=ALU.mult, op1=ALU.add),
                                 r=[("ps", bk), "gmod", "modc"], w=[("hT", t, kc)])
                        else:
                            S.op("act", lambda e, dst=dst, src=src, kc=kc: e.activation(
                                out=dst, in_=src, func=AF.Identity, scale=gmod[:, kc, 0:1], bias=modc[:, kc, 0:1]),
                                 r=[("ps", bk), "gmod", "modc"], w=[("hT", t, kc)])
                    else:
                        for s in range(NS):
                            dst = hT[:, kc, L + s * 8:L + s * 8 + 8]
                            src = PSB(bk)[:, kc * 128 + s * 8:kc * 128 + s * 8 + 8]
                            S.op("dve", lambda e, dst=dst, src=src, kc=kc, s=s: e.tensor_scalar(
                                out=dst, in0=src, scalar1=gmod[:, kc, 1 + s:2 + s], scalar2=modc[:, kc, 1 + s:2 + s],
                                op0=ALU.mult, op1=ALU.add), r=[("ps", bk), "gmod", "modc"], w=[("hT", t, kc)])

            load(0)
            load(1)
            sq(0)
            for t in range(ntile):
                if t + 2 < ntile:
                    load(t + 2)
                if t + 1 < ntile:
                    sq(t + 1)
                scale_T(t)

        HT_KEYS = [("hT", t, kc) for t in range(17) for kc in range(8)]

        R1.reset(); R2.reset(); R3.reset(); R4.reset()
        hT = R1.alloc([8, TT], BF16)
        ogT = R2.alloc([8, TT], BF16)
        R4g = Region(A, NB - 12288, NB)
        gate_p = R4g.alloc([D], F32)
        gate_s = R4g.alloc([D], F32, parts=32)
        x1s = R4g.alloc([D], F32, parts=32)
        R4 = Region(A, 140288, NB - 12288)

        class _Stop(Exception):
            pass

        def maybe_stop(tag):
            if stop == tag:
                S.finish()
                print("STOP at", tag, "instructions:", S.ninst, "sems:", S.nsem)
                raise _Stop()

        try:
            _build_rest = None
        finally:
            pass
        ada_phase(0, gate_p, gate_s, R3)
        R4.reset()
        if stop == "ada":
            S.finish(); print("STOP ada", S.ninst); return nc
        norm_phase(0, hT, None, R4)
        S.barrier()
        if stop == "norm":
            S.finish(); print("STOP norm", S.ninst); return nc
        R3.reset(); R4.reset()

        NCH = 16
        wab = R1.alloc([8, 16], BF16)
        S.dma("pool", wab, I["awin"][:, 4096:4112].rearrange("(c p) n -> p c n", p=128), w=["wab"])
        def fab(e):
            last = None
            for t in range(NCH):
                for kc in range(8):
                    last = e.matmul(PS(0)[:, t * 16:(t + 1) * 16], hT[:, kc, t * 128:(t + 1) * 128], wab[:, kc, :],
                                    start=(kc == 0), stop=(kc == 7))
            for s in range(NS):
                for kc in range(8):
                    last = e.matmul(PS(1)[0:8, s * 16:(s + 1) * 16], hT[:, kc, L + s * 8:L + s * 8 + 8], wab[:, kc, :],
                                    start=(kc == 0), stop=(kc == 7))
            return last
        S.op("pe", fab, r=["wab"] + HT_KEYS, w=[("ps", 0), ("ps", 1)])

        NCOL = NCH * 8 + NS * 8
        def galloc():
            return R1.alloc([NCOL], F32)
        xa, ax, ex, lx, g_t, beta_t, lbeta_t, gc_t, gcl_t, eg_t, gtot_t, ekd_t = [galloc() for _ in range(12)]
        nA = R1.alloc([8], F32)
        A_bc = hpar[:, 0:8]
        dt_bc = hpar[:, 8:16]
        outg_c = hpar[:, 16:17]
        def pv(tl):
            return tl[:, 0:128].rearrange("p (c h) -> p c h", h=8)
        def sv(tl):
            return tl[0:8, 128:160].rearrange("p (c h) -> p c h", h=8)
        abp = PS(0)[:, 0:256].rearrange("p (c k) -> p c k", k=16)
        abs_ = PS(1)[0:8, 0:64].rearrange("p (c k) -> p c k", k=16)
        S.op("dve", lambda e: e.tensor_tensor(out=pv(xa), in0=abp[:, :, 0:8], in1=dt_bc.unsqueeze(1).broadcast_to([128, 16, 8]), op=ALU.add),
             r=[("ps", 0), "hpar"], w=["xa_p"])
        S.op("dve", lambda e: e.tensor_tensor(out=sv(xa), in0=abs_[:, :, 0:8], in1=dt_bc[0:8].unsqueeze(1).broadcast_to([8, 4, 8]), op=ALU.add),
             r=[("ps", 1), "hpar"], w=["xa_s"])
        S.op("act", lambda e: e.activation(out=pv(ex), in_=abp[:, :, 8:16], func=AF.Exp, scale=-1.0), r=[("ps", 0)], w=["ex_p"])
        S.op("act", lambda e: e.activation(out=sv(ex), in_=abs_[:, :, 8:16], func=AF.Exp, scale=-1.0), r=[("ps", 1)], w=["ex_s"])
        GP = (128, slice(0, 128))
        GS = (8, slice(128, 160))
        for (p, cs_), tg in ((GP, "p"), (GS, "s")):
            def T(tl, p=p, cs_=cs_):
                return tl[0:p, cs_]
            S.op("act", lambda e, T=T: e.activation(out=T(lbeta_t), in_=T(ex), func=AF.Ln, bias=one_c[0:T(ex).shape[0]], scale=1.0),
                 r=["ex_" + tg, "cf"], w=["lbeta_" + tg])
            S.op("dve", lambda e, T=T: e.tensor_scalar(out=T(lbeta_t), in0=T(lbeta_t), scalar1=-1.0, scalar2=None, op0=ALU.mult),
                 r=["lbeta_" + tg], w=["lbeta_" + tg])
            S.op("act", lambda e, T=T: e.activation(out=T(beta_t), in_=T(lbeta_t), func=AF.Exp), r=["lbeta_" + tg], w=["beta_" + tg])
            S.op("dve", lambda e, T=T: e.tensor_scalar(out=T(ax), in0=T(xa), scalar1=-1.0, scalar2=None, op0=ALU.mult),
                 r=["xa_" + tg], w=["ax_" + tg])
            S.op("dve", lambda e, T=T: e.tensor_tensor(out=T(ax), in0=T(ax), in1=T(xa), op=ALU.max),
                 r=["xa_" + tg, "ax_" + tg], w=["ax_" + tg])
            S.op("act", lambda e, T=T: e.activation(out=T(ax), in_=T(ax), func=AF.Exp, scale=-1.0), r=["ax_" + tg], w=["ax_" + tg])
            S.op("act", lambda e, T=T: e.activation(out=T(lx), in_=T(ax), func=AF.Ln, bias=one_c[0:T(ax).shape[0]], scale=1.0),
                 r=["ax_" + tg, "cf"], w=["lx_" + tg])
            S.op("dve", lambda e, T=T: e.scalar_tensor_tensor(out=T(lx), in0=T(xa), scalar=0.0, in1=T(lx), op0=ALU.max, op1=ALU.add),
                 r=["xa_" + tg, "lx_" + tg], w=["lx_" + tg])
        S.op("act", lambda e: e.activation(out=nA, in_=A_bc, func=AF.Exp), r=["hpar"], w=["nA"])
        S.op("dve", lambda e: e.tensor_scalar(out=nA, in0=nA, scalar1=-1.0, scalar2=None, op0=ALU.mult), r=["nA"], w=["nA"])
        S.op("dve", lambda e: e.tensor_tensor(out=pv(g_t), in0=pv(lx), in1=nA.unsqueeze(1).broadcast_to([128, 16, 8]), op=ALU.mult),
             r=["lx_p", "nA"], w=["g_p"])
        S.op("dve", lambda e: e.tensor_tensor(out=sv(g_t), in0=sv(lx), in1=nA[0:8].unsqueeze(1).broadcast_to([8, 4, 8]), op=ALU.mult),
             r=["lx_s", "nA"], w=["g_s"])
        S.op("pe", lambda e: e.matmul(PS(2)[:, 0:128], U_f, g_t[:, 0:128], start=True, stop=True), r=["cf", "g_p"], w=[("ps", 2)])
        S.op("pe", lambda e: e.matmul(PS(3)[0:8, 0:32], U_f[0:8, 0:8], g_t[0:8, 128:160], start=True, stop=True), r=["cf", "g_s"], w=[("ps", 3)])
        S.op("act", lambda e: e.activation(out=gc_t[:, 0:128], in_=PS(2)[:, 0:128], func=AF.Copy), r=[("ps", 2)], w=["gc_p"])
        S.op("act", lambda e: e.activation(out=gc_t[0:8, 128:160], in_=PS(3)[0:8, 0:32], func=AF.Copy), r=[("ps", 3)], w=["gc_s"])
        S.op("pe", lambda e: e.matmul(PS(2)[:, 128:256], ident_f[:, 127:128].broadcast_to([128, 128]), gc_t[:, 0:128], start=True, stop=True),
             r=["cf", "gc_p"], w=[("ps", 2)])
        S.op("pe", lambda e: e.matmul(PS(3)[:, 128:160], ident_f[0:8, 7:8].broadcast_to([8, 128]), gc_t[0:8, 128:160], start=True, stop=True),
             r=["cf", "gc_s"], w=[("ps", 3)])
        S.op("act", lambda e: e.activation(out=gcl_t[:, 0:128], in_=PS(2)[:, 128:256], func=AF.Copy), r=[("ps", 2)], w=["gcl_p"])
        S.op("act", lambda e: e.activation(out=gcl_t[:, 128:160], in_=PS(3)[:, 128:160], func=AF.Copy), r=[("ps", 3)], w=["gcl_s"])
        for (p, cs_), tg in ((GP, "p"), (GS, "s")):
            def T(tl, p=p, cs_=cs_):
                return tl[0:p, cs_]
            S.op("act", lambda e, T=T: e.activation(out=T(eg_t), in_=T(gc_t), func=AF.Exp), r=["gc_" + tg], w=["eg_" + tg])
            S.op("act", lambda e, cs_=cs_: e.activation(out=gtot_t[:, cs_], in_=gcl_t[:, cs_], func=AF.Exp), r=["gcl_" + tg], w=["gtot_" + tg])
            S.op("dve", lambda e, T=T: e.tensor_tensor(out=T(ekd_t), in0=T(gcl_t), in1=T(gc_t), op=ALU.subtract),
                 r=["gcl_" + tg, "gc_" + tg], w=["ekd_" + tg])
        G_KEYS = [k + t for k in ("beta_", "lbeta_", "gc_", "gcl_", "eg_", "gtot_", "ekd_") for t in ("p", "s")]

        if stop == "G":
            S.finish(); print("STOP G", S.ninst); return nc
        NT = 16
        UW = 3 + L
        UT = UW + NS * (3 + LS)
        NUB = 3
        ubuf = [R4.alloc([UT], BF16) for _ in range(NUB)]
        Wh = [R1.alloc([8, 512], BF16) for _ in range(2)]
        diag = [R1.alloc([12, 128], BF16) for _ in range(2)]
        cwt = R1.alloc([24, 4], F32)
        S.dma("sp", cwt, I["cw"], w=["cwt"])
        sconv = R1.alloc([24, NS * 3], F32)
        S.dma("sp", sconv, I["sconv"].rearrange("p a s i -> p a (s i)"), w=["sconv"])
        cvo = [[R3.alloc([TT], BF16) for _ in range(4)] for _ in range(2)]
        sqb1 = R4.alloc([TT], BF16)
        sqb = [sqb1, sqb1]
        NHC = NT + NS
        def halloc(n=NHC):
            return [R4.alloc([n], F32) for _ in range(2)]
        ss_k, ss_q, lrnk, lrq, rows1, rows2, biasj, kbg_s, kdec_s, qdec_s = [halloc() for _ in range(10)]
        rowsT1 = R4.alloc([256], F32, parts=16)
        rowsT = [rowsT1, rowsT1]
        rowsTs = [R4.alloc([16], F32, parts=4) for _ in range(2)]
        rowsHL = [R4.alloc([2, 256], BF16, parts=16) for _ in range(2)]
        cst1 = R4.alloc([384], F32, parts=3)
        css1 = R4.alloc([384], F32, parts=32)
        cst = [cst1, cst1]
        css = [css1, css1]
        thb = [R4.alloc([512], BF16) for _ in range(2)]
        thc = [0]
        outg_h = R4.alloc([1], F32)
        S.op("dve", lambda e: e.tensor_scalar(out=outg_h, in0=outg_c, scalar1=0.5, scalar2=None, op0=ALU.mult), r=["hpar"], w=["outg_h"])
        for ub in range(NUB):
            S.op("pool", lambda e, ub=ub: e.memset(ubuf[ub][:, 0:3], 0.0), w=[("u", ub, "h")])

        ucount = [0]

        def P_head(h):
            hs_ = h % 2
            W = Wh[hs_]
            wkey = [("Wh", hs_, j) for j in range(4)]
            for j in range(4):
                col = (j * 1024 + h * 128)
                S.dma("pool", W[:, :, j * 128:(j + 1) * 128],
                      I["awin"][:, col:col + 128].rearrange("(c p) n -> p c n", p=128), w=[wkey[j]])
            dg = diag[hs_]
            for j in range(3):
                for i in range(4):
                    S.op("pool", lambda e, j=j, i=i: e.tensor_scalar(out=dg[:, j * 4 + i, :], in0=ident_f, scalar1=cwt[:, j * 8 + h, i:i + 1],
                                                                      scalar2=0.5, op0=ALU.mult, op1=ALU.mult),
                         r=["cf", "cwt"], w=[("diag", hs_, j)])
            yield 0.02
            blocks = [(q * 512, 512) for q in range(4)] + [(L, NS * LS)]
            step = 0
            nstep = 4 * 5 * 2.0
            for j in range(4):
                if j < 3:
                    ui = ucount[0] % NUB
                    ucount[0] += 1
                    ub = ubuf[ui]
                    ukey = None
                    ukeys = [("u", ui, bi_) for bi_ in range(5)]
                    S.op("act", lambda e, ub=ub, j=j: e.activation(
                        out=ub[:, UW:UT].rearrange("p (s i) -> p s i", i=3 + LS)[:, :, 0:3],
                        in_=sconv[:, j * 8 + h, :].rearrange("p (s i) -> p s i", i=3), func=AF.Copy),
                         r=["sconv"], w=[("u", ui, "sh")])
                for bi, (t0, n) in enumerate(blocks):
                    bk = 0
                    step += 1
                    def f(e, t0=t0, n=n, bk=bk, j=j):
                        last = None
                        for kc in range(8):
                            last = e.matmul(PS(bk)[:, 0:n], W[:, kc, j * 128:(j + 1) * 128], hT[:, kc, t0:t0 + n],
                                            start=(kc == 0), stop=(kc == 7))
                        return last
                    S.op("pe", f, r=[wkey[j]] + HT_KEYS, w=[("ps", bk)])
                    if j == 3:
                        tb = thb[thc[0] % 2]
                        tk = ("thb", thc[0] % 2)
                        thc[0] += 1
                        S.op("act", lambda e, n=n, bk=bk, tb=tb: e.activation(out=tb[:, 0:n], in_=PS(bk)[:, 0:n], func=AF.Tanh, scale=0.5),
                             r=[("ps", bk)], w=[tk])
                        S.op("dve", lambda e, t0=t0, n=n, bk=bk, tb=tb: e.scalar_tensor_tensor(out=cvo[hs_][3][:, t0:t0 + n], in0=tb[:, 0:n], scalar=1.0,
                                                                                              in1=PS(bk)[:, 0:n], op0=ALU.add, op1=ALU.mult),
                             r=[("ps", bk), tk], w=[("cvo", hs_, 3, bi)])
                    else:
                        if bi < 4:
                            dst = ub[:, 3 + t0:3 + t0 + n]
                            src = PS(bk)[:, 0:n]
                        else:
                            dst = ub[:, UW:UT].rearrange("p (s i) -> p s i", i=3 + LS)[:, :, 3:3 + LS]
                            src = PS(bk)[:, 0:n].rearrange("p (s i) -> p s i", i=LS)
                        S.op("act", lambda e, dst=dst, src=src: e.activation(out=dst, in_=src, func=AF.Copy), r=[("ps", bk)], w=[ukeys[bi]])
                        ck = 1
                        def fc(e, t0=t0, n=n, ck=ck, bi=bi, j=j, ub=ub):
                            last = None
                            for i in range(4):
                                if bi < 4:
                                    rhs = ub[:, t0 + i:t0 + i + n]
                                    out = PS(ck)[:, 0:n]
                                else:
                                    rhs = ub[:, UW:UT].rearrange("p (s i) -> p s i", i=3 + LS)[:, :, i:i + LS]
                                    out = PS(ck)[:, 0:n].rearrange("p (s i) -> p s i", i=LS)
                                last = e.matmul(out, dg[:, j * 4 + i, :], rhs, start=(i == 0), stop=(i == 3))
                            return last
                        rk = [ukeys[bi], ("diag", hs_, j)] + ([ukeys[bi - 1]] if 0 < bi < 4 else []) + ([("u", ui, "h")] if bi == 0 else []) + ([("u", ui, "sh")] if bi == 4 else [])
                        S.op("pe", fc, r=rk, w=[("ps", ck)])
                        tb = thb[thc[0] % 2]
                        tk = ("thb", thc[0] % 2)
                        thc[0] += 1
                        S.op("act", lambda e, n=n, ck=ck, tb=tb: e.activation(out=tb[:, 0:n], in_=PS(ck)[:, 0:n], func=AF.Tanh),
                             r=[("ps", ck)], w=[tk])
                        S.op("dve", lambda e, t0=t0, n=n, ck=ck, j=j, tb=tb: e.scalar_tensor_tensor(out=cvo[hs_][j][:, t0:t0 + n], in0=tb[:, 0:n], scalar=1.0,
                                                                                                   in1=PS(ck)[:, 0:n], op0=ALU.add, op1=ALU.mult),
                             r=[("ps", ck), tk], w=[("cvo", hs_, j, bi)])
                    yield 0.02 + 0.8 * step / 20.0
            def fcs(e):
                last = None
                for kc in range(8):
                    last = e.matmul(PS(0)[0:3, 0:384], hT[:, kc, L - 3:L], W[:, kc, 0:384], start=(kc == 0), stop=(kc == 7))
                for kc in range(8):
                    last = e.matmul(PS(1)[0:32, 0:384], hT[:, kc, L:TT], W[:, kc, 0:384], start=(kc == 0), stop=(kc == 7))
                return last
            S.op("pe", fcs, r=wkey + HT_KEYS, w=[("ps", 0), ("ps", 1)])
            S.op("act", lambda e: e.activation(out=cst[hs_], in_=PS(0)[0:3, 0:384], func=AF.Copy), r=[("ps", 0)], w=[("cst", 0)])
            S.op("act", lambda e: e.activation(out=css[hs_], in_=PS(1)[0:32, 0:384], func=AF.Copy), r=[("ps", 1)], w=[("css", 0)])
            S.dma("sp", O["cp"].rearrange("p (j c) -> p j c", j=3)[:, :, h * 128:(h + 1) * 128],
                  cst[hs_].rearrange("p (j c) -> p j c", j=3), r=[("cst", 0)])
            for s_ in range(NS):
                S.dma("sp", O["cs"].rearrange("p (j c) -> p j c", j=3)[3 * s_:3 * s_ + 3, :, h * 128:(h + 1) * 128],
                      css[hs_][8 * s_ + 5:8 * s_ + 8].rearrange("p (j c) -> p j c", j=3), r=[("css", 0)])
            for j, sst in ((1, ss_k[hs_]), (0, ss_q[hs_])):
                sb = sqb[j]
                S.op("act", lambda e, j=j, sb=sb: e.activation(out=sb, in_=cvo[hs_][j], func=AF.Square),
                     r=[("cvo", hs_, j, bi) for bi in range(5)], w=[("sqb", 0)])
                def fs(e, sb=sb, j=j):
                    last = None
                    for c in range(NT):
                        last = e.matmul(PS(j)[:, c:c + 1], sb[:, c * 128:(c + 1) * 128], ones_b[:, 0:1], start=True, stop=True)
                    for s in range(NS):
                        last = e.matmul(PS(j)[0:8, NT + s:NT + s + 1], sb[:, L + s * 8:L + s * 8 + 8], ones_b[:, 0:1], start=True, stop=True)
                    return last
                S.op("pe", fs, r=[("sqb", 0), "cb"], w=[("ps", j)])
                S.op("act", lambda e, j=j, sst=sst: e.activation(out=sst[:, 0:NT], in_=PS(j)[:, 0:NT], func=AF.Copy), r=[("ps", j)], w=[("ss", hs_, j)])
                S.op("act", lambda e, j=j, sst=sst: e.activation(out=sst[0:8, NT:NHC], in_=PS(j)[0:8, NT:NHC], func=AF.Copy), r=[("ps", j)], w=[("ss", hs_, j, "s")])
            yield 0.9
            def gcol(tl, which):
                if which == "p":
                    return tl[:, 0:128].rearrange("p (c hh) -> p c hh", hh=8)[:, :, h]
                return tl[0:8, 128:160].rearrange("p (c hh) -> p c hh", hh=8)[:, :, h]
            for which, p, cs_ in (("p", 128, slice(0, NT)), ("s", 8, slice(NT, NHC))):
                def T(tl, p=p, cs_=cs_):
                    return tl[hs_][0:p, cs_]
                hk = ("hs", hs_, which)
                S.op("act", lambda e, T=T, p=p: e.activation(out=T(lrnk), in_=T(ss_k), func=AF.Ln, bias=eps_c[0:p], scale=1.0),
                     r=[("ss", hs_, 1), ("ss", hs_, 1, "s"), "cf"], w=[hk + ("lrnk",)])
                S.op("act", lambda e, T=T, p=p: e.activation(out=T(lrq), in_=T(ss_q), func=AF.Ln, bias=eps_c[0:p], scale=1.0),
                     r=[("ss", hs_, 0), ("ss", hs_, 0, "s"), "cf"], w=[hk + ("lrq",)])
                S.op("dve", lambda e, T=T: e.tensor_scalar(out=T(lrnk), in0=T(lrnk), scalar1=-0.5, scalar2=None, op0=ALU.mult),
                     r=[hk + ("lrnk",)], w=[hk + ("lrnk",)])
                S.op("dve", lambda e, T=T, p=p: e.tensor_scalar(out=T(lrq), in0=T(lrq), scalar1=-0.5, scalar2=lnsc_c[0:p], op0=ALU.mult, op1=ALU.add),
                     r=[hk + ("lrq",), "cf"], w=[hk + ("lrq",)])
                S.op("dve", lambda e, T=T, which=which: e.tensor_tensor(out=T(rows1), in0=T(lrnk), in1=gcol(lbeta_t, which), op=ALU.add),
                     r=[hk + ("lrnk",), "lbeta_" + which], w=[hk + ("rows1",)])
                S.op("dve", lambda e, T=T, which=which: e.tensor_tensor(out=T(rows1), in0=T(rows1), in1=gcol(gc_t, which), op=ALU.add),
                     r=[hk + ("rows1",), "gc_" + which], w=[hk + ("rows1",)])
                S.op("dve", lambda e, T=T, which=which: e.tensor_tensor(out=T(rows2), in0=T(lrq), in1=gcol(gc_t, which), op=ALU.add),
                     r=[hk + ("lrq",), "gc_" + which], w=[hk + ("rows2",)])
                S.op("dve", lambda e, T=T, which=which: e.tensor_tensor(out=T(biasj), in0=T(lrnk), in1=gcol(gc_t, which), op=ALU.subtract),
                     r=[hk + ("lrnk",), "gc_" + which], w=[hk + ("biasj",)])
                S.op("act", lambda e, T=T: e.activation(out=T(kbg_s), in_=T(rows1), func=AF.Exp), r=[hk + ("rows1",)], w=[hk + ("kbg_s",)])
                S.op("act", lambda e, T=T: e.activation(out=T(qdec_s), in_=T(rows2), func=AF.Exp), r=[hk + ("rows2",)], w=[hk + ("qdec_s",)])
                S.op("dve", lambda e, T=T, which=which: e.tensor_tensor(out=T(kdec_s), in0=T(lrnk), in1=gcol(ekd_t, which), op=ALU.add),
                     r=[hk + ("lrnk",), "ekd_" + which], w=[hk + ("kdec_s",)])
                S.op("act", lambda e, T=T: e.activation(out=T(kdec_s), in_=T(kdec_s), func=AF.Exp), r=[hk + ("kdec_s",)], w=[hk + ("kdec_s",)])
            def ft(e):
                e.transpose(PS(0)[0:16, 0:128], rows2[hs_][:, 0:NT], ident_f)
                e.transpose(PS(0)[0:16, 128:256], rows1[hs_][:, 0:NT], ident_f)
                e.transpose(PS(1)[0:4, 0:8], rows2[hs_][0:8, NT:NHC], ident_f[0:8, 0:8])
                return e.transpose(PS(1)[0:4, 8:16], rows1[hs_][0:8, NT:NHC], ident_f[0:8, 0:8])
            S.op("pe", ft, r=[("hs", hs_, w_, n_) for w_ in ("p", "s") for n_ in ("rows1", "rows2")] + ["cf"], w=[("ps", 0), ("ps", 1)])
            S.op("act", lambda e: e.activation(out=rowsT[hs_], in_=PS(0)[0:16, 0:256], func=AF.Copy), r=[("ps", 0)], w=[("rowsT", 0)])
            S.op("act", lambda e: e.activation(out=rowsTs[hs_], in_=PS(1)[0:4, 0:16], func=AF.Copy), r=[("ps", 1)], w=[("rowsTs", hs_)])
            S.op("dve", lambda e: e.tensor_copy(out=rowsHL[hs_][:, 0, :], in_=rowsT[hs_]), r=[("rowsT", 0)], w=[("rowsHL", hs_, 0)])
            S.op("dve", lambda e: e.tensor_tensor(out=rowsHL[hs_][:, 1, :], in0=rowsT[hs_], in1=rowsHL[hs_][:, 0, :], op=ALU.subtract),
                 r=[("rowsT", 0), ("rowsHL", hs_, 0)], w=[("rowsHL", hs_, 1)])
            yield 1.0

        NSET = 3
        import os as _os
        NLANE = int(_os.environ.get("K_NLANE", "3"))
        NHO = 5
        STAG = int(_os.environ.get("K_STAG", "5"))
        LANE_BANK = (6, 2, 3)

        def lane_ws():
            d = {}
            d["t"] = R4.alloc([256], F32)
            d["D"] = d["t"]
            d["W"] = [R4.alloc([512], BF16) for _ in range(2)]
            d["BN"] = [w_[:, 0:256] for w_ in d["W"]]
            d["Q"] = [w_[:, 256:384] for w_ in d["W"]]
            d["M"] = [R4.alloc([256], BF16) for _ in range(3)]
            d["XY"] = R4.alloc([256], BF16)
            d["QTs"] = R4.alloc([128], BF16)
            return d

        def handoff():
            d = {}
            d["ktok"] = R4.alloc([128], BF16)
            d["vb"] = R4.alloc([128], BF16)
            d["NTQK"] = R4.alloc([384], BF16)
            d["kcTn"] = R4.alloc([128], BF16)
            d["QT"] = R4.alloc([128], BF16)
            return d
        HO_NAMES = ("ktok", "vb", "NTQK", "kcTn", "QT")
        lanes = [lane_ws() for _ in range(NLANE)]
        hos = [handoff() for _ in range(NHO)]
        u_t = [R4.alloc([128], BF16) for _ in range(2)]
        us_t = [R4.alloc([128], BF16) for _ in range(2)]
        t2_t = [R4.alloc([128], F32) for _ in range(2)]
        o_t = [R4.alloc([128], F32) for _ in range(2)]
        on_t = [R4.alloc([128], BF16) for _ in range(2)]
        junk2 = R4.alloc([128], BF16)
        oss = R4.alloc([2], F32)
        Sf = [R4.alloc([128], F32) for _ in range(2)]
        Sb = [R4.alloc([128], BF16) for _ in range(2)]
        sidx = [0]
        cidx = [0]
        cp_done = set()
        cr_done = [0]
        stg = [0]

        def chunk_specs(h):
            sp = [(128, c * 128, c, False, 0) for c in range(NT)]
            sp += [(8, L + s * 8, NT + s, True, s) for s in range(NS)]
            return sp

        def CP_chunk(h, spec, cs, lane, slot):
            RB = LANE_BANK[lane]
            def KK(name, *rest):
                return ((("ho", slot) if name in HO_NAMES else ("lw", lane)), name) + rest
            C, t0, col, smp, s = spec
            hs_ = h % 2
            qT, kT, vT = cvo[hs_][0], cvo[hs_][1], cvo[hs_][2]
            cvk = [("cvo", hs_, j, bi) for j in range(3) for bi in range(5)]
            hk = lambda n: ("hs", hs_, "s" if smp else "p", n)
            ksl = kT[:, t0:t0 + C]
            def f1(e):
                e.transpose(PSB(4)[0:C, 0:128], ksl, ident_b)
                return e.transpose(PSB(4)[0:C, 128:256], vT[:, t0:t0 + C], ident_b)
            S.op("pe", f1, r=cvk + ["cb"], w=[("ps", 4)])
            S.op("act", lambda e: e.activation(out=cs["ktok"][0:C], in_=PSB(4)[0:C, 0:128], func=AF.Copy), r=[("ps", 4)], w=[KK("ktok")])
            bcol = (beta_t[0:C, 128 + s * 8 + h:128 + s * 8 + h + 1] if smp else beta_t[:, col * 8 + h:col * 8 + h + 1])
            S.op("dve", lambda e: e.tensor_scalar(out=cs["vb"][0:C], in0=PSB(4)[0:C, 128:256], scalar1=bcol, scalar2=None, op0=ALU.mult),
                 r=[("ps", 4), "beta_s" if smp else "beta_p"], w=[KK("vb")])
            def f2(e):
                e.matmul(PS(RB)[0:C, 0:C], ksl, qT[:, t0:t0 + C], start=True, stop=True)
                e.matmul(PS(RB)[0:C, 128:128 + C], ksl, ksl, start=True, stop=True)
                if smp:
                    e.matmul(PS(RB)[0:C, 256:256 + C], ident_f[0:4, s:s + 1].broadcast_to([4, C]), rowsTs[hs_][:, 0:8], start=True, stop=True)
                    return e.matmul(PS(RB)[0:C, 384:384 + C], ident_f[0:4, s:s + 1].broadcast_to([4, C]), rowsTs[hs_][:, 8:16], start=True, stop=True)
                sel = ident_b[0:16, col:col + 1].broadcast_to([16, 128])
                e.matmul(PS(RB)[:, 256:512], sel, rowsHL[hs_][:, 0, :], start=True, stop=False)
                return e.matmul(PS(RB)[:, 256:512], sel, rowsHL[hs_][:, 1, :], start=False, stop=True)
            S.op("pe", f2, r=cvk + ["cf", "cb", ("rowsTs", hs_), ("rowsHL", hs_, 0), ("rowsHL", hs_, 1)], w=[("ps", RB)])
            t3 = cs["t"][0:C, :].rearrange("p (a b) -> p a b", a=2)[:, :, 0:C]
            D3 = cs["D"][0:C, :].rearrange("p (a b) -> p a b", a=2)[:, :, 0:C]
            N3 = cs["NTQK"][0:C, 0:256].rearrange("p (a b) -> p a b", a=2)[:, :, 0:C]
            m3 = mask2[0:C, :].rearrange("p (a b) -> p a b", a=2)[:, :, 0:C]
            E3 = PS(RB)[0:C, 256:512].rearrange("p (a b) -> p a b", a=2)[:, :, 0:C]
            raw3 = PS(RB)[0:C, 0:256].rearrange("p (a b) -> p a b", a=2)[:, :, 0:C]
            S.op("dve", lambda e: e.tensor_tensor(out=t3, in0=E3, in1=m3, op=ALU.add), r=[("ps", RB), "cf"], w=[KK("t")])
            S.op("act", lambda e: e.activation(out=D3, in_=t3, func=AF.Exp, bias=biasj[hs_][0:C, col:col + 1], scale=1.0),
                 r=[KK("t"), hk("biasj")], w=[KK("t")])
            S.op("dve", lambda e: e.tensor_tensor(out=N3, in0=raw3, in1=D3, op=ALU.mult), r=[("ps", RB), KK("t")], w=[KK("NTQK")])
            yield
            Bm = cs["NTQK"][0:C, 128:128 + C]
            S.op("pe", lambda e: e.transpose(PSB(4)[0:C, 256:256 + C], Bm, ident_b[0:C, 0:C]), r=[KK("NTQK"), "cb"], w=[("ps", 4)])
            if C == 128:
                W = cs["W"]
                wk = lambda i: KK("W", i)
                S.op("act", lambda e: e.activation(out=cs["NTQK"][:, 256:384], in_=PSB(4)[:, 256:384], func=AF.Copy), r=[("ps", 4)], w=[KK("NTQK")])
                BNf = cs["NTQK"][:, 128:384]
                v2 = lambda ap: ap.rearrange("p (a b) -> p a b", a=2)
                bd2 = bd_b.unsqueeze(1).broadcast_to([128, 2, 128])
                id2 = ident_b.unsqueeze(1).broadcast_to([128, 2, 128])
                W0bn = W[0].rearrange("p (a b) -> p a b", a=4)[:, 1::2, :]
                W1qt = W[1].rearrange("p (a b) -> p a b", a=4)[:, 0::2, :]
                S.op("dve", lambda e: e.tensor_tensor(out=W0bn, in0=v2(BNf), in1=bd2, op=ALU.mult), r=[KK("NTQK"), "cb"], w=[wk(0)])
                S.op("dve", lambda e: e.tensor_tensor(out=W1qt, in0=id2, in1=W0bn, op=ALU.subtract), r=[wk(0), "cb"], w=[wk(1)])
                for lv in range(3):
                    S.op("dve", lambda e, lv=lv: e.tensor_tensor(out=cs["M"][lv], in0=BNf, in1=ml_b[lv], op=ALU.mult),
                         r=[KK("NTQK"), "cb"], w=[KK("M", lv)])
                yield
                cur = 0
                for k in range(4):
                    nxt = 1 - cur
                    last = (k == 3)
                    Wc = W[cur]
                    Bk = Wc[:, 128:256]
                    Nk = Wc[:, 384:512]
                    src = Wc if k > 0 else None
                    def fr(e, k=k, Wc=Wc, Bk=Bk, Nk=Nk, last=last):
                        if k == 0:
                            e.matmul(PS(RB)[:, 128:256], Nk, Bk, start=True, stop=True)
                            return e.matmul(PS(RB)[:, 384:512], Bk, Nk, start=True, stop=True)
                        if last:
                            e.matmul(PS(RB)[:, 0:128], Nk, Wc[:, 0:128], start=True, stop=True)
                            return e.matmul(PS(RB)[:, 256:384], Bk, Wc[:, 256:384], start=True, stop=True)
                        e.matmul(PS(RB)[:, 0:256], Nk, Wc[:, 0:256], start=True, stop=True)
                        return e.matmul(PS(RB)[:, 256:512], Bk, Wc[:, 256:512], start=True, stop=True)
                    S.op("pe", fr, r=[wk(cur)], w=[("ps", RB)])
                    P4 = PS(RB).rearrange("p (a b) -> p a b", a=4)
                    Wn4 = W[nxt].rearrange("p (a b) -> p a b", a=4)
                    Wc4 = Wc.rearrange("p (a b) -> p a b", a=4)
                    if k == 0:
                        S.op("act", lambda e, P4=P4, Wn4=Wn4: e.activation(out=Wn4[:, 1::2, :], in_=P4[:, 1::2, :], func=AF.Copy),
                             r=[("ps", RB)], w=[wk(nxt)])
                    else:
                        if not last:
                            S.op("act", lambda e, P4=P4, Wn4=Wn4: e.activation(out=Wn4[:, 1::2, :], in_=P4[:, 1::2, :], func=AF.Copy),
                                 r=[("ps", RB)], w=[wk(nxt)])
                        S.op("dve", lambda e, P4=P4, Wn4=Wn4, Wc4=Wc4: e.tensor_tensor(out=Wn4[:, 0::2, :], in0=P4[:, 0::2, :], in1=Wc4[:, 0::2, :], op=ALU.add),
                             r=[("ps", RB), wk(cur)], w=[wk(nxt)])
                    cur = nxt
                    yield
                Wc = W[cur]
                Qv = Wc[:, 0:128]
                Tv = Wc[:, 256:384]
                XY = cs["XY"]
                for lv in range(3):
                    lastl = (lv == 2)
                    Bm_l = cs["M"][lv][:, 0:128]
                    Nm_l = cs["M"][lv][:, 128:256]
                    def f1_(e, Bm_l=Bm_l, Nm_l=Nm_l, lastl=lastl):
                        r_ = e.matmul(PS(RB)[:, 0:128], Nm_l, Qv, start=True, stop=True)
                        if not lastl:
                            r_ = e.matmul(PS(RB)[:, 128:256], Bm_l, Tv, start=True, stop=True)
                        return r_
                    S.op("pe", f1_, r=[wk(cur), KK("M", lv)], w=[("ps", RB)])
                    nx = 128 if lastl else 256
                    S.op("act", lambda e, nx=nx: e.activation(out=XY[:, 0:nx], in_=PS(RB)[:, 0:nx], func=AF.Copy), r=[("ps", RB)], w=[KK("XY")])
                    def f2_(e, lastl=lastl):
                        r_ = e.matmul(PS(RB)[:, 256:384], Tv, XY[:, 0:128], start=True, stop=True)
                        if not lastl:
                            r_ = e.matmul(PS(RB)[:, 384:512], Qv, XY[:, 128:256], start=True, stop=True)
                        return r_
                    S.op("pe", f2_, r=[wk(cur), KK("XY")], w=[("ps", RB)])
                    if lastl:
                        S.op("dve", lambda e: e.tensor_tensor(out=cs["QT"], in0=Qv, in1=PS(RB)[:, 256:384], op=ALU.subtract),
                             r=[("ps", RB), wk(cur)], w=[KK("QT")])
                    else:
                        Wq = Wc.rearrange("p (a b) -> p a b", a=4)[:, 0::2, :]
                        S.op("dve", lambda e, Wq=Wq: e.tensor_tensor(out=Wq, in0=Wq, in1=PS(RB)[:, 256:512].rearrange("p (a b) -> p a b", a=2), op=ALU.subtract),
                             r=[("ps", RB), wk(cur)], w=[wk(cur)])
                    yield
                QT = cs["QT"]
                qkey = KK("QT")
            else:
                BN = cs["BN"]
                Q = cs["Q"]
                S.op("act", lambda e: e.activation(out=BN[0][0:C, 128:128 + C], in_=PSB(4)[0:C, 256:256 + C], func=AF.Copy), r=[("ps", 4)], w=[KK("W", 0)])
                S.op("pool", lambda e: e.tensor_copy(out=BN[0][0:C, 0:C], in_=Bm), r=[KK("NTQK")], w=[KK("W", 0)])
                S.op("pool", lambda e: e.tensor_tensor(out=Q[0][0:C, 0:C], in0=ident_b[0:C, 0:C], in1=Bm, op=ALU.subtract),
                     r=[KK("NTQK"), "cb"], w=[KK("W", 0)])
                nr = 7 if C == 128 else 3
                cur = 0
                for k in range(nr):
                    nxt = 1 - cur
                    Bk = BN[cur][0:C, 0:C]
                    Nk = BN[cur][0:C, 128:128 + C]
                    last = (k == nr - 1)
                    def fr(e, k=k, Bk=Bk, Nk=Nk, cur=cur, last=last):
                        r_ = None
                        if k > 0:
                            r_ = e.matmul(PS(RB)[0:C, 0:C], Nk, Q[cur][0:C, 0:C], start=True, stop=True)
                        if not last:
                            r_ = e.matmul(PS(RB)[0:C, 128:128 + C], Nk, Bk, start=True, stop=True)
                            r_ = e.matmul(PS(RB)[0:C, 256:256 + C], Bk, Nk, start=True, stop=True)
                        return r_
                    S.op("pe", fr, r=[KK("W", cur)], w=[("ps", RB)])
                    if not last:
                        S.op("act", lambda e, nxt=nxt: e.activation(
                            out=BN[nxt][0:C, :].rearrange("p (a b) -> p a b", a=2)[:, :, 0:C],
                            in_=PS(RB)[0:C, 128:384].rearrange("p (a b) -> p a b", a=2)[:, :, 0:C], func=AF.Copy),
                             r=[("ps", RB)], w=[KK("W", nxt)])
                    if k > 0:
                        S.op("dve", lambda e, cur=cur, nxt=nxt: e.tensor_tensor(out=Q[nxt][0:C, 0:C], in0=PS(RB)[0:C, 0:C], in1=Q[cur][0:C, 0:C], op=ALU.add),
                             r=[("ps", RB), KK("W", cur)], w=[KK("W", nxt)])
                    else:
                        S.op("pool", lambda e, cur=cur, nxt=nxt: e.tensor_copy(out=Q[nxt][0:C, 0:C], in_=Q[cur][0:C, 0:C]),
                             r=[KK("W", cur)], w=[KK("W", nxt)])
                    cur = nxt
                    yield
                S.op("pool", lambda e, cur=cur: e.tensor_copy(out=cs["QT"][0:C, 0:C], in_=Q[cur][0:C, 0:C]), r=[KK("W", cur)], w=[KK("QT")])
                QT = cs["QT"][0:C, 0:C]
                qkey = KK("QT")
            S.op("dve", lambda e: e.tensor_scalar(out=cs["QTs"][0:C, 0:C], in0=QT, scalar1=kbg_s[hs_][0:C, col:col + 1], scalar2=None, op0=ALU.mult),
                 r=[qkey, hk("kbg_s")], w=[KK("QTs")])
            S.op("pe", lambda e: e.matmul(PS(4)[:, 256:256 + C], cs["ktok"][0:C, :], cs["QTs"][0:C, 0:C], start=True, stop=True),
                 r=[KK("ktok"), KK("QTs")], w=[("ps", 4)])
            S.op("act", lambda e: e.activation(out=cs["kcTn"][:, 0:C], in_=PS(4)[:, 256:256 + C], func=AF.Copy, scale=-1.0),
                 r=[("ps", 4)], w=[KK("kcTn")])
            yield

        def CR_chunk(h, spec, cs, slot, Sfl, Sbf, skey, n):
            def KK(name, *rest):
                return (("ho", slot), name) + rest
            C, t0, col, smp, s = spec
            hs_ = h % 2
            qT = cvo[hs_][0]
            zT = cvo[hs_][3]
            cvk = [("cvo", hs_, j, bi) for j in (0, 3) for bi in range(5)]
            hk = lambda nm: ("hs", hs_, "s" if smp else "p", nm)
            QT = cs["QT"][0:C, 0:C]
            qk = KK("QT")
            x = n % 2
            def fu(e):
                e.matmul(PS(7)[0:C, 0:128], QT, cs["vb"][0:C, :], start=True, stop=False)
                return e.matmul(PS(7)[0:C, 0:128], cs["kcTn"][:, 0:C], Sbf, start=False, stop=True)
            S.op("pe", fu, r=[qk, KK("vb"), KK("kcTn"), skey + ("b",)], w=[("ps", 7)])
            S.op("act", lambda e: e.activation(out=u_t[x][0:C], in_=PS(7)[0:C, 0:128], func=AF.Copy), r=[("ps", 7)], w=[("u_t", x)])
            S.op("dve", lambda e: e.tensor_scalar(out=us_t[x][0:C], in0=PS(7)[0:C, 0:128], scalar1=kdec_s[hs_][0:C, col:col + 1], scalar2=None, op0=ALU.mult),
                 r=[("ps", 7), hk("kdec_s")], w=[("us_t", x)])
            yield
            def fo(e):
                e.matmul(PS(7)[0:C, 128:256], qT[:, t0:t0 + C], Sbf, start=True, stop=True)
                e.matmul(PS(7)[0:C, 256:384], cs["NTQK"][0:C, 0:C], u_t[x][0:C], start=True, stop=True)
                return e.matmul(PS(7)[:, 384:512], cs["ktok"][0:C, :], us_t[x][0:C], start=True, stop=True)
            S.op("pe", fo, r=cvk + [skey + ("b",), KK("NTQK"), ("u_t", x), KK("ktok"), ("us_t", x)], w=[("ps", 7)])
            gcolumn = (gtot_t[:, 128 + s * 8 + h:128 + s * 8 + h + 1] if smp else gtot_t[:, col * 8 + h:col * 8 + h + 1])
            S.op("dve", lambda e: e.scalar_tensor_tensor(out=Sfl, in0=Sfl, scalar=gcolumn, in1=PS(7)[:, 384:512], op0=ALU.mult, op1=ALU.add),
                 r=[("ps", 7), skey + ("f",), "gtot_s" if smp else "gtot_p"], w=[skey + ("f",)])
            S.op("act", lambda e: e.activation(out=Sbf, in_=Sfl, func=AF.Copy), r=[skey + ("f",)], w=[skey + ("b",)])
            S.op("act", lambda e: e.activation(out=t2_t[x][0:C], in_=PS(7)[0:C, 256:384], func=AF.Copy), r=[("ps", 7)], w=[("t2", x)])
            S.op("dve", lambda e: e.scalar_tensor_tensor(out=o_t[x][0:C], in0=PS(7)[0:C, 128:256], scalar=qdec_s[hs_][0:C, col:col + 1],
                                                         in1=t2_t[x][0:C], op0=ALU.mult, op1=ALU.add),
                 r=[("ps", 7), ("t2", x), hk("qdec_s")], w=[("o_t", x)])
            yield
            S.op("act", lambda e: e.activation(out=junk2[0:C], in_=o_t[x][0:C], func=AF.Square, accum_out=oss[0:C, x:x + 1]),
                 r=[("o_t", x)], w=["junk2", ("oss", x)])
            S.op("pool", lambda e: e.tensor_scalar(out=oss[0:C, x:x + 1], in0=oss[0:C, x:x + 1], scalar1=1.0 / 128, scalar2=EPS, op0=ALU.mult, op1=ALU.add),
                 r=[("oss", x)], w=[("oss", x)])
            S.op("pool", lambda e: e.tensor_tensor(out=oss[0:C, x:x + 1], in0=oss[0:C, x:x + 1], in1=nhalf_c[0:C], op=ALU.pow),
                 r=[("oss", x), "cf"], w=[("oss", x)])
            S.op("act", lambda e: e.activation(out=on_t[x][0:C], in_=o_t[x][0:C], func=AF.Identity, scale=oss[0:C, x:x + 1]),
                 r=[("o_t", x), ("oss", x)], w=[("on_t", x)])
            S.op("pe", lambda e: e.transpose(PSB(5)[:, 0:C], on_t[x][0:C, :], ident_b[0:C, 0:C]), r=[("on_t", x), "cb"], w=[("ps", 5)])
            S.op("dve", lambda e: e.scalar_tensor_tensor(out=ogT[:, h, t0:t0 + C], in0=PSB(5)[:, 0:C], scalar=outg_h,
                                                         in1=zT[:, t0:t0 + C], op0=ALU.mult, op1=ALU.mult),
                 r=[("ps", 5), "outg_h"] + cvk, w=[("ogT", h, t0)])
            yield

        def CP_lane(h, lane):
            specs = chunk_specs(h)
            mine = list(range(lane, len(specs), NLANE))
            for _ in range(lane * STAG):
                yield 0.0
            for k_, n in enumerate(mine):
                spec = specs[n]
                gi = cidx[0] + n
                while gi - cr_done[0] >= NHO:
                    yield WAIT
                slot = gi % NHO
                cs = dict(lanes[lane])
                cs.update(hos[slot])
                for yi, _ in enumerate(CP_chunk(h, spec, cs, lane, slot)):
                    yield (k_ + min(0.95, (yi + 1) / 14.0)) / len(mine)
                cp_done.add(gi)
                yield (k_ + 1.0) / len(mine)

        def CR_head(h):
            specs = chunk_specs(h)
            yield 0.0
            for n, spec in enumerate(specs):
                C, t0, col, smp, s = spec
                gi = cidx[0] + n
                while gi not in cp_done:
                    yield WAIT
                slot = gi % NHO
                cs = hos[slot]
                if n == 0 or smp:
                    si = sidx[0] % 2
                    sidx[0] += 1
                    Sfl, Sbf, skey = Sf[si], Sb[si], ("S", si)
                    if smp:
                        S.dma("sp", Sfl, I["sdelta"][s, h], w=[skey + ("f",)])
                        S.op("act", lambda e, Sbf=Sbf, Sfl=Sfl: e.activation(out=Sbf, in_=Sfl, func=AF.Copy), r=[skey + ("f",)], w=[skey + ("b",)])
                    else:
                        S.op("pool", lambda e, Sfl=Sfl: e.memset(Sfl, 0.0), w=[skey + ("f",)])
                        S.op("pool", lambda e, Sbf=Sbf: e.memset(Sbf, 0.0), w=[skey + ("b",)])
                for yi, _ in enumerate(CR_chunk(h, spec, cs, slot, Sfl, Sbf, skey, gi)):
                    yield (n + (yi + 1) / 4.0) / len(specs) - 0.08
                if n == NT - 1 or smp:
                    dst = O["ds"][s, h] if smp else O["dp"][h]
                    S.dma("sp", dst, Sfl, r=[skey + ("f",)])
                cr_done[0] = gi + 1
                yield (n + 1.0) / len(specs) - 0.08

        S.barrier()
        run_tasks([P_head(0)])
        if stop == "P0":
            S.finish(); print("STOP P0", S.ninst); return nc
        for h in range(nheads):
            gens = [CP_lane(h, ln) for ln in range(NLANE)] + [CR_head(h)]
            if h + 1 < nheads:
                gens.append(P_head(h + 1))
            run_tasks(gens)
            cidx[0] += NT + NS
            if stop == ("H", h):
                S.finish(); print("STOP H", h, S.ninst); return nc
        S.barrier()

        if dbg:
            S.dma("sp", O["dbg_og"], ogT, r=[("ogT", h, t * 128) for h in range(8) for t in range(16)] + [("ogT", h, L + s * 8) for h in range(8) for s in range(NS)])
            S.dma("sp", O["dbg_hT"], hT, r=HT_KEYS)
            S.barrier()
        R1.reset(); R3.reset(); R4.reset()
        x1 = R1.alloc([16, D], F32)
        wo = R3.alloc([8, D], BF16)
        xst2 = [R3.alloc([D], F32) for _ in range(2)]
        xss = R4.alloc([D], F32, parts=32)
        ysb0 = R4.alloc([D], F32, parts=32)

        def wout_phase(l, wsrc, x_in, x_out_fn):
            S.dma("pool", wo, wsrc.rearrange("(c p) n -> p c n", p=128), w=["wo"])
            wos = wo
            if l == 0:
                S.dma("sp", xss, I["xs"], w=["xss"])
            for hb in range(2):
                def f(e, hb=hb):
                    last = None
                    for ec in range(8):
                        last = e.matmul(PS(hb)[0:32, :], ogT[:, ec, L:TT], wo[:, ec, hb * 512:(hb + 1) * 512], start=(ec == 0), stop=(ec == 7))
                    return last
                S.op("pe", f, r=["wo"] + [("ogT", h, L + s * 8) for h in range(8) for s in range(NS)], w=[("ps", hb)])
                ysb = xss if l == 1 else ysb0
                S.op("dve", lambda e, hb=hb, ysb=ysb: e.tensor_tensor(out=ysb[:, hb * 512:(hb + 1) * 512], in0=PS(hb)[0:32, :], in1=gate_s[:, hb * 512:(hb + 1) * 512], op=ALU.mult),
                     r=[("ps", hb), ("gate_s", l)], w=[("ysb", hb)])
                src = xss if l == 0 else x1s
                S.op("dve", lambda e, hb=hb, src=src, ysb=ysb: e.tensor_tensor(out=x1s[:, hb * 512:(hb + 1) * 512], in0=ysb[:, hb * 512:(hb + 1) * 512],
                                                                       in1=src[:, hb * 512:(hb + 1) * 512], op=ALU.add),
                     r=[("ysb", hb), "xss", ("x1s", hb)], w=[("x1s", hb)])
            for ec in range(8):
                S.op("pool", lambda e, ec=ec: e.tensor_tensor(out=wo[:, ec, :], in0=wo[:, ec, :], in1=gate_p, op=ALU.mult),
                     r=["wo", ("gate_p", l)], w=["wo"])
            for t in range(16):
                xb, xkey = x_in(t)
                for hb in range(2):
                    bk = (2 * t + hb) % 4
                    def f(e, hb=hb, bk=bk, t=t):
                        last = None
                        for ec in range(8):
                            last = e.matmul(PS(bk), ogT[:, ec, t * 128:(t + 1) * 128], wo[:, ec, hb * 512:(hb + 1) * 512], start=(ec == 0), stop=(ec == 7))
                        return last
                    S.op("pe", f, r=["wo"] + [("ogT", h, t * 128) for h in range(8)], w=[("ps", bk)])
                    S.op("dve", lambda e, hb=hb, bk=bk, t=t, xb=xb: e.tensor_tensor(out=x_out_fn(t)[:, hb * 512:(hb + 1) * 512], in0=PS(bk),
                                                                                     in1=xb[:, hb * 512:(hb + 1) * 512], op=ALU.add),
                         r=[("ps", bk)] + (xkey if isinstance(xkey, list) else [xkey]), w=[("x1", t, hb)])

        def x_in0(t):
            buf = xst2[t % 2]
            key = ("xst2", t % 2)
            S.dma("sp", buf, I["xp"][t * 128:(t + 1) * 128, :], w=[key])
            return buf, key

        wout_phase(0, I["awout"], x_in0, lambda t: x1[:, t, :])
        if dbg:
            for t in range(16):
                S.dma("sp", O["dbg_x1p"][t * 128:(t + 1) * 128, :], x1[:, t, :], r=[("x1", t, 0), ("x1", t, 1)])
            S.dma("sp", O["dbg_x1s"], x1s, r=[("x1s", 0), ("x1s", 1)])
        if not do_l1:
            S.finish()
            print("instructions:", S.ninst, "sems:", S.nsem)
            return nc
        ada_phase(1, gate_p, gate_s, R4, parts=("col",))
        S.barrier()
        R3.reset(); R4.reset()
        hT1 = R3.alloc([8, TT], BF16)
        RE = Region(A, NB - 12288, NB - 4096)
        qs_all = RE.alloc([8, 3, 32], BF16)
        zs_all = RE.alloc([8, 32], BF16)
        onew = RE.alloc([8, 32], F32)
        lnew = RE.alloc([8, 32], F32)
        ptn = RE.alloc([32], BF16, parts=32)

        def x_src1(t):
            if t < 16:
                return x1[:, t, :], [("x1", t, 0), ("x1", t, 1)]
            return x1s, [("x1s", 0), ("x1s", 1)]

        norm_phase(1, hT1, x_src1, R4)
        S.barrier()
        R4.reset()
        GRP = ((128, 1), (512, 4), (2048, 16))
        SCALE = 128.0 ** -0.5
        Wg = [R4.alloc([8, 384], BF16) for _ in range(2)]
        Wz = RC.alloc([8, 128], BF16)
        QK_ = [R4.alloc([2, TT], BF16) for _ in range(3)]
        QT_ = [qk_[:, 0, :] for qk_ in QK_]
        KT_ = [qk_[:, 1, :] for qk_ in QK_]
        qkb = [R4.alloc([256], BF16) for _ in range(2)]
        Vt_ = [R4.alloc([17, 128], BF16) for _ in range(3)]
        zTh = R4.alloc([1024], BF16)
        PTb = [R4.alloc([512], BF16) for _ in range(2)]
        stg1_ = R4.alloc([256], F32)
        stg_ = [stg1_, stg1_]
        fin1_ = R4.alloc([512], F32)
        fin_ = [fin1_, fin1_]
        print("L1 R4 used", R4.cur - R4.lo, "of", R4.hi - R4.lo)
        KVO = [O["kv0p"], O["kv1p"], O["kv2p"]]
        wcount = [0]
        stc = [0]
        ptc = [0]
        fnc = [0]

        def tok_ap(g, ti):
            d = GRP[g][1]
            if d == 1:
                return lambda kc: hT1[:, kc, ti * 128:(ti + 1) * 128]
            if d == 4:
                r, nb = ti // 4, ti % 4
                return lambda kc: hT1[:, kc, 0:L].rearrange("p (j r) -> p r j", r=4)[:, r, nb * 128:(nb + 1) * 128]
            return lambda kc: hT1[:, kc, 0:L].rearrange("p (j r) -> p r j", r=16)[:, ti, :]

        def blk_ap(g, blk):
            d = GRP[g][1]
            if d == 1:
                return lambda kc: hT1[:, kc, blk * 512:(blk + 1) * 512]
            if d == 4:
                return lambda kc: hT1[:, kc, 0:L].rearrange("p (j r) -> p r j", r=4)[:, blk, :]
            return lambda kc: hT1[:, kc, 0:L].rearrange("p (j r) -> p r j", r=16)[:, 4 * blk:4 * blk + 4, :]

        def kept_rows(g, ti):
            w_, d = GRP[g]
            if d == 1:
                return (0, 1) if ti == 15 else None
            if d == 4:
                r, nb = ti // 4, ti % 4
                return (r, 4) if nb == 3 else None
            return (ti, 16)

        def L1_proj(h, g):
            d = GRP[g][1]
            wb = Wg[wcount[0] % 2]
            wkey = ("Wg", wcount[0] % 2)
            wcount[0] += 1
            for t_ in range(3):
                col = t_ * 3072 + g * 1024 + h * 128
                S.dma("pool", wb[:, :, t_ * 128:(t_ + 1) * 128], I["bwin"][:, col:col + 128].rearrange("(c p) n -> p c n", p=128),
                      w=[wkey + (t_,)])
            wk = [wkey + (t_,) for t_ in range(3)]
            for ti in range(17):
                bk = ti % 2
                p = 128 if ti < 16 else 32
                ap = tok_ap(g, ti) if ti < 16 else (lambda kc: hT1[:, kc, L:TT])
                def f(e, ap=ap, bk=bk, p=p):
                    last = None
                    for kc in range(8):
                        last = e.matmul(PS(bk)[0:p, 0:384], ap(kc), wb[:, kc, 0:384], start=(kc == 0), stop=(kc == 7))
                    return last
                S.op("pe", f, r=wk + HT_KEYS, w=[("ps", bk)])
                qb = qkb[ti % 2]
                qbk = ("qkb", ti % 2)
                S.op("act", lambda e, bk=bk, p=p, qb=qb: e.activation(out=qb[0:p], in_=PS(bk)[0:p, 0:256], func=AF.Copy), r=[("ps", bk)], w=[qbk])
                S.op("dve", lambda e, bk=bk, p=p, ti=ti: e.tensor_copy(out=Vt_[g][0:p, ti, :], in_=PS(bk)[0:p, 256:384]),
                     r=[("ps", bk)], w=[("vt", g, ti)])
                kr = kept_rows(g, ti) if ti < 16 else None
                if kr is not None or ti == 16:
                    sb = stg_[stc[0] % 2]
                    skey = ("stg", 0)
                    stc[0] += 1
                    S.op("dve", lambda e, sb=sb, bk=bk, p=p: e.tensor_copy(out=sb[0:p], in_=PS(bk)[0:p, 128:384]), r=[("ps", bk)], w=[skey])
                    if ti < 16:
                        r0, st = kr
                        dst = KVO[g].rearrange("(q s) k hh e -> s q k hh e", s=st)[r0, :, :, h, :]
                        S.dma("sp", dst, sb.rearrange("p (k e) -> p k e", k=2), r=[skey])
                    else:
                        dst = O["kv%ds" % g].rearrange("s l k hh e -> (s l) k hh e")[:, :, h, :]
                        S.dma("sp", dst, sb[0:32].rearrange("p (k e) -> p k e", k=2), r=[skey])
                tb_ = 2 + (ti % 2)
                def ftp(e, qb=qb, p=p, tb_=tb_):
                    e.transpose(PSB(tb_)[:, 0:p], qb[0:p, 0:128], ident_b[0:p, 0:p])
                    return e.transpose(PSB(tb_)[:, 128:128 + p], qb[0:p, 128:256], ident_b[0:p, 0:p])
                S.op("pe", ftp, r=[qbk, "cb"], w=[("ps", tb_)])
                t0 = ti * 128 if ti < 16 else L
                eng = "act" if ti % 2 == 1 else "dve"
                dst = QK_[g][:, :, t0:t0 + p]
                src = PSB(tb_)[:, 0:256].rearrange("p (a b) -> p a b", a=2)[:, :, 0:p]
                if eng == "act":
                    S.op("act", lambda e, dst=dst, src=src: e.activation(out=dst, in_=src, func=AF.Copy), r=[("ps", tb_)], w=[("qkT", g, ti)])
                else:
                    S.op("dve", lambda e, dst=dst, src=src: e.tensor_copy(out=dst, in_=src), r=[("ps", tb_)], w=[("qkT", g, ti)])
                yield

        def L1_units(g, hf):
            d = GRP[g][1]
            units = []
            if d == 1:
                for nb in range(8 * hf, 8 * hf + 8):
                    col0 = 128 * (nb - 8 * hf)
                    outs = [(col0 // 512, (lambda B, c=col0 % 512: B[:, c:c + 128]), 0, 128)]
                    q = QT_[g][:, nb * 128:(nb + 1) * 128]
                    for pv, kt in ((True, nb - 1), (False, nb)):
                        if kt < 0:
                            continue
                        units.append((KT_[g][:, kt * 128:(kt + 1) * 128], Vt_[g][:, kt, :], 128, mT_prev if pv else mT_cur, q, 128, outs, ("vt", g, kt)))
            elif d == 4:
                for r in range(4):
                    for nb in range(2 * hf, 2 * hf + 2):
                        outs = [(nb - 2 * hf, (lambda B, r=r: B.rearrange("p (q r) -> p r q", r=4)[:, r, :]), 0, 128)]
                        q = QT_[g][:, r * 512 + nb * 128:r * 512 + (nb + 1) * 128]
                        for pv, kb in ((True, nb - 1), (False, nb)):
                            if kb < 0:
                                continue
                            kt = r * 4 + kb
                            units.append((KT_[g][:, kt * 128:(kt + 1) * 128], Vt_[g][:, kt, :], 128, mT_prev if pv else mT_cur, q, 128, outs, ("vt", g, kt)))
            else:
                for r in range(16):
                    outs = [(m, (lambda B, r=r: B.rearrange("p (j r) -> p r j", r=16)[:, r, :]), 32 * m, 32) for m in range(2)]
                    q = QT_[g][:, r * 128 + 64 * hf:r * 128 + 64 * hf + 64]
                    if hf == 0:
                        units.append((KT_[g][:, r * 128:r * 128 + 64], Vt_[g][0:64, r, :], 64, mT_cur[0:64, 0:64], q, 64, outs, ("vt", g, r)))
                    else:
                        units.append((KT_[g][:, r * 128:(r + 1) * 128], Vt_[g][:, r, :], 128, mT_cur[:, 64:128], q, 64, outs, ("vt", g, r)))
            return units

        OB = (4, 5)
        LB = (6, 7)

        def L1_pass(h, hf):
            for b_ in OB + LB:
                S.op("dve", lambda e, b_=b_: e.memset(PS(b_), 0.0), w=[("ps", b_)])
            for m in range(2):
                def f(e, m=m):
                    last = None
                    for kc in range(8):
                        last = e.matmul(PS(m), Wz[:, kc, :], hT1[:, kc, 1024 * hf + 512 * m:1024 * hf + 512 * (m + 1)], start=(kc == 0), stop=(kc == 7))
                    return last
                S.op("pe", f, r=["Wz"] + HT_KEYS, w=[("ps", m)])
                S.op("act", lambda e, m=m: e.activation(out=zTh[:, 512 * m:512 * (m + 1)], in_=PS(m), func=AF.Silu), r=[("ps", m)], w=[("zTh", m)])
            yield
            allu = []
            for g in range(3):
                allu += [(g, u) for u in L1_units(g, hf)]
            for i0 in range(0, len(allu), 4):
                grp = allu[i0:i0 + 4]
                sb_ = 2 + (ptc[0] % 2)
                pt = PTb[ptc[0] % 2]
                pkey = ("PT", ptc[0] % 2)
                ptc[0] += 1
                rk = []
                def fs(e, grp=grp, sb_=sb_):
                    last = None
                    for ui, (g, (kT, v, nk, mk, q, nq, outs, vkey)) in enumerate(grp):
                        o_ = PS(sb_)[0:nk, ui * 128:ui * 128 + nq]
                        e.matmul(o_, kT, q, start=True, stop=False)
                        last = e.matmul(o_, ident_b[0:nk, 0:nk], mk, start=False, stop=True)
                    return last
                for (g, u) in grp:
                    rk += [("qkT", g, ti_) for ti_ in range(16)]
                S.op("pe", fs, r=list(set(rk)) + ["cb"], w=[("ps", sb_)])
                S.op("act", lambda e, sb_=sb_, pt=pt: e.activation(out=pt, in_=PS(sb_), func=AF.Exp, scale=SCALE), r=[("ps", sb_)], w=[pkey])
                def fp(e, grp=grp, pt=pt):
                    last = None
                    for ui, (g, (kT, v, nk, mk, q, nq, outs, vkey)) in enumerate(grp):
                        for (bi_, ofn, c0, n_) in outs:
                            rhs = pt[0:nk, ui * 128 + c0:ui * 128 + c0 + n_]
                            e.matmul(ofn(PS(OB[bi_])), v, rhs, start=False, stop=False, skip_group_check=True)
                            last = e.matmul(ofn(PS(LB[bi_])), ones_b[0:nk, :], rhs, start=False, stop=False, skip_group_check=True)
                    return last
                S.op("pe", fp, r=[pkey, "cb"] + [u[7] for (_, u) in grp], w=[("ps", b_) for b_ in OB + LB])
                yield
            for m in range(2):
                fb = fin_[fnc[0] % 2]
                fkey = ("fin", 0)
                fnc[0] += 1
                S.op("act", lambda e, fb=fb, m=m: e.activation(out=fb, in_=PS(LB[m]), func=AF.Ln), r=[("ps", LB[m])], w=[fkey])
                S.op("act", lambda e, fb=fb: e.activation(out=fb, in_=fb, func=AF.Exp, scale=-1.0), r=[fkey], w=[fkey])
                S.op("dve", lambda e, fb=fb, m=m: e.tensor_tensor(out=fb, in0=PS(OB[m]), in1=fb, op=ALU.mult), r=[("ps", OB[m]), fkey], w=[fkey])
                p0 = 1024 * hf + 512 * m
                S.op("dve", lambda e, fb=fb, m=m, p0=p0: e.tensor_tensor(out=ogT[:, h, p0:p0 + 512], in0=fb, in1=zTh[:, 512 * m:512 * (m + 1)], op=ALU.mult),
                     r=[fkey, ("zTh", m)], w=[("ogT", h, p0 // 128 * 128 + i * 128) for i in range(4)])
            yield

        def L1_newrows(h):
            for g in range(3):
                S.op("pool", lambda e, g=g: e.tensor_copy(out=qs_all[:, h, g, :], in_=QT_[g][:, L:TT]), r=[("qkT", g, 16)], w=[("qs", h, g)])
                def fsn(e, g=g):
                    e.matmul(PS(2)[0:32, 0:32], KT_[g][:, L:TT], QT_[g][:, L:TT], start=True, stop=False)
                    return e.matmul(PS(2)[0:32, 0:32], ident_b[0:32, 0:32], cbf[0:32, CB_MN + g * 32:CB_MN + (g + 1) * 32], start=False, stop=True)
                S.op("pe", fsn, r=[("qkT", g, 16), "cb"], w=[("ps", 2)])
                S.op("act", lambda e: e.activation(out=ptn, in_=PS(2)[0:32, 0:32], func=AF.Exp, scale=SCALE), r=[("ps", 2)], w=["ptn"])
                def fpn(e, g=g):
                    e.matmul(PS(4)[:, 0:32], Vt_[g][0:32, 16, :], ptn, start=(g == 0), stop=(g == 2))
                    return e.matmul(PS(6)[:, 0:32], ones_b[0:32, :], ptn, start=(g == 0), stop=(g == 2))
                S.op("pe", fpn, r=["ptn", ("vt", g, 16), "cb"], w=[("ps", 4), ("ps", 6)])
            S.op("act", lambda e: e.activation(out=onew[:, h, :], in_=PS(4)[:, 0:32], func=AF.Copy), r=[("ps", 4)], w=[("onew", h)])
            S.op("dve", lambda e: e.tensor_copy(out=lnew[:, h, :], in_=PS(6)[:, 0:32]), r=[("ps", 6)], w=[("lnew", h)])
            def fz(e):
                last = None
                for kc in range(8):
                    last = e.matmul(PS(0)[:, 0:32], Wz[:, kc, :], hT1[:, kc, L:TT], start=(kc == 0), stop=(kc == 7))
                return last
            S.op("pe", fz, r=["Wz"] + HT_KEYS, w=[("ps", 0)])
            S.op("act", lambda e: e.activation(out=zs_all[:, h, :], in_=PS(0)[:, 0:32], func=AF.Silu), r=[("ps", 0)], w=[("zs", h)])

        def L1_head(h):
            S.dma("pool", Wz, I["bwin"][:, 9216 + h * 128:9216 + (h + 1) * 128].rearrange("(c p) n -> p c n", p=128), w=["Wz"])
            for g in range(3):
                for _ in L1_proj(h, g):
                    yield
            L1_newrows(h)
            yield
            for hf in range(2):
                for _ in L1_pass(h, hf):
                    yield

        for h in range(nheads):
            for _ in L1_head(h):
                pass
        S.barrier()

        R4.reset()
        NPF = 3
        NCL = (1, 4, 8)
        ckb = [[R4.alloc([NCL[g], 2, 128], BF16) for g in range(3)] for _ in range(NPF)]
        kTc = [R4.alloc([13, 128], BF16) for _ in range(2)]
        pts = [R4.alloc([24], BF16) for _ in range(2)]
        fo_ = R4.alloc([32], F32)
        fl_ = R4.alloc([32], F32)
        items = [(h, s_) for h in range(nheads) for s_ in range(NS)]

        def e2_load(i):
            h, s_ = items[i]
            for g in range(3):
                d = GRP[g][1]
                src = I["kc%d" % g][s_].rearrange("(k r) kv hh e -> k r kv hh e", r=d)[:, 0:NCL[g], :, h, :]
                S.dma("pool", ckb[i % NPF][g], src, w=[("ck", i % NPF, g)])

        for i in range(min(NPF - 1, len(items))):
            e2_load(i)
        for i, (h, s_) in enumerate(items):
            if i + NPF - 1 < len(items):
                e2_load(i + NPF - 1)
            buf = ckb[i % NPF]
            ckeys = [("ck", i % NPF, g) for g in range(3)]
            kt = kTc[i % 2]
            ktk = ("kTc", i % 2)
            pt = pts[i % 2]
            ptk = ("pts", i % 2)
            sb_ = 2 + (i % 2)
            tiles = [(0, 0)] + [(1, r) for r in range(4)] + [(2, r) for r in range(8)]
            if s_ == 0:
                S.op("dve", lambda e: e.memset(PS(4)[:, 0:32], 0.0), w=[("ps", 4)])
                S.op("dve", lambda e: e.memset(PS(6)[:, 0:32], 0.0), w=[("ps", 6)])
            def ftr(e, buf=buf):
                last = None
                for ti, (g, r) in enumerate(tiles):
                    last = e.transpose(PSB(ti // 8)[:, (ti % 8) * 128:(ti % 8 + 1) * 128], buf[g][:, r, 0, :], ident_b)
                return last
            S.op("pe", ftr, r=ckeys + ["cb"], w=[("ps", 0), ("ps", 1)])
            S.op("act", lambda e, kt=kt: e.activation(out=kt[:, 0:8, :], in_=PSB(0).rearrange("p (a b) -> p a b", a=8), func=AF.Copy), r=[("ps", 0)], w=[ktk + (0,)])
            S.op("dve", lambda e, kt=kt: e.tensor_copy(out=kt[:, 8:13, :], in_=PSB(1)[:, 0:640].rearrange("p (a b) -> p a b", a=5)), r=[("ps", 1)], w=[ktk + (1,)])
            def cls(g, r):
                d = GRP[g][1]
                nq = 8 // d if d <= 8 else 1
                c0 = (0, 8 + 2 * r, 16 + r)[g]
                if d == 1:
                    q = qs_all[:, h, g, 8 * s_:8 * s_ + 8]
                    mk = cbf[:, CB_MK:CB_MK + 8]
                    oc = lambda B: B[:, 8 * s_:8 * s_ + 8]
                elif d == 4:
                    q = qs_all[:, h, g, 8 * s_:8 * s_ + 8].rearrange("p (i r) -> p r i", r=4)[:, r, :]
                    mk = cbf[:, CB_MK + 8:CB_MK + 16].rearrange("p (i r) -> p r i", r=4)[:, r, :]
                    oc = lambda B: B[:, 8 * s_:8 * s_ + 8].rearrange("p (i r) -> p r i", r=4)[:, r, :]
                else:
                    q = qs_all[:, h, g, 8 * s_ + r:8 * s_ + r + 1]
                    mk = None
                    oc = lambda B: B[:, 8 * s_ + r:8 * s_ + r + 1]
                return nq, c0, q, mk, oc
            def fsc(e, kt=kt, sb_=sb_):
                last = None
                for ti, (g, r) in enumerate(tiles):
                    nq, c0, q, mk, oc = cls(g, r)
                    o_ = PS(sb_)[:, c0:c0 + nq]
                    last = e.matmul(o_, kt[:, ti, :], q, start=True, stop=(mk is None))
                    if mk is not None:
                        last = e.matmul(o_, ident_b, mk, start=False, stop=True)
                return last
            S.op("pe", fsc, r=[ktk + (0,), ktk + (1,), "cb"] + [("qs", h, g) for g in range(3)], w=[("ps", sb_)])
            S.op("act", lambda e, pt=pt, sb_=sb_: e.activation(out=pt, in_=PS(sb_)[:, 0:24], func=AF.Exp, scale=SCALE), r=[("ps", sb_)], w=[ptk])
            def fpv(e, buf=buf, pt=pt):
                last = None
                for ti, (g, r) in enumerate(tiles):
                    nq, c0, q, mk, oc = cls(g, r)
                    e.matmul(oc(PS(4)), buf[g][:, r, 1, :], pt[:, c0:c0 + nq], start=False, stop=False, skip_group_check=True)
                    last = e.matmul(oc(PS(6)), ones_b, pt[:, c0:c0 + nq], start=False, stop=False, skip_group_check=True)
                return last
            S.op("pe", fpv, r=ckeys + [ptk, "cb"], w=[("ps", 4), ("ps", 6)])
            if s_ == NS - 1:
                S.op("dve", lambda e, h=h: e.tensor_tensor(out=fo_, in0=PS(4)[:, 0:32], in1=onew[:, h, :], op=ALU.add), r=[("ps", 4), ("onew", h)], w=["fo_"])
                S.op("dve", lambda e, h=h: e.tensor_tensor(out=fl_, in0=PS(6)[:, 0:32], in1=lnew[:, h, :], op=ALU.add), r=[("ps", 6), ("lnew", h)], w=["fl_"])
                S.op("act", lambda e: e.activation(out=fl_, in_=fl_, func=AF.Ln), r=["fl_"], w=["fl_"])
                S.op("act", lambda e: e.activation(out=fl_, in_=fl_, func=AF.Exp, scale=-1.0), r=["fl_"], w=["fl_"])
                S.op("dve", lambda e: e.tensor_tensor(out=fo_, in0=fo_, in1=fl_, op=ALU.mult), r=["fo_", "fl_"], w=["fo_"])
                S.op("dve", lambda e, h=h: e.tensor_tensor(out=ogT[:, h, L:TT], in0=fo_, in1=zs_all[:, h, :], op=ALU.mult),
                     r=["fo_", ("zs", h)], w=[("ogT", h, L + q_ * 8) for q_ in range(NS)])
        S.barrier()
        R3.reset()
        ada_phase(1, gate_p, gate_s, R3, parts=("gate",))
        S.barrier()

        R4.reset()
        xss = R4.alloc([D], F32, parts=32)
        fng_bc = R4.alloc([D], F32)
        S.dma("sp", fng_bc, I["fng"].partition_broadcast(128), w=["fng"])
        wout_phase(1, I["bwout"], lambda t: (x1[:, t, :], [("x1", t, 0), ("x1", t, 1)]), lambda t: x1[:, t, :])
        ssq2 = R4.alloc([17], F32)
        junk3 = R4.alloc([D], BF16)
        ost = [R4.alloc([D], F32) for _ in range(2)]
        for t in range(17):
            p = 128 if t < 16 else 32
            xb, xk = x_src1(t)
            S.op("act", lambda e, xb=xb, p=p, t=t: e.activation(out=junk3[0:p], in_=xb[0:p], func=AF.Square, accum_out=ssq2[0:p, t:t + 1]),
                 r=xk, w=["junk3", ("ssq2", t)])
            S.op("pool", lambda e, p=p, t=t: e.tensor_scalar(out=ssq2[0:p, t:t + 1], in0=ssq2[0:p, t:t + 1], scalar1=1.0 / D, scalar2=EPS, op0=ALU.mult, op1=ALU.add),
                 r=[("ssq2", t)], w=[("ssq2", t)])
            S.op("pool", lambda e, p=p, t=t: e.tensor_tensor(out=ssq2[0:p, t:t + 1], in0=ssq2[0:p, t:t + 1], in1=nhalf_c[0:p], op=ALU.pow),
                 r=[("ssq2", t), "cf"], w=[("ssq2", t)])
            ob = ost[t % 2]
            okey = ("ost", t % 2)
            S.op("act", lambda e, xb=xb, p=p, t=t, ob=ob: e.activation(out=ob[0:p], in_=xb[0:p], func=AF.Identity, scale=ssq2[0:p, t:t + 1]),
                 r=xk + [("ssq2", t)], w=[okey])
            S.op("dve", lambda e, p=p, ob=ob: e.tensor_tensor(out=ob[0:p], in0=ob[0:p], in1=fng_bc[0:p], op=ALU.mult), r=[okey, "fng"], w=[okey])
            if t < 16:
                S.dma("sp", O["yp"][t * 128:(t + 1) * 128, :], ob, r=[okey])
            else:
                S.dma("sp", O["ys"], ob[0:32], r=[okey])
        S.finish()
        print("instructions:", S.ninst, "sems:", S.nsem)
    return nc


def prep_inputs(inp):
    f = lambda a: np.ascontiguousarray(np.asarray(a, dtype=np.float32))
    cf, cb = make_consts()
    ngc = f(np.asarray(inp["norm_g"]).reshape(2, 8, 128).transpose(0, 2, 1))
    adab = f(inp["ada_b"])
    adabc = f(adab[:, 0:2048].reshape(2, 16, 128).transpose(0, 2, 1))
    cw = f(np.asarray(inp["a_conv_w"])[0].reshape(4, 24, 128).transpose(2, 1, 0))
    hp = np.zeros((128, 17), np.float32)
    hp[:, 0:8] = np.asarray(inp["a_A_log"])[0][None, :]
    hp[:, 8:16] = np.asarray(inp["a_dt_bias"])[0][None, :]
    hp[:, 16] = np.asarray(inp["a_out_norm_g"])[0]
    shared = dict(ngc=ngc, adaw=f(inp["ada_w"]), adabc=adabc, adab=adab, awin=f(np.asarray(inp["a_w_in"])[0]), cw=cw, hp=hp,
                  awout=f(np.asarray(inp["a_w_out"])[0]), cf=cf, cb=cb, fng=f(inp["final_norm_g"]),
                  bwin=f(np.asarray(inp["b_w_in"])[0]), bwout=f(np.asarray(inp["b_w_out"])[0]))
    maps = []
    for c in range(NCORES):
        m = dict(shared)
        m["xp"] = f(np.asarray(inp["x_prompt"])[c])
        m["xs"] = f(np.asarray(inp["x_sample"])[4 * c:4 * c + 4].reshape(32, D))
        cc = np.concatenate([np.asarray(inp["c_prompt"])[c:c + 1], np.asarray(inp["c_sample"])[4 * c:4 * c + 4]], 0)
        m["cT"] = f(cc.T.reshape(8, 128, 5).transpose(1, 0, 2))
        sc = np.asarray(inp["state_conv"])[0, 4 * c:4 * c + 4]
        m["sconv"] = f(sc.reshape(4, 3, 24, 128).transpose(3, 2, 0, 1))
        m["sdelta"] = f(np.asarray(inp["state_delta"])[0, 4 * c:4 * c + 4])
        m["kc0"] = f(np.asarray(inp["cache_kv_w128"])[0, 4 * c:4 * c + 4])
        m["kc1"] = f(np.asarray(inp["cache_kv_w512"])[0, 4 * c:4 * c + 4])
        m["kc2"] = f(np.asarray(inp["cache_kv_w2048"])[0, 4 * c:4 * c + 4])
        maps.append(m)
    return maps


_NC_CACHE = {}


def kernel(**inp):
    if "nc" not in _NC_CACHE:
        _NC_CACHE["nc"] = build()
    nc = _NC_CACHE["nc"]
    maps = prep_inputs(inp)
    res = run_bass_kernel_spmd(nc, maps, core_ids=list(range(NCORES)))
    R = res.results
    g = lambda k: [np.asarray(R[c][k], dtype=np.float32) for c in range(NCORES)]
    y_prompt = np.stack(g("yp"), 0)
    y_sample = np.concatenate([a.reshape(NS, LS, D) for a in g("ys")], 0)
    delta_p = np.stack(g("dp"), 0)[None]
    delta_s = np.concatenate(g("ds"), 0)[None]
    conv_p = np.stack(g("cp"), 0)[None]
    conv_s = np.concatenate([a.reshape(NS, 3, 3072) for a in g("cs")], 0)[None]
    outs = [y_prompt, y_sample, delta_p, delta_s, conv_p, conv_s]
    for gi in range(3):
        outs.append(np.stack(g("kv%dp" % gi), 0)[None])
        outs.append(np.concatenate(g("kv%ds" % gi), 0)[None])
    return tuple(np.ascontiguousarray(o, dtype=np.float32) for o in outs)
```

```python
import numpy as np
import ml_dtypes
from contextlib import ExitStack
import concourse.bass as bass
import concourse.mybir as mybir
from concourse.bass_utils import run_bass_kernel_spmd

F32 = mybir.dt.float32
BF16 = mybir.dt.bfloat16
AF = mybir.ActivationFunctionType
ALU = mybir.AluOpType

NEG = -30000.0
NCORES = 8
L = 2048
NS = 4
LS = 8
TT = L + NS * LS
D = 1024
EPS = 1e-6


class Sched:
    ENG = ("pe", "act", "dve", "pool", "sp")

    def __init__(self, nc, stack, sempool):
        self.nc = nc
        self.stack = stack
        self.sempool = sempool
        self.eng = {"pe": nc.tensor, "act": nc.scalar, "dve": nc.vector, "pool": nc.gpsimd, "sp": nc.sync}
        self.gen = {e: 0 for e in self.ENG}
        self.cnt = {e: 0 for e in self.ENG}
        self.esem = {}
        self.seen = {e: {} for e in self.ENG}
        self.res = {}
        self.dsem = {}
        self.nsem = 0
        self.ninst = {e: 0 for e in self.ENG}

    def _newsem(self, name):
        self.nsem += 1
        return self.sempool.pop()

    def _R(self, k):
        r = self.res.get(k)
        if r is None:
            r = [None, {}]
            self.res[k] = r
        return r

    def _wait(self, en, events):
        need = {}
        for ev in events:
            if ev is None:
                continue
            if ev[0] == "E":
                if ev[1] == en and en == "pe":
                    continue
                k = ("E", ev[1])
                v = (ev[2], ev[3])
            else:
                k = ("D", ev[1])
                v = (0, ev[2])
            if need.get(k, (-1, -1)) < v:
                need[k] = v
        for k, v in need.items():
            if self.seen[en].get(k, (-1, -1)) >= v:
                continue
            self.seen[en][k] = v
            sem = self.esem[(k[1], v[0])] if k[0] == "E" else self.dsem[k[1]][0]
            self.eng[en].wait_ge(sem, v[1])

    def _deps(self, r, w):
        evs = []
        for k in r:
            R = self.res.get(k)
            if R is not None:
                evs.append(R[0])
                if isinstance(k, tuple) and k[0] == "ps":
                    evs.extend(R[1].values())
        for k in w:
            R = self.res.get(k)
            if R is not None:
                evs.append(R[0])
                evs.extend(R[1].values())
        return evs

    def _record(self, ev, semid, r, w):
        for k in r:
            self._R(k)[1][semid] = ev
        for k in w:
            R = self._R(k)
            R[0] = ev
            R[1] = {}

    def op(self, en, fn, r=(), w=()):
        self._wait(en, self._deps(r, w))
        inst = fn(self.eng[en])
        self.ninst[en] += 1
        if self.cnt[en] >= 12000:
            self.gen[en] += 1
            self.cnt[en] = 0
        self.cnt[en] += 1
        g = self.gen[en]
        if (en, g) not in self.esem:
            self.esem[(en, g)] = self._newsem(f"e_{en}_{g}")
        inst.then_inc(self.esem[(en, g)], 1)
        ev = ("E", en, g, self.cnt[en])
        self._record(ev, ("E", en), r, w)
        return inst

    def dma(self, q, out, in_, r=(), w=(), key=None, **kw):
        self._wait(q, self._deps(r, w))
        inst = self.eng[q].dma_start(out=out, in_=in_, **kw)
        if key is None:
            key = ("w", w[0]) if w else ("r", r[0])
        ds = self.dsem.get(key)
        if ds is None:
            ds = [self._newsem(f"d{len(self.dsem)}"), 0]
            self.dsem[key] = ds
        ds[1] += 16
        inst.then_inc(ds[0], 16)
        ev = ("D", key, ds[1])
        self._record(ev, ("D", key), r, w)
        return inst

    def _all_events(self):
        evs = []
        for e in self.ENG:
            if self.cnt[e] > 0 or self.gen[e] > 0:
                evs.append(("E", e, self.gen[e], self.cnt[e]))
        for k, ds in self.dsem.items():
            evs.append(("D", k, ds[1]))
        return evs

    def barrier(self):
        evs = self._all_events()
        for e in self.ENG:
            self._wait(e, evs)

    def finish(self):
        self._wait("sp", self._all_events())


WAIT = "WAIT"


def run_tasks(gens):
    act = list(gens)
    idle = 0
    i = 0
    while act:
        i %= len(act)
        g = act[i]
        try:
            v = next(g)
        except StopIteration:
            act.pop(i)
            idle = 0
            continue
        if v is WAIT:
            idle += 1
            assert idle <= 4 * len(act) + 4, "all tasks waiting"
        else:
            idle = 0
        i += 1


class Arena:
    def __init__(self, ap, nbytes):
        self.ap = ap
        self.nbytes = nbytes

    def view(self, off, parts, shape, dt):
        esz = 4 if dt == F32 else 2
        n = int(np.prod(shape))
        nb = n * esz
        assert off % 4 == 0 and off + nb <= self.nbytes, (off, nb, self.nbytes)
        nw = (nb + 3) // 4
        v = self.ap[0:parts, off // 4: off // 4 + nw]
        if dt != F32:
            v = v.bitcast(dt)
            if v.shape[1] != n:
                v = v[:, 0:n]
        if len(shape) == 2:
            return v.rearrange("p (a b) -> p a b", a=shape[0])
        if len(shape) == 3:
            return v.rearrange("p (a b c) -> p a b c", a=shape[0], b=shape[1])
        return v


class Region:
    def __init__(self, arena, lo, hi):
        self.arena, self.lo, self.hi = arena, lo, hi
        self.cur = lo

    def reset(self):
        self.cur = self.lo

    def alloc(self, shape, dt, parts=128):
        esz = 4 if dt == F32 else 2
        nb = (int(np.prod(shape)) * esz + 3) // 4 * 4
        off = self.cur
        assert off + nb <= self.hi, ("region overflow", off, nb, self.hi)
        self.cur += nb
        return self.arena.view(off, parts, shape, dt)


CF_ID, CF_U, CF_MASK, CF_EPS, CF_ONE, CF_NHALF, CF_LNSC, CF_ZERO, CF_N = 0, 128, 256, 512, 513, 514, 515, 516, 520
CB_ID, CB_ONES, CB_BD, CB_ML = 0, 128, 256, 384
CB_MC, CB_MP, CB_MN, CB_MK, CB_N = 1152, 1280, 1408, 1504, 1528


def make_consts():
    cf = np.zeros((128, CF_N), np.float32)
    cf[:, CF_ID:CF_ID + 128] = np.eye(128)
    k = np.arange(128)[:, None]
    i = np.arange(128)[None, :]
    cf[:, CF_U:CF_U + 128] = (k <= i)
    cf[:, CF_MASK:CF_MASK + 128] = np.where(i >= k, 0.0, NEG)
    cf[:, CF_MASK + 128:CF_MASK + 256] = np.where(i > k, 0.0, NEG)
    cf[:, CF_EPS] = EPS
    cf[:, CF_ONE] = 1.0
    cf[:, CF_NHALF] = -0.5
    cf[:, CF_LNSC] = np.log(128.0 ** -0.5)
    cb = np.zeros((128, CB_N), np.float32)
    cb[:, CB_ID:CB_ID + 128] = np.eye(128)
    cb[:, CB_ONES:CB_ONES + 128] = 1.0
    ii = np.arange(128)[:, None]
    jj = np.arange(128)[None, :]
    cb[:, CB_BD:CB_BD + 128] = (ii // 16 == jj // 16)
    for lv, b in enumerate((16, 32, 64)):
        off = (ii // (2 * b) == jj // (2 * b)) & (ii % (2 * b) >= b) & (jj % (2 * b) < b)
        cb[:, CB_ML + lv * 256:CB_ML + lv * 256 + 128] = off.T
        cb[:, CB_ML + lv * 256 + 128:CB_ML + lv * 256 + 256] = off
    cb[:, CB_MC:CB_MC + 128] = np.where(jj >= ii, 0.0, NEG)
    cb[:, CB_MP:CB_MP + 128] = np.where(jj <= ii, 0.0, NEG)
    for gi, d in enumerate((1, 4, 16)):
        a = np.arange(32)
        sk, jk = a[:, None] // 8, a[:, None] % 8
        sq, lq = a[None, :] // 8, a[None, :] % 8
        ok = (sk == sq) & (jk <= lq) & ((lq - jk) % d == 0)
        cb[0:32, CB_MN + gi * 32:CB_MN + (gi + 1) * 32] = np.where(ok, 0.0, NEG)
        cb[:, CB_MK + gi * 8:CB_MK + (gi + 1) * 8] = np.where(ii >= (np.arange(8)[None, :] // d), 0.0, NEG)
    return cf, cb.astype(ml_dtypes.bfloat16)


def build(dbg=False, nheads=8, do_l1=True, stop=None):
    nc = bass.Bass("TRN2", target_bir_lowering=False)

    def din(name, shape, dt=F32):
        return nc.dram_tensor(name, list(shape), dt, kind="ExternalInput").ap()

    def dout(name, shape, dt=F32):
        return nc.dram_tensor(name, list(shape), dt, kind="ExternalOutput").ap()

    I = dict(
        xp=din("xp", [L, D]), xs=din("xs", [NS * LS, D]),
        cT=din("cT", [128, 8, 5]), ngc=din("ngc", [2, 128, 8]),
        adaw=din("adaw", [2, D, 3 * D]), adabc=din("adabc", [2, 128, 16]), adab=din("adab", [2, 3 * D]),
        awin=din("awin", [D, 4112]), cw=din("cw", [128, 24, 4]), hp=din("hp", [128, 17]),
        sconv=din("sconv", [128, 24, NS, 3]), sdelta=din("sdelta", [NS, 8, 128, 128]),
        awout=din("awout", [D, D]),
        cf=din("cf", [128, CF_N]), cb=din("cb", [128, CB_N], BF16),
        fng=din("fng", [D]),
        bwin=din("bwin", [D, 10240]), bwout=din("bwout", [D, D]),
        kc0=din("kc0", [NS, 128, 2, 8, 128]), kc1=din("kc1", [NS, 512, 2, 8, 128]), kc2=din("kc2", [NS, 2048, 2, 8, 128]),
    )
    O = dict(
        yp=dout("yp", [L, D]), ys=dout("ys", [NS * LS, D]),
        dp=dout("dp", [8, 128, 128]), ds=dout("ds", [NS, 8, 128, 128]),
        cp=dout("cp", [3, 3072]), cs=dout("cs", [NS * 3, 3072]),
        kv0p=dout("kv0p", [128, 2, 8, 128]), kv1p=dout("kv1p", [512, 2, 8, 128]), kv2p=dout("kv2p", [2048, 2, 8, 128]),
        kv0s=dout("kv0s", [NS, LS, 2, 8, 128]), kv1s=dout("kv1s", [NS, LS, 2, 8, 128]), kv2s=dout("kv2s", [NS, LS, 2, 8, 128]),
    )
    if dbg:
        O["dbg_x1p"] = dout("dbg_x1p", [L, D])
        O["dbg_x1s"] = dout("dbg_x1s", [NS * LS, D])
        O["dbg_og"] = dout("dbg_og", [128, 8, TT], BF16)
        O["dbg_hT"] = dout("dbg_hT", [128, 8, TT], BF16)

    stack = ExitStack()
    with stack:
        NB = 212000
        arena_t = stack.enter_context(nc.sbuf_tensor("arena", [128, NB // 4], F32))
        banks = [stack.enter_context(nc.psum_tensor(f"bank{i}", [128, 512], F32)) for i in range(8)]
        sempool = [stack.enter_context(nc.semaphore(f"s{i}")) for i in range(96)]
        stack.enter_context(nc.Block())
        S = Sched(nc, stack, sempool)
        A = Arena(arena_t, NB)

        def PS(b):
            return banks[b][:, :]

        def PSB(b):
            return banks[b][:, :].bitcast(BF16)

        RC = Region(A, 0, 8192)
        R1 = Region(A, 8192, 73728)
        R2 = Region(A, 73728, 107008)
        R3 = Region(A, 107008, 140288)
        R4 = Region(A, 140288, NB)

        cf = RC.alloc([CF_N], F32)
        cbf = RC.alloc([CB_N], BF16)
        hpar = RC.alloc([17], F32)
        S.dma("sp", cf, I["cf"], w=["cf"])
        S.dma("sp", cbf, I["cb"], w=["cb"])
        S.dma("sp", hpar, I["hp"], w=["hpar"])
        ident_f = cf[:, CF_ID:CF_ID + 128]
        U_f = cf[:, CF_U:CF_U + 128]
        mask2 = cf[:, CF_MASK:CF_MASK + 256]
        eps_c = cf[:, CF_EPS:CF_EPS + 1]
        one_c = cf[:, CF_ONE:CF_ONE + 1]
        nhalf_c = cf[:, CF_NHALF:CF_NHALF + 1]
        lnsc_c = cf[:, CF_LNSC:CF_LNSC + 1]
        ident_b = cbf[:, CB_ID:CB_ID + 128]
        ones_b = cbf[:, CB_ONES:CB_ONES + 128]
        bd_b = cbf[:, CB_BD:CB_BD + 128]
        ml_b = [cbf[:, CB_ML + lv * 256:CB_ML + (lv + 1) * 256] for lv in range(3)]
        mT_cur = cbf[:, CB_MC:CB_MC + 128]
        mT_prev = cbf[:, CB_MP:CB_MP + 128]
        CK = ["cf", "cb", "hpar"]

        modc = RC.alloc([16, 5], F32)
        gmod = RC.alloc([8, 5], F32)
        ngc = RC.alloc([8], F32)
        adabc = RC.alloc([16], F32)
        cT = RC.alloc([8, 5], F32)
        scb = RC.alloc([8, 5], BF16)

        def ada_phase(l, gate_p, gate_s, reg, parts=("col", "gate")):
            S.dma("sp", ngc, I["ngc"][l], w=["ngc"])
            S.dma("sp", adabc, I["adabc"][l], w=["adabc"])
            if l == 0:
                S.dma("sp", cT, I["cT"], w=["cT"])
                S.op("act", lambda e: e.activation(out=scb, in_=cT, func=AF.Silu), r=["cT"], w=["scb"])
            scp = reg.alloc([8, 128], BF16)
            scs = reg.alloc([8, 32], BF16)
            S.op("act", lambda e: e.activation(out=scp, in_=cT[:, :, 0:1].broadcast_to([128, 8, 128]), func=AF.Silu),
                 r=["cT"], w=["scp"])
            for s in range(NS):
                S.op("act", lambda e: e.activation(out=scs[:, :, 8 * s:8 * s + 8],
                                                   in_=cT[:, :, 1 + s:2 + s].broadcast_to([128, 8, 8]), func=AF.Silu),
                     r=["cT"], w=[("scs", s)])
            gb = reg.alloc([D], F32)
            S.dma("sp", gb, I["adab"][l, 2 * D:3 * D].partition_broadcast(128), w=["gb"])
            wb = [reg.alloc([8, 512], BF16) for _ in range(2)]
            for blk in range(6):
                if (blk < 4 and "col" not in parts) or (blk >= 4 and "gate" not in parts):
                    continue
                buf = wb[blk % 2]
                key = ("adaw", blk % 2)
                S.dma("pool", buf, I["adaw"][l][:, blk * 512:(blk + 1) * 512].rearrange("(c p) n -> p c n", p=128),
                      w=[key])
                if blk < 4:
                    def f(e, blk=blk, buf=buf):
                        last = None
                        for ecl in range(4):
                            ec = blk * 4 + ecl
                            for kc in range(8):
                                last = e.matmul(PS(0)[:, ec * 5:ec * 5 + 5], buf[:, kc, ecl * 128:(ecl + 1) * 128],
                                                scb[:, kc, :], start=(kc == 0), stop=(kc == 7))
                        return last
                    S.op("pe", f, r=[key, "scb"], w=[("ps", 0)])
                else:
                    hb = blk - 4
                    bk = 1 + (hb % 2)
                    def f(e, buf=buf, bk=bk):
                        last = None
                        for kc in range(8):
                            last = e.matmul(PS(bk), scp[:, kc, :], buf[:, kc, :], start=(kc == 0), stop=(kc == 7))
                        return last
                    S.op("pe", f, r=[key, "scp"], w=[("ps", bk)])
                    S.op("dve", lambda e, bk=bk, hb=hb: e.tensor_tensor(out=gate_p[:, hb * 512:(hb + 1) * 512], in0=PS(bk),
                                                                         in1=gb[:, hb * 512:(hb + 1) * 512], op=ALU.add),
                         r=[("ps", bk), "gb"], w=[("gate_p", l)])
                    def f2(e, buf=buf, bk=bk):
                        last = None
                        for kc in range(8):
                            last = e.matmul(PS(bk)[0:32, :], scs[:, kc, :], buf[:, kc, :], start=(kc == 0), stop=(kc == 7))
                        return last
                    S.op("pe", f2, r=[key] + [("scs", s) for s in range(NS)], w=[("ps", bk)])
                    S.op("dve", lambda e, bk=bk, hb=hb: e.tensor_tensor(out=gate_s[:, hb * 512:(hb + 1) * 512], in0=PS(bk)[0:32, :],
                                                                         in1=gb[0:32, hb * 512:(hb + 1) * 512], op=ALU.add),
                         r=[("ps", bk), "gb"], w=[("gate_s", l)])
            if "col" not in parts:
                return
            S.op("dve", lambda e: e.tensor_tensor(out=modc, in0=PS(0)[:, 0:80].rearrange("p (a b) -> p a b", b=5),
                                                  in1=adabc.unsqueeze(2).broadcast_to([128, 16, 5]), op=ALU.add),
                 r=[("ps", 0), "adabc"], w=["modc"])
            S.op("dve", lambda e: e.scalar_tensor_tensor(out=gmod, in0=modc[:, 8:16, :], scalar=1.0,
                                                         in1=ngc.unsqueeze(2).broadcast_to([128, 8, 5]),
                                                         op0=ALU.add, op1=ALU.mult),
                 r=["modc", "ngc"], w=["gmod"])

        def norm_phase(l, hT, x_src, reg):
            ssq = reg.alloc([17], F32)
            rstd = reg.alloc([17], F32)
            junk = reg.alloc([D], BF16)
            xn = [reg.alloc([D], BF16) for _ in range(2)]
            ntile = 17
            tiles = []
            if l == 0:
                xst = [reg.alloc([D], F32) for _ in range(3)]

            def xt(t):
                if l == 0:
                    return xst[t % 3], ("xst", t % 3)
                return x_src(t)

            def load(t):
                if l != 0:
                    return
                buf, key = xt(t)
                if t < 16:
                    S.dma("sp", buf, I["xp"][t * 128:(t + 1) * 128, :], w=[key])
                else:
                    S.dma("sp", buf[0:32], I["xs"], w=[key])

            def sq(t):
                buf, key = xt(t)
                p = 128 if t < 16 else 32
                keys = key if isinstance(key, list) else [key]
                S.op("act", lambda e: e.activation(out=junk[0:p], in_=buf[0:p], func=AF.Square, accum_out=ssq[0:p, t:t + 1]),
                     r=keys, w=["junk", ("ssq", t)])
                S.op("pool", lambda e: e.tensor_scalar(out=rstd[0:p, t:t + 1], in0=ssq[0:p, t:t + 1], scalar1=1.0 / D, scalar2=EPS,
                                                       op0=ALU.mult, op1=ALU.add), r=[("ssq", t)], w=[("rstd", t)])
                S.op("pool", lambda e: e.tensor_tensor(out=rstd[0:p, t:t + 1], in0=rstd[0:p, t:t + 1], in1=nhalf_c[0:p], op=ALU.pow),
                     r=[("rstd", t), "cf"], w=[("rstd", t)])

            def scale_T(t):
                buf, key = xt(t)
                p = 128 if t < 16 else 32
                xb = xn[t % 2]
                keys = key if isinstance(key, list) else [key]
                S.op("act", lambda e: e.activation(out=xb[0:p], in_=buf[0:p], func=AF.Identity, scale=rstd[0:p, t:t + 1]),
                     r=keys + [("rstd", t)], w=[("xn", t % 2)])
                bk = 2 + (t % 2)
                def f(e):
                    last = None
                    for kc in range(8):
                        last = e.transpose(PSB(bk)[:, kc * 128:kc * 128 + p], xb[0:p, kc * 128:(kc + 1) * 128], ident_b[0:p, 0:p])
                    return last
                S.op("pe", f, r=[("xn", t % 2), "cb"], w=[("ps", bk)])
                for kc in range(8):
                    if t < 16:
                        dst = hT[:, kc, t * 128:(t + 1) * 128]
                        src = PSB(bk)[:, kc * 128:(kc + 1) * 128]
                        if kc % 2 == 0:
                            S.op("dve", lambda e, dst=dst, src=src, kc=kc: e.tensor_scalar(
                                out=dst, in0=src, scalar1=gmod[:, kc, 0:1], scalar2=modc[:, kc, 0:1], op0=ALU.mult, op1=ALU.add),
                                 r=[("ps", bk), "gmod", "modc"], w=[("hT", t, kc)])
                        else:
                            S.op("act", lambda e, dst=dst, src=src, kc=kc: e.activation(
                                out=dst, in_=src, func=AF.Identity, scale=gmod[:, kc, 0:1], bias=modc[:, kc, 0:1]),
                                 r=[("ps", bk), "gmod", "modc"], w=[("hT", t, kc)])
                    else:
                        for s in range(NS):
                            dst = hT[:, kc, L + s * 8:L + s * 8 + 8]
                            src = PSB(bk)[:, kc * 128 + s * 8:kc * 128 + s * 8 + 8]
                            S.op("dve", lambda e, dst=dst, src=src, kc=kc, s=s: e.tensor_scalar(
                                out=dst, in0=src, scalar1=gmod[:, kc, 1 + s:2 + s], scalar2=modc[:, kc, 1 + s:2 + s],
                                op0=ALU.mult, op1=ALU.add), r=[("ps", bk), "gmod", "modc"], w=[("hT", t, kc)])

            load(0)
            load(1)
            sq(0)
            for t in range(ntile):
                if t + 2 < ntile:
                    load(t + 2)
                if t + 1 < ntile:
                    sq(t + 1)
                scale_T(t)

        HT_KEYS = [("hT", t, kc) for t in range(17) for kc in range(8)]

        R1.reset(); R2.reset(); R3.reset(); R4.reset()
        hT = R1.alloc([8, TT], BF16)
        ogT = R2.alloc([8, TT], BF16)
        R4g = Region(A, NB - 12288, NB)
        gate_p = R4g.alloc([D], F32)
        gate_s = R4g.alloc([D], F32, parts=32)
        x1s = R4g.alloc([D], F32, parts=32)
        R4 = Region(A, 140288, NB - 12288)

        class _Stop(Exception):
            pass

        def maybe_stop(tag):
            if stop == tag:
                S.finish()
                print("STOP at", tag, "instructions:", S.ninst, "sems:", S.nsem)
                raise _Stop()

        try:
            _build_rest = None
        finally:
            pass
        ada_phase(0, gate_p, gate_s, R3)
        R4.reset()
        if stop == "ada":
            S.finish(); print("STOP ada", S.ninst); return nc
        norm_phase(0, hT, None, R4)
        S.barrier()
        if stop == "norm":
            S.finish(); print("STOP norm", S.ninst); return nc
        R3.reset(); R4.reset()

        NCH = 16
        wab = R1.alloc([8, 16], BF16)
        S.dma("pool", wab, I["awin"][:, 4096:4112].rearrange("(c p) n -> p c n", p=128), w=["wab"])
        def fab(e):
            last = None
            for t in range(NCH):
                for kc in range(8):
                    last = e.matmul(PS(0)[:, t * 16:(t + 1) * 16], hT[:, kc, t * 128:(t + 1) * 128], wab[:, kc, :],
                                    start=(kc == 0), stop=(kc == 7))
            for s in range(NS):
                for kc in range(8):
                    last = e.matmul(PS(1)[0:8, s * 16:(s + 1) * 16], hT[:, kc, L + s * 8:L + s * 8 + 8], wab[:, kc, :],
                                    start=(kc == 0), stop=(kc == 7))
            return last
        S.op("pe", fab, r=["wab"] + HT_KEYS, w=[("ps", 0), ("ps", 1)])

        NCOL = NCH * 8 + NS * 8
        def galloc():
            return R1.alloc([NCOL], F32)
        xa, ax, ex, lx, g_t, beta_t, lbeta_t, gc_t, gcl_t, eg_t, gtot_t, ekd_t = [galloc() for _ in range(12)]
        nA = R1.alloc([8], F32)
        A_bc = hpar[:, 0:8]
        dt_bc = hpar[:, 8:16]
        outg_c = hpar[:, 16:17]
        def pv(tl):
            return tl[:, 0:128].rearrange("p (c h) -> p c h", h=8)
        def sv(tl):
            return tl[0:8, 128:160].rearrange("p (c h) -> p c h", h=8)
        abp = PS(0)[:, 0:256].rearrange("p (c k) -> p c k", k=16)
        abs_ = PS(1)[0:8, 0:64].rearrange("p (c k) -> p c k", k=16)
        S.op("dve", lambda e: e.tensor_tensor(out=pv(xa), in0=abp[:, :, 0:8], in1=dt_bc.unsqueeze(1).broadcast_to([128, 16, 8]), op=ALU.add),
             r=[("ps", 0), "hpar"], w=["xa_p"])
        S.op("dve", lambda e: e.tensor_tensor(out=sv(xa), in0=abs_[:, :, 0:8], in1=dt_bc[0:8].unsqueeze(1).broadcast_to([8, 4, 8]), op=ALU.add),
             r=[("ps", 1), "hpar"], w=["xa_s"])
        S.op("act", lambda e: e.activation(out=pv(ex), in_=abp[:, :, 8:16], func=AF.Exp, scale=-1.0), r=[("ps", 0)], w=["ex_p"])
        S.op("act", lambda e: e.activation(out=sv(ex), in_=abs_[:, :, 8:16], func=AF.Exp, scale=-1.0), r=[("ps", 1)], w=["ex_s"])
        GP = (128, slice(0, 128))
        GS = (8, slice(128, 160))
        for (p, cs_), tg in ((GP, "p"), (GS, "s")):
            def T(tl, p=p, cs_=cs_):
                return tl[0:p, cs_]
            S.op("act", lambda e, T=T: e.activation(out=T(lbeta_t), in_=T(ex), func=AF.Ln, bias=one_c[0:T(ex).shape[0]], scale=1.0),
                 r=["ex_" + tg, "cf"], w=["lbeta_" + tg])
            S.op("dve", lambda e, T=T: e.tensor_scalar(out=T(lbeta_t), in0=T(lbeta_t), scalar1=-1.0, scalar2=None, op0=ALU.mult),
                 r=["lbeta_" + tg], w=["lbeta_" + tg])
            S.op("act", lambda e, T=T: e.activation(out=T(beta_t), in_=T(lbeta_t), func=AF.Exp), r=["lbeta_" + tg], w=["beta_" + tg])
            S.op("dve", lambda e, T=T: e.tensor_scalar(out=T(ax), in0=T(xa), scalar1=-1.0, scalar2=None, op0=ALU.mult),
                 r=["xa_" + tg], w=["ax_" + tg])
            S.op("dve", lambda e, T=T: e.tensor_tensor(out=T(ax), in0=T(ax), in1=T(xa), op=ALU.max),
                 r=["xa_" + tg, "ax_" + tg], w=["ax_" + tg])
            S.op("act", lambda e, T=T: e.activation(out=T(ax), in_=T(ax), func=AF.Exp, scale=-1.0), r=["ax_" + tg], w=["ax_" + tg])
            S.op("act", lambda e, T=T: e.activation(out=T(lx), in_=T(ax), func=AF.Ln, bias=one_c[0:T(ax).shape[0]], scale=1.0),
                 r=["ax_" + tg, "cf"], w=["lx_" + tg])
            S.op("dve", lambda e, T=T: e.scalar_tensor_tensor(out=T(lx), in0=T(xa), scalar=0.0, in1=T(lx), op0=ALU.max, op1=ALU.add),
                 r=["xa_" + tg, "lx_" + tg], w=["lx_" + tg])
        S.op("act", lambda e: e.activation(out=nA, in_=A_bc, func=AF.Exp), r=["hpar"], w=["nA"])
        S.op("dve", lambda e: e.tensor_scalar(out=nA, in0=nA, scalar1=-1.0, scalar2=None, op0=ALU.mult), r=["nA"], w=["nA"])
        S.op("dve", lambda e: e.tensor_tensor(out=pv(g_t), in0=pv(lx), in1=nA.unsqueeze(1).broadcast_to([128, 16, 8]), op=ALU.mult),
             r=["lx_p", "nA"], w=["g_p"])
        S.op("dve", lambda e: e.tensor_tensor(out=sv(g_t), in0=sv(lx), in1=nA[0:8].unsqueeze(1).broadcast_to([8, 4, 8]), op=ALU.mult),
             r=["lx_s", "nA"], w=["g_s"])
        S.op("pe", lambda e: e.matmul(PS(2)[:, 0:128], U_f, g_t[:, 0:128], start=True, stop=True), r=["cf", "g_p"], w=[("ps", 2)])
        S.op("pe", lambda e: e.matmul(PS(3)[0:8, 0:32], U_f[0:8, 0:8], g_t[0:8, 128:160], start=True, stop=True), r=["cf", "g_s"], w=[("ps", 3)])
        S.op("act", lambda e: e.activation(out=gc_t[:, 0:128], in_=PS(2)[:, 0:128], func=AF.Copy), r=[("ps", 2)], w=["gc_p"])
        S.op("act", lambda e: e.activation(out=gc_t[0:8, 128:160], in_=PS(3)[0:8, 0:32], func=AF.Copy), r=[("ps", 3)], w=["gc_s"])
        S.op("pe", lambda e: e.matmul(PS(2)[:, 128:256], ident_f[:, 127:128].broadcast_to([128, 128]), gc_t[:, 0:128], start=True, stop=True),
             r=["cf", "gc_p"], w=[("ps", 2)])
        S.op("pe", lambda e: e.matmul(PS(3)[:, 128:160], ident_f[0:8, 7:8].broadcast_to([8, 128]), gc_t[0:8, 128:160], start=True, stop=True),
             r=["cf", "gc_s"], w=[("ps", 3)])
        S.op("act", lambda e: e.activation(out=gcl_t[:, 0:128], in_=PS(2)[:, 128:256], func=AF.Copy), r=[("ps", 2)], w=["gcl_p"])
        S.op("act", lambda e: e.activation(out=gcl_t[:, 128:160], in_=PS(3)[:, 128:160], func=AF.Copy), r=[("ps", 3)], w=["gcl_s"])
        for (p, cs_), tg in ((GP, "p"), (GS, "s")):
            def T(tl, p=p, cs_=cs_):
                return tl[0:p, cs_]
            S.op("act", lambda e, T=T: e.activation(out=T(eg_t), in_=T(gc_t), func=AF.Exp), r=["gc_" + tg], w=["eg_" + tg])
            S.op("act", lambda e, cs_=cs_: e.activation(out=gtot_t[:, cs_], in_=gcl_t[:, cs_], func=AF.Exp), r=["gcl_" + tg], w=["gtot_" + tg])
            S.op("dve", lambda e, T=T: e.tensor_tensor(out=T(ekd_t), in0=T(gcl_t), in1=T(gc_t), op=ALU.subtract),
                 r=["gcl_" + tg, "gc_" + tg], w=["ekd_" + tg])
        G_KEYS = [k + t for k in ("beta_", "lbeta_", "gc_", "gcl_", "eg_", "gtot_", "ekd_") for t in ("p", "s")]

        if stop == "G":
            S.finish(); print("STOP G", S.ninst); return nc
        NT = 16
        UW = 3 + L
        UT = UW + NS * (3 + LS)
        NUB = 3
        ubuf = [R4.alloc([UT], BF16) for _ in range(NUB)]
        Wh = [R1.alloc([8, 512], BF16) for _ in range(2)]
        diag = [R1.alloc([12, 128], BF16) for _ in range(2)]
        cwt = R1.alloc([24, 4], F32)
        S.dma("sp", cwt, I["cw"], w=["cwt"])
        sconv = R1.alloc([24, NS * 3], F32)
        S.dma("sp", sconv, I["sconv"].rearrange("p a s i -> p a (s i)"), w=["sconv"])
        cvo = [[R3.alloc([TT], BF16) for _ in range(4)] for _ in range(2)]
        sqb1 = R4.alloc([TT], BF16)
        sqb = [sqb1, sqb1]
        NHC = NT + NS
        def halloc(n=NHC):
            return [R4.alloc([n], F32) for _ in range(2)]
        ss_k, ss_q, lrnk, lrq, rows1, rows2, biasj, kbg_s, kdec_s, qdec_s = [halloc() for _ in range(10)]
        rowsT = [R4.alloc([256], F32, parts=16) for _ in range(2)]
        rowsTs = [R4.alloc([16], F32, parts=4) for _ in range(2)]
        cst1 = R4.alloc([384], F32, parts=3)
        css1 = R4.alloc([384], F32, parts=32)
        cst = [cst1, cst1]
        css = [css1, css1]
        thb = [R4.alloc([512], BF16) for _ in range(2)]
        thc = [0]
        outg_h = R4.alloc([1], F32)
        S.op("dve", lambda e: e.tensor_scalar(out=outg_h, in0=outg_c, scalar1=0.5, scalar2=None, op0=ALU.mult), r=["hpar"], w=["outg_h"])
        for ub in range(NUB):
            S.op("pool", lambda e, ub=ub: e.memset(ubuf[ub][:, 0:3], 0.0), w=[("u", ub, "h")])

        ucount = [0]

        def P_head(h):
            hs_ = h % 2
            W = Wh[hs_]
            wkey = [("Wh", hs_, j) for j in range(4)]
            for j in range(4):
                col = (j * 1024 + h * 128)
                S.dma("pool", W[:, :, j * 128:(j + 1) * 128],
                      I["awin"][:, col:col + 128].rearrange("(c p) n -> p c n", p=128), w=[wkey[j]])
            dg = diag[hs_]
            for j in range(3):
                for i in range(4):
                    S.op("pool", lambda e, j=j, i=i: e.tensor_scalar(out=dg[:, j * 4 + i, :], in0=ident_f, scalar1=cwt[:, j * 8 + h, i:i + 1],
                                                                      scalar2=0.5, op0=ALU.mult, op1=ALU.mult),
                         r=["cf", "cwt"], w=[("diag", hs_, j)])
            yield 0.02
            blocks = [(q * 512, 512) for q in range(4)] + [(L, NS * LS)]
            step = 0
            nstep = 4 * 5 * 2.0
            for j in range(4):
                if j < 3:
                    ui = ucount[0] % NUB
                    ucount[0] += 1
                    ub = ubuf[ui]
                    ukey = None
                    ukeys = [("u", ui, bi_) for bi_ in range(5)]
                    S.op("act", lambda e, ub=ub, j=j: e.activation(
                        out=ub[:, UW:UT].rearrange("p (s i) -> p s i", i=3 + LS)[:, :, 0:3],
                        in_=sconv[:, j * 8 + h, :].rearrange("p (s i) -> p s i", i=3), func=AF.Copy),
                         r=["sconv"], w=[("u", ui, "sh")])
                for bi, (t0, n) in enumerate(blocks):
                    bk = 0
                    step += 1
                    def f(e, t0=t0, n=n, bk=bk, j=j):
                        last = None
                        for kc in range(8):
                            last = e.matmul(PS(bk)[:, 0:n], W[:, kc, j * 128:(j + 1) * 128], hT[:, kc, t0:t0 + n],
                                            start=(kc == 0), stop=(kc == 7))
                        return last
                    S.op("pe", f, r=[wkey[j]] + HT_KEYS, w=[("ps", bk)])
                    if j == 3:
                        tb = thb[thc[0] % 2]
                        tk = ("thb", thc[0] % 2)
                        thc[0] += 1
                        S.op("act", lambda e, n=n, bk=bk, tb=tb: e.activation(out=tb[:, 0:n], in_=PS(bk)[:, 0:n], func=AF.Tanh, scale=0.5),
                             r=[("ps", bk)], w=[tk])
                        S.op("dve", lambda e, t0=t0, n=n, bk=bk, tb=tb: e.scalar_tensor_tensor(out=cvo[hs_][3][:, t0:t0 + n], in0=tb[:, 0:n], scalar=1.0,
                                                                                              in1=PS(bk)[:, 0:n], op0=ALU.add, op1=ALU.mult),
                             r=[("ps", bk), tk], w=[("cvo", hs_, 3, bi)])
                    else:
                        if bi < 4:
                            dst = ub[:, 3 + t0:3 + t0 + n]
                            src = PS(bk)[:, 0:n]
                        else:
                            dst = ub[:, UW:UT].rearrange("p (s i) -> p s i", i=3 + LS)[:, :, 3:3 + LS]
                            src = PS(bk)[:, 0:n].rearrange("p (s i) -> p s i", i=LS)
                        S.op("act", lambda e, dst=dst, src=src: e.activation(out=dst, in_=src, func=AF.Copy), r=[("ps", bk)], w=[ukeys[bi]])
                        ck = 1
                        def fc(e, t0=t0, n=n, ck=ck, bi=bi, j=j, ub=ub):
                            last = None
                            for i in range(4):
                                if bi < 4:
                                    rhs = ub[:, t0 + i:t0 + i + n]
                                    out = PS(ck)[:, 0:n]
                                else:
                                    rhs = ub[:, UW:UT].rearrange("p (s i) -> p s i", i=3 + LS)[:, :, i:i + LS]
                                    out = PS(ck)[:, 0:n].rearrange("p (s i) -> p s i", i=LS)
                                last = e.matmul(out, dg[:, j * 4 + i, :], rhs, start=(i == 0), stop=(i == 3))
                            return last
                        rk = [ukeys[bi], ("diag", hs_, j)] + ([ukeys[bi - 1]] if 0 < bi < 4 else []) + ([("u", ui, "h")] if bi == 0 else []) + ([("u", ui, "sh")] if bi == 4 else [])
                        S.op("pe", fc, r=rk, w=[("ps", ck)])
                        tb = thb[thc[0] % 2]
                        tk = ("thb", thc[0] % 2)
                        thc[0] += 1
                        S.op("act", lambda e, n=n, ck=ck, tb=tb: e.activation(out=tb[:, 0:n], in_=PS(ck)[:, 0:n], func=AF.Tanh),
                             r=[("ps", ck)], w=[tk])
                        S.op("dve", lambda e, t0=t0, n=n, ck=ck, j=j, tb=tb: e.scalar_tensor_tensor(out=cvo[hs_][j][:, t0:t0 + n], in0=tb[:, 0:n], scalar=1.0,
                                                                                                   in1=PS(ck)[:, 0:n], op0=ALU.add, op1=ALU.mult),
                             r=[("ps", ck), tk], w=[("cvo", hs_, j, bi)])
                    yield 0.02 + 0.8 * step / 20.0
            def fcs(e):
                last = None
                for kc in range(8):
                    last = e.matmul(PS(0)[0:3, 0:384], hT[:, kc, L - 3:L], W[:, kc, 0:384], start=(kc == 0), stop=(kc == 7))
                for kc in range(8):
                    last = e.matmul(PS(1)[0:32, 0:384], hT[:, kc, L:TT], W[:, kc, 0:384], start=(kc == 0), stop=(kc == 7))
                return last
            S.op("pe", fcs, r=wkey + HT_KEYS, w=[("ps", 0), ("ps", 1)])
            S.op("act", lambda e: e.activation(out=cst[hs_], in_=PS(0)[0:3, 0:384], func=AF.Copy), r=[("ps", 0)], w=[("cst", 0)])
            S.op("act", lambda e: e.activation(out=css[hs_], in_=PS(1)[0:32, 0:384], func=AF.Copy), r=[("ps", 1)], w=[("css", 0)])
            S.dma("sp", O["cp"].rearrange("p (j c) -> p j c", j=3)[:, :, h * 128:(h + 1) * 128],
                  cst[hs_].rearrange("p (j c) -> p j c", j=3), r=[("cst", 0)])
            for s_ in range(NS):
                S.dma("sp", O["cs"].rearrange("p (j c) -> p j c", j=3)[3 * s_:3 * s_ + 3, :, h * 128:(h + 1) * 128],
                      css[hs_][8 * s_ + 5:8 * s_ + 8].rearrange("p (j c) -> p j c", j=3), r=[("css", 0)])
            for j, sst in ((1, ss_k[hs_]), (0, ss_q[hs_])):
                sb = sqb[j]
                S.op("act", lambda e, j=j, sb=sb: e.activation(out=sb, in_=cvo[hs_][j], func=AF.Square),
                     r=[("cvo", hs_, j, bi) for bi in range(5)], w=[("sqb", 0)])
                def fs(e, sb=sb, j=j):
                    last = None
                    for c in range(NT):
                        last = e.matmul(PS(j)[:, c:c + 1], sb[:, c * 128:(c + 1) * 128], ones_b[:, 0:1], start=True, stop=True)
                    for s in range(NS):
                        last = e.matmul(PS(j)[0:8, NT + s:NT + s + 1], sb[:, L + s * 8:L + s * 8 + 8], ones_b[:, 0:1], start=True, stop=True)
                    return last
                S.op("pe", fs, r=[("sqb", 0), "cb"], w=[("ps", j)])
                S.op("act", lambda e, j=j, sst=sst: e.activation(out=sst[:, 0:NT], in_=PS(j)[:, 0:NT], func=AF.Copy), r=[("ps", j)], w=[("ss", hs_, j)])
                S.op("act", lambda e, j=j, sst=sst: e.activation(out=sst[0:8, NT:NHC], in_=PS(j)[0:8, NT:NHC], func=AF.Copy), r=[("ps", j)], w=[("ss", hs_, j, "s")])
            yield 0.9
            def gcol(tl, which):
                if which == "p":
                    return tl[:, 0:128].rearrange("p (c hh) -> p c hh", hh=8)[:, :, h]
                return tl[0:8, 128:160].rearrange("p (c hh) -> p c hh", hh=8)[:, :, h]
            for which, p, cs_ in (("p", 128, slice(0, NT)), ("s", 8, slice(NT, NHC))):
                def T(tl, p=p, cs_=cs_):
                    return tl[hs_][0:p, cs_]
                hk = ("hs", hs_, which)
                S.op("act", lambda e, T=T, p=p: e.activation(out=T(lrnk), in_=T(ss_k), func=AF.Ln, bias=eps_c[0:p], scale=1.0),
                     r=[("ss", hs_, 1), ("ss", hs_, 1, "s"), "cf"], w=[hk + ("lrnk",)])
                S.op("act", lambda e, T=T, p=p: e.activation(out=T(lrq), in_=T(ss_q), func=AF.Ln, bias=eps_c[0:p], scale=1.0),
                     r=[("ss", hs_, 0), ("ss", hs_, 0, "s"), "cf"], w=[hk + ("lrq",)])
                S.op("dve", lambda e, T=T: e.tensor_scalar(out=T(lrnk), in0=T(lrnk), scalar1=-0.5, scalar2=None, op0=ALU.mult),
                     r=[hk + ("lrnk",)], w=[hk + ("lrnk",)])
                S.op("dve", lambda e, T=T, p=p: e.tensor_scalar(out=T(lrq), in0=T(lrq), scalar1=-0.5, scalar2=lnsc_c[0:p], op0=ALU.mult, op1=ALU.add),
                     r=[hk + ("lrq",), "cf"], w=[hk + ("lrq",)])
                S.op("dve", lambda e, T=T, which=which: e.tensor_tensor(out=T(rows1), in0=T(lrnk), in1=gcol(lbeta_t, which), op=ALU.add),
                     r=[hk + ("lrnk",), "lbeta_" + which], w=[hk + ("rows1",)])
                S.op("dve", lambda e, T=T, which=which: e.tensor_tensor(out=T(rows1), in0=T(rows1), in1=gcol(gc_t, which), op=ALU.add),
                     r=[hk + ("rows1",), "gc_" + which], w=[hk + ("rows1",)])
                S.op("dve", lambda e, T=T, which=which: e.tensor_tensor(out=T(rows2), in0=T(lrq), in1=gcol(gc_t, which), op=ALU.add),
                     r=[hk + ("lrq",), "gc_" + which], w=[hk + ("rows2",)])
                S.op("dve", lambda e, T=T, which=which: e.tensor_tensor(out=T(biasj), in0=T(lrnk), in1=gcol(gc_t, which), op=ALU.subtract),
                     r=[hk + ("lrnk",), "gc_" + which], w=[hk + ("biasj",)])
                S.op("act", lambda e, T=T: e.activation(out=T(kbg_s), in_=T(rows1), func=AF.Exp), r=[hk + ("rows1",)], w=[hk + ("kbg_s",)])
                S.op("act", lambda e, T=T: e.activation(out=T(qdec_s), in_=T(rows2), func=AF.Exp), r=[hk + ("rows2",)], w=[hk + ("qdec_s",)])
                S.op("dve", lambda e, T=T, which=which: e.tensor_tensor(out=T(kdec_s), in0=T(lrnk), in1=gcol(ekd_t, which), op=ALU.add),
                     r=[hk + ("lrnk",), "ekd_" + which], w=[hk + ("kdec_s",)])
                S.op("act", lambda e, T=T: e.activation(out=T(kdec_s), in_=T(kdec_s), func=AF.Exp), r=[hk + ("kdec_s",)], w=[hk + ("kdec_s",)])
            def ft(e):
                e.transpose(PS(0)[0:16, 0:128], rows2[hs_][:, 0:NT], ident_f)
                e.transpose(PS(0)[0:16, 128:256], rows1[hs_][:, 0:NT], ident_f)
                e.transpose(PS(1)[0:4, 0:8], rows2[hs_][0:8, NT:NHC], ident_f[0:8, 0:8])
                return e.transpose(PS(1)[0:4, 8:16], rows1[hs_][0:8, NT:NHC], ident_f[0:8, 0:8])
            S.op("pe", ft, r=[("hs", hs_, w_, n_) for w_ in ("p", "s") for n_ in ("rows1", "rows2")] + ["cf"], w=[("ps", 0), ("ps", 1)])
            S.op("act", lambda e: e.activation(out=rowsT[hs_], in_=PS(0)[0:16, 0:256], func=AF.Copy), r=[("ps", 0)], w=[("rowsT", hs_)])
            S.op("act", lambda e: e.activation(out=rowsTs[hs_], in_=PS(1)[0:4, 0:16], func=AF.Copy), r=[("ps", 1)], w=[("rowsTs", hs_)])
            yield 1.0

        NSET = 3
        import os as _os
        NLANE = int(_os.environ.get("K_NLANE", "3"))
        NHO = 6
        STAG = int(_os.environ.get("K_STAG", "5"))
        LANE_BANK = (6, 2, 3)

        def lane_ws():
            d = {}
            d["t"] = R4.alloc([256], F32)
            d["D"] = d["t"]
            d["W"] = [R4.alloc([512], BF16) for _ in range(2)]
            d["BN"] = [w_[:, 0:256] for w_ in d["W"]]
            d["Q"] = [w_[:, 256:384] for w_ in d["W"]]
            d["M"] = [R4.alloc([256], BF16) for _ in range(3)]
            d["XY"] = R4.alloc([256], BF16)
            d["QTs"] = R4.alloc([128], BF16)
            return d

        def handoff():
            d = {}
            d["ktok"] = R4.alloc([128], BF16)
            d["vb"] = R4.alloc([128], BF16)
            d["NTQK"] = R4.alloc([384], BF16)
            d["kcTn"] = R4.alloc([128], BF16)
            d["QT"] = R4.alloc([128], BF16)
            return d
        HO_NAMES = ("ktok", "vb", "NTQK", "kcTn", "QT")
        lanes = [lane_ws() for _ in range(NLANE)]
        hos = [handoff() for _ in range(NHO)]
        u_t = [R4.alloc([128], BF16) for _ in range(2)]
        us_t = [R4.alloc([128], BF16) for _ in range(2)]
        t2_t = [R4.alloc([128], F32) for _ in range(2)]
        o_t = [R4.alloc([128], F32) for _ in range(2)]
        on_t = [R4.alloc([128], BF16) for _ in range(2)]
        junk2 = R4.alloc([128], BF16)
        oss = R4.alloc([2], F32)
        Sf = [R4.alloc([128], F32) for _ in range(2)]
        Sb = [R4.alloc([128], BF16) for _ in range(2)]
        sidx = [0]
        cidx = [0]
        cp_done = set()
        cr_done = [0]
        stg = [0]

        def chunk_specs(h):
            sp = [(128, c * 128, c, False, 0) for c in range(NT)]
            sp += [(8, L + s * 8, NT + s, True, s) for s in range(NS)]
            return sp

        def CP_chunk(h, spec, cs, lane, slot):
            RB = LANE_BANK[lane]
            def KK(name, *rest):
                return ((("ho", slot) if name in HO_NAMES else ("lw", lane)), name) + rest
            C, t0, col, smp, s = spec
            hs_ = h % 2
            qT, kT, vT = cvo[hs_][0], cvo[hs_][1], cvo[hs_][2]
            cvk = [("cvo", hs_, j, bi) for j in range(3) for bi in range(5)]
            hk = lambda n: ("hs", hs_, "s" if smp else "p", n)
            ksl = kT[:, t0:t0 + C]
            def f1(e):
                e.transpose(PSB(4)[0:C, 0:128], ksl, ident_b)
                return e.transpose(PSB(4)[0:C, 128:256], vT[:, t0:t0 + C], ident_b)
            S.op("pe", f1, r=cvk + ["cb"], w=[("ps", 4)])
            S.op("act", lambda e: e.activation(out=cs["ktok"][0:C], in_=PSB(4)[0:C, 0:128], func=AF.Copy), r=[("ps", 4)], w=[KK("ktok")])
            bcol = (beta_t[0:C, 128 + s * 8 + h:128 + s * 8 + h + 1] if smp else beta_t[:, col * 8 + h:col * 8 + h + 1])
            S.op("dve", lambda e: e.tensor_scalar(out=cs["vb"][0:C], in0=PSB(4)[0:C, 128:256], scalar1=bcol, scalar2=None, op0=ALU.mult),
                 r=[("ps", 4), "beta_s" if smp else "beta_p"], w=[KK("vb")])
            def f2(e):
                e.matmul(PS(RB)[0:C, 0:C], ksl, qT[:, t0:t0 + C], start=True, stop=True)
                e.matmul(PS(RB)[0:C, 128:128 + C], ksl, ksl, start=True, stop=True)
                if smp:
                    e.matmul(PS(RB)[0:C, 256:256 + C], ident_f[0:4, s:s + 1].broadcast_to([4, C]), rowsTs[hs_][:, 0:8], start=True, stop=True)
                    return e.matmul(PS(RB)[0:C, 384:384 + C], ident_f[0:4, s:s + 1].broadcast_to([4, C]), rowsTs[hs_][:, 8:16], start=True, stop=True)
                return e.matmul(PS(RB)[:, 256:512], ident_f[0:16, col:col + 1].broadcast_to([16, 128]), rowsT[hs_], start=True, stop=True)
            S.op("pe", f2, r=cvk + ["cf", ("rowsTs" if smp else "rowsT", hs_)], w=[("ps", RB)])
            t3 = cs["t"][0:C, :].rearrange("p (a b) -> p a b", a=2)[:, :, 0:C]
            D3 = cs["D"][0:C, :].rearrange("p (a b) -> p a b", a=2)[:, :, 0:C]
            N3 = cs["NTQK"][0:C, 0:256].rearrange("p (a b) -> p a b", a=2)[:, :, 0:C]
            m3 = mask2[0:C, :].rearrange("p (a b) -> p a b", a=2)[:, :, 0:C]
            E3 = PS(RB)[0:C, 256:512].rearrange("p (a b) -> p a b", a=2)[:, :, 0:C]
            raw3 = PS(RB)[0:C, 0:256].rearrange("p (a b) -> p a b", a=2)[:, :, 0:C]
            S.op("dve", lambda e: e.tensor_tensor(out=t3, in0=E3, in1=m3, op=ALU.add), r=[("ps", RB), "cf"], w=[KK("t")])
            S.op("act", lambda e: e.activation(out=D3, in_=t3, func=AF.Exp, bias=biasj[hs_][0:C, col:col + 1], scale=1.0),
                 r=[KK("t"), hk("biasj")], w=[KK("t")])
            S.op("dve", lambda e: e.tensor_tensor(out=N3, in0=raw3, in1=D3, op=ALU.mult), r=[("ps", RB), KK("t")], w=[KK("NTQK")])
            yield
            Bm = cs["NTQK"][0:C, 128:128 + C]
            S.op("pe", lambda e: e.transpose(PSB(4)[0:C, 256:256 + C], Bm, ident_b[0:C, 0:C]), r=[KK("NTQK"), "cb"], w=[("ps", 4)])
            if C == 128:
                W = cs["W"]
                wk = lambda i: KK("W", i)
                S.op("act", lambda e: e.activation(out=cs["NTQK"][:, 256:384], in_=PSB(4)[:, 256:384], func=AF.Copy), r=[("ps", 4)], w=[KK("NTQK")])
                BNf = cs["NTQK"][:, 128:384]
                v2 = lambda ap: ap.rearrange("p (a b) -> p a b", a=2)
                bd2 = bd_b.unsqueeze(1).broadcast_to([128, 2, 128])
                id2 = ident_b.unsqueeze(1).broadcast_to([128, 2, 128])
                W0bn = W[0].rearrange("p (a b) -> p a b", a=4)[:, 1::2, :]
                W1qt = W[1].rearrange("p (a b) -> p a b", a=4)[:, 0::2, :]
                S.op("dve", lambda e: e.tensor_tensor(out=W0bn, in0=v2(BNf), in1=bd2, op=ALU.mult), r=[KK("NTQK"), "cb"], w=[wk(0)])
                S.op("dve", lambda e: e.tensor_tensor(out=W1qt, in0=id2, in1=W0bn, op=ALU.subtract), r=[wk(0), "cb"], w=[wk(1)])
                for lv in range(3):
                    S.op("dve", lambda e, lv=lv: e.tensor_tensor(out=cs["M"][lv], in0=BNf, in1=ml_b[lv], op=ALU.mult),
                         r=[KK("NTQK"), "cb"], w=[KK("M", lv)])
                yield
                cur = 0
                for k in range(4):
                    nxt = 1 - cur
                    last = (k == 3)
                    Wc = W[cur]
                    Bk = Wc[:, 128:256]
                    Nk = Wc[:, 384:512]
                    src = Wc if k > 0 else None
                    def fr(e, k=k, Wc=Wc, Bk=Bk, Nk=Nk, last=last):
                        if k == 0:
                            e.matmul(PS(RB)[:, 128:256], Nk, Bk, start=True, stop=True)
                            return e.matmul(PS(RB)[:, 384:512], Bk, Nk, start=True, stop=True)
                        if last:
                            e.matmul(PS(RB)[:, 0:128], Nk, Wc[:, 0:128], start=True, stop=True)
                            return e.matmul(PS(RB)[:, 256:384], Bk, Wc[:, 256:384], start=True, stop=True)
                        e.matmul(PS(RB)[:, 0:256], Nk, Wc[:, 0:256], start=True, stop=True)
                        return e.matmul(PS(RB)[:, 256:512], Bk, Wc[:, 256:512], start=True, stop=True)
                    S.op("pe", fr, r=[wk(cur)], w=[("ps", RB)])
                    P4 = PS(RB).rearrange("p (a b) -> p a b", a=4)
                    Wn4 = W[nxt].rearrange("p (a b) -> p a b", a=4)
                    Wc4 = Wc.rearrange("p (a b) -> p a b", a=4)
                    if k == 0:
                        S.op("act", lambda e, P4=P4, Wn4=Wn4: e.activation(out=Wn4[:, 1::2, :], in_=P4[:, 1::2, :], func=AF.Copy),
                             r=[("ps", RB)], w=[wk(nxt)])
                    else:
                        if not last:
                            S.op("act", lambda e, P4=P4, Wn4=Wn4: e.activation(out=Wn4[:, 1::2, :], in_=P4[:, 1::2, :], func=AF.Copy),
                                 r=[("ps", RB)], w=[wk(nxt)])
                        S.op("dve", lambda e, P4=P4, Wn4=Wn4, Wc4=Wc4: e.tensor_tensor(out=Wn4[:, 0::2, :], in0=P4[:, 0::2, :], in1=Wc4[:, 0::2, :], op=ALU.add),
                             r=[("ps", RB), wk(cur)], w=[wk(nxt)])
                    cur = nxt
                    yield
                Wc = W[cur]
                Qv = Wc[:, 0:128]
                Tv = Wc[:, 256:384]
                XY = cs["XY"]
                for lv in range(3):
                    lastl = (lv == 2)
                    Bm_l = cs["M"][lv][:, 0:128]
                    Nm_l = cs["M"][lv][:, 128:256]
                    def f1_(e, Bm_l=Bm_l, Nm_l=Nm_l, lastl=lastl):
                        r_ = e.matmul(PS(RB)[:, 0:128], Nm_l, Qv, start=True, stop=True)
                        if not lastl:
                            r_ = e.matmul(PS(RB)[:, 128:256], Bm_l, Tv, start=True, stop=True)
                        return r_
                    S.op("pe", f1_, r=[wk(cur), KK("M", lv)], w=[("ps", RB)])
                    nx = 128 if lastl else 256
                    S.op("act", lambda e, nx=nx: e.activation(out=XY[:, 0:nx], in_=PS(RB)[:, 0:nx], func=AF.Copy), r=[("ps", RB)], w=[KK("XY")])
                    def f2_(e, lastl=lastl):
                        r_ = e.matmul(PS(RB)[:, 256:384], Tv, XY[:, 0:128], start=True, stop=True)
                        if not lastl:
                            r_ = e.matmul(PS(RB)[:, 384:512], Qv, XY[:, 128:256], start=True, stop=True)
                        return r_
                    S.op("pe", f2_, r=[wk(cur), KK("XY")], w=[("ps", RB)])
                    if lastl:
                        S.op("dve", lambda e: e.tensor_tensor(out=cs["QT"], in0=Qv, in1=PS(RB)[:, 256:384], op=ALU.subtract),
                             r=[("ps", RB), wk(cur)], w=[KK("QT")])
                    else:
                        Wq = Wc.rearrange("p (a b) -> p a b", a=4)[:, 0::2, :]
                        S.op("dve", lambda e, Wq=Wq: e.tensor_tensor(out=Wq, in0=Wq, in1=PS(RB)[:, 256:512].rearrange("p (a b) -> p a b", a=2), op=ALU.subtract),
                             r=[("ps", RB), wk(cur)], w=[wk(cur)])
                    yield
                QT = cs["QT"]
                qkey = KK("QT")
            else:
                BN = cs["BN"]
                Q = cs["Q"]
                S.op("act", lambda e: e.activation(out=BN[0][0:C, 128:128 + C], in_=PSB(4)[0:C, 256:256 + C], func=AF.Copy), r=[("ps", 4)], w=[KK("W", 0)])
                S.op("pool", lambda e: e.tensor_copy(out=BN[0][0:C, 0:C], in_=Bm), r=[KK("NTQK")], w=[KK("W", 0)])
                S.op("pool", lambda e: e.tensor_tensor(out=Q[0][0:C, 0:C], in0=ident_b[0:C, 0:C], in1=Bm, op=ALU.subtract),
                     r=[KK("NTQK"), "cb"], w=[KK("W", 0)])
                nr = 7 if C == 128 else 3
                cur = 0
                for k in range(nr):
                    nxt = 1 - cur
                    Bk = BN[cur][0:C, 0:C]
                    Nk = BN[cur][0:C, 128:128 + C]
                    last = (k == nr - 1)
                    def fr(e, k=k, Bk=Bk, Nk=Nk, cur=cur, last=last):
                        r_ = None
                        if k > 0:
                            r_ = e.matmul(PS(RB)[0:C, 0:C], Nk, Q[cur][0:C, 0:C], start=True, stop=True)
                        if not last:
                            r_ = e.matmul(PS(RB)[0:C, 128:128 + C], Nk, Bk, start=True, stop=True)
                            r_ = e.matmul(PS(RB)[0:C, 256:256 + C], Bk, Nk, start=True, stop=True)
                        return r_
                    S.op("pe", fr, r=[KK("W", cur)], w=[("ps", RB)])
                    if not last:
                        S.op("act", lambda e, nxt=nxt: e.activation(
                            out=BN[nxt][0:C, :].rearrange("p (a b) -> p a b", a=2)[:, :, 0:C],
                            in_=PS(RB)[0:C, 128:384].rearrange("p (a b) -> p a b", a=2)[:, :, 0:C], func=AF.Copy),
                             r=[("ps", RB)], w=[KK("W", nxt)])
                    if k > 0:
                        S.op("dve", lambda e, cur=cur, nxt=nxt: e.tensor_tensor(out=Q[nxt][0:C, 0:C], in0=PS(RB)[0:C, 0:C], in1=Q[cur][0:C, 0:C], op=ALU.add),
                             r=[("ps", RB), KK("W", cur)], w=[KK("W", nxt)])
                    else:
                        S.op("pool", lambda e, cur=cur, nxt=nxt: e.tensor_copy(out=Q[nxt][0:C, 0:C], in_=Q[cur][0:C, 0:C]),
                             r=[KK("W", cur)], w=[KK("W", nxt)])
                    cur = nxt
                    yield
                S.op("pool", lambda e, cur=cur: e.tensor_copy(out=cs["QT"][0:C, 0:C], in_=Q[cur][0:C, 0:C]), r=[KK("W", cur)], w=[KK("QT")])
                QT = cs["QT"][0:C, 0:C]
                qkey = KK("QT")
            S.op("dve", lambda e: e.tensor_scalar(out=cs["QTs"][0:C, 0:C], in0=QT, scalar1=kbg_s[hs_][0:C, col:col + 1], scalar2=None, op0=ALU.mult),
                 r=[qkey, hk("kbg_s")], w=[KK("QTs")])
            S.op("pe", lambda e: e.matmul(PS(4)[:, 256:256 + C], cs["ktok"][0:C, :], cs["QTs"][0:C, 0:C], start=True, stop=True),
                 r=[KK("ktok"), KK("QTs")], w=[("ps", 4)])
            S.op("act", lambda e: e.activation(out=cs["kcTn"][:, 0:C], in_=PS(4)[:, 256:256 + C], func=AF.Copy, scale=-1.0),
                 r=[("ps", 4)], w=[KK("kcTn")])
            yield

        def CR_chunk(h, spec, cs, slot, Sfl, Sbf, skey, n):
            def KK(name, *rest):
                return (("ho", slot), name) + rest
            C, t0, col, smp, s = spec
            hs_ = h % 2
            qT = cvo[hs_][0]
            zT = cvo[hs_][3]
            cvk = [("cvo", hs_, j, bi) for j in (0, 3) for bi in range(5)]
            hk = lambda nm: ("hs", hs_, "s" if smp else "p", nm)
            QT = cs["QT"][0:C, 0:C]
            qk = KK("QT")
            x = n % 2
            def fu(e):
                e.matmul(PS(7)[0:C, 0:128], QT, cs["vb"][0:C, :], start=True, stop=False)
                return e.matmul(PS(7)[0:C, 0:128], cs["kcTn"][:, 0:C], Sbf, start=False, stop=True)
            S.op("pe", fu, r=[qk, KK("vb"), KK("kcTn"), skey + ("b",)], w=[("ps", 7)])
            S.op("act", lambda e: e.activation(out=u_t[x][0:C], in_=PS(7)[0:C, 0:128], func=AF.Copy), r=[("ps", 7)], w=[("u_t", x)])
            S.op("dve", lambda e: e.tensor_scalar(out=us_t[x][0:C], in0=PS(7)[0:C, 0:128], scalar1=kdec_s[hs_][0:C, col:col + 1], scalar2=None, op0=ALU.mult),
                 r=[("ps", 7), hk("kdec_s")], w=[("us_t", x)])
            yield
            def fo(e):
                e.matmul(PS(7)[0:C, 128:256], qT[:, t0:t0 + C], Sbf, start=True, stop=True)
                e.matmul(PS(7)[0:C, 256:384], cs["NTQK"][0:C, 0:C], u_t[x][0:C], start=True, stop=True)
                return e.matmul(PS(7)[:, 384:512], cs["ktok"][0:C, :], us_t[x][0:C], start=True, stop=True)
            S.op("pe", fo, r=cvk + [skey + ("b",), KK("NTQK"), ("u_t", x), KK("ktok"), ("us_t", x)], w=[("ps", 7)])
            gcolumn = (gtot_t[:, 128 + s * 8 + h:128 + s * 8 + h + 1] if smp else gtot_t[:, col * 8 + h:col * 8 + h + 1])
            S.op("dve", lambda e: e.scalar_tensor_tensor(out=Sfl, in0=Sfl, scalar=gcolumn, in1=PS(7)[:, 384:512], op0=ALU.mult, op1=ALU.add),
                 r=[("ps", 7), skey + ("f",), "gtot_s" if smp else "gtot_p"], w=[skey + ("f",)])
            S.op("act", lambda e: e.activation(out=Sbf, in_=Sfl, func=AF.Copy), r=[skey + ("f",)], w=[skey + ("b",)])
            S.op("act", lambda e: e.activation(out=t2_t[x][0:C], in_=PS(7)[0:C, 256:384], func=AF.Copy), r=[("ps", 7)], w=[("t2", x)])
            S.op("dve", lambda e: e.scalar_tensor_tensor(out=o_t[x][0:C], in0=PS(7)[0:C, 128:256], scalar=qdec_s[hs_][0:C, col:col + 1],
                                                         in1=t2_t[x][0:C], op0=ALU.mult, op1=ALU.add),
                 r=[("ps", 7), ("t2", x), hk("qdec_s")], w=[("o_t", x)])
            yield
            S.op("act", lambda e: e.activation(out=junk2[0:C], in_=o_t[x][0:C], func=AF.Square, accum_out=oss[0:C, x:x + 1]),
                 r=[("o_t", x)], w=["junk2", ("oss", x)])
            S.op("pool", lambda e: e.tensor_scalar(out=oss[0:C, x:x + 1], in0=oss[0:C, x:x + 1], scalar1=1.0 / 128, scalar2=EPS, op0=ALU.mult, op1=ALU.add),
                 r=[("oss", x)], w=[("oss", x)])
            S.op("pool", lambda e: e.tensor_tensor(out=oss[0:C, x:x + 1], in0=oss[0:C, x:x + 1], in1=nhalf_c[0:C], op=ALU.pow),
                 r=[("oss", x), "cf"], w=[("oss", x)])
            S.op("act", lambda e: e.activation(out=on_t[x][0:C], in_=o_t[x][0:C], func=AF.Identity, scale=oss[0:C, x:x + 1]),
                 r=[("o_t", x), ("oss", x)], w=[("on_t", x)])
            S.op("pe", lambda e: e.transpose(PSB(5)[:, 0:C], on_t[x][0:C, :], ident_b[0:C, 0:C]), r=[("on_t", x), "cb"], w=[("ps", 5)])
            S.op("dve", lambda e: e.scalar_tensor_tensor(out=ogT[:, h, t0:t0 + C], in0=PSB(5)[:, 0:C], scalar=outg_h,
                                                         in1=zT[:, t0:t0 + C], op0=ALU.mult, op1=ALU.mult),
                 r=[("ps", 5), "outg_h"] + cvk, w=[("ogT", h, t0)])
            yield

        def CP_lane(h, lane):
            specs = chunk_specs(h)
            mine = list(range(lane, len(specs), NLANE))
            for _ in range(lane * STAG):
                yield 0.0
            for k_, n in enumerate(mine):
                spec = specs[n]
                gi = cidx[0] + n
                while gi - cr_done[0] >= NHO:
                    yield WAIT
                slot = gi % NHO
                cs = dict(lanes[lane])
                cs.update(hos[slot])
                for yi, _ in enumerate(CP_chunk(h, spec, cs, lane, slot)):
                    yield (k_ + min(0.95, (yi + 1) / 14.0)) / len(mine)
                cp_done.add(gi)
                yield (k_ + 1.0) / len(mine)

        def CR_head(h):
            specs = chunk_specs(h)
            yield 0.0
            for n, spec in enumerate(specs):
                C, t0, col, smp, s = spec
                gi = cidx[0] + n
                while gi not in cp_done:
                    yield WAIT
                slot = gi % NHO
                cs = hos[slot]
                if n == 0 or smp:
                    si = sidx[0] % 2
                    sidx[0] += 1
                    Sfl, Sbf, skey = Sf[si], Sb[si], ("S", si)
                    if smp:
                        S.dma("sp", Sfl, I["sdelta"][s, h], w=[skey + ("f",)])
                        S.op("act", lambda e, Sbf=Sbf, Sfl=Sfl: e.activation(out=Sbf, in_=Sfl, func=AF.Copy), r=[skey + ("f",)], w=[skey + ("b",)])
                    else:
                        S.op("pool", lambda e, Sfl=Sfl: e.memset(Sfl, 0.0), w=[skey + ("f",)])
                        S.op("pool", lambda e, Sbf=Sbf: e.memset(Sbf, 0.0), w=[skey + ("b",)])
                for yi, _ in enumerate(CR_chunk(h, spec, cs, slot, Sfl, Sbf, skey, gi)):
                    yield (n + (yi + 1) / 4.0) / len(specs) - 0.08
                if n == NT - 1 or smp:
                    dst = O["ds"][s, h] if smp else O["dp"][h]
                    S.dma("sp", dst, Sfl, r=[skey + ("f",)])
                cr_done[0] = gi + 1
                yield (n + 1.0) / len(specs) - 0.08

        S.barrier()
        run_tasks([P_head(0)])
        if stop == "P0":
            S.finish(); print("STOP P0", S.ninst); return nc
        for h in range(nheads):
            gens = [CP_lane(h, ln) for ln in range(NLANE)] + [CR_head(h)]
            if h + 1 < nheads:
                gens.append(P_head(h + 1))
            run_tasks(gens)
            cidx[0] += NT + NS
            if stop == ("H", h):
                S.finish(); print("STOP H", h, S.ninst); return nc
        S.barrier()

        if dbg:
            S.dma("sp", O["dbg_og"], ogT, r=[("ogT", h, t * 128) for h in range(8) for t in range(16)] + [("ogT", h, L + s * 8) for h in range(8) for s in range(NS)])
            S.dma("sp", O["dbg_hT"], hT, r=HT_KEYS)
            S.barrier()
        R1.reset(); R3.reset(); R4.reset()
        x1 = R1.alloc([16, D], F32)
        wo = R3.alloc([8, D], BF16)
        xst2 = [R3.alloc([D], F32) for _ in range(2)]
        xss = R4.alloc([D], F32, parts=32)
        ysb0 = R4.alloc([D], F32, parts=32)

        def wout_phase(l, wsrc, x_in, x_out_fn):
            S.dma("pool", wo, wsrc.rearrange("(c p) n -> p c n", p=128), w=["wo"])
            wos = wo
            if l == 0:
                S.dma("sp", xss, I["xs"], w=["xss"])
            for hb in range(2):
                def f(e, hb=hb):
                    last = None
                    for ec in range(8):
                        last = e.matmul(PS(hb)[0:32, :], ogT[:, ec, L:TT], wo[:, ec, hb * 512:(hb + 1) * 512], start=(ec == 0), stop=(ec == 7))
                    return last
                S.op("pe", f, r=["wo"] + [("ogT", h, L + s * 8) for h in range(8) for s in range(NS)], w=[("ps", hb)])
                ysb = xss if l == 1 else ysb0
                S.op("dve", lambda e, hb=hb, ysb=ysb: e.tensor_tensor(out=ysb[:, hb * 512:(hb + 1) * 512], in0=PS(hb)[0:32, :], in1=gate_s[:, hb * 512:(hb + 1) * 512], op=ALU.mult),
                     r=[("ps", hb), ("gate_s", l)], w=[("ysb", hb)])
                src = xss if l == 0 else x1s
                S.op("dve", lambda e, hb=hb, src=src, ysb=ysb: e.tensor_tensor(out=x1s[:, hb * 512:(hb + 1) * 512], in0=ysb[:, hb * 512:(hb + 1) * 512],
                                                                       in1=src[:, hb * 512:(hb + 1) * 512], op=ALU.add),
                     r=[("ysb", hb), "xss", ("x1s", hb)], w=[("x1s", hb)])
            for ec in range(8):
                S.op("pool", lambda e, ec=ec: e.tensor_tensor(out=wo[:, ec, :], in0=wo[:, ec, :], in1=gate_p, op=ALU.mult),
                     r=["wo", ("gate_p", l)], w=["wo"])
            for t in range(16):
                xb, xkey = x_in(t)
                for hb in range(2):
                    bk = (2 * t + hb) % 4
                    def f(e, hb=hb, bk=bk, t=t):
                        last = None
                        for ec in range(8):
                            last = e.matmul(PS(bk), ogT[:, ec, t * 128:(t + 1) * 128], wo[:, ec, hb * 512:(hb + 1) * 512], start=(ec == 0), stop=(ec == 7))
                        return last
                    S.op("pe", f, r=["wo"] + [("ogT", h, t * 128) for h in range(8)], w=[("ps", bk)])
                    S.op("dve", lambda e, hb=hb, bk=bk, t=t, xb=xb: e.tensor_tensor(out=x_out_fn(t)[:, hb * 512:(hb + 1) * 512], in0=PS(bk),
                                                                                     in1=xb[:, hb * 512:(hb + 1) * 512], op=ALU.add),
                         r=[("ps", bk)] + (xkey if isinstance(xkey, list) else [xkey]), w=[("x1", t, hb)])

        def x_in0(t):
            buf = xst2[t % 2]
            key = ("xst2", t % 2)
            S.dma("sp", buf, I["xp"][t * 128:(t + 1) * 128, :], w=[key])
            return buf, key

        wout_phase(0, I["awout"], x_in0, lambda t: x1[:, t, :])
        if dbg:
            for t in range(16):
                S.dma("sp", O["dbg_x1p"][t * 128:(t + 1) * 128, :], x1[:, t, :], r=[("x1", t, 0), ("x1", t, 1)])
            S.dma("sp", O["dbg_x1s"], x1s, r=[("x1s", 0), ("x1s", 1)])
        if not do_l1:
            S.finish()
            print("instructions:", S.ninst, "sems:", S.nsem)
            return nc
        ada_phase(1, gate_p, gate_s, R4, parts=("col",))
        S.barrier()
        R3.reset(); R4.reset()
        hT1 = R3.alloc([8, TT], BF16)
        RE = Region(A, NB - 12288, NB - 4096)
        qs_all = RE.alloc([8, 3, 32], BF16)
        zs_all = RE.alloc([8, 32], BF16)
        onew = RE.alloc([8, 32], F32)
        lnew = RE.alloc([8, 32], F32)
        ptn = RE.alloc([32], BF16, parts=32)

        def x_src1(t):
            if t < 16:
                return x1[:, t, :], [("x1", t, 0), ("x1", t, 1)]
            return x1s, [("x1s", 0), ("x1s", 1)]

        norm_phase(1, hT1, x_src1, R4)
        S.barrier()
        R4.reset()
        GRP = ((128, 1), (512, 4), (2048, 16))
        SCALE = 128.0 ** -0.5
        Wg = [R4.alloc([8, 384], BF16) for _ in range(2)]
        Wz = RC.alloc([8, 128], BF16)
        QK_ = [R4.alloc([2, TT], BF16) for _ in range(3)]
        QT_ = [qk_[:, 0, :] for qk_ in QK_]
        KT_ = [qk_[:, 1, :] for qk_ in QK_]
        qkb = [R4.alloc([256], BF16) for _ in range(2)]
        Vt_ = [R4.alloc([17, 128], BF16) for _ in range(3)]
        zTh = R4.alloc([1024], BF16)
        PTb = [R4.alloc([512], BF16) for _ in range(2)]
        stg1_ = R4.alloc([256], F32)
        stg_ = [stg1_, stg1_]
        fin1_ = R4.alloc([512], F32)
        fin_ = [fin1_, fin1_]
        print("L1 R4 used", R4.cur - R4.lo, "of", R4.hi - R4.lo)
        KVO = [O["kv0p"], O["kv1p"], O["kv2p"]]
        wcount = [0]
        stc = [0]
        ptc = [0]
        fnc = [0]

        def tok_ap(g, ti):
            d = GRP[g][1]
            if d == 1:
                return lambda kc: hT1[:, kc, ti * 128:(ti + 1) * 128]
            if d == 4:
                r, nb = ti // 4, ti % 4
                return lambda kc: hT1[:, kc, 0:L].rearrange("p (j r) -> p r j", r=4)[:, r, nb * 128:(nb + 1) * 128]
            return lambda kc: hT1[:, kc, 0:L].rearrange("p (j r) -> p r j", r=16)[:, ti, :]

        def blk_ap(g, blk):
            d = GRP[g][1]
            if d == 1:
                return lambda kc: hT1[:, kc, blk * 512:(blk + 1) * 512]
            if d == 4:
                return lambda kc: hT1[:, kc, 0:L].rearrange("p (j r) -> p r j", r=4)[:, blk, :]
            return lambda kc: hT1[:, kc, 0:L].rearrange("p (j r) -> p r j", r=16)[:, 4 * blk:4 * blk + 4, :]

        def kept_rows(g, ti):
            w_, d = GRP[g]
            if d == 1:
                return (0, 1) if ti == 15 else None
            if d == 4:
                r, nb = ti // 4, ti % 4
                return (r, 4) if nb == 3 else None
            return (ti, 16)

        def L1_proj(h, g):
            d = GRP[g][1]
            wb = Wg[wcount[0] % 2]
            wkey = ("Wg", wcount[0] % 2)
            wcount[0] += 1
            for t_ in range(3):
                col = t_ * 3072 + g * 1024 + h * 128
                S.dma("pool", wb[:, :, t_ * 128:(t_ + 1) * 128], I["bwin"][:, col:col + 128].rearrange("(c p) n -> p c n", p=128),
                      w=[wkey + (t_,)])
            wk = [wkey + (t_,) for t_ in range(3)]
            pend = []

            def emit_tr(ti, p, qb, qbk):
                tb_ = 2 + (ti % 2)
                def ftp(e):
                    e.transpose(PSB(tb_)[:, 0:p], qb[0:p, 0:128], ident_b[0:p, 0:p])
                    return e.transpose(PSB(tb_)[:, 128:128 + p], qb[0:p, 128:256], ident_b[0:p, 0:p])
                S.op("pe", ftp, r=[qbk, "cb"], w=[("ps", tb_)])
                t0 = ti * 128 if ti < 16 else L
                dst = QK_[g][:, :, t0:t0 + p]
                src = PSB(tb_)[:, 0:256].rearrange("p (a b) -> p a b", a=2)[:, :, 0:p]
                if ti % 2 == 1:
                    S.op("act", lambda e: e.activation(out=dst, in_=src, func=AF.Copy), r=[("ps", tb_)], w=[("qkT", g, ti)])
                else:
                    S.op("dve", lambda e: e.tensor_copy(out=dst, in_=src), r=[("ps", tb_)], w=[("qkT", g, ti)])

            for ti in range(17):
                bk = ti % 2
                p = 128 if ti < 16 else 32
                ap = tok_ap(g, ti) if ti < 16 else (lambda kc: hT1[:, kc, L:TT])
                def f(e, ap=ap, bk=bk, p=p):
                    last = None
                    for kc in range(8):
                        last = e.matmul(PS(bk)[0:p, 0:384], ap(kc), wb[:, kc, 0:384], start=(kc == 0), stop=(kc == 7))
                    return last
                S.op("pe", f, r=wk + HT_KEYS, w=[("ps", bk)])
                qb = qkb[ti % 2]
                qbk = ("qkb", ti % 2)
                S.op("act", lambda e, bk=bk, p=p, qb=qb: e.activation(out=qb[0:p], in_=PS(bk)[0:p, 0:256], func=AF.Copy), r=[("ps", bk)], w=[qbk])
                S.op("dve", lambda e, bk=bk, p=p, ti=ti: e.tensor_copy(out=Vt_[g][0:p, ti, :], in_=PS(bk)[0:p, 256:384]),
                     r=[("ps", bk)], w=[("vt", g, ti)])
                kr = kept_rows(g, ti) if ti < 16 else None
                if kr is not None or ti == 16:
                    sb = stg_[stc[0] % 2]
                    skey = ("stg", 0)
                    stc[0] += 1
                    S.op("dve", lambda e, sb=sb, bk=bk, p=p: e.tensor_copy(out=sb[0:p], in_=PS(bk)[0:p, 128:384]), r=[("ps", bk)], w=[skey])
                    if ti < 16:
                        r0, st = kr
                        dst = KVO[g].rearrange("(q s) k hh e -> s q k hh e", s=st)[r0, :, :, h, :]
                        S.dma("sp", dst, sb.rearrange("p (k e) -> p k e", k=2), r=[skey])
                    else:
                        dst = O["kv%ds" % g].rearrange("s l k hh e -> (s l) k hh e")[:, :, h, :]
                        S.dma("sp", dst, sb[0:32].rearrange("p (k e) -> p k e", k=2), r=[skey])
                pend.append((ti, p, qb, qbk))
                if len(pend) > 1:
                    emit_tr(*pend.pop(0))
                yield
            while pend:
                emit_tr(*pend.pop(0))
            yield

        def L1_units(g, hf):
            d = GRP[g][1]
            units = []
            if d == 1:
                for nb in range(8 * hf, 8 * hf + 8):
                    col0 = 128 * (nb - 8 * hf)
                    outs = [(col0 // 512, (lambda B, c=col0 % 512: B[:, c:c + 128]), 0, 128)]
                    q = QT_[g][:, nb * 128:(nb + 1) * 128]
                    for pv, kt in ((True, nb - 1), (False, nb)):
                        if kt < 0:
                            continue
                        units.append((KT_[g][:, kt * 128:(kt + 1) * 128], Vt_[g][:, kt, :], 128, mT_prev if pv else mT_cur, q, 128, outs, ("vt", g, kt)))
            elif d == 4:
                for r in range(4):
                    for nb in range(2 * hf, 2 * hf + 2):
                        outs = [(nb - 2 * hf, (lambda B, r=r: B.rearrange("p (q r) -> p r q", r=4)[:, r, :]), 0, 128)]
                        q = QT_[g][:, r * 512 + nb * 128:r * 512 + (nb + 1) * 128]
                        for pv, kb in ((True, nb - 1), (False, nb)):
                            if kb < 0:
                                continue
                            kt = r * 4 + kb
                            units.append((KT_[g][:, kt * 128:(kt + 1) * 128], Vt_[g][:, kt, :], 128, mT_prev if pv else mT_cur, q, 128, outs, ("vt", g, kt)))
            else:
                for r in range(16):
                    outs = [(m, (lambda B, r=r: B.rearrange("p (j r) -> p r j", r=16)[:, r, :]), 32 * m, 32) for m in range(2)]
                    q = QT_[g][:, r * 128 + 64 * hf:r * 128 + 64 * hf + 64]
                    if hf == 0:
                        units.append((KT_[g][:, r * 128:r * 128 + 64], Vt_[g][0:64, r, :], 64, mT_cur[0:64, 0:64], q, 64, outs, ("vt", g, r)))
                    else:
                        units.append((KT_[g][:, r * 128:(r + 1) * 128], Vt_[g][:, r, :], 128, mT_cur[:, 64:128], q, 64, outs, ("vt", g, r)))
            return units

        OB = (4, 5)
        LB = (6, 7)

        def L1_pass(h, hf):
            for b_ in OB + LB:
                S.op("dve", lambda e, b_=b_: e.memset(PS(b_), 0.0), w=[("ps", b_)])
            for m in range(2):
                def f(e, m=m):
                    last = None
                    for kc in range(8):
                        last = e.matmul(PS(m), Wz[:, kc, :], hT1[:, kc, 1024 * hf + 512 * m:1024 * hf + 512 * (m + 1)], start=(kc == 0), stop=(kc == 7))
                    return last
                S.op("pe", f, r=["Wz"] + HT_KEYS, w=[("ps", m)])
                S.op("act", lambda e, m=m: e.activation(out=zTh[:, 512 * m:512 * (m + 1)], in_=PS(m), func=AF.Silu), r=[("ps", m)], w=[("zTh", m)])
            yield
            allu = []
            for g in range(3):
                allu += [(g, u) for u in L1_units(g, hf)]
            for i0 in range(0, len(allu), 4):
                grp = allu[i0:i0 + 4]
                sb_ = 2 + (ptc[0] % 2)
                pt = PTb[ptc[0] % 2]
                pkey = ("PT", ptc[0] % 2)
                ptc[0] += 1
                rk = []
                def fs(e, grp=grp, sb_=sb_):
                    last = None
                    for ui, (g, (kT, v, nk, mk, q, nq, outs, vkey)) in enumerate(grp):
                        o_ = PS(sb_)[0:nk, ui * 128:ui * 128 + nq]
                        e.matmul(o_, kT, q, start=True, stop=False)
                        last = e.matmul(o_, ident_b[0:nk, 0:nk], mk, start=False, stop=True)
                    return last
                for (g, u) in grp:
                    rk += [("qkT", g, ti_) for ti_ in range(16)]
                S.op("pe", fs, r=list(set(rk)) + ["cb"], w=[("ps", sb_)])
                S.op("act", lambda e, sb_=sb_, pt=pt: e.activation(out=pt, in_=PS(sb_), func=AF.Exp, scale=SCALE), r=[("ps", sb_)], w=[pkey])
                def fp(e, grp=grp, pt=pt):
                    last = None
                    for ui, (g, (kT, v, nk, mk, q, nq, outs, vkey)) in enumerate(grp):
                        for (bi_, ofn, c0, n_) in outs:
                            rhs = pt[0:nk, ui * 128 + c0:ui * 128 + c0 + n_]
                            e.matmul(ofn(PS(OB[bi_])), v, rhs, start=False, stop=False, skip_group_check=True)
                            last = e.matmul(ofn(PS(LB[bi_])), ones_b[0:nk, :], rhs, start=False, stop=False, skip_group_check=True)
                    return last
                S.op("pe", fp, r=[pkey, "cb"] + [u[7] for (_, u) in grp], w=[("ps", b_) for b_ in OB + LB])
                yield
            for m in range(2):
                fb = fin_[fnc[0] % 2]
                fkey = ("fin", 0)
                fnc[0] += 1
                S.op("act", lambda e, fb=fb, m=m: e.activation(out=fb, in_=PS(LB[m]), func=AF.Ln), r=[("ps", LB[m])], w=[fkey])
                S.op("act", lambda e, fb=fb: e.activation(out=fb, in_=fb, func=AF.Exp, scale=-1.0), r=[fkey], w=[fkey])
                S.op("dve", lambda e, fb=fb, m=m: e.tensor_tensor(out=fb, in0=PS(OB[m]), in1=fb, op=ALU.mult), r=[("ps", OB[m]), fkey], w=[fkey])
                p0 = 1024 * hf + 512 * m
                S.op("dve", lambda e, fb=fb, m=m, p0=p0: e.tensor_tensor(out=ogT[:, h, p0:p0 + 512], in0=fb, in1=zTh[:, 512 * m:512 * (m + 1)], op=ALU.mult),
                     r=[fkey, ("zTh", m)], w=[("ogT", h, p0 // 128 * 128 + i * 128) for i in range(4)])
            yield

        def L1_newrows(h):
            for g in range(3):
                S.op("pool", lambda e, g=g: e.tensor_copy(out=qs_all[:, h, g, :], in_=QT_[g][:, L:TT]), r=[("qkT", g, 16)], w=[("qs", h, g)])
                def fsn(e, g=g):
                    e.matmul(PS(2)[0:32, 0:32], KT_[g][:, L:TT], QT_[g][:, L:TT], start=True, stop=False)
                    return e.matmul(PS(2)[0:32, 0:32], ident_b[0:32, 0:32], cbf[0:32, CB_MN + g * 32:CB_MN + (g + 1) * 32], start=False, stop=True)
                S.op("pe", fsn, r=[("qkT", g, 16), "cb"], w=[("ps", 2)])
                S.op("act", lambda e: e.activation(out=ptn, in_=PS(2)[0:32, 0:32], func=AF.Exp, scale=SCALE), r=[("ps", 2)], w=["ptn"])
                def fpn(e, g=g):
                    e.matmul(PS(4)[:, 0:32], Vt_[g][0:32, 16, :], ptn, start=(g == 0), stop=(g == 2))
                    return e.matmul(PS(6)[:, 0:32], ones_b[0:32, :], ptn, start=(g == 0), stop=(g == 2))
                S.op("pe", fpn, r=["ptn", ("vt", g, 16), "cb"], w=[("ps", 4), ("ps", 6)])
            S.op("act", lambda e: e.activation(out=onew[:, h, :], in_=PS(4)[:, 0:32], func=AF.Copy), r=[("ps", 4)], w=[("onew", h)])
            S.op("dve", lambda e: e.tensor_copy(out=lnew[:, h, :], in_=PS(6)[:, 0:32]), r=[("ps", 6)], w=[("lnew", h)])
            def fz(e):
                last = None
                for kc in range(8):
                    last = e.matmul(PS(0)[:, 0:32], Wz[:, kc, :], hT1[:, kc, L:TT], start=(kc == 0), stop=(kc == 7))
                return last
            S.op("pe", fz, r=["Wz"] + HT_KEYS, w=[("ps", 0)])
            S.op("act", lambda e: e.activation(out=zs_all[:, h, :], in_=PS(0)[:, 0:32], func=AF.Silu), r=[("ps", 0)], w=[("zs", h)])

        def L1_head(h):
            S.dma("pool", Wz, I["bwin"][:, 9216 + h * 128:9216 + (h + 1) * 128].rearrange("(c p) n -> p c n", p=128), w=["Wz"])
            for g in range(3):
                for _ in L1_proj(h, g):
                    yield
            L1_newrows(h)
            yield
            for hf in range(2):
                for _ in L1_pass(h, hf):
                    yield

        for h in range(nheads):
            for _ in L1_head(h):
                pass
        S.barrier()

        R4.reset()
        NPF = 3
        NCL = (1, 4, 8)
        ckb = [[R4.alloc([NCL[g], 2, 128], BF16) for g in range(3)] for _ in range(NPF)]
        kTc = [R4.alloc([13, 128], BF16) for _ in range(2)]
        pts = [R4.alloc([24], BF16) for _ in range(2)]
        fo_ = R4.alloc([32], F32)
        fl_ = R4.alloc([32], F32)
        items = [(h, s_) for h in range(nheads) for s_ in range(NS)]

        def e2_load(i):
            h, s_ = items[i]
            for g in range(3):
                d = GRP[g][1]
                src = I["kc%d" % g][s_].rearrange("(k r) kv hh e -> k r kv hh e", r=d)[:, 0:NCL[g], :, h, :]
                S.dma("pool", ckb[i % NPF][g], src, w=[("ck", i % NPF, g)])

        for i in range(min(NPF - 1, len(items))):
            e2_load(i)
        for i, (h, s_) in enumerate(items):
            if i + NPF - 1 < len(items):
                e2_load(i + NPF - 1)
            buf = ckb[i % NPF]
            ckeys = [("ck", i % NPF, g) for g in range(3)]
            kt = kTc[i % 2]
            ktk = ("kTc", i % 2)
            pt = pts[i % 2]
            ptk = ("pts", i % 2)
            sb_ = 2 + (i % 2)
            tiles = [(0, 0)] + [(1, r) for r in range(4)] + [(2, r) for r in range(8)]
            if s_ == 0:
                S.op("dve", lambda e: e.memset(PS(4)[:, 0:32], 0.0), w=[("ps", 4)])
                S.op("dve", lambda e: e.memset(PS(6)[:, 0:32], 0.0), w=[("ps", 6)])
            def ftr(e, buf=buf):
                last = None
                for ti, (g, r) in enumerate(tiles):
                    last = e.transpose(PSB(ti // 8)[:, (ti % 8) * 128:(ti % 8 + 1) * 128], buf[g][:, r, 0, :], ident_b)
                return last
            S.op("pe", ftr, r=ckeys + ["cb"], w=[("ps", 0), ("ps", 1)])
            S.op("act", lambda e, kt=kt: e.activation(out=kt[:, 0:8, :], in_=PSB(0).rearrange("p (a b) -> p a b", a=8), func=AF.Copy), r=[("ps", 0)], w=[ktk + (0,)])
            S.op("dve", lambda e, kt=kt: e.tensor_copy(out=kt[:, 8:13, :], in_=PSB(1)[:, 0:640].rearrange("p (a b) -> p a b", a=5)), r=[("ps", 1)], w=[ktk + (1,)])
            def cls(g, r):
                d = GRP[g][1]
                nq = 8 // d if d <= 8 else 1
                c0 = (0, 8 + 2 * r, 16 + r)[g]
                if d == 1:
                    q = qs_all[:, h, g, 8 * s_:8 * s_ + 8]
                    mk = cbf[:, CB_MK:CB_MK + 8]
                    oc = lambda B: B[:, 8 * s_:8 * s_ + 8]
                elif d == 4:
                    q = qs_all[:, h, g, 8 * s_:8 * s_ + 8].rearrange("p (i r) -> p r i", r=4)[:, r, :]
                    mk = cbf[:, CB_MK + 8:CB_MK + 16].rearrange("p (i r) -> p r i", r=4)[:, r, :]
                    oc = lambda B: B[:, 8 * s_:8 * s_ + 8].rearrange("p (i r) -> p r i", r=4)[:, r, :]
                else:
                    q = qs_all[:, h, g, 8 * s_ + r:8 * s_ + r + 1]
                    mk = None
                    oc = lambda B: B[:, 8 * s_ + r:8 * s_ + r + 1]
                return nq, c0, q, mk, oc
            def fsc(e, kt=kt, sb_=sb_):
                last = None
                for ti, (g, r) in enumerate(tiles):
                    nq, c0, q, mk, oc = cls(g, r)
                    o_ = PS(sb_)[:, c0:c0 + nq]
                    last = e.matmul(o_, kt[:, ti, :], q, start=True, stop=(mk is None))
                    if mk is not None:
                        last = e.matmul(o_, ident_b, mk, start=False, stop=True)
                return last
            S.op("pe", fsc, r=[ktk + (0,), ktk + (1,), "cb"] + [("qs", h, g) for g in range(3)], w=[("ps", sb_)])
            S.op("act", lambda e, pt=pt, sb_=sb_: e.activation(out=pt, in_=PS(sb_)[:, 0:24], func=AF.Exp, scale=SCALE), r=[("ps", sb_)], w=[ptk])
            def fpv(e, buf=buf, pt=pt):
                last = None
                for ti, (g, r) in enumerate(tiles):
                    nq, c0, q, mk, oc = cls(g, r)
                    e.matmul(oc(PS(4)), buf[g][:, r, 1, :], pt[:, c0:c0 + nq], start=False, stop=False, skip_group_check=True)
                    last = e.matmul(oc(PS(6)), ones_b, pt[:, c0:c0 + nq], start=False, stop=False, skip_group_check=True)
                return last
            S.op("pe", fpv, r=ckeys + [ptk, "cb"], w=[("ps", 4), ("ps", 6)])
            if s_ == NS - 1:
                S.op("dve", lambda e, h=h: e.tensor_tensor(out=fo_, in0=PS(4)[:, 0:32], in1=onew[:, h, :], op=ALU.add), r=[("ps", 4), ("onew", h)], w=["fo_"])
                S.op("dve", lambda e, h=h: e.tensor_tensor(out=fl_, in0=PS(6)[:, 0:32], in1=lnew[:, h, :], op=ALU.add), r=[("ps", 6), ("lnew", h)], w=["fl_"])
                S.op("act", lambda e: e.activation(out=fl_, in_=fl_, func=AF.Ln), r=["fl_"], w=["fl_"])
                S.op("act", lambda e: e.activation(out=fl_, in_=fl_, func=AF.Exp, scale=-1.0), r=["fl_"], w=["fl_"])
                S.op("dve", lambda e: e.tensor_tensor(out=fo_, in0=fo_, in1=fl_, op=ALU.mult), r=["fo_", "fl_"], w=["fo_"])
                S.op("dve", lambda e, h=h: e.tensor_tensor(out=ogT[:, h, L:TT], in0=fo_, in1=zs_all[:, h, :], op=ALU.mult),
                     r=["fo_", ("zs", h)], w=[("ogT", h, L + q_ * 8) for q_ in range(NS)])
        S.barrier()
        R3.reset()
        ada_phase(1, gate_p, gate_s, R3, parts=("gate",))
        S.barrier()

        R4.reset()
        xss = R4.alloc([D], F32, parts=32)
        fng_bc = R4.alloc([D], F32)
        S.dma("sp", fng_bc, I["fng"].partition_broadcast(128), w=["fng"])
        wout_phase(1, I["bwout"], lambda t: (x1[:, t, :], [("x1", t, 0), ("x1", t, 1)]), lambda t: x1[:, t, :])
        ssq2 = R4.alloc([17], F32)
        junk3 = R4.alloc([D], BF16)
        ost = [R4.alloc([D], F32) for _ in range(2)]
        for t in range(17):
            p = 128 if t < 16 else 32
            xb, xk = x_src1(t)
            S.op("act", lambda e, xb=xb, p=p, t=t: e.activation(out=junk3[0:p], in_=xb[0:p], func=AF.Square, accum_out=ssq2[0:p, t:t + 1]),
                 r=xk, w=["junk3", ("ssq2", t)])
            S.op("pool", lambda e, p=p, t=t: e.tensor_scalar(out=ssq2[0:p, t:t + 1], in0=ssq2[0:p, t:t + 1], scalar1=1.0 / D, scalar2=EPS, op0=ALU.mult, op1=ALU.add),
                 r=[("ssq2", t)], w=[("ssq2", t)])
            S.op("pool", lambda e, p=p, t=t: e.tensor_tensor(out=ssq2[0:p, t:t + 1], in0=ssq2[0:p, t:t + 1], in1=nhalf_c[0:p], op=ALU.pow),
                 r=[("ssq2", t), "cf"], w=[("ssq2", t)])
            ob = ost[t % 2]
            okey = ("ost", t % 2)
            S.op("act", lambda e, xb=xb, p=p, t=t, ob=ob: e.activation(out=ob[0:p], in_=xb[0:p], func=AF.Identity, scale=ssq2[0:p, t:t + 1]),
                 r=xk + [("ssq2", t)], w=[okey])
            S.op("dve", lambda e, p=p, ob=ob: e.tensor_tensor(out=ob[0:p], in0=ob[0:p], in1=fng_bc[0:p], op=ALU.mult), r=[okey, "fng"], w=[okey])
            if t < 16:
                S.dma("sp", O["yp"][t * 128:(t + 1) * 128, :], ob, r=[okey])
            else:
                S.dma("sp", O["ys"], ob[0:32], r=[okey])
        S.finish()
        print("instructions:", S.ninst, "sems:", S.nsem)
    return nc


def prep_inputs(inp):
    f = lambda a: np.ascontiguousarray(np.asarray(a, dtype=np.float32))
    cf, cb = make_consts()
    ngc = f(np.asarray(inp["norm_g"]).reshape(2, 8, 128).transpose(0, 2, 1))
    adab = f(inp["ada_b"])
    adabc = f(adab[:, 0:2048].reshape(2, 16, 128).transpose(0, 2, 1))
    cw = f(np.asarray(inp["a_conv_w"])[0].reshape(4, 24, 128).transpose(2, 1, 0))
    hp = np.zeros((128, 17), np.float32)
    hp[:, 0:8] = np.asarray(inp["a_A_log"])[0][None, :]
    hp[:, 8:16] = np.asarray(inp["a_dt_bias"])[0][None, :]
    hp[:, 16] = np.asarray(inp["a_out_norm_g"])[0]
    shared = dict(ngc=ngc, adaw=f(inp["ada_w"]), adabc=adabc, adab=adab, awin=f(np.asarray(inp["a_w_in"])[0]), cw=cw, hp=hp,
                  awout=f(np.asarray(inp["a_w_out"])[0]), cf=cf, cb=cb, fng=f(inp["final_norm_g"]),
                  bwin=f(np.asarray(inp["b_w_in"])[0]), bwout=f(np.asarray(inp["b_w_out"])[0]))
    maps = []
    for c in range(NCORES):
        m = dict(shared)
        m["xp"] = f(np.asarray(inp["x_prompt"])[c])
        m["xs"] = f(np.asarray(inp["x_sample"])[4 * c:4 * c + 4].reshape(32, D))
        cc = np.concatenate([np.asarray(inp["c_prompt"])[c:c + 1], np.asarray(inp["c_sample"])[4 * c:4 * c + 4]], 0)
        m["cT"] = f(cc.T.reshape(8, 128, 5).transpose(1, 0, 2))
        sc = np.asarray(inp["state_conv"])[0, 4 * c:4 * c + 4]
        m["sconv"] = f(sc.reshape(4, 3, 24, 128).transpose(3, 2, 0, 1))
        m["sdelta"] = f(np.asarray(inp["state_delta"])[0, 4 * c:4 * c + 4])
        m["kc0"] = f(np.asarray(inp["cache_kv_w128"])[0, 4 * c:4 * c + 4])
        m["kc1"] = f(np.asarray(inp["cache_kv_w512"])[0, 4 * c:4 * c + 4])
        m["kc2"] = f(np.asarray(inp["cache_kv_w2048"])[0, 4 * c:4 * c + 4])
        maps.append(m)
    return maps


_NC_CACHE = {}


def kernel(**inp):
    if "nc" not in _NC_CACHE:
        _NC_CACHE["nc"] = build()
    nc = _NC_CACHE["nc"]
    maps = prep_inputs(inp)
    res = run_bass_kernel_spmd(nc, maps, core_ids=list(range(NCORES)))
    R = res.results
    g = lambda k: [np.asarray(R[c][k], dtype=np.float32) for c in range(NCORES)]
    y_prompt = np.stack(g("yp"), 0)
    y_sample = np.concatenate([a.reshape(NS, LS, D) for a in g("ys")], 0)
    delta_p = np.stack(g("dp"), 0)[None]
    delta_s = np.concatenate(g("ds"), 0)[None]
    conv_p = np.stack(g("cp"), 0)[None]
    conv_s = np.concatenate([a.reshape(NS, 3, 3072) for a in g("cs")], 0)[None]
    outs = [y_prompt, y_sample, delta_p, delta_s, conv_p, conv_s]
    for gi in range(3):
        outs.append(np.stack(g("kv%dp" % gi), 0)[None])
        outs.append(np.concatenate(g("kv%ds" % gi), 0)[None])
    return tuple(np.ascontiguousarray(o, dtype=np.float32) for o in outs)
```

```python
import numpy as np
import ml_dtypes
from contextlib import ExitStack
import concourse.bass as bass
import concourse.mybir as mybir
from concourse.bass_utils import run_bass_kernel_spmd

F32 = mybir.dt.float32
BF16 = mybir.dt.bfloat16
AF = mybir.ActivationFunctionType
ALU = mybir.AluOpType

NEG = -30000.0
NCORES = 8
L = 2048
NS = 4
LS = 8
TT = L + NS * LS
D = 1024
EPS = 1e-6


class Sched:
    ENG = ("pe", "act", "dve", "pool", "sp")

    def __init__(self, nc, stack, sempool):
        self.nc = nc
        self.stack = stack
        self.sempool = sempool
        self.eng = {"pe": nc.tensor, "act": nc.scalar, "dve": nc.vector, "pool": nc.gpsimd, "sp": nc.sync}
        self.gen = {e: 0 for e in self.ENG}
        self.cnt = {e: 0 for e in self.ENG}
        self.esem = {}
        self.seen = {e: {} for e in self.ENG}
        self.res = {}
        self.dsem = {}
        self.nsem = 0
        self.ninst = {e: 0 for e in self.ENG}

    def _newsem(self, name):
        self.nsem += 1
        return self.sempool.pop()

    def _R(self, k):
        r = self.res.get(k)
        if r is None:
            r = [None, {}]
            self.res[k] = r
        return r

    def _wait(self, en, events):
        need = {}
        for ev in events:
            if ev is None:
                continue
            if ev[0] == "E":
                if ev[1] == en and en == "pe":
                    continue
                k = ("E", ev[1])
                v = (ev[2], ev[3])
            else:
                k = ("D", ev[1])
                v = (0, ev[2])
            if need.get(k, (-1, -1)) < v:
                need[k] = v
        for k, v in need.items():
            if self.seen[en].get(k, (-1, -1)) >= v:
                continue
            self.seen[en][k] = v
            sem = self.esem[(k[1], v[0])] if k[0] == "E" else self.dsem[k[1]][0]
            self.eng[en].wait_ge(sem, v[1])

    def _deps(self, r, w):
        evs = []
        for k in r:
            R = self.res.get(k)
            if R is not None:
                evs.append(R[0])
                if isinstance(k, tuple) and k[0] == "ps":
                    evs.extend(R[1].values())
        for k in w:
            R = self.res.get(k)
            if R is not None:
                evs.append(R[0])
                evs.extend(R[1].values())
        return evs

    def _record(self, ev, semid, r, w):
        for k in r:
            self._R(k)[1][semid] = ev
        for k in w:
            R = self._R(k)
            R[0] = ev
            R[1] = {}

    def op(self, en, fn, r=(), w=()):
        self._wait(en, self._deps(r, w))
        inst = fn(self.eng[en])
        self.ninst[en] += 1
        if self.cnt[en] >= 12000:
            self.gen[en] += 1
            self.cnt[en] = 0
        self.cnt[en] += 1
        g = self.gen[en]
        if (en, g) not in self.esem:
            self.esem[(en, g)] = self._newsem(f"e_{en}_{g}")
        inst.then_inc(self.esem[(en, g)], 1)
        ev = ("E", en, g, self.cnt[en])
        self._record(ev, ("E", en), r, w)
        return inst

    def dma(self, q, out, in_, r=(), w=(), key=None, **kw):
        self._wait(q, self._deps(r, w))
        inst = self.eng[q].dma_start(out=out, in_=in_, **kw)
        if key is None:
            key = ("w", w[0]) if w else ("r", r[0])
        ds = self.dsem.get(key)
        if ds is None:
            ds = [self._newsem(f"d{len(self.dsem)}"), 0]
            self.dsem[key] = ds
        ds[1] += 16
        inst.then_inc(ds[0], 16)
        ev = ("D", key, ds[1])
        self._record(ev, ("D", key), r, w)
        return inst

    def _all_events(self):
        evs = []
        for e in self.ENG:
            if self.cnt[e] > 0 or self.gen[e] > 0:
                evs.append(("E", e, self.gen[e], self.cnt[e]))
        for k, ds in self.dsem.items():
            evs.append(("D", k, ds[1]))
        return evs

    def barrier(self):
        evs = self._all_events()
        for e in self.ENG:
            self._wait(e, evs)

    def finish(self):
        self._wait("sp", self._all_events())


WAIT = "WAIT"


def run_tasks(gens):
    act = list(gens)
    idle = 0
    i = 0
    while act:
        i %= len(act)
        g = act[i]
        try:
            v = next(g)
        except StopIteration:
            act.pop(i)
            idle = 0
            continue
        if v is WAIT:
            idle += 1
            assert idle <= 4 * len(act) + 4, "all tasks waiting"
        else:
            idle = 0
        i += 1


class Arena:
    def __init__(self, ap, nbytes):
        self.ap = ap
        self.nbytes = nbytes

    def view(self, off, parts, shape, dt):
        esz = 4 if dt == F32 else 2
        n = int(np.prod(shape))
        nb = n * esz
        assert off % 4 == 0 and off + nb <= self.nbytes, (off, nb, self.nbytes)
        nw = (nb + 3) // 4
        v = self.ap[0:parts, off // 4: off // 4 + nw]
        if dt != F32:
            v = v.bitcast(dt)
            if v.shape[1] != n:
                v = v[:, 0:n]
        if len(shape) == 2:
            return v.rearrange("p (a b) -> p a b", a=shape[0])
        if len(shape) == 3:
            return v.rearrange("p (a b c) -> p a b c", a=shape[0], b=shape[1])
        return v


class Region:
    def __init__(self, arena, lo, hi):
        self.arena, self.lo, self.hi = arena, lo, hi
        self.cur = lo

    def reset(self):
        self.cur = self.lo

    def alloc(self, shape, dt, parts=128):
        esz = 4 if dt == F32 else 2
        nb = (int(np.prod(shape)) * esz + 3) // 4 * 4
        off = self.cur
        assert off + nb <= self.hi, ("region overflow", off, nb, self.hi)
        self.cur += nb
        return self.arena.view(off, parts, shape, dt)


CF_ID, CF_U, CF_MASK, CF_EPS, CF_ONE, CF_NHALF, CF_LNSC, CF_ZERO, CF_N = 0, 128, 256, 512, 513, 514, 515, 516, 520
CB_ID, CB_ONES, CB_BD, CB_ML = 0, 128, 256, 384
CB_MC, CB_MP, CB_MN, CB_MK, CB_N = 1152, 1280, 1408, 1504, 1528


def make_consts():
    cf = np.zeros((128, CF_N), np.float32)
    cf[:, CF_ID:CF_ID + 128] = np.eye(128)
    k = np.arange(128)[:, None]
    i = np.arange(128)[None, :]
    cf[:, CF_U:CF_U + 128] = (k <= i)
    cf[:, CF_MASK:CF_MASK + 128] = np.where(i >= k, 0.0, NEG)
    cf[:, CF_MASK + 128:CF_MASK + 256] = np.where(i > k, 0.0, NEG)
    cf[:, CF_EPS] = EPS
    cf[:, CF_ONE] = 1.0
    cf[:, CF_NHALF] = -0.5
    cf[:, CF_LNSC] = np.log(128.0 ** -0.5)
    cb = np.zeros((128, CB_N), np.float32)
    cb[:, CB_ID:CB_ID + 128] = np.eye(128)
    cb[:, CB_ONES:CB_ONES + 128] = 1.0
    ii = np.arange(128)[:, None]
    jj = np.arange(128)[None, :]
    cb[:, CB_BD:CB_BD + 128] = (ii // 16 == jj // 16)
    for lv, b in enumerate((16, 32, 64)):
        off = (ii // (2 * b) == jj // (2 * b)) & (ii % (2 * b) >= b) & (jj % (2 * b) < b)
        cb[:, CB_ML + lv * 256:CB_ML + lv * 256 + 128] = off.T
        cb[:, CB_ML + lv * 256 + 128:CB_ML + lv * 256 + 256] = off
    cb[:, CB_MC:CB_MC + 128] = np.where(jj >= ii, 0.0, NEG)
    cb[:, CB_MP:CB_MP + 128] = np.where(jj <= ii, 0.0, NEG)
    for gi, d in enumerate((1, 4, 16)):
        a = np.arange(32)
        sk, jk = a[:, None] // 8, a[:, None] % 8
        sq, lq = a[None, :] // 8, a[None, :] % 8
        ok = (sk == sq) & (jk <= lq) & ((lq - jk) % d == 0)
        cb[0:32, CB_MN + gi * 32:CB_MN + (gi + 1) * 32] = np.where(ok, 0.0, NEG)
        cb[:, CB_MK + gi * 8:CB_MK + (gi + 1) * 8] = np.where(ii >= (np.arange(8)[None, :] // d), 0.0, NEG)
    return cf, cb.astype(ml_dtypes.bfloat16)


def build(dbg=False, nheads=8, do_l1=True, stop=None):
    nc = bass.Bass("TRN2", target_bir_lowering=False)

    def din(name, shape, dt=F32):
        return nc.dram_tensor(name, list(shape), dt, kind="ExternalInput").ap()

    def dout(name, shape, dt=F32):
        return nc.dram_tensor(name, list(shape), dt, kind="ExternalOutput").ap()

    I = dict(
        xp=din("xp", [L, D]), xs=din("xs", [NS * LS, D]),
        cT=din("cT", [128, 8, 5]), ngc=din("ngc", [2, 128, 8]),
        adaw=din("adaw", [2, D, 3 * D]), adabc=din("adabc", [2, 128, 16]), adab=din("adab", [2, 3 * D]),
        awin=din("awin", [D, 4112]), cw=din("cw", [128, 24, 4]), hp=din("hp", [128, 17]),
        sconv=din("sconv", [128, 24, NS, 3]), sdelta=din("sdelta", [NS, 8, 128, 128]),
        awout=din("awout", [D, D]),
        cf=din("cf", [128, CF_N]), cb=din("cb", [128, CB_N], BF16),
        fng=din("fng", [D]),
        bwin=din("bwin", [D, 10240]), bwout=din("bwout", [D, D]),
        kc0=din("kc0", [NS, 128, 2, 8, 128]), kc1=din("kc1", [NS, 512, 2, 8, 128]), kc2=din("kc2", [NS, 2048, 2, 8, 128]),
    )
    O = dict(
        yp=dout("yp", [L, D]), ys=dout("ys", [NS * LS, D]),
        dp=dout("dp", [8, 128, 128]), ds=dout("ds", [NS, 8, 128, 128]),
        cp=dout("cp", [3, 3072]), cs=dout("cs", [NS * 3, 3072]),
        kv0p=dout("kv0p", [128, 2, 8, 128]), kv1p=dout("kv1p", [512, 2, 8, 128]), kv2p=dout("kv2p", [2048, 2, 8, 128]),
        kv0s=dout("kv0s", [NS, LS, 2, 8, 128]), kv1s=dout("kv1s", [NS, LS, 2, 8, 128]), kv2s=dout("kv2s", [NS, LS, 2, 8, 128]),
    )
    if dbg:
        O["dbg_x1p"] = dout("dbg_x1p", [L, D])
        O["dbg_x1s"] = dout("dbg_x1s", [NS * LS, D])
        O["dbg_og"] = dout("dbg_og", [128, 8, TT], BF16)
        O["dbg_hT"] = dout("dbg_hT", [128, 8, TT], BF16)

    stack = ExitStack()
    with stack:
        NB = 212000
        arena_t = stack.enter_context(nc.sbuf_tensor("arena", [128, NB // 4], F32))
        banks = [stack.enter_context(nc.psum_tensor(f"bank{i}", [128, 512], F32)) for i in range(8)]
        sempool = [stack.enter_context(nc.semaphore(f"s{i}")) for i in range(96)]
        stack.enter_context(nc.Block())
        S = Sched(nc, stack, sempool)
        A = Arena(arena_t, NB)

        def PS(b):
            return banks[b][:, :]

        def PSB(b):
            return banks[b][:, :].bitcast(BF16)

        RC = Region(A, 0, 8192)
        R1 = Region(A, 8192, 73728)
        R2 = Region(A, 73728, 107008)
        R3 = Region(A, 107008, 140288)
        R4 = Region(A, 140288, NB)

        cf = RC.alloc([CF_N], F32)
        cbf = RC.alloc([CB_N], BF16)
        hpar = RC.alloc([17], F32)
        S.dma("sp", cf, I["cf"], w=["cf"])
        S.dma("sp", cbf, I["cb"], w=["cb"])
        S.dma("sp", hpar, I["hp"], w=["hpar"])
        ident_f = cf[:, CF_ID:CF_ID + 128]
        U_f = cf[:, CF_U:CF_U + 128]
        mask2 = cf[:, CF_MASK:CF_MASK + 256]
        eps_c = cf[:, CF_EPS:CF_EPS + 1]
        one_c = cf[:, CF_ONE:CF_ONE + 1]
        nhalf_c = cf[:, CF_NHALF:CF_NHALF + 1]
        lnsc_c = cf[:, CF_LNSC:CF_LNSC + 1]
        ident_b = cbf[:, CB_ID:CB_ID + 128]
        ones_b = cbf[:, CB_ONES:CB_ONES + 128]
        bd_b = cbf[:, CB_BD:CB_BD + 128]
        ml_b = [cbf[:, CB_ML + lv * 256:CB_ML + (lv + 1) * 256] for lv in range(3)]
        mT_cur = cbf[:, CB_MC:CB_MC + 128]
        mT_prev = cbf[:, CB_MP:CB_MP + 128]
        CK = ["cf", "cb", "hpar"]

        modc = RC.alloc([16, 5], F32)
        gmod = RC.alloc([8, 5], F32)
        ngc = RC.alloc([8], F32)
        adabc = RC.alloc([16], F32)
        cT = RC.alloc([8, 5], F32)
        scb = RC.alloc([8, 5], BF16)

        def ada_phase(l, gate_p, gate_s, reg, parts=("col", "gate")):
            S.dma("sp", ngc, I["ngc"][l], w=["ngc"])
            S.dma("sp", adabc, I["adabc"][l], w=["adabc"])
            if l == 0:
                S.dma("sp", cT, I["cT"], w=["cT"])
                S.op("act", lambda e: e.activation(out=scb, in_=cT, func=AF.Silu), r=["cT"], w=["scb"])
            scp = reg.alloc([8, 128], BF16)
            scs = reg.alloc([8, 32], BF16)
            S.op("act", lambda e: e.activation(out=scp, in_=cT[:, :, 0:1].broadcast_to([128, 8, 128]), func=AF.Silu),
                 r=["cT"], w=["scp"])
            for s in range(NS):
                S.op("act", lambda e: e.activation(out=scs[:, :, 8 * s:8 * s + 8],
                                                   in_=cT[:, :, 1 + s:2 + s].broadcast_to([128, 8, 8]), func=AF.Silu),
                     r=["cT"], w=[("scs", s)])
            gb = reg.alloc([D], F32)
            S.dma("sp", gb, I["adab"][l, 2 * D:3 * D].partition_broadcast(128), w=["gb"])
            wb = [reg.alloc([8, 512], BF16) for _ in range(2)]
            for blk in range(6):
                if (blk < 4 and "col" not in parts) or (blk >= 4 and "gate" not in parts):
                    continue
                buf = wb[blk % 2]
                key = ("adaw", blk % 2)
                S.dma("pool", buf, I["adaw"][l][:, blk * 512:(blk + 1) * 512].rearrange("(c p) n -> p c n", p=128),
                      w=[key])
                if blk < 4:
                    def f(e, blk=blk, buf=buf):
                        last = None
                        for ecl in range(4):
                            ec = blk * 4 + ecl
                            for kc in range(8):
                                last = e.matmul(PS(0)[:, ec * 5:ec * 5 + 5], buf[:, kc, ecl * 128:(ecl + 1) * 128],
                                                scb[:, kc, :], start=(kc == 0), stop=(kc == 7))
                        return last
                    S.op("pe", f, r=[key, "scb"], w=[("ps", 0)])
                else:
                    hb = blk - 4
                    bk = 1 + (hb % 2)
                    def f(e, buf=buf, bk=bk):
                        last = None
                        for kc in range(8):
                            last = e.matmul(PS(bk), scp[:, kc, :], buf[:, kc, :], start=(kc == 0), stop=(kc == 7))
                        return last
                    S.op("pe", f, r=[key, "scp"], w=[("ps", bk)])
                    S.op("dve", lambda e, bk=bk, hb=hb: e.tensor_tensor(out=gate_p[:, hb * 512:(hb + 1) * 512], in0=PS(bk),
                                                                         in1=gb[:, hb * 512:(hb + 1) * 512], op=ALU.add),
                         r=[("ps", bk), "gb"], w=[("gate_p", l)])
                    def f2(e, buf=buf, bk=bk):
                        last = None
                        for kc in range(8):
                            last = e.matmul(PS(bk)[0:32, :], scs[:, kc, :], buf[:, kc, :], start=(kc == 0), stop=(kc == 7))
                        return last
                    S.op("pe", f2, r=[key] + [("scs", s) for s in range(NS)], w=[("ps", bk)])
                    S.op("dve", lambda e, bk=bk, hb=hb: e.tensor_tensor(out=gate_s[:, hb * 512:(hb + 1) * 512], in0=PS(bk)[0:32, :],
                                                                         in1=gb[0:32, hb * 512:(hb + 1) * 512], op=ALU.add),
                         r=[("ps", bk), "gb"], w=[("gate_s", l)])
            if "col" not in parts:
                return
            S.op("dve", lambda e: e.tensor_tensor(out=modc, in0=PS(0)[:, 0:80].rearrange("p (a b) -> p a b", b=5),
                                                  in1=adabc.unsqueeze(2).broadcast_to([128, 16, 5]), op=ALU.add),
                 r=[("ps", 0), "adabc"], w=["modc"])
            S.op("dve", lambda e: e.scalar_tensor_tensor(out=gmod, in0=modc[:, 8:16, :], scalar=1.0,
                                                         in1=ngc.unsqueeze(2).broadcast_to([128, 8, 5]),
                                                         op0=ALU.add, op1=ALU.mult),
                 r=["modc", "ngc"], w=["gmod"])

        def norm_phase(l, hT, x_src, reg):
            ssq = reg.alloc([17], F32)
            rstd = reg.alloc([17], F32)
            junk = reg.alloc([D], BF16)
            xn = [reg.alloc([D], BF16) for _ in range(2)]
            ntile = 17
            tiles = []
            if l == 0:
                xst = [reg.alloc([D], F32) for _ in range(3)]

            def xt(t):
                if l == 0:
                    return xst[t % 3], ("xst", t % 3)
                return x_src(t)

            def load(t):
                if l != 0:
                    return
                buf, key = xt(t)
                if t < 16:
                    S.dma("sp", buf, I["xp"][t * 128:(t + 1) * 128, :], w=[key])
                else:
                    S.dma("sp", buf[0:32], I["xs"], w=[key])

            def sq(t):
                buf, key = xt(t)
                p = 128 if t < 16 else 32
                keys = key if isinstance(key, list) else [key]
                S.op("act", lambda e: e.activation(out=junk[0:p], in_=buf[0:p], func=AF.Square, accum_out=ssq[0:p, t:t + 1]),
                     r=keys, w=["junk", ("ssq", t)])
                S.op("pool", lambda e: e.tensor_scalar(out=rstd[0:p, t:t + 1], in0=ssq[0:p, t:t + 1], scalar1=1.0 / D, scalar2=EPS,
                                                       op0=ALU.mult, op1=ALU.add), r=[("ssq", t)], w=[("rstd", t)])
                S.op("pool", lambda e: e.tensor_tensor(out=rstd[0:p, t:t + 1], in0=rstd[0:p, t:t + 1], in1=nhalf_c[0:p], op=ALU.pow),
                     r=[("rstd", t), "cf"], w=[("rstd", t)])

            def scale_T(t):
                buf, key = xt(t)
                p = 128 if t < 16 else 32
                xb = xn[t % 2]
                keys = key if isinstance(key, list) else [key]
                S.op("act", lambda e: e.activation(out=xb[0:p], in_=buf[0:p], func=AF.Identity, scale=rstd[0:p, t:t + 1]),
                     r=keys + [("rstd", t)], w=[("xn", t % 2)])
                bk = 2 + (t % 2)
                def f(e):
                    last = None
                    for kc in range(8):
                        last = e.transpose(PSB(bk)[:, kc * 128:kc * 128 + p], xb[0:p, kc * 128:(kc + 1) * 128], ident_b[0:p, 0:p])
                    return last
                S.op("pe", f, r=[("xn", t % 2), "cb"], w=[("ps", bk)])
                for kc in range(8):
                    if t < 16:
                        dst = hT[:, kc, t * 128:(t + 1) * 128]
                        src = PSB(bk)[:, kc * 128:(kc + 1) * 128]
                        if kc % 2 == 0:
                            S.op("dve", lambda e, dst=dst, src=src, kc=kc: e.tensor_scalar(
                                out=dst, in0=src, scalar1=gmod[:, kc, 0:1], scalar2=modc[:, kc, 0:1], op0=ALU.mult, op1=ALU.add),
                                 r=[("ps", bk), "gmod", "modc"], w=[("hT", t, kc)])
                        else:
                            S.op("act", lambda e, dst=dst, src=src, kc=kc: e.activation(
                                out=dst, in_=src, func=AF.Identity, scale=gmod[:, kc, 0:1], bias=modc[:, kc, 0:1]),
                                 r=[("ps", bk), "gmod", "modc"], w=[("hT", t, kc)])
                    else:
                        for s in range(NS):
                            dst = hT[:, kc, L + s * 8:L + s * 8 + 8]
                            src = PSB(bk)[:, kc * 128 + s * 8:kc * 128 + s * 8 + 8]
                            S.op("dve", lambda e, dst=dst, src=src, kc=kc, s=s: e.tensor_scalar(
                                out=dst, in0=src, scalar1=gmod[:, kc, 1 + s:2 + s], scalar2=modc[:, kc, 1 + s:2 + s],
                                op0=ALU.mult, op1=ALU.add), r=[("ps", bk), "gmod", "modc"], w=[("hT", t, kc)])

            load(0)
            load(1)
            sq(0)
            for t in range(ntile):
                if t + 2 < ntile:
                    load(t + 2)
                if t + 1 < ntile:
                    sq(t + 1)
                scale_T(t)

        HT_KEYS = [("hT", t, kc) for t in range(17) for kc in range(8)]

        R1.reset(); R2.reset(); R3.reset(); R4.reset()
        hT = R1.alloc([8, TT], BF16)
        ogT = R2.alloc([8, TT], BF16)
        R4g = Region(A, NB - 12288, NB)
        gate_p = R4g.alloc([D], F32)
        gate_s = R4g.alloc([D], F32, parts=32)
        x1s = R4g.alloc([D], F32, parts=32)
        R4 = Region(A, 140288, NB - 12288)

        class _Stop(Exception):
            pass

        def maybe_stop(tag):
            if stop == tag:
                S.finish()
                print("STOP at", tag, "instructions:", S.ninst, "sems:", S.nsem)
                raise _Stop()

        try:
            _build_rest = None
        finally:
            pass
        ada_phase(0, gate_p, gate_s, R3)
        R4.reset()
        if stop == "ada":
            S.finish(); print("STOP ada", S.ninst); return nc
        norm_phase(0, hT, None, R4)
        S.barrier()
        if stop == "norm":
            S.finish(); print("STOP norm", S.ninst); return nc
        R3.reset(); R4.reset()

        NCH = 16
        wab = R1.alloc([8, 16], BF16)
        S.dma("pool", wab, I["awin"][:, 4096:4112].rearrange("(c p) n -> p c n", p=128), w=["wab"])
        def fab(e):
            last = None
            for t in range(NCH):
                for kc in range(8):
                    last = e.matmul(PS(0)[:, t * 16:(t + 1) * 16], hT[:, kc, t * 128:(t + 1) * 128], wab[:, kc, :],
                                    start=(kc == 0), stop=(kc == 7))
            for s in range(NS):
                for kc in range(8):
                    last = e.matmul(PS(1)[0:8, s * 16:(s + 1) * 16], hT[:, kc, L + s * 8:L + s * 8 + 8], wab[:, kc, :],
                                    start=(kc == 0), stop=(kc == 7))
            return last
        S.op("pe", fab, r=["wab"] + HT_KEYS, w=[("ps", 0), ("ps", 1)])

        NCOL = NCH * 8 + NS * 8
        def galloc():
            return R1.alloc([NCOL], F32)
        xa, ax, ex, lx, g_t, beta_t, lbeta_t, gc_t, gcl_t, eg_t, gtot_t, ekd_t = [galloc() for _ in range(12)]
        nA = R1.alloc([8], F32)
        A_bc = hpar[:, 0:8]
        dt_bc = hpar[:, 8:16]
        outg_c = hpar[:, 16:17]
        def pv(tl):
            return tl[:, 0:128].rearrange("p (c h) -> p c h", h=8)
        def sv(tl):
            return tl[0:8, 128:160].rearrange("p (c h) -> p c h", h=8)
        abp = PS(0)[:, 0:256].rearrange("p (c k) -> p c k", k=16)
        abs_ = PS(1)[0:8, 0:64].rearrange("p (c k) -> p c k", k=16)
        S.op("dve", lambda e: e.tensor_tensor(out=pv(xa), in0=abp[:, :, 0:8], in1=dt_bc.unsqueeze(1).broadcast_to([128, 16, 8]), op=ALU.add),
             r=[("ps", 0), "hpar"], w=["xa_p"])
        S.op("dve", lambda e: e.tensor_tensor(out=sv(xa), in0=abs_[:, :, 0:8], in1=dt_bc[0:8].unsqueeze(1).broadcast_to([8, 4, 8]), op=ALU.add),
             r=[("ps", 1), "hpar"], w=["xa_s"])
        S.op("act", lambda e: e.activation(out=pv(ex), in_=abp[:, :, 8:16], func=AF.Exp, scale=-1.0), r=[("ps", 0)], w=["ex_p"])
        S.op("act", lambda e: e.activation(out=sv(ex), in_=abs_[:, :, 8:16], func=AF.Exp, scale=-1.0), r=[("ps", 1)], w=["ex_s"])
        GP = (128, slice(0, 128))
        GS = (8, slice(128, 160))
        for (p, cs_), tg in ((GP, "p"), (GS, "s")):
            def T(tl, p=p, cs_=cs_):
                return tl[0:p, cs_]
            S.op("act", lambda e, T=T: e.activation(out=T(lbeta_t), in_=T(ex), func=AF.Ln, bias=one_c[0:T(ex).shape[0]], scale=1.0),
                 r=["ex_" + tg, "cf"], w=["lbeta_" + tg])
            S.op("dve", lambda e, T=T: e.tensor_scalar(out=T(lbeta_t), in0=T(lbeta_t), scalar1=-1.0, scalar2=None, op0=ALU.mult),
                 r=["lbeta_" + tg], w=["lbeta_" + tg])
            S.op("act", lambda e, T=T: e.activation(out=T(beta_t), in_=T(lbeta_t), func=AF.Exp), r=["lbeta_" + tg], w=["beta_" + tg])
            S.op("dve", lambda e, T=T: e.tensor_scalar(out=T(ax), in0=T(xa), scalar1=-1.0, scalar2=None, op0=ALU.mult),
                 r=["xa_" + tg], w=["ax_" + tg])
            S.op("dve", lambda e, T=T: e.tensor_tensor(out=T(ax), in0=T(ax), in1=T(xa), op=ALU.max),
                 r=["xa_" + tg, "ax_" + tg], w=["ax_" + tg])
            S.op("act", lambda e, T=T: e.activation(out=T(ax), in_=T(ax), func=AF.Exp, scale=-1.0), r=["ax_" + tg], w=["ax_" + tg])
            S.op("act", lambda e, T=T: e.activation(out=T(lx), in_=T(ax), func=AF.Ln, bias=one_c[0:T(ax).shape[0]], scale=1.0),
                 r=["ax_" + tg, "cf"], w=["lx_" + tg])
            S.op("dve", lambda e, T=T: e.scalar_tensor_tensor(out=T(lx), in0=T(xa), scalar=0.0, in1=T(lx), op0=ALU.max, op1=ALU.add),
                 r=["xa_" + tg, "lx_" + tg], w=["lx_" + tg])
        S.op("act", lambda e: e.activation(out=nA, in_=A_bc, func=AF.Exp), r=["hpar"], w=["nA"])
        S.op("dve", lambda e: e.tensor_scalar(out=nA, in0=nA, scalar1=-1.0, scalar2=None, op0=ALU.mult), r=["nA"], w=["nA"])
        S.op("dve", lambda e: e.tensor_tensor(out=pv(g_t), in0=pv(lx), in1=nA.unsqueeze(1).broadcast_to([128, 16, 8]), op=ALU.mult),
             r=["lx_p", "nA"], w=["g_p"])
        S.op("dve", lambda e: e.tensor_tensor(out=sv(g_t), in0=sv(lx), in1=nA[0:8].unsqueeze(1).broadcast_to([8, 4, 8]), op=ALU.mult),
             r=["lx_s", "nA"], w=["g_s"])
        S.op("pe", lambda e: e.matmul(PS(2)[:, 0:128], U_f, g_t[:, 0:128], start=True, stop=True), r=["cf", "g_p"], w=[("ps", 2)])
        S.op("pe", lambda e: e.matmul(PS(3)[0:8, 0:32], U_f[0:8, 0:8], g_t[0:8, 128:160], start=True, stop=True), r=["cf", "g_s"], w=[("ps", 3)])
        S.op("act", lambda e: e.activation(out=gc_t[:, 0:128], in_=PS(2)[:, 0:128], func=AF.Copy), r=[("ps", 2)], w=["gc_p"])
        S.op("act", lambda e: e.activation(out=gc_t[0:8, 128:160], in_=PS(3)[0:8, 0:32], func=AF.Copy), r=[("ps", 3)], w=["gc_s"])
        S.op("pe", lambda e: e.matmul(PS(2)[:, 128:256], ident_f[:, 127:128].broadcast_to([128, 128]), gc_t[:, 0:128], start=True, stop=True),
             r=["cf", "gc_p"], w=[("ps", 2)])
        S.op("pe", lambda e: e.matmul(PS(3)[:, 128:160], ident_f[0:8, 7:8].broadcast_to([8, 128]), gc_t[0:8, 128:160], start=True, stop=True),
             r=["cf", "gc_s"], w=[("ps", 3)])
        S.op("act", lambda e: e.activation(out=gcl_t[:, 0:128], in_=PS(2)[:, 128:256], func=AF.Copy), r=[("ps", 2)], w=["gcl_p"])
        S.op("act", lambda e: e.activation(out=gcl_t[:, 128:160], in_=PS(3)[:, 128:160], func=AF.Copy), r=[("ps", 3)], w=["gcl_s"])
        for (p, cs_), tg in ((GP, "p"), (GS, "s")):
            def T(tl, p=p, cs_=cs_):
                return tl[0:p, cs_]
            S.op("act", lambda e, T=T: e.activation(out=T(eg_t), in_=T(gc_t), func=AF.Exp), r=["gc_" + tg], w=["eg_" + tg])
            S.op("act", lambda e, cs_=cs_: e.activation(out=gtot_t[:, cs_], in_=gcl_t[:, cs_], func=AF.Exp), r=["gcl_" + tg], w=["gtot_" + tg])
            S.op("dve", lambda e, T=T: e.tensor_tensor(out=T(ekd_t), in0=T(gcl_t), in1=T(gc_t), op=ALU.subtract),
                 r=["gcl_" + tg, "gc_" + tg], w=["ekd_" + tg])
        G_KEYS = [k + t for k in ("beta_", "lbeta_", "gc_", "gcl_", "eg_", "gtot_", "ekd_") for t in ("p", "s")]

        if stop == "G":
            S.finish(); print("STOP G", S.ninst); return nc
        NT = 16
        UW = 3 + L
        UT = UW + NS * (3 + LS)
        NUB = 3
        ubuf = [R4.alloc([UT], BF16) for _ in range(NUB)]
        Wh = [R1.alloc([8, 512], BF16) for _ in range(2)]
        diag = [R1.alloc([12, 128], BF16) for _ in range(2)]
        cwt = R1.alloc([24, 4], F32)
        S.dma("sp", cwt, I["cw"], w=["cwt"])
        sconv = R1.alloc([24, NS * 3], F32)
        S.dma("sp", sconv, I["sconv"].rearrange("p a s i -> p a (s i)"), w=["sconv"])
        cvo = [[R3.alloc([TT], BF16) for _ in range(4)] for _ in range(2)]
        sqb1 = R4.alloc([TT], BF16)
        sqb = [sqb1, sqb1]
        NHC = NT + NS
        def halloc(n=NHC):
            return [R4.alloc([n], F32) for _ in range(2)]
        ss_k, ss_q, lrnk, lrq, rows1, rows2, biasj, kbg_s, kdec_s, qdec_s = [halloc() for _ in range(10)]
        rowsT = [R4.alloc([256], F32, parts=16) for _ in range(2)]
        rowsTs = [R4.alloc([16], F32, parts=4) for _ in range(2)]
        cst1 = R4.alloc([384], F32, parts=3)
        css1 = R4.alloc([384], F32, parts=32)
        cst = [cst1, cst1]
        css = [css1, css1]
        thb = [R4.alloc([512], BF16) for _ in range(2)]
        thc = [0]
        outg_h = R4.alloc([1], F32)
        S.op("dve", lambda e: e.tensor_scalar(out=outg_h, in0=outg_c, scalar1=0.5, scalar2=None, op0=ALU.mult), r=["hpar"], w=["outg_h"])
        for ub in range(NUB):
            S.op("pool", lambda e, ub=ub: e.memset(ubuf[ub][:, 0:3], 0.0), w=[("u", ub, "h")])

        ucount = [0]

        def P_head(h):
            hs_ = h % 2
            W = Wh[hs_]
            wkey = [("Wh", hs_, j) for j in range(4)]
            for j in range(4):
                col = (j * 1024 + h * 128)
                S.dma("pool", W[:, :, j * 128:(j + 1) * 128],
                      I["awin"][:, col:col + 128].rearrange("(c p) n -> p c n", p=128), w=[wkey[j]])
            dg = diag[hs_]
            for j in range(3):
                for i in range(4):
                    S.op("pool", lambda e, j=j, i=i: e.tensor_scalar(out=dg[:, j * 4 + i, :], in0=ident_f, scalar1=cwt[:, j * 8 + h, i:i + 1],
                                                                      scalar2=0.5, op0=ALU.mult, op1=ALU.mult),
                         r=["cf", "cwt"], w=[("diag", hs_, j)])
            yield 0.02
            blocks = [(q * 512, 512) for q in range(4)] + [(L, NS * LS)]
            step = 0
            nstep = 4 * 5 * 2.0
            for j in range(4):
                if j < 3:
                    ui = ucount[0] % NUB
                    ucount[0] += 1
                    ub = ubuf[ui]
                    ukey = None
                    ukeys = [("u", ui, bi_) for bi_ in range(5)]
                    S.op("act", lambda e, ub=ub, j=j: e.activation(
                        out=ub[:, UW:UT].rearrange("p (s i) -> p s i", i=3 + LS)[:, :, 0:3],
                        in_=sconv[:, j * 8 + h, :].rearrange("p (s i) -> p s i", i=3), func=AF.Copy),
                         r=["sconv"], w=[("u", ui, "sh")])
                for bi, (t0, n) in enumerate(blocks):
                    bk = 0
                    step += 1
                    def f(e, t0=t0, n=n, bk=bk, j=j):
                        last = None
                        for kc in range(8):
                            last = e.matmul(PS(bk)[:, 0:n], W[:, kc, j * 128:(j + 1) * 128], hT[:, kc, t0:t0 + n],
                                            start=(kc == 0), stop=(kc == 7))
                        return last
                    S.op("pe", f, r=[wkey[j]] + HT_KEYS, w=[("ps", bk)])
                    if j == 3:
                        tb = thb[thc[0] % 2]
                        tk = ("thb", thc[0] % 2)
                        thc[0] += 1
                        S.op("act", lambda e, n=n, bk=bk, tb=tb: e.activation(out=tb[:, 0:n], in_=PS(bk)[:, 0:n], func=AF.Tanh, scale=0.5),
                             r=[("ps", bk)], w=[tk])
                        S.op("dve", lambda e, t0=t0, n=n, bk=bk, tb=tb: e.scalar_tensor_tensor(out=cvo[hs_][3][:, t0:t0 + n], in0=tb[:, 0:n], scalar=1.0,
                                                                                              in1=PS(bk)[:, 0:n], op0=ALU.add, op1=ALU.mult),
                             r=[("ps", bk), tk], w=[("cvo", hs_, 3, bi)])
                    else:
                        if bi < 4:
                            dst = ub[:, 3 + t0:3 + t0 + n]
                            src = PS(bk)[:, 0:n]
                        else:
                            dst = ub[:, UW:UT].rearrange("p (s i) -> p s i", i=3 + LS)[:, :, 3:3 + LS]
                            src = PS(bk)[:, 0:n].rearrange("p (s i) -> p s i", i=LS)
                        S.op("act", lambda e, dst=dst, src=src: e.activation(out=dst, in_=src, func=AF.Copy), r=[("ps", bk)], w=[ukeys[bi]])
                        ck = 1
                        def fc(e, t0=t0, n=n, ck=ck, bi=bi, j=j, ub=ub):
                            last = None
                            for i in range(4):
                                if bi < 4:
                                    rhs = ub[:, t0 + i:t0 + i + n]
                                    out = PS(ck)[:, 0:n]
                                else:
                                    rhs = ub[:, UW:UT].rearrange("p (s i) -> p s i", i=3 + LS)[:, :, i:i + LS]
                                    out = PS(ck)[:, 0:n].rearrange("p (s i) -> p s i", i=LS)
                                last = e.matmul(out, dg[:, j * 4 + i, :], rhs, start=(i == 0), stop=(i == 3))
                            return last
                        rk = [ukeys[bi], ("diag", hs_, j)] + ([ukeys[bi - 1]] if 0 < bi < 4 else []) + ([("u", ui, "h")] if bi == 0 else []) + ([("u", ui, "sh")] if bi == 4 else [])
                        S.op("pe", fc, r=rk, w=[("ps", ck)])
                        tb = thb[thc[0] % 2]
                        tk = ("thb", thc[0] % 2)
                        thc[0] += 1
                        S.op("act", lambda e, n=n, ck=ck, tb=tb: e.activation(out=tb[:, 0:n], in_=PS(ck)[:, 0:n], func=AF.Tanh),
                             r=[("ps", ck)], w=[tk])
                        S.op("dve", lambda e, t0=t0, n=n, ck=ck, j=j, tb=tb: e.scalar_tensor_tensor(out=cvo[hs_][j][:, t0:t0 + n], in0=tb[:, 0:n], scalar=1.0,
                                                                                                   in1=PS(ck)[:, 0:n], op0=ALU.add, op1=ALU.mult),
                             r=[("ps", ck), tk], w=[("cvo", hs_, j, bi)])
                    yield 0.02 + 0.8 * step / 20.0
            def fcs(e):
                last = None
                for kc in range(8):
                    last = e.matmul(PS(0)[0:3, 0:384], hT[:, kc, L - 3:L], W[:, kc, 0:384], start=(kc == 0), stop=(kc == 7))
                for kc in range(8):
                    last = e.matmul(PS(1)[0:32, 0:384], hT[:, kc, L:TT], W[:, kc, 0:384], start=(kc == 0), stop=(kc == 7))
                return last
            S.op("pe", fcs, r=wkey + HT_KEYS, w=[("ps", 0), ("ps", 1)])
            S.op("act", lambda e: e.activation(out=cst[hs_], in_=PS(0)[0:3, 0:384], func=AF.Copy), r=[("ps", 0)], w=[("cst", 0)])
            S.op("act", lambda e: e.activation(out=css[hs_], in_=PS(1)[0:32, 0:384], func=AF.Copy), r=[("ps", 1)], w=[("css", 0)])
            S.dma("sp", O["cp"].rearrange("p (j c) -> p j c", j=3)[:, :, h * 128:(h + 1) * 128],
                  cst[hs_].rearrange("p (j c) -> p j c", j=3), r=[("cst", 0)])
            for s_ in range(NS):
                S.dma("sp", O["cs"].rearrange("p (j c) -> p j c", j=3)[3 * s_:3 * s_ + 3, :, h * 128:(h + 1) * 128],
                      css[hs_][8 * s_ + 5:8 * s_ + 8].rearrange("p (j c) -> p j c", j=3), r=[("css", 0)])
            for j, sst in ((1, ss_k[hs_]), (0, ss_q[hs_])):
                sb = sqb[j]
                S.op("act", lambda e, j=j, sb=sb: e.activation(out=sb, in_=cvo[hs_][j], func=AF.Square),
                     r=[("cvo", hs_, j, bi) for bi in range(5)], w=[("sqb", 0)])
                def fs(e, sb=sb, j=j):
                    last = None
                    for c in range(NT):
                        last = e.matmul(PS(j)[:, c:c + 1], sb[:, c * 128:(c + 1) * 128], ones_b[:, 0:1], start=True, stop=True)
                    for s in range(NS):
                        last = e.matmul(PS(j)[0:8, NT + s:NT + s + 1], sb[:, L + s * 8:L + s * 8 + 8], ones_b[:, 0:1], start=True, stop=True)
                    return last
                S.op("pe", fs, r=[("sqb", 0), "cb"], w=[("ps", j)])
                S.op("act", lambda e, j=j, sst=sst: e.activation(out=sst[:, 0:NT], in_=PS(j)[:, 0:NT], func=AF.Copy), r=[("ps", j)], w=[("ss", hs_, j)])
                S.op("act", lambda e, j=j, sst=sst: e.activation(out=sst[0:8, NT:NHC], in_=PS(j)[0:8, NT:NHC], func=AF.Copy), r=[("ps", j)], w=[("ss", hs_, j, "s")])
            yield 0.9
            def gcol(tl, which):
                if which == "p":
                    return tl[:, 0:128].rearrange("p (c hh) -> p c hh", hh=8)[:, :, h]
                return tl[0:8, 128:160].rearrange("p (c hh) -> p c hh", hh=8)[:, :, h]
            for which, p, cs_ in (("p", 128, slice(0, NT)), ("s", 8, slice(NT, NHC))):
                def T(tl, p=p, cs_=cs_):
                    return tl[hs_][0:p, cs_]
                hk = ("hs", hs_, which)
                S.op("act", lambda e, T=T, p=p: e.activation(out=T(lrnk), in_=T(ss_k), func=AF.Ln, bias=eps_c[0:p], scale=1.0),
                     r=[("ss", hs_, 1), ("ss", hs_, 1, "s"), "cf"], w=[hk + ("lrnk",)])
                S.op("act", lambda e, T=T, p=p: e.activation(out=T(lrq), in_=T(ss_q), func=AF.Ln, bias=eps_c[0:p], scale=1.0),
                     r=[("ss", hs_, 0), ("ss", hs_, 0, "s"), "cf"], w=[hk + ("lrq",)])
                S.op("dve", lambda e, T=T: e.tensor_scalar(out=T(lrnk), in0=T(lrnk), scalar1=-0.5, scalar2=None, op0=ALU.mult),
                     r=[hk + ("lrnk",)], w=[hk + ("lrnk",)])
                S.op("dve", lambda e, T=T, p=p: e.tensor_scalar(out=T(lrq), in0=T(lrq), scalar1=-0.5, scalar2=lnsc_c[0:p], op0=ALU.mult, op1=ALU.add),
                     r=[hk + ("lrq",), "cf"], w=[hk + ("lrq",)])
                S.op("dve", lambda e, T=T, which=which: e.tensor_tensor(out=T(rows1), in0=T(lrnk), in1=gcol(lbeta_t, which), op=ALU.add),
                     r=[hk + ("lrnk",), "lbeta_" + which], w=[hk + ("rows1",)])
                S.op("dve", lambda e, T=T, which=which: e.tensor_tensor(out=T(rows1), in0=T(rows1), in1=gcol(gc_t, which), op=ALU.add),
                     r=[hk + ("rows1",), "gc_" + which], w=[hk + ("rows1",)])
                S.op("dve", lambda e, T=T, which=which: e.tensor_tensor(out=T(rows2), in0=T(lrq), in1=gcol(gc_t, which), op=ALU.add),
                     r=[hk + ("lrq",), "gc_" + which], w=[hk + ("rows2",)])
                S.op("dve", lambda e, T=T, which=which: e.tensor_tensor(out=T(biasj), in0=T(lrnk), in1=gcol(gc_t, which), op=ALU.subtract),
                     r=[hk + ("lrnk",), "gc_" + which], w=[hk + ("biasj",)])
                S.op("act", lambda e, T=T: e.activation(out=T(kbg_s), in_=T(rows1), func=AF.Exp), r=[hk + ("rows1",)], w=[hk + ("kbg_s",)])
                S.op("act", lambda e, T=T: e.activation(out=T(qdec_s), in_=T(rows2), func=AF.Exp), r=[hk + ("rows2",)], w=[hk + ("qdec_s",)])
                S.op("dve", lambda e, T=T, which=which: e.tensor_tensor(out=T(kdec_s), in0=T(lrnk), in1=gcol(ekd_t, which), op=ALU.add),
                     r=[hk + ("lrnk",), "ekd_" + which], w=[hk + ("kdec_s",)])
                S.op("act", lambda e, T=T: e.activation(out=T(kdec_s), in_=T(kdec_s), func=AF.Exp), r=[hk + ("kdec_s",)], w=[hk + ("kdec_s",)])
            def ft(e):
                e.transpose(PS(0)[0:16, 0:128], rows2[hs_][:, 0:NT], ident_f)
                e.transpose(PS(0)[0:16, 128:256], rows1[hs_][:, 0:NT], ident_f)
                e.transpose(PS(1)[0:4, 0:8], rows2[hs_][0:8, NT:NHC], ident_f[0:8, 0:8])
                return e.transpose(PS(1)[0:4, 8:16], rows1[hs_][0:8, NT:NHC], ident_f[0:8, 0:8])
            S.op("pe", ft, r=[("hs", hs_, w_, n_) for w_ in ("p", "s") for n_ in ("rows1", "rows2")] + ["cf"], w=[("ps", 0), ("ps", 1)])
            S.op("act", lambda e: e.activation(out=rowsT[hs_], in_=PS(0)[0:16, 0:256], func=AF.Copy), r=[("ps", 0)], w=[("rowsT", hs_)])
            S.op("act", lambda e: e.activation(out=rowsTs[hs_], in_=PS(1)[0:4, 0:16], func=AF.Copy), r=[("ps", 1)], w=[("rowsTs", hs_)])
            yield 1.0

        NSET = 3
        import os as _os
        NLANE = int(_os.environ.get("K_NLANE", "3"))
        NHO = 6
        STAG = int(_os.environ.get("K_STAG", "5"))
        LANE_BANK = (6, 2, 3)

        def lane_ws():
            d = {}
            d["t"] = R4.alloc([256], F32)
            d["D"] = d["t"]
            d["W"] = [R4.alloc([512], BF16) for _ in range(2)]
            d["BN"] = [w_[:, 0:256] for w_ in d["W"]]
            d["Q"] = [w_[:, 256:384] for w_ in d["W"]]
            d["M"] = [R4.alloc([256], BF16) for _ in range(3)]
            d["XY"] = R4.alloc([256], BF16)
            d["QTs"] = R4.alloc([128], BF16)
            return d

        def handoff():
            d = {}
            d["ktok"] = R4.alloc([128], BF16)
            d["vb"] = R4.alloc([128], BF16)
            d["NTQK"] = R4.alloc([384], BF16)
            d["kcTn"] = R4.alloc([128], BF16)
            d["QT"] = R4.alloc([128], BF16)
            return d
        HO_NAMES = ("ktok", "vb", "NTQK", "kcTn", "QT")
        lanes = [lane_ws() for _ in range(NLANE)]
        hos = [handoff() for _ in range(NHO)]
        u_t = [R4.alloc([128], BF16) for _ in range(2)]
        us_t = [R4.alloc([128], BF16) for _ in range(2)]
        t2_t = [R4.alloc([128], F32) for _ in range(2)]
        o_t = [R4.alloc([128], F32) for _ in range(2)]
        on_t = [R4.alloc([128], BF16) for _ in range(2)]
        junk2 = R4.alloc([128], BF16)
        oss = R4.alloc([2], F32)
        Sf = [R4.alloc([128], F32) for _ in range(2)]
        Sb = [R4.alloc([128], BF16) for _ in range(2)]
        sidx = [0]
        cidx = [0]
        cp_done = set()
        cr_done = [0]
        stg = [0]

        def chunk_specs(h):
            sp = [(128, c * 128, c, False, 0) for c in range(NT)]
            sp += [(8, L + s * 8, NT + s, True, s) for s in range(NS)]
            return sp

        def CP_chunk(h, spec, cs, lane, slot):
            RB = LANE_BANK[lane]
            def KK(name, *rest):
                return ((("ho", slot) if name in HO_NAMES else ("lw", lane)), name) + rest
            C, t0, col, smp, s = spec
            hs_ = h % 2
            qT, kT, vT = cvo[hs_][0], cvo[hs_][1], cvo[hs_][2]
            cvk = [("cvo", hs_, j, bi) for j in range(3) for bi in range(5)]
            hk = lambda n: ("hs", hs_, "s" if smp else "p", n)
            ksl = kT[:, t0:t0 + C]
            def f1(e):
                e.transpose(PSB(4)[0:C, 0:128], ksl, ident_b)
                return e.transpose(PSB(4)[0:C, 128:256], vT[:, t0:t0 + C], ident_b)
            S.op("pe", f1, r=cvk + ["cb"], w=[("ps", 4)])
            S.op("act", lambda e: e.activation(out=cs["ktok"][0:C], in_=PSB(4)[0:C, 0:128], func=AF.Copy), r=[("ps", 4)], w=[KK("ktok")])
            bcol = (beta_t[0:C, 128 + s * 8 + h:128 + s * 8 + h + 1] if smp else beta_t[:, col * 8 + h:col * 8 + h + 1])
            S.op("dve", lambda e: e.tensor_scalar(out=cs["vb"][0:C], in0=PSB(4)[0:C, 128:256], scalar1=bcol, scalar2=None, op0=ALU.mult),
                 r=[("ps", 4), "beta_s" if smp else "beta_p"], w=[KK("vb")])
            def f2(e):
                e.matmul(PS(RB)[0:C, 0:C], ksl, qT[:, t0:t0 + C], start=True, stop=True)
                e.matmul(PS(RB)[0:C, 128:128 + C], ksl, ksl, start=True, stop=True)
                if smp:
                    e.matmul(PS(RB)[0:C, 256:256 + C], ident_f[0:4, s:s + 1].broadcast_to([4, C]), rowsTs[hs_][:, 0:8], start=True, stop=True)
                    return e.matmul(PS(RB)[0:C, 384:384 + C], ident_f[0:4, s:s + 1].broadcast_to([4, C]), rowsTs[hs_][:, 8:16], start=True, stop=True)
                return e.matmul(PS(RB)[:, 256:512], ident_f[0:16, col:col + 1].broadcast_to([16, 128]), rowsT[hs_], start=True, stop=True)
            S.op("pe", f2, r=cvk + ["cf", ("rowsTs" if smp else "rowsT", hs_)], w=[("ps", RB)])
            t3 = cs["t"][0:C, :].rearrange("p (a b) -> p a b", a=2)[:, :, 0:C]
            D3 = cs["D"][0:C, :].rearrange("p (a b) -> p a b", a=2)[:, :, 0:C]
            N3 = cs["NTQK"][0:C, 0:256].rearrange("p (a b) -> p a b", a=2)[:, :, 0:C]
            m3 = mask2[0:C, :].rearrange("p (a b) -> p a b", a=2)[:, :, 0:C]
            E3 = PS(RB)[0:C, 256:512].rearrange("p (a b) -> p a b", a=2)[:, :, 0:C]
            raw3 = PS(RB)[0:C, 0:256].rearrange("p (a b) -> p a b", a=2)[:, :, 0:C]
            S.op("dve", lambda e: e.tensor_tensor(out=t3, in0=E3, in1=m3, op=ALU.add), r=[("ps", RB), "cf"], w=[KK("t")])
            S.op("act", lambda e: e.activation(out=D3, in_=t3, func=AF.Exp, bias=biasj[hs_][0:C, col:col + 1], scale=1.0),
                 r=[KK("t"), hk("biasj")], w=[KK("t")])
            S.op("dve", lambda e: e.tensor_tensor(out=N3, in0=raw3, in1=D3, op=ALU.mult), r=[("ps", RB), KK("t")], w=[KK("NTQK")])
            yield
            Bm = cs["NTQK"][0:C, 128:128 + C]
            S.op("pe", lambda e: e.transpose(PSB(4)[0:C, 256:256 + C], Bm, ident_b[0:C, 0:C]), r=[KK("NTQK"), "cb"], w=[("ps", 4)])
            if C == 128:
                W = cs["W"]
                wk = lambda i: KK("W", i)
                S.op("act", lambda e: e.activation(out=cs["NTQK"][:, 256:384], in_=PSB(4)[:, 256:384], func=AF.Copy), r=[("ps", 4)], w=[KK("NTQK")])
                BNf = cs["NTQK"][:, 128:384]
                v2 = lambda ap: ap.rearrange("p (a b) -> p a b", a=2)
                bd2 = bd_b.unsqueeze(1).broadcast_to([128, 2, 128])
                id2 = ident_b.unsqueeze(1).broadcast_to([128, 2, 128])
                W0bn = W[0].rearrange("p (a b) -> p a b", a=4)[:, 1::2, :]
                W1qt = W[1].rearrange("p (a b) -> p a b", a=4)[:, 0::2, :]
                S.op("dve", lambda e: e.tensor_tensor(out=W0bn, in0=v2(BNf), in1=bd2, op=ALU.mult), r=[KK("NTQK"), "cb"], w=[wk(0)])
                S.op("dve", lambda e: e.tensor_tensor(out=W1qt, in0=id2, in1=W0bn, op=ALU.subtract), r=[wk(0), "cb"], w=[wk(1)])
                for lv in range(3):
                    S.op("dve", lambda e, lv=lv: e.tensor_tensor(out=cs["M"][lv], in0=BNf, in1=ml_b[lv], op=ALU.mult),
                         r=[KK("NTQK"), "cb"], w=[KK("M", lv)])
                yield
                cur = 0
                for k in range(4):
                    nxt = 1 - cur
                    last = (k == 3)
                    Wc = W[cur]
                    Bk = Wc[:, 128:256]
                    Nk = Wc[:, 384:512]
                    src = Wc if k > 0 else None
                    def fr(e, k=k, Wc=Wc, Bk=Bk, Nk=Nk, last=last):
                        if k == 0:
                            e.matmul(PS(RB)[:, 128:256], Nk, Bk, start=True, stop=True)
                            return e.matmul(PS(RB)[:, 384:512], Bk, Nk, start=True, stop=True)
                        if last:
                            e.matmul(PS(RB)[:, 0:128], Nk, Wc[:, 0:128], start=True, stop=True)
                            return e.matmul(PS(RB)[:, 256:384], Bk, Wc[:, 256:384], start=True, stop=True)
                        e.matmul(PS(RB)[:, 0:256], Nk, Wc[:, 0:256], start=True, stop=True)
                        return e.matmul(PS(RB)[:, 256:512], Bk, Wc[:, 256:512], start=True, stop=True)
                    S.op("pe", fr, r=[wk(cur)], w=[("ps", RB)])
                    P4 = PS(RB).rearrange("p (a b) -> p a b", a=4)
                    Wn4 = W[nxt].rearrange("p (a b) -> p a b", a=4)
                    Wc4 = Wc.rearrange("p (a b) -> p a b", a=4)
                    if k == 0:
                        S.op("act", lambda e, P4=P4, Wn4=Wn4: e.activation(out=Wn4[:, 1::2, :], in_=P4[:, 1::2, :], func=AF.Copy),
                             r=[("ps", RB)], w=[wk(nxt)])
                    else:
                        if not last:
                            S.op("act", lambda e, P4=P4, Wn4=Wn4: e.activation(out=Wn4[:, 1::2, :], in_=P4[:, 1::2, :], func=AF.Copy),
                                 r=[("ps", RB)], w=[wk(nxt)])
                        S.op("dve", lambda e, P4=P4, Wn4=Wn4, Wc4=Wc4: e.tensor_tensor(out=Wn4[:, 0::2, :], in0=P4[:, 0::2, :], in1=Wc4[:, 0::2, :], op=ALU.add),
                             r=[("ps", RB), wk(cur)], w=[wk(nxt)])
                    cur = nxt
                    yield
                Wc = W[cur]
                Qv = Wc[:, 0:128]
                Tv = Wc[:, 256:384]
                XY = cs["XY"]
                for lv in range(3):
                    lastl = (lv == 2)
                    Bm_l = cs["M"][lv][:, 0:128]
                    Nm_l = cs["M"][lv][:, 128:256]
                    def f1_(e, Bm_l=Bm_l, Nm_l=Nm_l, lastl=lastl):
                        r_ = e.matmul(PS(RB)[:, 0:128], Nm_l, Qv, start=True, stop=True)
                        if not lastl:
                            r_ = e.matmul(PS(RB)[:, 128:256], Bm_l, Tv, start=True, stop=True)
                        return r_
                    S.op("pe", f1_, r=[wk(cur), KK("M", lv)], w=[("ps", RB)])
                    nx = 128 if lastl else 256
                    S.op("act", lambda e, nx=nx: e.activation(out=XY[:, 0:nx], in_=PS(RB)[:, 0:nx], func=AF.Copy), r=[("ps", RB)], w=[KK("XY")])
                    def f2_(e, lastl=lastl):
                        r_ = e.matmul(PS(RB)[:, 256:384], Tv, XY[:, 0:128], start=True, stop=True)
                        if not lastl:
                            r_ = e.matmul(PS(RB)[:, 384:512], Qv, XY[:, 128:256], start=True, stop=True)
                        return r_
                    S.op("pe", f2_, r=[wk(cur), KK("XY")], w=[("ps", RB)])
                    if lastl:
                        S.op("dve", lambda e: e.tensor_tensor(out=cs["QT"], in0=Qv, in1=PS(RB)[:, 256:384], op=ALU.subtract),
                             r=[("ps", RB), wk(cur)], w=[KK("QT")])
                    else:
                        Wq = Wc.rearrange("p (a b) -> p a b", a=4)[:, 0::2, :]
                        S.op("dve", lambda e, Wq=Wq: e.tensor_tensor(out=Wq, in0=Wq, in1=PS(RB)[:, 256:512].rearrange("p (a b) -> p a b", a=2), op=ALU.subtract),
                             r=[("ps", RB), wk(cur)], w=[wk(cur)])
                    yield
                QT = cs["QT"]
                qkey = KK("QT")
            else:
                BN = cs["BN"]
                Q = cs["Q"]
                S.op("act", lambda e: e.activation(out=BN[0][0:C, 128:128 + C], in_=PSB(4)[0:C, 256:256 + C], func=AF.Copy), r=[("ps", 4)], w=[KK("W", 0)])
                S.op("pool", lambda e: e.tensor_copy(out=BN[0][0:C, 0:C], in_=Bm), r=[KK("NTQK")], w=[KK("W", 0)])
                S.op("pool", lambda e: e.tensor_tensor(out=Q[0][0:C, 0:C], in0=ident_b[0:C, 0:C], in1=Bm, op=ALU.subtract),
                     r=[KK("NTQK"), "cb"], w=[KK("W", 0)])
                nr = 7 if C == 128 else 3
                cur = 0
                for k in range(nr):
                    nxt = 1 - cur
                    Bk = BN[cur][0:C, 0:C]
                    Nk = BN[cur][0:C, 128:128 + C]
                    last = (k == nr - 1)
                    def fr(e, k=k, Bk=Bk, Nk=Nk, cur=cur, last=last):
                        r_ = None
                        if k > 0:
                            r_ = e.matmul(PS(RB)[0:C, 0:C], Nk, Q[cur][0:C, 0:C], start=True, stop=True)
                        if not last:
                            r_ = e.matmul(PS(RB)[0:C, 128:128 + C], Nk, Bk, start=True, stop=True)
                            r_ = e.matmul(PS(RB)[0:C, 256:256 + C], Bk, Nk, start=True, stop=True)
                        return r_
                    S.op("pe", fr, r=[KK("W", cur)], w=[("ps", RB)])
                    if not last:
                        S.op("act", lambda e, nxt=nxt: e.activation(
                            out=BN[nxt][0:C, :].rearrange("p (a b) -> p a b", a=2)[:, :, 0:C],
                            in_=PS(RB)[0:C, 128:384].rearrange("p (a b) -> p a b", a=2)[:, :, 0:C], func=AF.Copy),
                             r=[("ps", RB)], w=[KK("W", nxt)])
                    if k > 0:
                        S.op("dve", lambda e, cur=cur, nxt=nxt: e.tensor_tensor(out=Q[nxt][0:C, 0:C], in0=PS(RB)[0:C, 0:C], in1=Q[cur][0:C, 0:C], op=ALU.add),
                             r=[("ps", RB), KK("W", cur)], w=[KK("W", nxt)])
                    else:
                        S.op("pool", lambda e, cur=cur, nxt=nxt: e.tensor_copy(out=Q[nxt][0:C, 0:C], in_=Q[cur][0:C, 0:C]),
                             r=[KK("W", cur)], w=[KK("W", nxt)])
                    cur = nxt
                    yield
                S.op("pool", lambda e, cur=cur: e.tensor_copy(out=cs["QT"][0:C, 0:C], in_=Q[cur][0:C, 0:C]), r=[KK("W", cur)], w=[KK("QT")])
                QT = cs["QT"][0:C, 0:C]
                qkey = KK("QT")
            S.op("dve", lambda e: e.tensor_scalar(out=cs["QTs"][0:C, 0:C], in0=QT, scalar1=kbg_s[hs_][0:C, col:col + 1], scalar2=None, op0=ALU.mult),
                 r=[qkey, hk("kbg_s")], w=[KK("QTs")])
            S.op("pe", lambda e: e.matmul(PS(4)[:, 256:256 + C], cs["ktok"][0:C, :], cs["QTs"][0:C, 0:C], start=True, stop=True),
                 r=[KK("ktok"), KK("QTs")], w=[("ps", 4)])
            S.op("act", lambda e: e.activation(out=cs["kcTn"][:, 0:C], in_=PS(4)[:, 256:256 + C], func=AF.Copy, scale=-1.0),
                 r=[("ps", 4)], w=[KK("kcTn")])
            yield

        def CR_chunk(h, spec, cs, slot, Sfl, Sbf, skey, n):
            def KK(name, *rest):
                return (("ho", slot), name) + rest
            C, t0, col, smp, s = spec
            hs_ = h % 2
            qT = cvo[hs_][0]
            zT = cvo[hs_][3]
            cvk = [("cvo", hs_, j, bi) for j in (0, 3) for bi in range(5)]
            hk = lambda nm: ("hs", hs_, "s" if smp else "p", nm)
            QT = cs["QT"][0:C, 0:C]
            qk = KK("QT")
            x = n % 2
            def fu(e):
                e.matmul(PS(7)[0:C, 0:128], QT, cs["vb"][0:C, :], start=True, stop=False)
                return e.matmul(PS(7)[0:C, 0:128], cs["kcTn"][:, 0:C], Sbf, start=False, stop=True)
            S.op("pe", fu, r=[qk, KK("vb"), KK("kcTn"), skey + ("b",)], w=[("ps", 7)])
            S.op("act", lambda e: e.activation(out=u_t[x][0:C], in_=PS(7)[0:C, 0:128], func=AF.Copy), r=[("ps", 7)], w=[("u_t", x)])
            S.op("dve", lambda e: e.tensor_scalar(out=us_t[x][0:C], in0=PS(7)[0:C, 0:128], scalar1=kdec_s[hs_][0:C, col:col + 1], scalar2=None, op0=ALU.mult),
                 r=[("ps", 7), hk("kdec_s")], w=[("us_t", x)])
            yield
            def fo(e):
                e.matmul(PS(7)[0:C, 128:256], qT[:, t0:t0 + C], Sbf, start=True, stop=True)
                e.matmul(PS(7)[0:C, 256:384], cs["NTQK"][0:C, 0:C], u_t[x][0:C], start=True, stop=True)
                return e.matmul(PS(7)[:, 384:512], cs["ktok"][0:C, :], us_t[x][0:C], start=True, stop=True)
            S.op("pe", fo, r=cvk + [skey + ("b",), KK("NTQK"), ("u_t", x), KK("ktok"), ("us_t", x)], w=[("ps", 7)])
            gcolumn = (gtot_t[:, 128 + s * 8 + h:128 + s * 8 + h + 1] if smp else gtot_t[:, col * 8 + h:col * 8 + h + 1])
            S.op("dve", lambda e: e.scalar_tensor_tensor(out=Sfl, in0=Sfl, scalar=gcolumn, in1=PS(7)[:, 384:512], op0=ALU.mult, op1=ALU.add),
                 r=[("ps", 7), skey + ("f",), "gtot_s" if smp else "gtot_p"], w=[skey + ("f",)])
            S.op("act", lambda e: e.activation(out=Sbf, in_=Sfl, func=AF.Copy), r=[skey + ("f",)], w=[skey + ("b",)])
            S.op("act", lambda e: e.activation(out=t2_t[x][0:C], in_=PS(7)[0:C, 256:384], func=AF.Copy), r=[("ps", 7)], w=[("t2", x)])
            S.op("dve", lambda e: e.scalar_tensor_tensor(out=o_t[x][0:C], in0=PS(7)[0:C, 128:256], scalar=qdec_s[hs_][0:C, col:col + 1],
                                                         in1=t2_t[x][0:C], op0=ALU.mult, op1=ALU.add),
                 r=[("ps", 7), ("t2", x), hk("qdec_s")], w=[("o_t", x)])
            yield
            S.op("act", lambda e: e.activation(out=junk2[0:C], in_=o_t[x][0:C], func=AF.Square, accum_out=oss[0:C, x:x + 1]),
                 r=[("o_t", x)], w=["junk2", ("oss", x)])
            S.op("pool", lambda e: e.tensor_scalar(out=oss[0:C, x:x + 1], in0=oss[0:C, x:x + 1], scalar1=1.0 / 128, scalar2=EPS, op0=ALU.mult, op1=ALU.add),
                 r=[("oss", x)], w=[("oss", x)])
            S.op("pool", lambda e: e.tensor_tensor(out=oss[0:C, x:x + 1], in0=oss[0:C, x:x + 1], in1=nhalf_c[0:C], op=ALU.pow),
                 r=[("oss", x), "cf"], w=[("oss", x)])
            S.op("act", lambda e: e.activation(out=on_t[x][0:C], in_=o_t[x][0:C], func=AF.Identity, scale=oss[0:C, x:x + 1]),
                 r=[("o_t", x), ("oss", x)], w=[("on_t", x)])
            S.op("pe", lambda e: e.transpose(PSB(5)[:, 0:C], on_t[x][0:C, :], ident_b[0:C, 0:C]), r=[("on_t", x), "cb"], w=[("ps", 5)])
            S.op("dve", lambda e: e.scalar_tensor_tensor(out=ogT[:, h, t0:t0 + C], in0=PSB(5)[:, 0:C], scalar=outg_h,
                                                         in1=zT[:, t0:t0 + C], op0=ALU.mult, op1=ALU.mult),
                 r=[("ps", 5), "outg_h"] + cvk, w=[("ogT", h, t0)])
            yield

        def CP_lane(h, lane):
            specs = chunk_specs(h)
            mine = list(range(lane, len(specs), NLANE))
            for _ in range(lane * STAG):
                yield 0.0
            for k_, n in enumerate(mine):
                spec = specs[n]
                gi = cidx[0] + n
                while gi - cr_done[0] >= NHO:
                    yield WAIT
                slot = gi % NHO
                cs = dict(lanes[lane])
                cs.update(hos[slot])
                for yi, _ in enumerate(CP_chunk(h, spec, cs, lane, slot)):
                    yield (k_ + min(0.95, (yi + 1) / 14.0)) / len(mine)
                cp_done.add(gi)
                yield (k_ + 1.0) / len(mine)

        def CR_head(h):
            specs = chunk_specs(h)
            yield 0.0
            for n, spec in enumerate(specs):
                C, t0, col, smp, s = spec
                gi = cidx[0] + n
                while gi not in cp_done:
                    yield WAIT
                slot = gi % NHO
                cs = hos[slot]
                if n == 0 or smp:
                    si = sidx[0] % 2
                    sidx[0] += 1
                    Sfl, Sbf, skey = Sf[si], Sb[si], ("S", si)
                    if smp:
                        S.dma("sp", Sfl, I["sdelta"][s, h], w=[skey + ("f",)])
                        S.op("act", lambda e, Sbf=Sbf, Sfl=Sfl: e.activation(out=Sbf, in_=Sfl, func=AF.Copy), r=[skey + ("f",)], w=[skey + ("b",)])
                    else:
                        S.op("pool", lambda e, Sfl=Sfl: e.memset(Sfl, 0.0), w=[skey + ("f",)])
                        S.op("pool", lambda e, Sbf=Sbf: e.memset(Sbf, 0.0), w=[skey + ("b",)])
                for yi, _ in enumerate(CR_chunk(h, spec, cs, slot, Sfl, Sbf, skey, gi)):
                    yield (n + (yi + 1) / 4.0) / len(specs) - 0.08
                if n == NT - 1 or smp:
                    dst = O["ds"][s, h] if smp else O["dp"][h]
                    S.dma("sp", dst, Sfl, r=[skey + ("f",)])
                cr_done[0] = gi + 1
                yield (n + 1.0) / len(specs) - 0.08

        S.barrier()
        run_tasks([P_head(0)])
        if stop == "P0":
            S.finish(); print("STOP P0", S.ninst); return nc
        for h in range(nheads):
            gens = [CP_lane(h, ln) for ln in range(NLANE)] + [CR_head(h)]
            if h + 1 < nheads:
                gens.append(P_head(h + 1))
            run_tasks(gens)
            cidx[0] += NT + NS
            if stop == ("H", h):
                S.finish(); print("STOP H", h, S.ninst); return nc
        S.barrier()

        if dbg:
            S.dma("sp", O["dbg_og"], ogT, r=[("ogT", h, t * 128) for h in range(8) for t in range(16)] + [("ogT", h, L + s * 8) for h in range(8) for s in range(NS)])
            S.dma("sp", O["dbg_hT"], hT, r=HT_KEYS)
            S.barrier()
        R1.reset(); R3.reset(); R4.reset()
        x1 = R1.alloc([16, D], F32)
        wo = R3.alloc([8, D], BF16)
        xst2 = [R3.alloc([D], F32) for _ in range(2)]
        xss = R4.alloc([D], F32, parts=32)
        ysb0 = R4.alloc([D], F32, parts=32)

        def wout_phase(l, wsrc, x_in, x_out_fn):
            S.dma("pool", wo, wsrc.rearrange("(c p) n -> p c n", p=128), w=["wo"])
            wos = wo
            if l == 0:
                S.dma("sp", xss, I["xs"], w=["xss"])
            for hb in range(2):
                def f(e, hb=hb):
                    last = None
                    for ec in range(8):
                        last = e.matmul(PS(hb)[0:32, :], ogT[:, ec, L:TT], wo[:, ec, hb * 512:(hb + 1) * 512], start=(ec == 0), stop=(ec == 7))
                    return last
                S.op("pe", f, r=["wo"] + [("ogT", h, L + s * 8) for h in range(8) for s in range(NS)], w=[("ps", hb)])
                ysb = xss if l == 1 else ysb0
                S.op("dve", lambda e, hb=hb, ysb=ysb: e.tensor_tensor(out=ysb[:, hb * 512:(hb + 1) * 512], in0=PS(hb)[0:32, :], in1=gate_s[:, hb * 512:(hb + 1) * 512], op=ALU.mult),
                     r=[("ps", hb), ("gate_s", l)], w=[("ysb", hb)])
                src = xss if l == 0 else x1s
                S.op("dve", lambda e, hb=hb, src=src, ysb=ysb: e.tensor_tensor(out=x1s[:, hb * 512:(hb + 1) * 512], in0=ysb[:, hb * 512:(hb + 1) * 512],
                                                                       in1=src[:, hb * 512:(hb + 1) * 512], op=ALU.add),
                     r=[("ysb", hb), "xss", ("x1s", hb)], w=[("x1s", hb)])
            for ec in range(8):
                S.op("pool", lambda e, ec=ec: e.tensor_tensor(out=wo[:, ec, :], in0=wo[:, ec, :], in1=gate_p, op=ALU.mult),
                     r=["wo", ("gate_p", l)], w=["wo"])
            for t in range(16):
                xb, xkey = x_in(t)
                for hb in range(2):
                    bk = (2 * t + hb) % 4
                    def f(e, hb=hb, bk=bk, t=t):
                        last = None
                        for ec in range(8):
                            last = e.matmul(PS(bk), ogT[:, ec, t * 128:(t + 1) * 128], wo[:, ec, hb * 512:(hb + 1) * 512], start=(ec == 0), stop=(ec == 7))
                        return last
                    S.op("pe", f, r=["wo"] + [("ogT", h, t * 128) for h in range(8)], w=[("ps", bk)])
                    S.op("dve", lambda e, hb=hb, bk=bk, t=t, xb=xb: e.tensor_tensor(out=x_out_fn(t)[:, hb * 512:(hb + 1) * 512], in0=PS(bk),
                                                                                     in1=xb[:, hb * 512:(hb + 1) * 512], op=ALU.add),
                         r=[("ps", bk)] + (xkey if isinstance(xkey, list) else [xkey]), w=[("x1", t, hb)])

        def x_in0(t):
            buf = xst2[t % 2]
            key = ("xst2", t % 2)
            S.dma("sp", buf, I["xp"][t * 128:(t + 1) * 128, :], w=[key])
            return buf, key

        wout_phase(0, I["awout"], x_in0, lambda t: x1[:, t, :])
        if dbg:
            for t in range(16):
                S.dma("sp", O["dbg_x1p"][t * 128:(t + 1) * 128, :], x1[:, t, :], r=[("x1", t, 0), ("x1", t, 1)])
            S.dma("sp", O["dbg_x1s"], x1s, r=[("x1s", 0), ("x1s", 1)])
        if not do_l1:
            S.finish()
            print("instructions:", S.ninst, "sems:", S.nsem)
            return nc
        ada_phase(1, gate_p, gate_s, R4, parts=("col",))
        S.barrier()
        R3.reset(); R4.reset()
        hT1 = R3.alloc([8, TT], BF16)
        RE = Region(A, NB - 12288, NB - 4096)
        qs_all = RE.alloc([8, 3, 32], BF16)
        zs_all = RE.alloc([8, 32], BF16)
        onew = RE.alloc([8, 32], F32)
        lnew = RE.alloc([8, 32], F32)
        ptn = RE.alloc([32], BF16, parts=32)

        def x_src1(t):
            if t < 16:
                return x1[:, t, :], [("x1", t, 0), ("x1", t, 1)]
            return x1s, [("x1s", 0), ("x1s", 1)]

        norm_phase(1, hT1, x_src1, R4)
        S.barrier()
        R4.reset()
        GRP = ((128, 1), (512, 4), (2048, 16))
        SCALE = 128.0 ** -0.5
        Wg = [R4.alloc([8, 384], BF16) for _ in range(2)]
        Wz = RC.alloc([8, 128], BF16)
        QK_ = [R4.alloc([2, TT], BF16) for _ in range(3)]
        QT_ = [qk_[:, 0, :] for qk_ in QK_]
        KT_ = [qk_[:, 1, :] for qk_ in QK_]
        qkb = [R4.alloc([256], BF16) for _ in range(2)]
        Vt_ = [R4.alloc([17, 128], BF16) for _ in range(3)]
        zTh = R4.alloc([1024], BF16)
        PTb = [R4.alloc([512], BF16) for _ in range(2)]
        stg1_ = R4.alloc([256], F32)
        stg_ = [stg1_, stg1_]
        fin1_ = R4.alloc([512], F32)
        fin_ = [fin1_, fin1_]
        print("L1 R4 used", R4.cur - R4.lo, "of", R4.hi - R4.lo)
        KVO = [O["kv0p"], O["kv1p"], O["kv2p"]]
        wcount = [0]
        stc = [0]
        ptc = [0]
        fnc = [0]

        def tok_ap(g, ti):
            d = GRP[g][1]
            if d == 1:
                return lambda kc: hT1[:, kc, ti * 128:(ti + 1) * 128]
            if d == 4:
                r, nb = ti // 4, ti % 4
                return lambda kc: hT1[:, kc, 0:L].rearrange("p (j r) -> p r j", r=4)[:, r, nb * 128:(nb + 1) * 128]
            return lambda kc: hT1[:, kc, 0:L].rearrange("p (j r) -> p r j", r=16)[:, ti, :]

        def blk_ap(g, blk):
            d = GRP[g][1]
            if d == 1:
                return lambda kc: hT1[:, kc, blk * 512:(blk + 1) * 512]
            if d == 4:
                return lambda kc: hT1[:, kc, 0:L].rearrange("p (j r) -> p r j", r=4)[:, blk, :]
            return lambda kc: hT1[:, kc, 0:L].rearrange("p (j r) -> p r j", r=16)[:, 4 * blk:4 * blk + 4, :]

        def kept_rows(g, ti):
            w_, d = GRP[g]
            if d == 1:
                return (0, 1) if ti == 15 else None
            if d == 4:
                r, nb = ti // 4, ti % 4
                return (r, 4) if nb == 3 else None
            return (ti, 16)

        def L1_proj(h, g):
            d = GRP[g][1]
            wb = Wg[wcount[0] % 2]
            wkey = ("Wg", wcount[0] % 2)
            wcount[0] += 1
            for t_ in range(3):
                col = t_ * 3072 + g * 1024 + h * 128
                S.dma("pool", wb[:, :, t_ * 128:(t_ + 1) * 128], I["bwin"][:, col:col + 128].rearrange("(c p) n -> p c n", p=128),
                      w=[wkey + (t_,)])
            wk = [wkey + (t_,) for t_ in range(3)]
            pend = []

            def emit_tr(ti, p, qb, qbk):
                tb_ = 2 + (ti % 2)
                def ftp(e):
                    e.transpose(PSB(tb_)[:, 0:p], qb[0:p, 0:128], ident_b[0:p, 0:p])
                    return e.transpose(PSB(tb_)[:, 128:128 + p], qb[0:p, 128:256], ident_b[0:p, 0:p])
                S.op("pe", ftp, r=[qbk, "cb"], w=[("ps", tb_)])
                t0 = ti * 128 if ti < 16 else L
                dst = QK_[g][:, :, t0:t0 + p]
                src = PSB(tb_)[:, 0:256].rearrange("p (a b) -> p a b", a=2)[:, :, 0:p]
                if ti % 2 == 1:
                    S.op("act", lambda e: e.activation(out=dst, in_=src, func=AF.Copy), r=[("ps", tb_)], w=[("qkT", g, ti)])
                else:
                    S.op("dve", lambda e: e.tensor_copy(out=dst, in_=src), r=[("ps", tb_)], w=[("qkT", g, ti)])

            for ti in range(17):
                bk = ti % 2
                p = 128 if ti < 16 else 32
                ap = tok_ap(g, ti) if ti < 16 else (lambda kc: hT1[:, kc, L:TT])
                def f(e, ap=ap, bk=bk, p=p):
                    last = None
                    for kc in range(8):
                        last = e.matmul(PS(bk)[0:p, 0:384], ap(kc), wb[:, kc, 0:384], start=(kc == 0), stop=(kc == 7))
                    return last
                S.op("pe", f, r=wk + HT_KEYS, w=[("ps", bk)])
                qb = qkb[ti % 2]
                qbk = ("qkb", ti % 2)
                S.op("act", lambda e, bk=bk, p=p, qb=qb: e.activation(out=qb[0:p], in_=PS(bk)[0:p, 0:256], func=AF.Copy), r=[("ps", bk)], w=[qbk])
                S.op("dve", lambda e, bk=bk, p=p, ti=ti: e.tensor_copy(out=Vt_[g][0:p, ti, :], in_=PS(bk)[0:p, 256:384]),
                     r=[("ps", bk)], w=[("vt", g, ti)])
                kr = kept_rows(g, ti) if ti < 16 else None
                if kr is not None or ti == 16:
                    sb = stg_[stc[0] % 2]
                    skey = ("stg", 0)
                    stc[0] += 1
                    S.op("dve", lambda e, sb=sb, bk=bk, p=p: e.tensor_copy(out=sb[0:p], in_=PS(bk)[0:p, 128:384]), r=[("ps", bk)], w=[skey])
                    if ti < 16:
                        r0, st = kr
                        dst = KVO[g].rearrange("(q s) k hh e -> s q k hh e", s=st)[r0, :, :, h, :]
                        S.dma("sp", dst, sb.rearrange("p (k e) -> p k e", k=2), r=[skey])
                    else:
                        dst = O["kv%ds" % g].rearrange("s l k hh e -> (s l) k hh e")[:, :, h, :]
                        S.dma("sp", dst, sb[0:32].rearrange("p (k e) -> p k e", k=2), r=[skey])
                pend.append((ti, p, qb, qbk))
                if len(pend) > 1:
                    emit_tr(*pend.pop(0))
                yield
            while pend:
                emit_tr(*pend.pop(0))
            yield

        def L1_units(g, hf):
            d = GRP[g][1]
            units = []
            if d == 1:
                for nb in range(8 * hf, 8 * hf + 8):
                    col0 = 128 * (nb - 8 * hf)
                    outs = [(col0 // 512, (lambda B, c=col0 % 512: B[:, c:c + 128]), 0, 128)]
                    q = QT_[g][:, nb * 128:(nb + 1) * 128]
                    for pv, kt in ((True, nb - 1), (False, nb)):
                        if kt < 0:
                            continue
                        units.append((KT_[g][:, kt * 128:(kt + 1) * 128], Vt_[g][:, kt, :], 128, mT_prev if pv else mT_cur, q, 128, outs, ("vt", g, kt)))
            elif d == 4:
                for r in range(4):
                    for nb in range(2 * hf, 2 * hf + 2):
                        outs = [(nb - 2 * hf, (lambda B, r=r: B.rearrange("p (q r) -> p r q", r=4)[:, r, :]), 0, 128)]
                        q = QT_[g][:, r * 512 + nb * 128:r * 512 + (nb + 1) * 128]
                        for pv, kb in ((True, nb - 1), (False, nb)):
                            if kb < 0:
                                continue
                            kt = r * 4 + kb
                            units.append((KT_[g][:, kt * 128:(kt + 1) * 128], Vt_[g][:, kt, :], 128, mT_prev if pv else mT_cur, q, 128, outs, ("vt", g, kt)))
            else:
                for r in range(16):
                    outs = [(m, (lambda B, r=r: B.rearrange("p (j r) -> p r j", r=16)[:, r, :]), 32 * m, 32) for m in range(2)]
                    q = QT_[g][:, r * 128 + 64 * hf:r * 128 + 64 * hf + 64]
                    if hf == 0:
                        units.append((KT_[g][:, r * 128:r * 128 + 64], Vt_[g][0:64, r, :], 64, mT_cur[0:64, 0:64], q, 64, outs, ("vt", g, r)))
                    else:
                        units.append((KT_[g][:, r * 128:(r + 1) * 128], Vt_[g][:, r, :], 128, mT_cur[:, 64:128], q, 64, outs, ("vt", g, r)))
            return units

        OB = (4, 5)
        LB = (6, 7)

        def L1_pass(h, hf):
            for b_ in OB + LB:
                S.op("dve", lambda e, b_=b_: e.memset(PS(b_), 0.0), w=[("ps", b_)])
            for m in range(2):
                def f(e, m=m):
                    last = None
                    for kc in range(8):
                        last = e.matmul(PS(m), Wz[:, kc, :], hT1[:, kc, 1024 * hf + 512 * m:1024 * hf + 512 * (m + 1)], start=(kc == 0), stop=(kc == 7))
                    return last
                S.op("pe", f, r=["Wz"] + HT_KEYS, w=[("ps", m)])
                S.op("act", lambda e, m=m: e.activation(out=zTh[:, 512 * m:512 * (m + 1)], in_=PS(m), func=AF.Silu), r=[("ps", m)], w=[("zTh", m)])
            yield
            allu = []
            pend_pv = []
            for g in range(3):
                allu += [(g, u) for u in L1_units(g, hf)]
            for i0 in range(0, len(allu), 4):
                grp = allu[i0:i0 + 4]
                sb_ = 2 + (ptc[0] % 2)
                pt = PTb[ptc[0] % 2]
                pkey = ("PT", ptc[0] % 2)
                ptc[0] += 1
                rk = []
                def fs(e, grp=grp, sb_=sb_):
                    last = None
                    for ui, (g, (kT, v, nk, mk, q, nq, outs, vkey)) in enumerate(grp):
                        o_ = PS(sb_)[0:nk, ui * 128:ui * 128 + nq]
                        e.matmul(o_, kT, q, start=True, stop=False)
                        last = e.matmul(o_, ident_b[0:nk, 0:nk], mk, start=False, stop=True)
                    return last
                for (g, u) in grp:
                    rk += [("qkT", g, ti_) for ti_ in range(16)]
                S.op("pe", fs, r=list(set(rk)) + ["cb"], w=[("ps", sb_)])
                S.op("act", lambda e, sb_=sb_, pt=pt: e.activation(out=pt, in_=PS(sb_), func=AF.Exp, scale=SCALE), r=[("ps", sb_)], w=[pkey])
                def fp(e, grp=grp, pt=pt):
                    last = None
                    for ui, (g, (kT, v, nk, mk, q, nq, outs, vkey)) in enumerate(grp):
                        for (bi_, ofn, c0, n_) in outs:
                            rhs = pt[0:nk, ui * 128 + c0:ui * 128 + c0 + n_]
                            e.matmul(ofn(PS(OB[bi_])), v, rhs, start=False, stop=False, skip_group_check=True)
                            last = e.matmul(ofn(PS(LB[bi_])), ones_b[0:nk, :], rhs, start=False, stop=False, skip_group_check=True)
                    return last
                pend_pv.append((fp, [pkey, "cb"] + [u[7] for (_, u) in grp]))
                if len(pend_pv) > 1:
                    fq, rq_ = pend_pv.pop(0)
                    S.op("pe", fq, r=rq_, w=[("ps", b_) for b_ in OB + LB])
                yield
            while pend_pv:
                fq, rq_ = pend_pv.pop(0)
                S.op("pe", fq, r=rq_, w=[("ps", b_) for b_ in OB + LB])
            for m in range(2):
                fb = fin_[fnc[0] % 2]
                fkey = ("fin", 0)
                fnc[0] += 1
                S.op("act", lambda e, fb=fb, m=m: e.activation(out=fb, in_=PS(LB[m]), func=AF.Ln), r=[("ps", LB[m])], w=[fkey])
                S.op("act", lambda e, fb=fb: e.activation(out=fb, in_=fb, func=AF.Exp, scale=-1.0), r=[fkey], w=[fkey])
                S.op("dve", lambda e, fb=fb, m=m: e.tensor_tensor(out=fb, in0=PS(OB[m]), in1=fb, op=ALU.mult), r=[("ps", OB[m]), fkey], w=[fkey])
                p0 = 1024 * hf + 512 * m
                S.op("dve", lambda e, fb=fb, m=m, p0=p0: e.tensor_tensor(out=ogT[:, h, p0:p0 + 512], in0=fb, in1=zTh[:, 512 * m:512 * (m + 1)], op=ALU.mult),
                     r=[fkey, ("zTh", m)], w=[("ogT", h, p0 // 128 * 128 + i * 128) for i in range(4)])
            yield

        def L1_newrows(h):
            for g in range(3):
                S.op("pool", lambda e, g=g: e.tensor_copy(out=qs_all[:, h, g, :], in_=QT_[g][:, L:TT]), r=[("qkT", g, 16)], w=[("qs", h, g)])
                def fsn(e, g=g):
                    e.matmul(PS(2)[0:32, 0:32], KT_[g][:, L:TT], QT_[g][:, L:TT], start=True, stop=False)
                    return e.matmul(PS(2)[0:32, 0:32], ident_b[0:32, 0:32], cbf[0:32, CB_MN + g * 32:CB_MN + (g + 1) * 32], start=False, stop=True)
                S.op("pe", fsn, r=[("qkT", g, 16), "cb"], w=[("ps", 2)])
                S.op("act", lambda e: e.activation(out=ptn, in_=PS(2)[0:32, 0:32], func=AF.Exp, scale=SCALE), r=[("ps", 2)], w=["ptn"])
                def fpn(e, g=g):
                    e.matmul(PS(4)[:, 0:32], Vt_[g][0:32, 16, :], ptn, start=(g == 0), stop=(g == 2))
                    return e.matmul(PS(6)[:, 0:32], ones_b[0:32, :], ptn, start=(g == 0), stop=(g == 2))
                S.op("pe", fpn, r=["ptn", ("vt", g, 16), "cb"], w=[("ps", 4), ("ps", 6)])
            S.op("act", lambda e: e.activation(out=onew[:, h, :], in_=PS(4)[:, 0:32], func=AF.Copy), r=[("ps", 4)], w=[("onew", h)])
            S.op("dve", lambda e: e.tensor_copy(out=lnew[:, h, :], in_=PS(6)[:, 0:32]), r=[("ps", 6)], w=[("lnew", h)])
            def fz(e):
                last = None
                for kc in range(8):
                    last = e.matmul(PS(0)[:, 0:32], Wz[:, kc, :], hT1[:, kc, L:TT], start=(kc == 0), stop=(kc == 7))
                return last
            S.op("pe", fz, r=["Wz"] + HT_KEYS, w=[("ps", 0)])
            S.op("act", lambda e: e.activation(out=zs_all[:, h, :], in_=PS(0)[:, 0:32], func=AF.Silu), r=[("ps", 0)], w=[("zs", h)])

        def L1_head(h):
            S.dma("pool", Wz, I["bwin"][:, 9216 + h * 128:9216 + (h + 1) * 128].rearrange("(c p) n -> p c n", p=128), w=["Wz"])
            for g in range(3):
                for _ in L1_proj(h, g):
                    yield
            L1_newrows(h)
            yield
            for hf in range(2):
                for _ in L1_pass(h, hf):
                    yield

        for h in range(nheads):
            for _ in L1_head(h):
                pass
        S.barrier()

        R4.reset()
        NPF = 3
        NCL = (1, 4, 8)
        ckb = [[R4.alloc([NCL[g], 2, 128], BF16) for g in range(3)] for _ in range(NPF)]
        kTc = [R4.alloc([13, 128], BF16) for _ in range(2)]
        pts = [R4.alloc([24], BF16) for _ in range(2)]
        fo_ = R4.alloc([32], F32)
        fl_ = R4.alloc([32], F32)
        items = [(h, s_) for h in range(nheads) for s_ in range(NS)]

        def e2_load(i):
            h, s_ = items[i]
            for g in range(3):
                d = GRP[g][1]
                src = I["kc%d" % g][s_].rearrange("(k r) kv hh e -> k r kv hh e", r=d)[:, 0:NCL[g], :, h, :]
                S.dma("pool", ckb[i % NPF][g], src, w=[("ck", i % NPF, g)])

        for i in range(min(NPF - 1, len(items))):
            e2_load(i)
        for i, (h, s_) in enumerate(items):
            if i + NPF - 1 < len(items):
                e2_load(i + NPF - 1)
            buf = ckb[i % NPF]
            ckeys = [("ck", i % NPF, g) for g in range(3)]
            kt = kTc[i % 2]
            ktk = ("kTc", i % 2)
            pt = pts[i % 2]
            ptk = ("pts", i % 2)
            sb_ = 2 + (i % 2)
            tiles = [(0, 0)] + [(1, r) for r in range(4)] + [(2, r) for r in range(8)]
            if s_ == 0:
                S.op("dve", lambda e: e.memset(PS(4)[:, 0:32], 0.0), w=[("ps", 4)])
                S.op("dve", lambda e: e.memset(PS(6)[:, 0:32], 0.0), w=[("ps", 6)])
            def ftr(e, buf=buf):
                last = None
                for ti, (g, r) in enumerate(tiles):
                    last = e.transpose(PSB(ti // 8)[:, (ti % 8) * 128:(ti % 8 + 1) * 128], buf[g][:, r, 0, :], ident_b)
                return last
            S.op("pe", ftr, r=ckeys + ["cb"], w=[("ps", 0), ("ps", 1)])
            S.op("act", lambda e, kt=kt: e.activation(out=kt[:, 0:8, :], in_=PSB(0).rearrange("p (a b) -> p a b", a=8), func=AF.Copy), r=[("ps", 0)], w=[ktk + (0,)])
            S.op("dve", lambda e, kt=kt: e.tensor_copy(out=kt[:, 8:13, :], in_=PSB(1)[:, 0:640].rearrange("p (a b) -> p a b", a=5)), r=[("ps", 1)], w=[ktk + (1,)])
            def cls(g, r):
                d = GRP[g][1]
                nq = 8 // d if d <= 8 else 1
                c0 = (0, 8 + 2 * r, 16 + r)[g]
                if d == 1:
                    q = qs_all[:, h, g, 8 * s_:8 * s_ + 8]
                    mk = cbf[:, CB_MK:CB_MK + 8]
                    oc = lambda B: B[:, 8 * s_:8 * s_ + 8]
                elif d == 4:
                    q = qs_all[:, h, g, 8 * s_:8 * s_ + 8].rearrange("p (i r) -> p r i", r=4)[:, r, :]
                    mk = cbf[:, CB_MK + 8:CB_MK + 16].rearrange("p (i r) -> p r i", r=4)[:, r, :]
                    oc = lambda B: B[:, 8 * s_:8 * s_ + 8].rearrange("p (i r) -> p r i", r=4)[:, r, :]
                else:
                    q = qs_all[:, h, g, 8 * s_ + r:8 * s_ + r + 1]
                    mk = None
                    oc = lambda B: B[:, 8 * s_ + r:8 * s_ + r + 1]
                return nq, c0, q, mk, oc
            def fsc(e, kt=kt, sb_=sb_):
                last = None
                for ti, (g, r) in enumerate(tiles):
                    nq, c0, q, mk, oc = cls(g, r)
                    o_ = PS(sb_)[:, c0:c0 + nq]
                    last = e.matmul(o_, kt[:, ti, :], q, start=True, stop=(mk is None))
                    if mk is not None:
                        last = e.matmul(o_, ident_b, mk, start=False, stop=True)
                return last
            S.op("pe", fsc, r=[ktk + (0,), ktk + (1,), "cb"] + [("qs", h, g) for g in range(3)], w=[("ps", sb_)])
            S.op("act", lambda e, pt=pt, sb_=sb_: e.activation(out=pt, in_=PS(sb_)[:, 0:24], func=AF.Exp, scale=SCALE), r=[("ps", sb_)], w=[ptk])
            def fpv(e, buf=buf, pt=pt):
                last = None
                for ti, (g, r) in enumerate(tiles):
                    nq, c0, q, mk, oc = cls(g, r)
                    e.matmul(oc(PS(4)), buf[g][:, r, 1, :], pt[:, c0:c0 + nq], start=False, stop=False, skip_group_check=True)
                    last = e.matmul(oc(PS(6)), ones_b, pt[:, c0:c0 + nq], start=False, stop=False, skip_group_check=True)
                return last
            S.op("pe", fpv, r=ckeys + [ptk, "cb"], w=[("ps", 4), ("ps", 6)])
            if s_ == NS - 1:
                S.op("dve", lambda e, h=h: e.tensor_tensor(out=fo_, in0=PS(4)[:, 0:32], in1=onew[:, h, :], op=ALU.add), r=[("ps", 4), ("onew", h)], w=["fo_"])
                S.op("dve", lambda e, h=h: e.tensor_tensor(out=fl_, in0=PS(6)[:, 0:32], in1=lnew[:, h, :], op=ALU.add), r=[("ps", 6), ("lnew", h)], w=["fl_"])
                S.op("act", lambda e: e.activation(out=fl_, in_=fl_, func=AF.Ln), r=["fl_"], w=["fl_"])
                S.op("act", lambda e: e.activation(out=fl_, in_=fl_, func=AF.Exp, scale=-1.0), r=["fl_"], w=["fl_"])
                S.op("dve", lambda e: e.tensor_tensor(out=fo_, in0=fo_, in1=fl_, op=ALU.mult), r=["fo_", "fl_"], w=["fo_"])
                S.op("dve", lambda e, h=h: e.tensor_tensor(out=ogT[:, h, L:TT], in0=fo_, in1=zs_all[:, h, :], op=ALU.mult),
                     r=["fo_", ("zs", h)], w=[("ogT", h, L + q_ * 8) for q_ in range(NS)])
        S.barrier()
        R3.reset()
        ada_phase(1, gate_p, gate_s, R3, parts=("gate",))
        S.barrier()

        R4.reset()
        xss = R4.alloc([D], F32, parts=32)
        fng_bc = R4.alloc([D], F32)
        S.dma("sp", fng_bc, I["fng"].partition_broadcast(128), w=["fng"])
        wout_phase(1, I["bwout"], lambda t: (x1[:, t, :], [("x1", t, 0), ("x1", t, 1)]), lambda t: x1[:, t, :])
        ssq2 = R4.alloc([17], F32)
        junk3 = R4.alloc([D], BF16)
        ost = [R4.alloc([D], F32) for _ in range(2)]
        for t in range(17):
            p = 128 if t < 16 else 32
            xb, xk = x_src1(t)
            S.op("act", lambda e, xb=xb, p=p, t=t: e.activation(out=junk3[0:p], in_=xb[0:p], func=AF.Square, accum_out=ssq2[0:p, t:t + 1]),
                 r=xk, w=["junk3", ("ssq2", t)])
            S.op("pool", lambda e, p=p, t=t: e.tensor_scalar(out=ssq2[0:p, t:t + 1], in0=ssq2[0:p, t:t + 1], scalar1=1.0 / D, scalar2=EPS, op0=ALU.mult, op1=ALU.add),
                 r=[("ssq2", t)], w=[("ssq2", t)])
            S.op("pool", lambda e, p=p, t=t: e.tensor_tensor(out=ssq2[0:p, t:t + 1], in0=ssq2[0:p, t:t + 1], in1=nhalf_c[0:p], op=ALU.pow),
                 r=[("ssq2", t), "cf"], w=[("ssq2", t)])
            ob = ost[t % 2]
            okey = ("ost", t % 2)
            S.op("act", lambda e, xb=xb, p=p, t=t, ob=ob: e.activation(out=ob[0:p], in_=xb[0:p], func=AF.Identity, scale=ssq2[0:p, t:t + 1]),
                 r=xk + [("ssq2", t)], w=[okey])
            S.op("dve", lambda e, p=p, ob=ob: e.tensor_tensor(out=ob[0:p], in0=ob[0:p], in1=fng_bc[0:p], op=ALU.mult), r=[okey, "fng"], w=[okey])
            if t < 16:
                S.dma("sp", O["yp"][t * 128:(t + 1) * 128, :], ob, r=[okey])
            else:
                S.dma("sp", O["ys"], ob[0:32], r=[okey])
        S.finish()
        print("instructions:", S.ninst, "sems:", S.nsem)
    return nc


def prep_inputs(inp):
    f = lambda a: np.ascontiguousarray(np.asarray(a, dtype=np.float32))
    cf, cb = make_consts()
    ngc = f(np.asarray(inp["norm_g"]).reshape(2, 8, 128).transpose(0, 2, 1))
    adab = f(inp["ada_b"])
    adabc = f(adab[:, 0:2048].reshape(2, 16, 128).transpose(0, 2, 1))
    cw = f(np.asarray(inp["a_conv_w"])[0].reshape(4, 24, 128).transpose(2, 1, 0))
    hp = np.zeros((128, 17), np.float32)
    hp[:, 0:8] = np.asarray(inp["a_A_log"])[0][None, :]
    hp[:, 8:16] = np.asarray(inp["a_dt_bias"])[0][None, :]
    hp[:, 16] = np.asarray(inp["a_out_norm_g"])[0]
    shared = dict(ngc=ngc, adaw=f(inp["ada_w"]), adabc=adabc, adab=adab, awin=f(np.asarray(inp["a_w_in"])[0]), cw=cw, hp=hp,
                  awout=f(np.asarray(inp["a_w_out"])[0]), cf=cf, cb=cb, fng=f(inp["final_norm_g"]),
                  bwin=f(np.asarray(inp["b_w_in"])[0]), bwout=f(np.asarray(inp["b_w_out"])[0]))
    maps = []
    for c in range(NCORES):
        m = dict(shared)
        m["xp"] = f(np.asarray(inp["x_prompt"])[c])
        m["xs"] = f(np.asarray(inp["x_sample"])[4 * c:4 * c + 4].reshape(32, D))
        cc = np.concatenate([np.asarray(inp["c_prompt"])[c:c + 1], np.asarray(inp["c_sample"])[4 * c:4 * c + 4]], 0)
        m["cT"] = f(cc.T.reshape(8, 128, 5).transpose(1, 0, 2))
        sc = np.asarray(inp["state_conv"])[0, 4 * c:4 * c + 4]
        m["sconv"] = f(sc.reshape(4, 3, 24, 128).transpose(3, 2, 0, 1))
        m["sdelta"] = f(np.asarray(inp["state_delta"])[0, 4 * c:4 * c + 4])
        m["kc0"] = f(np.asarray(inp["cache_kv_w128"])[0, 4 * c:4 * c + 4])
        m["kc1"] = f(np.asarray(inp["cache_kv_w512"])[0, 4 * c:4 * c + 4])
        m["kc2"] = f(np.asarray(inp["cache_kv_w2048"])[0, 4 * c:4 * c + 4])
        maps.append(m)
    return maps


_NC_CACHE = {}


def kernel(**inp):
    if "nc" not in _NC_CACHE:
        _NC_CACHE["nc"] = build()
    nc = _NC_CACHE["nc"]
    maps = prep_inputs(inp)
    res = run_bass_kernel_spmd(nc, maps, core_ids=list(range(NCORES)))
    R = res.results
    g = lambda k: [np.asarray(R[c][k], dtype=np.float32) for c in range(NCORES)]
    y_prompt = np.stack(g("yp"), 0)
    y_sample = np.concatenate([a.reshape(NS, LS, D) for a in g("ys")], 0)
    delta_p = np.stack(g("dp"), 0)[None]
    delta_s = np.concatenate(g("ds"), 0)[None]
    conv_p = np.stack(g("cp"), 0)[None]
    conv_s = np.concatenate([a.reshape(NS, 3, 3072) for a in g("cs")], 0)[None]
    outs = [y_prompt, y_sample, delta_p, delta_s, conv_p, conv_s]
    for gi in range(3):
        outs.append(np.stack(g("kv%dp" % gi), 0)[None])
        outs.append(np.concatenate(g("kv%ds" % gi), 0)[None])
    return tuple(np.ascontiguousarray(o, dtype=np.float32) for o in outs)
```

```python
import numpy as np
import ml_dtypes
from contextlib import ExitStack
import concourse.bass as bass
import concourse.mybir as mybir
from concourse.bass_utils import run_bass_kernel_spmd

F32 = mybir.dt.float32
BF16 = mybir.dt.bfloat16
AF = mybir.ActivationFunctionType
ALU = mybir.AluOpType

NEG = -30000.0
NCORES = 8
L = 2048
NS = 4
LS = 8
TT = L + NS * LS
D = 1024
EPS = 1e-6


class Sched:
    ENG = ("pe", "act", "dve", "pool", "sp")

    def __init__(self, nc, stack, sempool):
        self.nc = nc
        self.stack = stack
        self.sempool = sempool
        self.eng = {"pe": nc.tensor, "act": nc.scalar, "dve": nc.vector, "pool": nc.gpsimd, "sp": nc.sync}
        self.gen = {e: 0 for e in self.ENG}
        self.cnt = {e: 0 for e in self.ENG}
        self.esem = {}
        self.seen = {e: {} for e in self.ENG}
        self.res = {}
        self.dsem = {}
        self.nsem = 0
        self.ninst = {e: 0 for e in self.ENG}

    def _newsem(self, name):
        self.nsem += 1
        return self.sempool.pop()

    def _R(self, k):
        r = self.res.get(k)
        if r is None:
            r = [None, {}]
            self.res[k] = r
        return r

    def _wait(self, en, events):
        need = {}
        for ev in events:
            if ev is None:
                continue
            if ev[0] == "E":
                if ev[1] == en and en == "pe":
                    continue
                k = ("E", ev[1])
                v = (ev[2], ev[3])
            else:
                k = ("D", ev[1])
                v = (0, ev[2])
            if need.get(k, (-1, -1)) < v:
                need[k] = v
        for k, v in need.items():
            if self.seen[en].get(k, (-1, -1)) >= v:
                continue
            self.seen[en][k] = v
            sem = self.esem[(k[1], v[0])] if k[0] == "E" else self.dsem[k[1]][0]
            self.eng[en].wait_ge(sem, v[1])

    def _deps(self, r, w):
        evs = []
        for k in r:
            R = self.res.get(k)
            if R is not None:
                evs.append(R[0])
                if isinstance(k, tuple) and k[0] == "ps":
                    evs.extend(R[1].values())
        for k in w:
            R = self.res.get(k)
            if R is not None:
                evs.append(R[0])
                evs.extend(R[1].values())
        return evs

    def _record(self, ev, semid, r, w):
        for k in r:
            self._R(k)[1][semid] = ev
        for k in w:
            R = self._R(k)
            R[0] = ev
            R[1] = {}

    def op(self, en, fn, r=(), w=()):
        self._wait(en, self._deps(r, w))
        inst = fn(self.eng[en])
        self.ninst[en] += 1
        if self.cnt[en] >= 12000:
            self.gen[en] += 1
            self.cnt[en] = 0
        self.cnt[en] += 1
        g = self.gen[en]
        if (en, g) not in self.esem:
            self.esem[(en, g)] = self._newsem(f"e_{en}_{g}")
        inst.then_inc(self.esem[(en, g)], 1)
        ev = ("E", en, g, self.cnt[en])
        self._record(ev, ("E", en), r, w)
        return inst

    def dma(self, q, out, in_, r=(), w=(), key=None, **kw):
        self._wait(q, self._deps(r, w))
        inst = self.eng[q].dma_start(out=out, in_=in_, **kw)
        if key is None:
            key = ("w", w[0]) if w else ("r", r[0])
        ds = self.dsem.get(key)
        if ds is None:
            ds = [self._newsem(f"d{len(self.dsem)}"), 0]
            self.dsem[key] = ds
        ds[1] += 16
        inst.then_inc(ds[0], 16)
        ev = ("D", key, ds[1])
        self._record(ev, ("D", key), r, w)
        return inst

    def _all_events(self):
        evs = []
        for e in self.ENG:
            if self.cnt[e] > 0 or self.gen[e] > 0:
                evs.append(("E", e, self.gen[e], self.cnt[e]))
        for k, ds in self.dsem.items():
            evs.append(("D", k, ds[1]))
        return evs

    def barrier(self):
        evs = self._all_events()
        for e in self.ENG:
            self._wait(e, evs)

    def finish(self):
        self._wait("sp", self._all_events())


WAIT = "WAIT"


def run_tasks(gens):
    act = list(gens)
    idle = 0
    i = 0
    while act:
        i %= len(act)
        g = act[i]
        try:
            v = next(g)
        except StopIteration:
            act.pop(i)
            idle = 0
            continue
        if v is WAIT:
            idle += 1
            assert idle <= 4 * len(act) + 4, "all tasks waiting"
        else:
            idle = 0
        i += 1


class Arena:
    def __init__(self, ap, nbytes):
        self.ap = ap
        self.nbytes = nbytes

    def view(self, off, parts, shape, dt):
        esz = 4 if dt == F32 else 2
        n = int(np.prod(shape))
        nb = n * esz
        assert off % 4 == 0 and off + nb <= self.nbytes, (off, nb, self.nbytes)
        nw = (nb + 3) // 4
        v = self.ap[0:parts, off // 4: off // 4 + nw]
        if dt != F32:
            v = v.bitcast(dt)
            if v.shape[1] != n:
                v = v[:, 0:n]
        if len(shape) == 2:
            return v.rearrange("p (a b) -> p a b", a=shape[0])
        if len(shape) == 3:
            return v.rearrange("p (a b c) -> p a b c", a=shape[0], b=shape[1])
        return v


class Region:
    def __init__(self, arena, lo, hi):
        self.arena, self.lo, self.hi = arena, lo, hi
        self.cur = lo

    def reset(self):
        self.cur = self.lo

    def alloc(self, shape, dt, parts=128):
        esz = 4 if dt == F32 else 2
        nb = (int(np.prod(shape)) * esz + 3) // 4 * 4
        off = self.cur
        assert off + nb <= self.hi, ("region overflow", off, nb, self.hi)
        self.cur += nb
        return self.arena.view(off, parts, shape, dt)


CF_ID, CF_U, CF_MASK, CF_EPS, CF_ONE, CF_NHALF, CF_LNSC, CF_ZERO, CF_N = 0, 128, 256, 512, 513, 514, 515, 516, 520
CB_ID, CB_ONES, CB_BD, CB_ML = 0, 128, 256, 384
CB_MC, CB_MP, CB_MN, CB_MK, CB_N = 1152, 1280, 1408, 1504, 1528


def make_consts():
    cf = np.zeros((128, CF_N), np.float32)
    cf[:, CF_ID:CF_ID + 128] = np.eye(128)
    k = np.arange(128)[:, None]
    i = np.arange(128)[None, :]
    cf[:, CF_U:CF_U + 128] = (k <= i)
    cf[:, CF_MASK:CF_MASK + 128] = np.where(i >= k, 0.0, NEG)
    cf[:, CF_MASK + 128:CF_MASK + 256] = np.where(i > k, 0.0, NEG)
    cf[:, CF_EPS] = EPS
    cf[:, CF_ONE] = 1.0
    cf[:, CF_NHALF] = -0.5
    cf[:, CF_LNSC] = np.log(128.0 ** -0.5)
    cb = np.zeros((128, CB_N), np.float32)
    cb[:, CB_ID:CB_ID + 128] = np.eye(128)
    cb[:, CB_ONES:CB_ONES + 128] = 1.0
    ii = np.arange(128)[:, None]
    jj = np.arange(128)[None, :]
    cb[:, CB_BD:CB_BD + 128] = (ii // 16 == jj // 16)
    for lv, b in enumerate((16, 32, 64)):
        off = (ii // (2 * b) == jj // (2 * b)) & (ii % (2 * b) >= b) & (jj % (2 * b) < b)
        cb[:, CB_ML + lv * 256:CB_ML + lv * 256 + 128] = off.T
        cb[:, CB_ML + lv * 256 + 128:CB_ML + lv * 256 + 256] = off
    cb[:, CB_MC:CB_MC + 128] = np.where(jj >= ii, 0.0, NEG)
    cb[:, CB_MP:CB_MP + 128] = np.where(jj <= ii, 0.0, NEG)
    for gi, d in enumerate((1, 4, 16)):
        a = np.arange(32)
        sk, jk = a[:, None] // 8, a[:, None] % 8
        sq, lq = a[None, :] // 8, a[None, :] % 8
        ok = (sk == sq) & (jk <= lq) & ((lq - jk) % d == 0)
        cb[0:32, CB_MN + gi * 32:CB_MN + (gi + 1) * 32] = np.where(ok, 0.0, NEG)
        cb[:, CB_MK + gi * 8:CB_MK + (gi + 1) * 8] = np.where(ii >= (np.arange(8)[None, :] // d), 0.0, NEG)
    return cf, cb.astype(ml_dtypes.bfloat16)


def build(dbg=False, nheads=8, do_l1=True, stop=None):
    nc = bass.Bass("TRN2", target_bir_lowering=False)

    def din(name, shape, dt=F32):
        return nc.dram_tensor(name, list(shape), dt, kind="ExternalInput").ap()

    def dout(name, shape, dt=F32):
        return nc.dram_tensor(name, list(shape), dt, kind="ExternalOutput").ap()

    I = dict(
        xp=din("xp", [L, D]), xs=din("xs", [NS * LS, D]),
        cT=din("cT", [128, 8, 5]), ngc=din("ngc", [2, 128, 8]),
        adaw=din("adaw", [2, D, 3 * D]), adabc=din("adabc", [2, 128, 16]), adab=din("adab", [2, 3 * D]),
        awin=din("awin", [D, 4112]), cw=din("cw", [128, 24, 4]), hp=din("hp", [128, 17]),
        sconv=din("sconv", [128, 24, NS, 3]), sdelta=din("sdelta", [NS, 8, 128, 128]),
        awout=din("awout", [D, D]),
        cf=din("cf", [128, CF_N]), cb=din("cb", [128, CB_N], BF16),
        fng=din("fng", [D]),
        bwin=din("bwin", [D, 10240]), bwout=din("bwout", [D, D]),
        kc0=din("kc0", [NS, 128, 2, 8, 128]), kc1=din("kc1", [NS, 512, 2, 8, 128]), kc2=din("kc2", [NS, 2048, 2, 8, 128]),
    )
    O = dict(
        yp=dout("yp", [L, D]), ys=dout("ys", [NS * LS, D]),
        dp=dout("dp", [8, 128, 128]), ds=dout("ds", [NS, 8, 128, 128]),
        cp=dout("cp", [3, 3072]), cs=dout("cs", [NS * 3, 3072]),
        kv0p=dout("kv0p", [128, 2, 8, 128]), kv1p=dout("kv1p", [512, 2, 8, 128]), kv2p=dout("kv2p", [2048, 2, 8, 128]),
        kv0s=dout("kv0s", [NS, LS, 2, 8, 128]), kv1s=dout("kv1s", [NS, LS, 2, 8, 128]), kv2s=dout("kv2s", [NS, LS, 2, 8, 128]),
    )
    if dbg:
        O["dbg_x1p"] = dout("dbg_x1p", [L, D])
        O["dbg_x1s"] = dout("dbg_x1s", [NS * LS, D])
        O["dbg_og"] = dout("dbg_og", [128, 8, TT], BF16)
        O["dbg_hT"] = dout("dbg_hT", [128, 8, TT], BF16)

    stack = ExitStack()
    with stack:
        NB = 212000
        arena_t = stack.enter_context(nc.sbuf_tensor("arena", [128, NB // 4], F32))
        banks = [stack.enter_context(nc.psum_tensor(f"bank{i}", [128, 512], F32)) for i in range(8)]
        sempool = [stack.enter_context(nc.semaphore(f"s{i}")) for i in range(96)]
        stack.enter_context(nc.Block())
        S = Sched(nc, stack, sempool)
        A = Arena(arena_t, NB)

        def PS(b):
            return banks[b][:, :]

        def PSB(b):
            return banks[b][:, :].bitcast(BF16)

        RC = Region(A, 0, 8192)
        R1 = Region(A, 8192, 73728)
        R2 = Region(A, 73728, 107008)
        R3 = Region(A, 107008, 140288)
        R4 = Region(A, 140288, NB)

        cf = RC.alloc([CF_N], F32)
        cbf = RC.alloc([CB_N], BF16)
        hpar = RC.alloc([17], F32)
        S.dma("sp", cf, I["cf"], w=["cf"])
        S.dma("sp", cbf, I["cb"], w=["cb"])
        S.dma("sp", hpar, I["hp"], w=["hpar"])
        ident_f = cf[:, CF_ID:CF_ID + 128]
        U_f = cf[:, CF_U:CF_U + 128]
        mask2 = cf[:, CF_MASK:CF_MASK + 256]
        eps_c = cf[:, CF_EPS:CF_EPS + 1]
        one_c = cf[:, CF_ONE:CF_ONE + 1]
        nhalf_c = cf[:, CF_NHALF:CF_NHALF + 1]
        lnsc_c = cf[:, CF_LNSC:CF_LNSC + 1]
        ident_b = cbf[:, CB_ID:CB_ID + 128]
        ones_b = cbf[:, CB_ONES:CB_ONES + 128]
        bd_b = cbf[:, CB_BD:CB_BD + 128]
        ml_b = [cbf[:, CB_ML + lv * 256:CB_ML + (lv + 1) * 256] for lv in range(3)]
        mT_cur = cbf[:, CB_MC:CB_MC + 128]
        mT_prev = cbf[:, CB_MP:CB_MP + 128]
        CK = ["cf", "cb", "hpar"]

        modc = RC.alloc([16, 5], F32)
        gmod = RC.alloc([8, 5], F32)
        ngc = RC.alloc([8], F32)
        adabc = RC.alloc([16], F32)
        cT = RC.alloc([8, 5], F32)
        scb = RC.alloc([8, 5], BF16)

        def ada_phase(l, gate_p, gate_s, reg, parts=("col", "gate")):
            S.dma("sp", ngc, I["ngc"][l], w=["ngc"])
            S.dma("sp", adabc, I["adabc"][l], w=["adabc"])
            if l == 0:
                S.dma("sp", cT, I["cT"], w=["cT"])
                S.op("act", lambda e: e.activation(out=scb, in_=cT, func=AF.Silu), r=["cT"], w=["scb"])
            scp = reg.alloc([8, 128], BF16)
            scs = reg.alloc([8, 32], BF16)
            S.op("act", lambda e: e.activation(out=scp, in_=cT[:, :, 0:1].broadcast_to([128, 8, 128]), func=AF.Silu),
                 r=["cT"], w=["scp"])
            for s in range(NS):
                S.op("act", lambda e: e.activation(out=scs[:, :, 8 * s:8 * s + 8],
                                                   in_=cT[:, :, 1 + s:2 + s].broadcast_to([128, 8, 8]), func=AF.Silu),
                     r=["cT"], w=[("scs", s)])
            gb = reg.alloc([D], F32)
            S.dma("sp", gb, I["adab"][l, 2 * D:3 * D].partition_broadcast(128), w=["gb"])
            wb = [reg.alloc([8, 512], BF16) for _ in range(2)]
            for blk in range(6):
                if (blk < 4 and "col" not in parts) or (blk >= 4 and "gate" not in parts):
                    continue
                buf = wb[blk % 2]
                key = ("adaw", blk % 2)
                S.dma("pool", buf, I["adaw"][l][:, blk * 512:(blk + 1) * 512].rearrange("(c p) n -> p c n", p=128),
                      w=[key])
                if blk < 4:
                    def f(e, blk=blk, buf=buf):
                        last = None
                        for ecl in range(4):
                            ec = blk * 4 + ecl
                            for kc in range(8):
                                last = e.matmul(PS(0)[:, ec * 5:ec * 5 + 5], buf[:, kc, ecl * 128:(ecl + 1) * 128],
                                                scb[:, kc, :], start=(kc == 0), stop=(kc == 7))
                        return last
                    S.op("pe", f, r=[key, "scb"], w=[("ps", 0)])
                else:
                    hb = blk - 4
                    bk = 1 + (hb % 2)
                    def f(e, buf=buf, bk=bk):
                        last = None
                        for kc in range(8):
                            last = e.matmul(PS(bk), scp[:, kc, :], buf[:, kc, :], start=(kc == 0), stop=(kc == 7))
                        return last
                    S.op("pe", f, r=[key, "scp"], w=[("ps", bk)])
                    S.op("dve", lambda e, bk=bk, hb=hb: e.tensor_tensor(out=gate_p[:, hb * 512:(hb + 1) * 512], in0=PS(bk),
                                                                         in1=gb[:, hb * 512:(hb + 1) * 512], op=ALU.add),
                         r=[("ps", bk), "gb"], w=[("gate_p", l)])
                    def f2(e, buf=buf, bk=bk):
                        last = None
                        for kc in range(8):
                            last = e.matmul(PS(bk)[0:32, :], scs[:, kc, :], buf[:, kc, :], start=(kc == 0), stop=(kc == 7))
                        return last
                    S.op("pe", f2, r=[key] + [("scs", s) for s in range(NS)], w=[("ps", bk)])
                    S.op("dve", lambda e, bk=bk, hb=hb: e.tensor_tensor(out=gate_s[:, hb * 512:(hb + 1) * 512], in0=PS(bk)[0:32, :],
                                                                         in1=gb[0:32, hb * 512:(hb + 1) * 512], op=ALU.add),
                         r=[("ps", bk), "gb"], w=[("gate_s", l)])
            if "col" not in parts:
                return
            S.op("dve", lambda e: e.tensor_tensor(out=modc, in0=PS(0)[:, 0:80].rearrange("p (a b) -> p a b", b=5),
                                                  in1=adabc.unsqueeze(2).broadcast_to([128, 16, 5]), op=ALU.add),
                 r=[("ps", 0), "adabc"], w=["modc"])
            S.op("dve", lambda e: e.scalar_tensor_tensor(out=gmod, in0=modc[:, 8:16, :], scalar=1.0,
                                                         in1=ngc.unsqueeze(2).broadcast_to([128, 8, 5]),
                                                         op0=ALU.add, op1=ALU.mult),
                 r=["modc", "ngc"], w=["gmod"])

        def norm_phase(l, hT, x_src, reg):
            ssq = reg.alloc([17], F32)
            rstd = reg.alloc([17], F32)
            junk = reg.alloc([D], BF16)
            xn = [reg.alloc([D], BF16) for _ in range(2)]
            ntile = 17
            tiles = []
            if l == 0:
                xst = [reg.alloc([D], F32) for _ in range(3)]

            def xt(t):
                if l == 0:
                    return xst[t % 3], ("xst", t % 3)
                return x_src(t)

            def load(t):
                if l != 0:
                    return
                buf, key = xt(t)
                if t < 16:
                    S.dma("sp", buf, I["xp"][t * 128:(t + 1) * 128, :], w=[key])
                else:
                    S.dma("sp", buf[0:32], I["xs"], w=[key])

            def sq(t):
                buf, key = xt(t)
                p = 128 if t < 16 else 32
                keys = key if isinstance(key, list) else [key]
                S.op("act", lambda e: e.activation(out=junk[0:p], in_=buf[0:p], func=AF.Square, accum_out=ssq[0:p, t:t + 1]),
                     r=keys, w=["junk", ("ssq", t)])
                S.op("pool", lambda e: e.tensor_scalar(out=rstd[0:p, t:t + 1], in0=ssq[0:p, t:t + 1], scalar1=1.0 / D, scalar2=EPS,
                                                       op0=ALU.mult, op1=ALU.add), r=[("ssq", t)], w=[("rstd", t)])
                S.op("pool", lambda e: e.tensor_tensor(out=rstd[0:p, t:t + 1], in0=rstd[0:p, t:t + 1], in1=nhalf_c[0:p], op=ALU.pow),
                     r=[("rstd", t), "cf"], w=[("rstd", t)])

            def scale_T(t):
                buf, key = xt(t)
                p = 128 if t < 16 else 32
                xb = xn[t % 2]
                keys = key if isinstance(key, list) else [key]
                S.op("act", lambda e: e.activation(out=xb[0:p], in_=buf[0:p], func=AF.Identity, scale=rstd[0:p, t:t + 1]),
                     r=keys + [("rstd", t)], w=[("xn", t % 2)])
                bk = 2 + (t % 2)
                def f(e):
                    last = None
                    for kc in range(8):
                        last = e.transpose(PSB(bk)[:, kc * 128:kc * 128 + p], xb[0:p, kc * 128:(kc + 1) * 128], ident_b[0:p, 0:p])
                    return last
                S.op("pe", f, r=[("xn", t % 2), "cb"], w=[("ps", bk)])
                for kc in range(8):
                    if t < 16:
                        dst = hT[:, kc, t * 128:(t + 1) * 128]
                        src = PSB(bk)[:, kc * 128:(kc + 1) * 128]
                        if kc % 2 == 0:
                            S.op("dve", lambda e, dst=dst, src=src, kc=kc: e.tensor_scalar(
                                out=dst, in0=src, scalar1=gmod[:, kc, 0:1], scalar2=modc[:, kc, 0:1], op0=ALU.mult, op1=ALU.add),
                                 r=[("ps", bk), "gmod", "modc"], w=[("hT", t, kc)])
                        else:
                            S.op("act", lambda e, dst=dst, src=src, kc=kc: e.activation(
                                out=dst, in_=src, func=AF.Identity, scale=gmod[:, kc, 0:1], bias=modc[:, kc, 0:1]),
                                 r=[("ps", bk), "gmod", "modc"], w=[("hT", t, kc)])
                    else:
                        for s in range(NS):
                            dst = hT[:, kc, L + s * 8:L + s * 8 + 8]
                            src = PSB(bk)[:, kc * 128 + s * 8:kc * 128 + s * 8 + 8]
                            S.op("dve", lambda e, dst=dst, src=src, kc=kc, s=s: e.tensor_scalar(
                                out=dst, in0=src, scalar1=gmod[:, kc, 1 + s:2 + s], scalar2=modc[:, kc, 1 + s:2 + s],
                                op0=ALU.mult, op1=ALU.add), r=[("ps", bk), "gmod", "modc"], w=[("hT", t, kc)])

            load(0)
            load(1)
            sq(0)
            for t in range(ntile):
                if t + 2 < ntile:
                    load(t + 2)
                if t + 1 < ntile:
                    sq(t + 1)
                scale_T(t)

        HT_KEYS = [("hT", t, kc) for t in range(17) for kc in range(8)]

        R1.reset(); R2.reset(); R3.reset(); R4.reset()
        hT = R1.alloc([8, TT], BF16)
        ogT = R2.alloc([8, TT], BF16)
        R4g = Region(A, NB - 12288, NB)
        gate_p = R4g.alloc([D], F32)
        gate_s = R4g.alloc([D], F32, parts=32)
        x1s = R4g.alloc([D], F32, parts=32)
        R4 = Region(A, 140288, NB - 12288)

        class _Stop(Exception):
            pass

        def maybe_stop(tag):
            if stop == tag:
                S.finish()
                print("STOP at", tag, "instructions:", S.ninst, "sems:", S.nsem)
                raise _Stop()

        try:
            _build_rest = None
        finally:
            pass
        ada_phase(0, gate_p, gate_s, R3)
        R4.reset()
        if stop == "ada":
            S.finish(); print("STOP ada", S.ninst); return nc
        norm_phase(0, hT, None, R4)
        S.barrier()
        if stop == "norm":
            S.finish(); print("STOP norm", S.ninst); return nc
        R3.reset(); R4.reset()

        NCH = 16
        wab = R1.alloc([8, 16], BF16)
        S.dma("pool", wab, I["awin"][:, 4096:4112].rearrange("(c p) n -> p c n", p=128), w=["wab"])
        def fab(e):
            last = None
            for t in range(NCH):
                for kc in range(8):
                    last = e.matmul(PS(0)[:, t * 16:(t + 1) * 16], hT[:, kc, t * 128:(t + 1) * 128], wab[:, kc, :],
                                    start=(kc == 0), stop=(kc == 7))
            for s in range(NS):
                for kc in range(8):
                    last = e.matmul(PS(1)[0:8, s * 16:(s + 1) * 16], hT[:, kc, L + s * 8:L + s * 8 + 8], wab[:, kc, :],
                                    start=(kc == 0), stop=(kc == 7))
            return last
        S.op("pe", fab, r=["wab"] + HT_KEYS, w=[("ps", 0), ("ps", 1)])

        NCOL = NCH * 8 + NS * 8
        def galloc():
            return R1.alloc([NCOL], F32)
        xa, ax, ex, lx, g_t, beta_t, lbeta_t, gc_t, gcl_t, eg_t, gtot_t, ekd_t = [galloc() for _ in range(12)]
        nA = R1.alloc([8], F32)
        A_bc = hpar[:, 0:8]
        dt_bc = hpar[:, 8:16]
        outg_c = hpar[:, 16:17]
        def pv(tl):
            return tl[:, 0:128].rearrange("p (c h) -> p c h", h=8)
        def sv(tl):
            return tl[0:8, 128:160].rearrange("p (c h) -> p c h", h=8)
        abp = PS(0)[:, 0:256].rearrange("p (c k) -> p c k", k=16)
        abs_ = PS(1)[0:8, 0:64].rearrange("p (c k) -> p c k", k=16)
        S.op("dve", lambda e: e.tensor_tensor(out=pv(xa), in0=abp[:, :, 0:8], in1=dt_bc.unsqueeze(1).broadcast_to([128, 16, 8]), op=ALU.add),
             r=[("ps", 0), "hpar"], w=["xa_p"])
        S.op("dve", lambda e: e.tensor_tensor(out=sv(xa), in0=abs_[:, :, 0:8], in1=dt_bc[0:8].unsqueeze(1).broadcast_to([8, 4, 8]), op=ALU.add),
             r=[("ps", 1), "hpar"], w=["xa_s"])
        S.op("act", lambda e: e.activation(out=pv(ex), in_=abp[:, :, 8:16], func=AF.Exp, scale=-1.0), r=[("ps", 0)], w=["ex_p"])
        S.op("act", lambda e: e.activation(out=sv(ex), in_=abs_[:, :, 8:16], func=AF.Exp, scale=-1.0), r=[("ps", 1)], w=["ex_s"])
        GP = (128, slice(0, 128))
        GS = (8, slice(128, 160))
        for (p, cs_), tg in ((GP, "p"), (GS, "s")):
            def T(tl, p=p, cs_=cs_):
                return tl[0:p, cs_]
            S.op("act", lambda e, T=T: e.activation(out=T(lbeta_t), in_=T(ex), func=AF.Ln, bias=one_c[0:T(ex).shape[0]], scale=1.0),
                 r=["ex_" + tg, "cf"], w=["lbeta_" + tg])
            S.op("dve", lambda e, T=T: e.tensor_scalar(out=T(lbeta_t), in0=T(lbeta_t), scalar1=-1.0, scalar2=None, op0=ALU.mult),
                 r=["lbeta_" + tg], w=["lbeta_" + tg])
            S.op("act", lambda e, T=T: e.activation(out=T(beta_t), in_=T(lbeta_t), func=AF.Exp), r=["lbeta_" + tg], w=["beta_" + tg])
            S.op("dve", lambda e, T=T: e.tensor_scalar(out=T(ax), in0=T(xa), scalar1=-1.0, scalar2=None, op0=ALU.mult),
                 r=["xa_" + tg], w=["ax_" + tg])
            S.op("dve", lambda e, T=T: e.tensor_tensor(out=T(ax), in0=T(ax), in1=T(xa), op=ALU.max),
                 r=["xa_" + tg, "ax_" + tg], w=["ax_" + tg])
            S.op("act", lambda e, T=T: e.activation(out=T(ax), in_=T(ax), func=AF.Exp, scale=-1.0), r=["ax_" + tg], w=["ax_" + tg])
            S.op("act", lambda e, T=T: e.activation(out=T(lx), in_=T(ax), func=AF.Ln, bias=one_c[0:T(ax).shape[0]], scale=1.0),
                 r=["ax_" + tg, "cf"], w=["lx_" + tg])
            S.op("dve", lambda e, T=T: e.scalar_tensor_tensor(out=T(lx), in0=T(xa), scalar=0.0, in1=T(lx), op0=ALU.max, op1=ALU.add),
                 r=["xa_" + tg, "lx_" + tg], w=["lx_" + tg])
        S.op("act", lambda e: e.activation(out=nA, in_=A_bc, func=AF.Exp), r=["hpar"], w=["nA"])
        S.op("dve", lambda e: e.tensor_scalar(out=nA, in0=nA, scalar1=-1.0, scalar2=None, op0=ALU.mult), r=["nA"], w=["nA"])
        S.op("dve", lambda e: e.tensor_tensor(out=pv(g_t), in0=pv(lx), in1=nA.unsqueeze(1).broadcast_to([128, 16, 8]), op=ALU.mult),
             r=["lx_p", "nA"], w=["g_p"])
        S.op("dve", lambda e: e.tensor_tensor(out=sv(g_t), in0=sv(lx), in1=nA[0:8].unsqueeze(1).broadcast_to([8, 4, 8]), op=ALU.mult),
             r=["lx_s", "nA"], w=["g_s"])
        S.op("pe", lambda e: e.matmul(PS(2)[:, 0:128], U_f, g_t[:, 0:128], start=True, stop=True), r=["cf", "g_p"], w=[("ps", 2)])
        S.op("pe", lambda e: e.matmul(PS(3)[0:8, 0:32], U_f[0:8, 0:8], g_t[0:8, 128:160], start=True, stop=True), r=["cf", "g_s"], w=[("ps", 3)])
        S.op("act", lambda e: e.activation(out=gc_t[:, 0:128], in_=PS(2)[:, 0:128], func=AF.Copy), r=[("ps", 2)], w=["gc_p"])
        S.op("act", lambda e: e.activation(out=gc_t[0:8, 128:160], in_=PS(3)[0:8, 0:32], func=AF.Copy), r=[("ps", 3)], w=["gc_s"])
        S.op("pe", lambda e: e.matmul(PS(2)[:, 128:256], ident_f[:, 127:128].broadcast_to([128, 128]), gc_t[:, 0:128], start=True, stop=True),
             r=["cf", "gc_p"], w=[("ps", 2)])
        S.op("pe", lambda e: e.matmul(PS(3)[:, 128:160], ident_f[0:8, 7:8].broadcast_to([8, 128]), gc_t[0:8, 128:160], start=True, stop=True),
             r=["cf", "gc_s"], w=[("ps", 3)])
        S.op("act", lambda e: e.activation(out=gcl_t[:, 0:128], in_=PS(2)[:, 128:256], func=AF.Copy), r=[("ps", 2)], w=["gcl_p"])
        S.op("act", lambda e: e.activation(out=gcl_t[:, 128:160], in_=PS(3)[:, 128:160], func=AF.Copy), r=[("ps", 3)], w=["gcl_s"])
        for (p, cs_), tg in ((GP, "p"), (GS, "s")):
            def T(tl, p=p, cs_=cs_):
                return tl[0:p, cs_]
            S.op("act", lambda e, T=T: e.activation(out=T(eg_t), in_=T(gc_t), func=AF.Exp), r=["gc_" + tg], w=["eg_" + tg])
            S.op("act", lambda e, cs_=cs_: e.activation(out=gtot_t[:, cs_], in_=gcl_t[:, cs_], func=AF.Exp), r=["gcl_" + tg], w=["gtot_" + tg])
            S.op("dve", lambda e, T=T: e.tensor_tensor(out=T(ekd_t), in0=T(gcl_t), in1=T(gc_t), op=ALU.subtract),
                 r=["gcl_" + tg, "gc_" + tg], w=["ekd_" + tg])
        G_KEYS = [k + t for k in ("beta_", "lbeta_", "gc_", "gcl_", "eg_", "gtot_", "ekd_") for t in ("p", "s")]

        if stop == "G":
            S.finish(); print("STOP G", S.ninst); return nc
        NT = 16
        UW = 3 + L
        UT = UW + NS * (3 + LS)
        NUB = 3
        ubuf = [R4.alloc([UT], BF16) for _ in range(NUB)]
        Wh = [R1.alloc([8, 512], BF16) for _ in range(2)]
        diag = [R1.alloc([12, 128], BF16) for _ in range(2)]
        cwt = R1.alloc([24, 4], F32)
        S.dma("sp", cwt, I["cw"], w=["cwt"])
        sconv = R1.alloc([24, NS * 3], F32)
        S.dma("sp", sconv, I["sconv"].rearrange("p a s i -> p a (s i)"), w=["sconv"])
        cvo = [[R3.alloc([TT], BF16) for _ in range(4)] for _ in range(2)]
        sqb1 = R4.alloc([TT], BF16)
        sqb = [sqb1, sqb1]
        NHC = NT + NS
        def halloc(n=NHC):
            return [R4.alloc([n], F32) for _ in range(2)]
        ss_k, ss_q, lrnk, lrq, rows1, rows2, biasj, kbg_s, kdec_s, qdec_s = [halloc() for _ in range(10)]
        rowsT = [R4.alloc([256], F32, parts=16) for _ in range(2)]
        rowsTs = [R4.alloc([16], F32, parts=4) for _ in range(2)]
        cst1 = R4.alloc([384], F32, parts=3)
        css1 = R4.alloc([384], F32, parts=32)
        cst = [cst1, cst1]
        css = [css1, css1]
        thb = [R4.alloc([512], BF16) for _ in range(2)]
        thc = [0]
        outg_h = R4.alloc([1], F32)
        S.op("dve", lambda e: e.tensor_scalar(out=outg_h, in0=outg_c, scalar1=0.5, scalar2=None, op0=ALU.mult), r=["hpar"], w=["outg_h"])
        for ub in range(NUB):
            S.op("pool", lambda e, ub=ub: e.memset(ubuf[ub][:, 0:3], 0.0), w=[("u", ub, "h")])

        ucount = [0]

        def P_head(h):
            hs_ = h % 2
            W = Wh[hs_]
            wkey = [("Wh", hs_, j) for j in range(4)]
            for j in range(4):
                col = (j * 1024 + h * 128)
                S.dma("pool", W[:, :, j * 128:(j + 1) * 128],
                      I["awin"][:, col:col + 128].rearrange("(c p) n -> p c n", p=128), w=[wkey[j]])
            dg = diag[hs_]
            for j in range(3):
                for i in range(4):
                    S.op("pool", lambda e, j=j, i=i: e.tensor_scalar(out=dg[:, j * 4 + i, :], in0=ident_f, scalar1=cwt[:, j * 8 + h, i:i + 1],
                                                                      scalar2=0.5, op0=ALU.mult, op1=ALU.mult),
                         r=["cf", "cwt"], w=[("diag", hs_, j)])
            yield 0.02
            blocks = [(q * 512, 512) for q in range(4)] + [(L, NS * LS)]
            step = 0
            nstep = 4 * 5 * 2.0
            for j in range(4):
                if j < 3:
                    ui = ucount[0] % NUB
                    ucount[0] += 1
                    ub = ubuf[ui]
                    ukey = None
                    ukeys = [("u", ui, bi_) for bi_ in range(5)]
                    S.op("act", lambda e, ub=ub, j=j: e.activation(
                        out=ub[:, UW:UT].rearrange("p (s i) -> p s i", i=3 + LS)[:, :, 0:3],
                        in_=sconv[:, j * 8 + h, :].rearrange("p (s i) -> p s i", i=3), func=AF.Copy),
                         r=["sconv"], w=[("u", ui, "sh")])
                for bi, (t0, n) in enumerate(blocks):
                    bk = 0
                    step += 1
                    def f(e, t0=t0, n=n, bk=bk, j=j):
                        last = None
                        for kc in range(8):
                            last = e.matmul(PS(bk)[:, 0:n], W[:, kc, j * 128:(j + 1) * 128], hT[:, kc, t0:t0 + n],
                                            start=(kc == 0), stop=(kc == 7))
                        return last
                    S.op("pe", f, r=[wkey[j]] + HT_KEYS, w=[("ps", bk)])
                    if j == 3:
                        tb = thb[thc[0] % 2]
                        tk = ("thb", thc[0] % 2)
                        thc[0] += 1
                        S.op("act", lambda e, n=n, bk=bk, tb=tb: e.activation(out=tb[:, 0:n], in_=PS(bk)[:, 0:n], func=AF.Tanh, scale=0.5),
                             r=[("ps", bk)], w=[tk])
                        S.op("dve", lambda e, t0=t0, n=n, bk=bk, tb=tb: e.scalar_tensor_tensor(out=cvo[hs_][3][:, t0:t0 + n], in0=tb[:, 0:n], scalar=1.0,
                                                                                              in1=PS(bk)[:, 0:n], op0=ALU.add, op1=ALU.mult),
                             r=[("ps", bk), tk], w=[("cvo", hs_, 3, bi)])
                    else:
                        if bi < 4:
                            dst = ub[:, 3 + t0:3 + t0 + n]
                            src = PS(bk)[:, 0:n]
                        else:
                            dst = ub[:, UW:UT].rearrange("p (s i) -> p s i", i=3 + LS)[:, :, 3:3 + LS]
                            src = PS(bk)[:, 0:n].rearrange("p (s i) -> p s i", i=LS)
                        S.op("act", lambda e, dst=dst, src=src: e.activation(out=dst, in_=src, func=AF.Copy), r=[("ps", bk)], w=[ukeys[bi]])
                        ck = 1
                        def fc(e, t0=t0, n=n, ck=ck, bi=bi, j=j, ub=ub):
                            last = None
                            for i in range(4):
                                if bi < 4:
                                    rhs = ub[:, t0 + i:t0 + i + n]
                                    out = PS(ck)[:, 0:n]
                                else:
                                    rhs = ub[:, UW:UT].rearrange("p (s i) -> p s i", i=3 + LS)[:, :, i:i + LS]
                                    out = PS(ck)[:, 0:n].rearrange("p (s i) -> p s i", i=LS)
                                last = e.matmul(out, dg[:, j * 4 + i, :], rhs, start=(i == 0), stop=(i == 3))
                            return last
                        rk = [ukeys[bi], ("diag", hs_, j)] + ([ukeys[bi - 1]] if 0 < bi < 4 else []) + ([("u", ui, "h")] if bi == 0 else []) + ([("u", ui, "sh")] if bi == 4 else [])
                        S.op("pe", fc, r=rk, w=[("ps", ck)])
                        tb = thb[thc[0] % 2]
                        tk = ("thb", thc[0] % 2)
                        thc[0] += 1
                        S.op("act", lambda e, n=n, ck=ck, tb=tb: e.activation(out=tb[:, 0:n], in_=PS(ck)[:, 0:n], func=AF.Tanh),
                             r=[("ps", ck)], w=[tk])
                        S.op("dve", lambda e, t0=t0, n=n, ck=ck, j=j, tb=tb: e.scalar_tensor_tensor(out=cvo[hs_][j][:, t0:t0 + n], in0=tb[:, 0:n], scalar=1.0,
                                                                                                   in1=PS(ck)[:, 0:n], op0=ALU.add, op1=ALU.mult),
                             r=[("ps", ck), tk], w=[("cvo", hs_, j, bi)])
                    yield 0.02 + 0.8 * step / 20.0
            def fcs(e):
                last = None
                for kc in range(8):
                    last = e.matmul(PS(0)[0:3, 0:384], hT[:, kc, L - 3:L], W[:, kc, 0:384], start=(kc == 0), stop=(kc == 7))
                for kc in range(8):
                    last = e.matmul(PS(1)[0:32, 0:384], hT[:, kc, L:TT], W[:, kc, 0:384], start=(kc == 0), stop=(kc == 7))
                return last
            S.op("pe", fcs, r=wkey + HT_KEYS, w=[("ps", 0), ("ps", 1)])
            S.op("act", lambda e: e.activation(out=cst[hs_], in_=PS(0)[0:3, 0:384], func=AF.Copy), r=[("ps", 0)], w=[("cst", 0)])
            S.op("act", lambda e: e.activation(out=css[hs_], in_=PS(1)[0:32, 0:384], func=AF.Copy), r=[("ps", 1)], w=[("css", 0)])
            S.dma("sp", O["cp"].rearrange("p (j c) -> p j c", j=3)[:, :, h * 128:(h + 1) * 128],
                  cst[hs_].rearrange("p (j c) -> p j c", j=3), r=[("cst", 0)])
            for s_ in range(NS):
                S.dma("sp", O["cs"].rearrange("p (j c) -> p j c", j=3)[3 * s_:3 * s_ + 3, :, h * 128:(h + 1) * 128],
                      css[hs_][8 * s_ + 5:8 * s_ + 8].rearrange("p (j c) -> p j c", j=3), r=[("css", 0)])
            for j, sst in ((1, ss_k[hs_]), (0, ss_q[hs_])):
                sb = sqb[j]
                S.op("act", lambda e, j=j, sb=sb: e.activation(out=sb, in_=cvo[hs_][j], func=AF.Square),
                     r=[("cvo", hs_, j, bi) for bi in range(5)], w=[("sqb", 0)])
                def fs(e, sb=sb, j=j):
                    last = None
                    for c in range(NT):
                        last = e.matmul(PS(j)[:, c:c + 1], sb[:, c * 128:(c + 1) * 128], ones_b[:, 0:1], start=True, stop=True)
                    for s in range(NS):
                        last = e.matmul(PS(j)[0:8, NT + s:NT + s + 1], sb[:, L + s * 8:L + s * 8 + 8], ones_b[:, 0:1], start=True, stop=True)
                    return last
                S.op("pe", fs, r=[("sqb", 0), "cb"], w=[("ps", j)])
                S.op("act", lambda e, j=j, sst=sst: e.activation(out=sst[:, 0:NT], in_=PS(j)[:, 0:NT], func=AF.Copy), r=[("ps", j)], w=[("ss", hs_, j)])
                S.op("act", lambda e, j=j, sst=sst: e.activation(out=sst[0:8, NT:NHC], in_=PS(j)[0:8, NT:NHC], func=AF.Copy), r=[("ps", j)], w=[("ss", hs_, j, "s")])
            yield 0.9
            def gcol(tl, which):
                if which == "p":
                    return tl[:, 0:128].rearrange("p (c hh) -> p c hh", hh=8)[:, :, h]
                return tl[0:8, 128:160].rearrange("p (c hh) -> p c hh", hh=8)[:, :, h]
            for which, p, cs_ in (("p", 128, slice(0, NT)), ("s", 8, slice(NT, NHC))):
                def T(tl, p=p, cs_=cs_):
                    return tl[hs_][0:p, cs_]
                hk = ("hs", hs_, which)
                S.op("act", lambda e, T=T, p=p: e.activation(out=T(lrnk), in_=T(ss_k), func=AF.Ln, bias=eps_c[0:p], scale=1.0),
                     r=[("ss", hs_, 1), ("ss", hs_, 1, "s"), "cf"], w=[hk + ("lrnk",)])
                S.op("act", lambda e, T=T, p=p: e.activation(out=T(lrq), in_=T(ss_q), func=AF.Ln, bias=eps_c[0:p], scale=1.0),
                     r=[("ss", hs_, 0), ("ss", hs_, 0, "s"), "cf"], w=[hk + ("lrq",)])
                S.op("dve", lambda e, T=T: e.tensor_scalar(out=T(lrnk), in0=T(lrnk), scalar1=-0.5, scalar2=None, op0=ALU.mult),
                     r=[hk + ("lrnk",)], w=[hk + ("lrnk",)])
                S.op("dve", lambda e, T=T, p=p: e.tensor_scalar(out=T(lrq), in0=T(lrq), scalar1=-0.5, scalar2=lnsc_c[0:p], op0=ALU.mult, op1=ALU.add),
                     r=[hk + ("lrq",), "cf"], w=[hk + ("lrq",)])
                S.op("dve", lambda e, T=T, which=which: e.tensor_tensor(out=T(rows1), in0=T(lrnk), in1=gcol(lbeta_t, which), op=ALU.add),
                     r=[hk + ("lrnk",), "lbeta_" + which], w=[hk + ("rows1",)])
                S.op("dve", lambda e, T=T, which=which: e.tensor_tensor(out=T(rows1), in0=T(rows1), in1=gcol(gc_t, which), op=ALU.add),
                     r=[hk + ("rows1",), "gc_" + which], w=[hk + ("rows1",)])
                S.op("dve", lambda e, T=T, which=which: e.tensor_tensor(out=T(rows2), in0=T(lrq), in1=gcol(gc_t, which), op=ALU.add),
                     r=[hk + ("lrq",), "gc_" + which], w=[hk + ("rows2",)])
                S.op("dve", lambda e, T=T, which=which: e.tensor_tensor(out=T(biasj), in0=T(lrnk), in1=gcol(gc_t, which), op=ALU.subtract),
                     r=[hk + ("lrnk",), "gc_" + which], w=[hk + ("biasj",)])
                S.op("act", lambda e, T=T: e.activation(out=T(kbg_s), in_=T(rows1), func=AF.Exp), r=[hk + ("rows1",)], w=[hk + ("kbg_s",)])
                S.op("act", lambda e, T=T: e.activation(out=T(qdec_s), in_=T(rows2), func=AF.Exp), r=[hk + ("rows2",)], w=[hk + ("qdec_s",)])
                S.op("dve", lambda e, T=T, which=which: e.tensor_tensor(out=T(kdec_s), in0=T(lrnk), in1=gcol(ekd_t, which), op=ALU.add),
                     r=[hk + ("lrnk",), "ekd_" + which], w=[hk + ("kdec_s",)])
                S.op("act", lambda e, T=T: e.activation(out=T(kdec_s), in_=T(kdec_s), func=AF.Exp), r=[hk + ("kdec_s",)], w=[hk + ("kdec_s",)])
            def ft(e):
                e.transpose(PS(0)[0:16, 0:128], rows2[hs_][:, 0:NT], ident_f)
                e.transpose(PS(0)[0:16, 128:256], rows1[hs_][:, 0:NT], ident_f)
                e.transpose(PS(1)[0:4, 0:8], rows2[hs_][0:8, NT:NHC], ident_f[0:8, 0:8])
                return e.transpose(PS(1)[0:4, 8:16], rows1[hs_][0:8, NT:NHC], ident_f[0:8, 0:8])
            S.op("pe", ft, r=[("hs", hs_, w_, n_) for w_ in ("p", "s") for n_ in ("rows1", "rows2")] + ["cf"], w=[("ps", 0), ("ps", 1)])
            S.op("act", lambda e: e.activation(out=rowsT[hs_], in_=PS(0)[0:16, 0:256], func=AF.Copy), r=[("ps", 0)], w=[("rowsT", hs_)])
            S.op("act", lambda e: e.activation(out=rowsTs[hs_], in_=PS(1)[0:4, 0:16], func=AF.Copy), r=[("ps", 1)], w=[("rowsTs", hs_)])
            yield 1.0

        NSET = 3
        import os as _os
        NLANE = int(_os.environ.get("K_NLANE", "3"))
        NHO = 6
        STAG = int(_os.environ.get("K_STAG", "5"))
        LANE_BANK = (6, 2, 3)

        def lane_ws():
            d = {}
            d["t"] = R4.alloc([256], F32)
            d["D"] = d["t"]
            d["W"] = [R4.alloc([512], BF16) for _ in range(2)]
            d["BN"] = [w_[:, 0:256] for w_ in d["W"]]
            d["Q"] = [w_[:, 256:384] for w_ in d["W"]]
            d["M"] = [R4.alloc([256], BF16) for _ in range(3)]
            d["XY"] = R4.alloc([256], BF16)
            d["QTs"] = R4.alloc([128], BF16)
            return d

        def handoff():
            d = {}
            d["ktok"] = R4.alloc([128], BF16)
            d["vb"] = R4.alloc([128], BF16)
            d["NTQK"] = R4.alloc([384], BF16)
            d["kcTn"] = R4.alloc([128], BF16)
            d["QT"] = R4.alloc([128], BF16)
            return d
        HO_NAMES = ("ktok", "vb", "NTQK", "kcTn", "QT")
        lanes = [lane_ws() for _ in range(NLANE)]
        hos = [handoff() for _ in range(NHO)]
        u_t = [R4.alloc([128], BF16) for _ in range(2)]
        us_t = [R4.alloc([128], BF16) for _ in range(2)]
        t2_t = [R4.alloc([128], F32) for _ in range(2)]
        o_t = [R4.alloc([128], F32) for _ in range(2)]
        on_t = [R4.alloc([128], BF16) for _ in range(2)]
        junk2 = R4.alloc([128], BF16)
        oss = R4.alloc([2], F32)
        Sf = [R4.alloc([128], F32) for _ in range(2)]
        Sb = [R4.alloc([128], BF16) for _ in range(2)]
        sidx = [0]
        cidx = [0]
        cp_done = set()
        cr_done = [0]
        cr_o = [0]
        cn_done = [0]
        stg = [0]

        def chunk_specs(h):
            sp = [(128, c * 128, c, False, 0) for c in range(NT)]
            sp += [(8, L + s * 8, NT + s, True, s) for s in range(NS)]
            return sp

        def CP_chunk(h, spec, cs, lane, slot):
            RB = LANE_BANK[lane]
            def KK(name, *rest):
                return ((("ho", slot) if name in HO_NAMES else ("lw", lane)), name) + rest
            C, t0, col, smp, s = spec
            hs_ = h % 2
            qT, kT, vT = cvo[hs_][0], cvo[hs_][1], cvo[hs_][2]
            cvk = [("cvo", hs_, j, bi) for j in range(3) for bi in range(5)]
            hk = lambda n: ("hs", hs_, "s" if smp else "p", n)
            ksl = kT[:, t0:t0 + C]
            def f1(e):
                e.transpose(PSB(4)[0:C, 0:128], ksl, ident_b)
                return e.transpose(PSB(4)[0:C, 128:256], vT[:, t0:t0 + C], ident_b)
            S.op("pe", f1, r=cvk + ["cb"], w=[("ps", 4)])
            S.op("act", lambda e: e.activation(out=cs["ktok"][0:C], in_=PSB(4)[0:C, 0:128], func=AF.Copy), r=[("ps", 4)], w=[KK("ktok")])
            bcol = (beta_t[0:C, 128 + s * 8 + h:128 + s * 8 + h + 1] if smp else beta_t[:, col * 8 + h:col * 8 + h + 1])
            S.op("dve", lambda e: e.tensor_scalar(out=cs["vb"][0:C], in0=PSB(4)[0:C, 128:256], scalar1=bcol, scalar2=None, op0=ALU.mult),
                 r=[("ps", 4), "beta_s" if smp else "beta_p"], w=[KK("vb")])
            def f2(e):
                e.matmul(PS(RB)[0:C, 0:C], ksl, qT[:, t0:t0 + C], start=True, stop=True)
                e.matmul(PS(RB)[0:C, 128:128 + C], ksl, ksl, start=True, stop=True)
                if smp:
                    e.matmul(PS(RB)[0:C, 256:256 + C], ident_f[0:4, s:s + 1].broadcast_to([4, C]), rowsTs[hs_][:, 0:8], start=True, stop=True)
                    return e.matmul(PS(RB)[0:C, 384:384 + C], ident_f[0:4, s:s + 1].broadcast_to([4, C]), rowsTs[hs_][:, 8:16], start=True, stop=True)
                return e.matmul(PS(RB)[:, 256:512], ident_f[0:16, col:col + 1].broadcast_to([16, 128]), rowsT[hs_], start=True, stop=True)
            S.op("pe", f2, r=cvk + ["cf", ("rowsTs" if smp else "rowsT", hs_)], w=[("ps", RB)])
            t3 = cs["t"][0:C, :].rearrange("p (a b) -> p a b", a=2)[:, :, 0:C]
            D3 = cs["D"][0:C, :].rearrange("p (a b) -> p a b", a=2)[:, :, 0:C]
            N3 = cs["NTQK"][0:C, 0:256].rearrange("p (a b) -> p a b", a=2)[:, :, 0:C]
            m3 = mask2[0:C, :].rearrange("p (a b) -> p a b", a=2)[:, :, 0:C]
            E3 = PS(RB)[0:C, 256:512].rearrange("p (a b) -> p a b", a=2)[:, :, 0:C]
            raw3 = PS(RB)[0:C, 0:256].rearrange("p (a b) -> p a b", a=2)[:, :, 0:C]
            yield
            S.op("dve", lambda e: e.tensor_tensor(out=t3, in0=E3, in1=m3, op=ALU.add), r=[("ps", RB), "cf"], w=[KK("t")])
            yield
            S.op("act", lambda e: e.activation(out=D3, in_=t3, func=AF.Exp, bias=biasj[hs_][0:C, col:col + 1], scale=1.0),
                 r=[KK("t"), hk("biasj")], w=[KK("t")])
            yield
            S.op("dve", lambda e: e.tensor_tensor(out=N3, in0=raw3, in1=D3, op=ALU.mult), r=[("ps", RB), KK("t")], w=[KK("NTQK")])
            yield
            Bm = cs["NTQK"][0:C, 128:128 + C]
            S.op("pe", lambda e: e.transpose(PSB(4)[0:C, 256:256 + C], Bm, ident_b[0:C, 0:C]), r=[KK("NTQK"), "cb"], w=[("ps", 4)])
            if C == 128:
                W = cs["W"]
                wk = lambda i: KK("W", i)
                S.op("act", lambda e: e.activation(out=cs["NTQK"][:, 256:384], in_=PSB(4)[:, 256:384], func=AF.Copy), r=[("ps", 4)], w=[KK("NTQK")])
                BNf = cs["NTQK"][:, 128:384]
                v2 = lambda ap: ap.rearrange("p (a b) -> p a b", a=2)
                bd2 = bd_b.unsqueeze(1).broadcast_to([128, 2, 128])
                id2 = ident_b.unsqueeze(1).broadcast_to([128, 2, 128])
                W0bn = W[0].rearrange("p (a b) -> p a b", a=4)[:, 1::2, :]
                W1qt = W[1].rearrange("p (a b) -> p a b", a=4)[:, 0::2, :]
                S.op("dve", lambda e: e.tensor_tensor(out=W0bn, in0=v2(BNf), in1=bd2, op=ALU.mult), r=[KK("NTQK"), "cb"], w=[wk(0)])
                S.op("dve", lambda e: e.tensor_tensor(out=W1qt, in0=id2, in1=W0bn, op=ALU.subtract), r=[wk(0), "cb"], w=[wk(1)])
                for lv in range(3):
                    S.op("dve", lambda e, lv=lv: e.tensor_tensor(out=cs["M"][lv], in0=BNf, in1=ml_b[lv], op=ALU.mult),
                         r=[KK("NTQK"), "cb"], w=[KK("M", lv)])
                yield
                cur = 0
                for k in range(4):
                    nxt = 1 - cur
                    last = (k == 3)
                    Wc = W[cur]
                    Bk = Wc[:, 128:256]
                    Nk = Wc[:, 384:512]
                    src = Wc if k > 0 else None
                    def fr(e, k=k, Wc=Wc, Bk=Bk, Nk=Nk, last=last):
                        if k == 0:
                            e.matmul(PS(RB)[:, 128:256], Nk, Bk, start=True, stop=True)
                            return e.matmul(PS(RB)[:, 384:512], Bk, Nk, start=True, stop=True)
                        if last:
                            e.matmul(PS(RB)[:, 0:128], Nk, Wc[:, 0:128], start=True, stop=True)
                            return e.matmul(PS(RB)[:, 256:384], Bk, Wc[:, 256:384], start=True, stop=True)
                        e.matmul(PS(RB)[:, 0:256], Nk, Wc[:, 0:256], start=True, stop=True)
                        return e.matmul(PS(RB)[:, 256:512], Bk, Wc[:, 256:512], start=True, stop=True)
                    S.op("pe", fr, r=[wk(cur)], w=[("ps", RB)])
                    P4 = PS(RB).rearrange("p (a b) -> p a b", a=4)
                    Wn4 = W[nxt].rearrange("p (a b) -> p a b", a=4)
                    Wc4 = Wc.rearrange("p (a b) -> p a b", a=4)
                    if k == 0:
                        S.op("act", lambda e, P4=P4, Wn4=Wn4: e.activation(out=Wn4[:, 1::2, :], in_=P4[:, 1::2, :], func=AF.Copy),
                             r=[("ps", RB)], w=[wk(nxt)])
                    else:
                        if not last:
                            S.op("act", lambda e, P4=P4, Wn4=Wn4: e.activation(out=Wn4[:, 1::2, :], in_=P4[:, 1::2, :], func=AF.Copy),
                                 r=[("ps", RB)], w=[wk(nxt)])
                        S.op("dve", lambda e, P4=P4, Wn4=Wn4, Wc4=Wc4: e.tensor_tensor(out=Wn4[:, 0::2, :], in0=P4[:, 0::2, :], in1=Wc4[:, 0::2, :], op=ALU.add),
                             r=[("ps", RB), wk(cur)], w=[wk(nxt)])
                    cur = nxt
                    yield
                Wc = W[cur]
                Qv = Wc[:, 0:128]
                Tv = Wc[:, 256:384]
                XY = cs["XY"]
                for lv in range(3):
                    lastl = (lv == 2)
                    Bm_l = cs["M"][lv][:, 0:128]
                    Nm_l = cs["M"][lv][:, 128:256]
                    def f1_(e, Bm_l=Bm_l, Nm_l=Nm_l, lastl=lastl):
                        r_ = e.matmul(PS(RB)[:, 0:128], Nm_l, Qv, start=True, stop=True)
                        if not lastl:
                            r_ = e.matmul(PS(RB)[:, 128:256], Bm_l, Tv, start=True, stop=True)
                        return r_
                    S.op("pe", f1_, r=[wk(cur), KK("M", lv)], w=[("ps", RB)])
                    nx = 128 if lastl else 256
                    S.op("act", lambda e, nx=nx: e.activation(out=XY[:, 0:nx], in_=PS(RB)[:, 0:nx], func=AF.Copy), r=[("ps", RB)], w=[KK("XY")])
                    def f2_(e, lastl=lastl):
                        r_ = e.matmul(PS(RB)[:, 256:384], Tv, XY[:, 0:128], start=True, stop=True)
                        if not lastl:
                            r_ = e.matmul(PS(RB)[:, 384:512], Qv, XY[:, 128:256], start=True, stop=True)
                        return r_
                    S.op("pe", f2_, r=[wk(cur), KK("XY")], w=[("ps", RB)])
                    if lastl:
                        S.op("dve", lambda e: e.tensor_tensor(out=cs["QT"], in0=Qv, in1=PS(RB)[:, 256:384], op=ALU.subtract),
                             r=[("ps", RB), wk(cur)], w=[KK("QT")])
                    else:
                        Wq = Wc.rearrange("p (a b) -> p a b", a=4)[:, 0::2, :]
                        S.op("dve", lambda e, Wq=Wq: e.tensor_tensor(out=Wq, in0=Wq, in1=PS(RB)[:, 256:512].rearrange("p (a b) -> p a b", a=2), op=ALU.subtract),
                             r=[("ps", RB), wk(cur)], w=[wk(cur)])
                    yield
                QT = cs["QT"]
                qkey = KK("QT")
            else:
                BN = cs["BN"]
                Q = cs["Q"]
                S.op("act", lambda e: e.activation(out=BN[0][0:C, 128:128 + C], in_=PSB(4)[0:C, 256:256 + C], func=AF.Copy), r=[("ps", 4)], w=[KK("W", 0)])
                S.op("pool", lambda e: e.tensor_copy(out=BN[0][0:C, 0:C], in_=Bm), r=[KK("NTQK")], w=[KK("W", 0)])
                S.op("pool", lambda e: e.tensor_tensor(out=Q[0][0:C, 0:C], in0=ident_b[0:C, 0:C], in1=Bm, op=ALU.subtract),
                     r=[KK("NTQK"), "cb"], w=[KK("W", 0)])
                nr = 7 if C == 128 else 3
                cur = 0
                for k in range(nr):
                    nxt = 1 - cur
                    Bk = BN[cur][0:C, 0:C]
                    Nk = BN[cur][0:C, 128:128 + C]
                    last = (k == nr - 1)
                    def fr(e, k=k, Bk=Bk, Nk=Nk, cur=cur, last=last):
                        r_ = None
                        if k > 0:
                            r_ = e.matmul(PS(RB)[0:C, 0:C], Nk, Q[cur][0:C, 0:C], start=True, stop=True)
                        if not last:
                            r_ = e.matmul(PS(RB)[0:C, 128:128 + C], Nk, Bk, start=True, stop=True)
                            r_ = e.matmul(PS(RB)[0:C, 256:256 + C], Bk, Nk, start=True, stop=True)
                        return r_
                    S.op("pe", fr, r=[KK("W", cur)], w=[("ps", RB)])
                    if not last:
                        S.op("act", lambda e, nxt=nxt: e.activation(
                            out=BN[nxt][0:C, :].rearrange("p (a b) -> p a b", a=2)[:, :, 0:C],
                            in_=PS(RB)[0:C, 128:384].rearrange("p (a b) -> p a b", a=2)[:, :, 0:C], func=AF.Copy),
                             r=[("ps", RB)], w=[KK("W", nxt)])
                    if k > 0:
                        S.op("dve", lambda e, cur=cur, nxt=nxt: e.tensor_tensor(out=Q[nxt][0:C, 0:C], in0=PS(RB)[0:C, 0:C], in1=Q[cur][0:C, 0:C], op=ALU.add),
                             r=[("ps", RB), KK("W", cur)], w=[KK("W", nxt)])
                    else:
                        S.op("pool", lambda e, cur=cur, nxt=nxt: e.tensor_copy(out=Q[nxt][0:C, 0:C], in_=Q[cur][0:C, 0:C]),
                             r=[KK("W", cur)], w=[KK("W", nxt)])
                    cur = nxt
                    yield
                S.op("pool", lambda e, cur=cur: e.tensor_copy(out=cs["QT"][0:C, 0:C], in_=Q[cur][0:C, 0:C]), r=[KK("W", cur)], w=[KK("QT")])
                QT = cs["QT"][0:C, 0:C]
                qkey = KK("QT")
            S.op("dve", lambda e: e.tensor_scalar(out=cs["QTs"][0:C, 0:C], in0=QT, scalar1=kbg_s[hs_][0:C, col:col + 1], scalar2=None, op0=ALU.mult),
                 r=[qkey, hk("kbg_s")], w=[KK("QTs")])
            S.op("pe", lambda e: e.matmul(PS(4)[:, 256:256 + C], cs["ktok"][0:C, :], cs["QTs"][0:C, 0:C], start=True, stop=True),
                 r=[KK("ktok"), KK("QTs")], w=[("ps", 4)])
            S.op("act", lambda e: e.activation(out=cs["kcTn"][:, 0:C], in_=PS(4)[:, 256:256 + C], func=AF.Copy, scale=-1.0),
                 r=[("ps", 4)], w=[KK("kcTn")])
            yield

        def CR_chunk(h, spec, cs, slot, Sfl, Sbf, skey, n):
            def KK(name, *rest):
                return (("ho", slot), name) + rest
            C, t0, col, smp, s = spec
            hs_ = h % 2
            qT = cvo[hs_][0]
            zT = cvo[hs_][3]
            cvk = [("cvo", hs_, j, bi) for j in (0, 3) for bi in range(5)]
            hk = lambda nm: ("hs", hs_, "s" if smp else "p", nm)
            QT = cs["QT"][0:C, 0:C]
            qk = KK("QT")
            x = n % 2
            def fu(e):
                e.matmul(PS(7)[0:C, 0:128], QT, cs["vb"][0:C, :], start=True, stop=False)
                return e.matmul(PS(7)[0:C, 0:128], cs["kcTn"][:, 0:C], Sbf, start=False, stop=True)
            S.op("pe", fu, r=[qk, KK("vb"), KK("kcTn"), skey + ("b",)], w=[("ps", 7)])
            S.op("act", lambda e: e.activation(out=u_t[x][0:C], in_=PS(7)[0:C, 0:128], func=AF.Copy), r=[("ps", 7)], w=[("u_t", x)])
            S.op("dve", lambda e: e.tensor_scalar(out=us_t[x][0:C], in0=PS(7)[0:C, 0:128], scalar1=kdec_s[hs_][0:C, col:col + 1], scalar2=None, op0=ALU.mult),
                 r=[("ps", 7), hk("kdec_s")], w=[("us_t", x)])
            yield
            def fo(e):
                e.matmul(PS(7)[0:C, 128:256], qT[:, t0:t0 + C], Sbf, start=True, stop=True)
                e.matmul(PS(7)[0:C, 256:384], cs["NTQK"][0:C, 0:C], u_t[x][0:C], start=True, stop=True)
                return e.matmul(PS(7)[:, 384:512], cs["ktok"][0:C, :], us_t[x][0:C], start=True, stop=True)
            S.op("pe", fo, r=cvk + [skey + ("b",), KK("NTQK"), ("u_t", x), KK("ktok"), ("us_t", x)], w=[("ps", 7)])
            gcolumn = (gtot_t[:, 128 + s * 8 + h:128 + s * 8 + h + 1] if smp else gtot_t[:, col * 8 + h:col * 8 + h + 1])
            S.op("dve", lambda e: e.scalar_tensor_tensor(out=Sfl, in0=Sfl, scalar=gcolumn, in1=PS(7)[:, 384:512], op0=ALU.mult, op1=ALU.add),
                 r=[("ps", 7), skey + ("f",), "gtot_s" if smp else "gtot_p"], w=[skey + ("f",)])
            S.op("act", lambda e: e.activation(out=Sbf, in_=Sfl, func=AF.Copy), r=[skey + ("f",)], w=[skey + ("b",)])
            while n - cn_done[0] >= 2:
                yield WAIT
            S.op("act", lambda e: e.activation(out=t2_t[x][0:C], in_=PS(7)[0:C, 256:384], func=AF.Copy), r=[("ps", 7)], w=[("t2", x)])
            S.op("dve", lambda e: e.scalar_tensor_tensor(out=o_t[x][0:C], in0=PS(7)[0:C, 128:256], scalar=qdec_s[hs_][0:C, col:col + 1],
                                                         in1=t2_t[x][0:C], op0=ALU.mult, op1=ALU.add),
                 r=[("ps", 7), ("t2", x), hk("qdec_s")], w=[("o_t", x)])
            yield
        def CN_chunk(h, spec, n):
            C, t0, col, smp, s = spec
            hs_ = h % 2
            zT = cvo[hs_][3]
            cvk = [("cvo", hs_, j, bi) for j in (0, 3) for bi in range(5)]
            x = n % 2
            S.op("act", lambda e: e.activation(out=junk2[0:C], in_=o_t[x][0:C], func=AF.Square, accum_out=oss[0:C, x:x + 1]),
                 r=[("o_t", x)], w=["junk2", ("oss", x)])
            yield
            S.op("pool", lambda e: e.tensor_scalar(out=oss[0:C, x:x + 1], in0=oss[0:C, x:x + 1], scalar1=1.0 / 128, scalar2=EPS, op0=ALU.mult, op1=ALU.add),
                 r=[("oss", x)], w=[("oss", x)])
            S.op("pool", lambda e: e.tensor_tensor(out=oss[0:C, x:x + 1], in0=oss[0:C, x:x + 1], in1=nhalf_c[0:C], op=ALU.pow),
                 r=[("oss", x), "cf"], w=[("oss", x)])
            yield
            S.op("act", lambda e: e.activation(out=on_t[x][0:C], in_=o_t[x][0:C], func=AF.Identity, scale=oss[0:C, x:x + 1]),
                 r=[("o_t", x), ("oss", x)], w=[("on_t", x)])
            yield
            S.op("pe", lambda e: e.transpose(PSB(5)[:, 0:C], on_t[x][0:C, :], ident_b[0:C, 0:C]), r=[("on_t", x), "cb"], w=[("ps", 5)])
            S.op("dve", lambda e: e.scalar_tensor_tensor(out=ogT[:, h, t0:t0 + C], in0=PSB(5)[:, 0:C], scalar=outg_h,
                                                         in1=zT[:, t0:t0 + C], op0=ALU.mult, op1=ALU.mult),
                 r=[("ps", 5), "outg_h"] + cvk, w=[("ogT", h, t0)])
            yield

        def CN_head(h):
            specs = chunk_specs(h)
            for n, spec in enumerate(specs):
                gi = cidx[0] + n
                while cr_o[0] <= gi:
                    yield WAIT
                for _ in CN_chunk(h, spec, gi):
                    yield 0.5
                cn_done[0] = gi + 1
                yield 0.5

        def CP_lane(h, lane):
            specs = chunk_specs(h)
            mine = list(range(lane, len(specs), NLANE))
            for _ in range(lane * STAG):
                yield 0.0
            for k_, n in enumerate(mine):
                spec = specs[n]
                gi = cidx[0] + n
                while gi - cr_done[0] >= NHO:
                    yield WAIT
                slot = gi % NHO
                cs = dict(lanes[lane])
                cs.update(hos[slot])
                for yi, _ in enumerate(CP_chunk(h, spec, cs, lane, slot)):
                    yield (k_ + min(0.95, (yi + 1) / 14.0)) / len(mine)
                cp_done.add(gi)
                yield (k_ + 1.0) / len(mine)

        def CR_head(h):
            specs = chunk_specs(h)
            yield 0.0
            for n, spec in enumerate(specs):
                C, t0, col, smp, s = spec
                gi = cidx[0] + n
                while gi not in cp_done:
                    yield WAIT
                slot = gi % NHO
                cs = hos[slot]
                if n == 0 or smp:
                    si = sidx[0] % 2
                    sidx[0] += 1
                    Sfl, Sbf, skey = Sf[si], Sb[si], ("S", si)
                    if smp:
                        S.dma("sp", Sfl, I["sdelta"][s, h], w=[skey + ("f",)])
                        S.op("act", lambda e, Sbf=Sbf, Sfl=Sfl: e.activation(out=Sbf, in_=Sfl, func=AF.Copy), r=[skey + ("f",)], w=[skey + ("b",)])
                    else:
                        S.op("pool", lambda e, Sfl=Sfl: e.memset(Sfl, 0.0), w=[skey + ("f",)])
                        S.op("pool", lambda e, Sbf=Sbf: e.memset(Sbf, 0.0), w=[skey + ("b",)])
                for v_ in CR_chunk(h, spec, cs, slot, Sfl, Sbf, skey, gi):
                    yield (WAIT if v_ is WAIT else 0.5)
                cr_o[0] = gi + 1
                if n == NT - 1 or smp:
                    dst = O["ds"][s, h] if smp else O["dp"][h]
                    S.dma("sp", dst, Sfl, r=[skey + ("f",)])
                cr_done[0] = gi + 1
                yield (n + 1.0) / len(specs) - 0.08

        S.barrier()
        run_tasks([P_head(0)])
        if stop == "P0":
            S.finish(); print("STOP P0", S.ninst); return nc
        for h in range(nheads):
            gens = [CP_lane(h, ln) for ln in range(NLANE)] + [CR_head(h), CN_head(h)]
            if h + 1 < nheads:
                gens.append(P_head(h + 1))
            run_tasks(gens)
            cidx[0] += NT + NS
            if stop == ("H", h):
                S.finish(); print("STOP H", h, S.ninst); return nc
        S.barrier()

        if dbg:
            S.dma("sp", O["dbg_og"], ogT, r=[("ogT", h, t * 128) for h in range(8) for t in range(16)] + [("ogT", h, L + s * 8) for h in range(8) for s in range(NS)])
            S.dma("sp", O["dbg_hT"], hT, r=HT_KEYS)
            S.barrier()
        R1.reset(); R3.reset(); R4.reset()
        x1 = R1.alloc([16, D], F32)
        wo = R3.alloc([8, D], BF16)
        xst2 = [R3.alloc([D], F32) for _ in range(2)]
        xss = R4.alloc([D], F32, parts=32)
        ysb0 = R4.alloc([D], F32, parts=32)

        def wout_phase(l, wsrc, x_in, x_out_fn):
            S.dma("pool", wo, wsrc.rearrange("(c p) n -> p c n", p=128), w=["wo"])
            wos = wo
            if l == 0:
                S.dma("sp", xss, I["xs"], w=["xss"])
            for hb in range(2):
                def f(e, hb=hb):
                    last = None
                    for ec in range(8):
                        last = e.matmul(PS(hb)[0:32, :], ogT[:, ec, L:TT], wo[:, ec, hb * 512:(hb + 1) * 512], start=(ec == 0), stop=(ec == 7))
                    return last
                S.op("pe", f, r=["wo"] + [("ogT", h, L + s * 8) for h in range(8) for s in range(NS)], w=[("ps", hb)])
                ysb = xss if l == 1 else ysb0
                S.op("dve", lambda e, hb=hb, ysb=ysb: e.tensor_tensor(out=ysb[:, hb * 512:(hb + 1) * 512], in0=PS(hb)[0:32, :], in1=gate_s[:, hb * 512:(hb + 1) * 512], op=ALU.mult),
                     r=[("ps", hb), ("gate_s", l)], w=[("ysb", hb)])
                src = xss if l == 0 else x1s
                S.op("dve", lambda e, hb=hb, src=src, ysb=ysb: e.tensor_tensor(out=x1s[:, hb * 512:(hb + 1) * 512], in0=ysb[:, hb * 512:(hb + 1) * 512],
                                                                       in1=src[:, hb * 512:(hb + 1) * 512], op=ALU.add),
                     r=[("ysb", hb), "xss", ("x1s", hb)], w=[("x1s", hb)])
            for ec in range(8):
                S.op("pool", lambda e, ec=ec: e.tensor_tensor(out=wo[:, ec, :], in0=wo[:, ec, :], in1=gate_p, op=ALU.mult),
                     r=["wo", ("gate_p", l)], w=["wo"])
            for t in range(16):
                xb, xkey = x_in(t)
                for hb in range(2):
                    bk = (2 * t + hb) % 4
                    def f(e, hb=hb, bk=bk, t=t):
                        last = None
                        for ec in range(8):
                            last = e.matmul(PS(bk), ogT[:, ec, t * 128:(t + 1) * 128], wo[:, ec, hb * 512:(hb + 1) * 512], start=(ec == 0), stop=(ec == 7))
                        return last
                    S.op("pe", f, r=["wo"] + [("ogT", h, t * 128) for h in range(8)], w=[("ps", bk)])
                    S.op("dve", lambda e, hb=hb, bk=bk, t=t, xb=xb: e.tensor_tensor(out=x_out_fn(t)[:, hb * 512:(hb + 1) * 512], in0=PS(bk),
                                                                                     in1=xb[:, hb * 512:(hb + 1) * 512], op=ALU.add),
                         r=[("ps", bk)] + (xkey if isinstance(xkey, list) else [xkey]), w=[("x1", t, hb)])

        def x_in0(t):
            buf = xst2[t % 2]
            key = ("xst2", t % 2)
            S.dma("sp", buf, I["xp"][t * 128:(t + 1) * 128, :], w=[key])
            return buf, key

        wout_phase(0, I["awout"], x_in0, lambda t: x1[:, t, :])
        if dbg:
            for t in range(16):
                S.dma("sp", O["dbg_x1p"][t * 128:(t + 1) * 128, :], x1[:, t, :], r=[("x1", t, 0), ("x1", t, 1)])
            S.dma("sp", O["dbg_x1s"], x1s, r=[("x1s", 0), ("x1s", 1)])
        if not do_l1:
            S.finish()
            print("instructions:", S.ninst, "sems:", S.nsem)
            return nc
        ada_phase(1, gate_p, gate_s, R4, parts=("col",))
        S.barrier()
        R3.reset(); R4.reset()
        hT1 = R3.alloc([8, TT], BF16)
        RE = Region(A, NB - 12288, NB - 4096)
        qs_all = RE.alloc([8, 3, 32], BF16)
        zs_all = RE.alloc([8, 32], BF16)
        onew = RE.alloc([8, 32], F32)
        lnew = RE.alloc([8, 32], F32)
        ptn = RE.alloc([32], BF16, parts=32)

        def x_src1(t):
            if t < 16:
                return x1[:, t, :], [("x1", t, 0), ("x1", t, 1)]
            return x1s, [("x1s", 0), ("x1s", 1)]

        norm_phase(1, hT1, x_src1, R4)
        S.barrier()
        R4.reset()
        GRP = ((128, 1), (512, 4), (2048, 16))
        SCALE = 128.0 ** -0.5
        Wg = [R4.alloc([8, 384], BF16) for _ in range(2)]
        Wz = RC.alloc([8, 128], BF16)
        QK_ = [R4.alloc([2, TT], BF16) for _ in range(3)]
        QT_ = [qk_[:, 0, :] for qk_ in QK_]
        KT_ = [qk_[:, 1, :] for qk_ in QK_]
        qkb = [R4.alloc([256], BF16) for _ in range(2)]
        Vt_ = [R4.alloc([17, 128], BF16) for _ in range(3)]
        zTh = R4.alloc([1024], BF16)
        PTb = [R4.alloc([512], BF16) for _ in range(2)]
        stg1_ = R4.alloc([256], F32)
        stg_ = [stg1_, stg1_]
        fin1_ = R4.alloc([512], F32)
        fin_ = [fin1_, fin1_]
        print("L1 R4 used", R4.cur - R4.lo, "of", R4.hi - R4.lo)
        KVO = [O["kv0p"], O["kv1p"], O["kv2p"]]
        wcount = [0]
        stc = [0]
        ptc = [0]
        fnc = [0]

        def tok_ap(g, ti):
            d = GRP[g][1]
            if d == 1:
                return lambda kc: hT1[:, kc, ti * 128:(ti + 1) * 128]
            if d == 4:
                r, nb = ti // 4, ti % 4
                return lambda kc: hT1[:, kc, 0:L].rearrange("p (j r) -> p r j", r=4)[:, r, nb * 128:(nb + 1) * 128]
            return lambda kc: hT1[:, kc, 0:L].rearrange("p (j r) -> p r j", r=16)[:, ti, :]

        def blk_ap(g, blk):
            d = GRP[g][1]
            if d == 1:
                return lambda kc: hT1[:, kc, blk * 512:(blk + 1) * 512]
            if d == 4:
                return lambda kc: hT1[:, kc, 0:L].rearrange("p (j r) -> p r j", r=4)[:, blk, :]
            return lambda kc: hT1[:, kc, 0:L].rearrange("p (j r) -> p r j", r=16)[:, 4 * blk:4 * blk + 4, :]

        def kept_rows(g, ti):
            w_, d = GRP[g]
            if d == 1:
                return (0, 1) if ti == 15 else None
            if d == 4:
                r, nb = ti // 4, ti % 4
                return (r, 4) if nb == 3 else None
            return (ti, 16)

        def L1_proj(h, g):
            d = GRP[g][1]
            wb = Wg[wcount[0] % 2]
            wkey = ("Wg", wcount[0] % 2)
            wcount[0] += 1
            for t_ in range(3):
                col = t_ * 3072 + g * 1024 + h * 128
                S.dma("pool", wb[:, :, t_ * 128:(t_ + 1) * 128], I["bwin"][:, col:col + 128].rearrange("(c p) n -> p c n", p=128),
                      w=[wkey + (t_,)])
            wk = [wkey + (t_,) for t_ in range(3)]
            pend = []

            def emit_tr(ti, p, qb, qbk):
                tb_ = 2 + (ti % 2)
                def ftp(e):
                    e.transpose(PSB(tb_)[:, 0:p], qb[0:p, 0:128], ident_b[0:p, 0:p])
                    return e.transpose(PSB(tb_)[:, 128:128 + p], qb[0:p, 128:256], ident_b[0:p, 0:p])
                S.op("pe", ftp, r=[qbk, "cb"], w=[("ps", tb_)])
                t0 = ti * 128 if ti < 16 else L
                dst = QK_[g][:, :, t0:t0 + p]
                src = PSB(tb_)[:, 0:256].rearrange("p (a b) -> p a b", a=2)[:, :, 0:p]
                if ti % 2 == 1:
                    S.op("act", lambda e: e.activation(out=dst, in_=src, func=AF.Copy), r=[("ps", tb_)], w=[("qkT", g, ti)])
                else:
                    S.op("dve", lambda e: e.tensor_copy(out=dst, in_=src), r=[("ps", tb_)], w=[("qkT", g, ti)])

            for ti in range(17):
                bk = ti % 2
                p = 128 if ti < 16 else 32
                ap = tok_ap(g, ti) if ti < 16 else (lambda kc: hT1[:, kc, L:TT])
                def f(e, ap=ap, bk=bk, p=p):
                    last = None
                    for kc in range(8):
                        last = e.matmul(PS(bk)[0:p, 0:384], ap(kc), wb[:, kc, 0:384], start=(kc == 0), stop=(kc == 7))
                    return last
                S.op("pe", f, r=wk + HT_KEYS, w=[("ps", bk)])
                qb = qkb[ti % 2]
                qbk = ("qkb", ti % 2)
                S.op("act", lambda e, bk=bk, p=p, qb=qb: e.activation(out=qb[0:p], in_=PS(bk)[0:p, 0:256], func=AF.Copy), r=[("ps", bk)], w=[qbk])
                S.op("dve", lambda e, bk=bk, p=p, ti=ti: e.tensor_copy(out=Vt_[g][0:p, ti, :], in_=PS(bk)[0:p, 256:384]),
                     r=[("ps", bk)], w=[("vt", g, ti)])
                kr = kept_rows(g, ti) if ti < 16 else None
                if kr is not None or ti == 16:
                    sb = stg_[stc[0] % 2]
                    skey = ("stg", 0)
                    stc[0] += 1
                    S.op("dve", lambda e, sb=sb, bk=bk, p=p: e.tensor_copy(out=sb[0:p], in_=PS(bk)[0:p, 128:384]), r=[("ps", bk)], w=[skey])
                    if ti < 16:
                        r0, st = kr
                        dst = KVO[g].rearrange("(q s) k hh e -> s q k hh e", s=st)[r0, :, :, h, :]
                        S.dma("sp", dst, sb.rearrange("p (k e) -> p k e", k=2), r=[skey])
                    else:
                        dst = O["kv%ds" % g].rearrange("s l k hh e -> (s l) k hh e")[:, :, h, :]
                        S.dma("sp", dst, sb[0:32].rearrange("p (k e) -> p k e", k=2), r=[skey])
                pend.append((ti, p, qb, qbk))
                if len(pend) > 1:
                    emit_tr(*pend.pop(0))
                yield
            while pend:
                emit_tr(*pend.pop(0))
            yield

        def L1_units(g, hf):
            d = GRP[g][1]
            units = []
            if d == 1:
                for nb in range(8 * hf, 8 * hf + 8):
                    col0 = 128 * (nb - 8 * hf)
                    outs = [(col0 // 512, (lambda B, c=col0 % 512: B[:, c:c + 128]), 0, 128)]
                    q = QT_[g][:, nb * 128:(nb + 1) * 128]
                    for pv, kt in ((True, nb - 1), (False, nb)):
                        if kt < 0:
                            continue
                        units.append((KT_[g][:, kt * 128:(kt + 1) * 128], Vt_[g][:, kt, :], 128, mT_prev if pv else mT_cur, q, 128, outs, ("vt", g, kt)))
            elif d == 4:
                for r in range(4):
                    for nb in range(2 * hf, 2 * hf + 2):
                        outs = [(nb - 2 * hf, (lambda B, r=r: B.rearrange("p (q r) -> p r q", r=4)[:, r, :]), 0, 128)]
                        q = QT_[g][:, r * 512 + nb * 128:r * 512 + (nb + 1) * 128]
                        for pv, kb in ((True, nb - 1), (False, nb)):
                            if kb < 0:
                                continue
                            kt = r * 4 + kb
                            units.append((KT_[g][:, kt * 128:(kt + 1) * 128], Vt_[g][:, kt, :], 128, mT_prev if pv else mT_cur, q, 128, outs, ("vt", g, kt)))
            else:
                for r in range(16):
                    outs = [(m, (lambda B, r=r: B.rearrange("p (j r) -> p r j", r=16)[:, r, :]), 32 * m, 32) for m in range(2)]
                    q = QT_[g][:, r * 128 + 64 * hf:r * 128 + 64 * hf + 64]
                    if hf == 0:
                        units.append((KT_[g][:, r * 128:r * 128 + 64], Vt_[g][0:64, r, :], 64, mT_cur[0:64, 0:64], q, 64, outs, ("vt", g, r)))
                    else:
                        units.append((KT_[g][:, r * 128:(r + 1) * 128], Vt_[g][:, r, :], 128, mT_cur[:, 64:128], q, 64, outs, ("vt", g, r)))
            return units

        OB = (4, 5)
        LB = (6, 7)

        def L1_pass(h, hf):
            for b_ in OB + LB:
                S.op("dve", lambda e, b_=b_: e.memset(PS(b_), 0.0), w=[("ps", b_)])
            for m in range(2):
                def f(e, m=m):
                    last = None
                    for kc in range(8):
                        last = e.matmul(PS(m), Wz[:, kc, :], hT1[:, kc, 1024 * hf + 512 * m:1024 * hf + 512 * (m + 1)], start=(kc == 0), stop=(kc == 7))
                    return last
                S.op("pe", f, r=["Wz"] + HT_KEYS, w=[("ps", m)])
                S.op("act", lambda e, m=m: e.activation(out=zTh[:, 512 * m:512 * (m + 1)], in_=PS(m), func=AF.Silu), r=[("ps", m)], w=[("zTh", m)])
            yield
            allu = []
            pend_pv = []
            for g in range(3):
                allu += [(g, u) for u in L1_units(g, hf)]
            for i0 in range(0, len(allu), 4):
                grp = allu[i0:i0 + 4]
                sb_ = 2 + (ptc[0] % 2)
                pt = PTb[ptc[0] % 2]
                pkey = ("PT", ptc[0] % 2)
                ptc[0] += 1
                rk = []
                def fs(e, grp=grp, sb_=sb_):
                    last = None
                    for ui, (g, (kT, v, nk, mk, q, nq, outs, vkey)) in enumerate(grp):
                        o_ = PS(sb_)[0:nk, ui * 128:ui * 128 + nq]
                        e.matmul(o_, kT, q, start=True, stop=False)
                        last = e.matmul(o_, ident_b[0:nk, 0:nk], mk, start=False, stop=True)
                    return last
                for (g, u) in grp:
                    rk += [("qkT", g, ti_) for ti_ in range(16)]
                S.op("pe", fs, r=list(set(rk)) + ["cb"], w=[("ps", sb_)])
                S.op("act", lambda e, sb_=sb_, pt=pt: e.activation(out=pt, in_=PS(sb_), func=AF.Exp, scale=SCALE), r=[("ps", sb_)], w=[pkey])
                def fp(e, grp=grp, pt=pt):
                    last = None
                    for ui, (g, (kT, v, nk, mk, q, nq, outs, vkey)) in enumerate(grp):
                        for (bi_, ofn, c0, n_) in outs:
                            rhs = pt[0:nk, ui * 128 + c0:ui * 128 + c0 + n_]
                            e.matmul(ofn(PS(OB[bi_])), v, rhs, start=False, stop=False, skip_group_check=True)
                            last = e.matmul(ofn(PS(LB[bi_])), ones_b[0:nk, :], rhs, start=False, stop=False, skip_group_check=True)
                    return last
                pend_pv.append((fp, [pkey, "cb"] + [u[7] for (_, u) in grp]))
                if len(pend_pv) > 1:
                    fq, rq_ = pend_pv.pop(0)
                    S.op("pe", fq, r=rq_, w=[("ps", b_) for b_ in OB + LB])
                yield
            while pend_pv:
                fq, rq_ = pend_pv.pop(0)
                S.op("pe", fq, r=rq_, w=[("ps", b_) for b_ in OB + LB])
            for m in range(2):
                fb = fin_[fnc[0] % 2]
                fkey = ("fin", 0)
                fnc[0] += 1
                S.op("act", lambda e, fb=fb, m=m: e.activation(out=fb, in_=PS(LB[m]), func=AF.Ln), r=[("ps", LB[m])], w=[fkey])
                S.op("act", lambda e, fb=fb: e.activation(out=fb, in_=fb, func=AF.Exp, scale=-1.0), r=[fkey], w=[fkey])
                S.op("dve", lambda e, fb=fb, m=m: e.tensor_tensor(out=fb, in0=PS(OB[m]), in1=fb, op=ALU.mult), r=[("ps", OB[m]), fkey], w=[fkey])
                p0 = 1024 * hf + 512 * m
                S.op("dve", lambda e, fb=fb, m=m, p0=p0: e.tensor_tensor(out=ogT[:, h, p0:p0 + 512], in0=fb, in1=zTh[:, 512 * m:512 * (m + 1)], op=ALU.mult),
                     r=[fkey, ("zTh", m)], w=[("ogT", h, p0 // 128 * 128 + i * 128) for i in range(4)])
            yield

        def L1_newrows(h):
            for g in range(3):
                S.op("pool", lambda e, g=g: e.tensor_copy(out=qs_all[:, h, g, :], in_=QT_[g][:, L:TT]), r=[("qkT", g, 16)], w=[("qs", h, g)])
                def fsn(e, g=g):
                    e.matmul(PS(2)[0:32, 0:32], KT_[g][:, L:TT], QT_[g][:, L:TT], start=True, stop=False)
                    return e.matmul(PS(2)[0:32, 0:32], ident_b[0:32, 0:32], cbf[0:32, CB_MN + g * 32:CB_MN + (g + 1) * 32], start=False, stop=True)
                S.op("pe", fsn, r=[("qkT", g, 16), "cb"], w=[("ps", 2)])
                S.op("act", lambda e: e.activation(out=ptn, in_=PS(2)[0:32, 0:32], func=AF.Exp, scale=SCALE), r=[("ps", 2)], w=["ptn"])
                def fpn(e, g=g):
                    e.matmul(PS(4)[:, 0:32], Vt_[g][0:32, 16, :], ptn, start=(g == 0), stop=(g == 2))
                    return e.matmul(PS(6)[:, 0:32], ones_b[0:32, :], ptn, start=(g == 0), stop=(g == 2))
                S.op("pe", fpn, r=["ptn", ("vt", g, 16), "cb"], w=[("ps", 4), ("ps", 6)])
            S.op("act", lambda e: e.activation(out=onew[:, h, :], in_=PS(4)[:, 0:32], func=AF.Copy), r=[("ps", 4)], w=[("onew", h)])
            S.op("dve", lambda e: e.tensor_copy(out=lnew[:, h, :], in_=PS(6)[:, 0:32]), r=[("ps", 6)], w=[("lnew", h)])
            def fz(e):
                last = None
                for kc in range(8):
                    last = e.matmul(PS(0)[:, 0:32], Wz[:, kc, :], hT1[:, kc, L:TT], start=(kc == 0), stop=(kc == 7))
                return last
            S.op("pe", fz, r=["Wz"] + HT_KEYS, w=[("ps", 0)])
            S.op("act", lambda e: e.activation(out=zs_all[:, h, :], in_=PS(0)[:, 0:32], func=AF.Silu), r=[("ps", 0)], w=[("zs", h)])

        def L1_head(h):
            S.dma("pool", Wz, I["bwin"][:, 9216 + h * 128:9216 + (h + 1) * 128].rearrange("(c p) n -> p c n", p=128), w=["Wz"])
            for g in range(3):
                for _ in L1_proj(h, g):
                    yield
            L1_newrows(h)
            yield
            for hf in range(2):
                for _ in L1_pass(h, hf):
                    yield

        for h in range(nheads):
            for _ in L1_head(h):
                pass
        S.barrier()

        R4.reset()
        NPF = 3
        NCL = (1, 4, 8)
        ckb = [[R4.alloc([NCL[g], 2, 128], BF16) for g in range(3)] for _ in range(NPF)]
        kTc = [R4.alloc([13, 128], BF16) for _ in range(2)]
        pts = [R4.alloc([24], BF16) for _ in range(2)]
        fo_ = R4.alloc([32], F32)
        fl_ = R4.alloc([32], F32)
        items = [(h, s_) for h in range(nheads) for s_ in range(NS)]

        def e2_load(i):
            h, s_ = items[i]
            for g in range(3):
                d = GRP[g][1]
                src = I["kc%d" % g][s_].rearrange("(k r) kv hh e -> k r kv hh e", r=d)[:, 0:NCL[g], :, h, :]
                S.dma("pool", ckb[i % NPF][g], src, w=[("ck", i % NPF, g)])

        for i in range(min(NPF - 1, len(items))):
            e2_load(i)
        for i, (h, s_) in enumerate(items):
            if i + NPF - 1 < len(items):
                e2_load(i + NPF - 1)
            buf = ckb[i % NPF]
            ckeys = [("ck", i % NPF, g) for g in range(3)]
            kt = kTc[i % 2]
            ktk = ("kTc", i % 2)
            pt = pts[i % 2]
            ptk = ("pts", i % 2)
            sb_ = 2 + (i % 2)
            tiles = [(0, 0)] + [(1, r) for r in range(4)] + [(2, r) for r in range(8)]
            if s_ == 0:
                S.op("dve", lambda e: e.memset(PS(4)[:, 0:32], 0.0), w=[("ps", 4)])
                S.op("dve", lambda e: e.memset(PS(6)[:, 0:32], 0.0), w=[("ps", 6)])
            def ftr(e, buf=buf):
                last = None
                for ti, (g, r) in enumerate(tiles):
                    last = e.transpose(PSB(ti // 8)[:, (ti % 8) * 128:(ti % 8 + 1) * 128], buf[g][:, r, 0, :], ident_b)
                return last
            S.op("pe", ftr, r=ckeys + ["cb"], w=[("ps", 0), ("ps", 1)])
            S.op("act", lambda e, kt=kt: e.activation(out=kt[:, 0:8, :], in_=PSB(0).rearrange("p (a b) -> p a b", a=8), func=AF.Copy), r=[("ps", 0)], w=[ktk + (0,)])
            S.op("dve", lambda e, kt=kt: e.tensor_copy(out=kt[:, 8:13, :], in_=PSB(1)[:, 0:640].rearrange("p (a b) -> p a b", a=5)), r=[("ps", 1)], w=[ktk + (1,)])
            def cls(g, r):
                d = GRP[g][1]
                nq = 8 // d if d <= 8 else 1
                c0 = (0, 8 + 2 * r, 16 + r)[g]
                if d == 1:
                    q = qs_all[:, h, g, 8 * s_:8 * s_ + 8]
                    mk = cbf[:, CB_MK:CB_MK + 8]
                    oc = lambda B: B[:, 8 * s_:8 * s_ + 8]
                elif d == 4:
                    q = qs_all[:, h, g, 8 * s_:8 * s_ + 8].rearrange("p (i r) -> p r i", r=4)[:, r, :]
                    mk = cbf[:, CB_MK + 8:CB_MK + 16].rearrange("p (i r) -> p r i", r=4)[:, r, :]
                    oc = lambda B: B[:, 8 * s_:8 * s_ + 8].rearrange("p (i r) -> p r i", r=4)[:, r, :]
                else:
                    q = qs_all[:, h, g, 8 * s_ + r:8 * s_ + r + 1]
                    mk = None
                    oc = lambda B: B[:, 8 * s_ + r:8 * s_ + r + 1]
                return nq, c0, q, mk, oc
            def fsc(e, kt=kt, sb_=sb_):
                last = None
                for ti, (g, r) in enumerate(tiles):
                    nq, c0, q, mk, oc = cls(g, r)
                    o_ = PS(sb_)[:, c0:c0 + nq]
                    last = e.matmul(o_, kt[:, ti, :], q, start=True, stop=(mk is None))
                    if mk is not None:
                        last = e.matmul(o_, ident_b, mk, start=False, stop=True)
                return last
            S.op("pe", fsc, r=[ktk + (0,), ktk + (1,), "cb"] + [("qs", h, g) for g in range(3)], w=[("ps", sb_)])
            S.op("act", lambda e, pt=pt, sb_=sb_: e.activation(out=pt, in_=PS(sb_)[:, 0:24], func=AF.Exp, scale=SCALE), r=[("ps", sb_)], w=[ptk])
            def fpv(e, buf=buf, pt=pt):
                last = None
                for ti, (g, r) in enumerate(tiles):
                    nq, c0, q, mk, oc = cls(g, r)
                    e.matmul(oc(PS(4)), buf[g][:, r, 1, :], pt[:, c0:c0 + nq], start=False, stop=False, skip_group_check=True)
                    last = e.matmul(oc(PS(6)), ones_b, pt[:, c0:c0 + nq], start=False, stop=False, skip_group_check=True)
                return last
            S.op("pe", fpv, r=ckeys + [ptk, "cb"], w=[("ps", 4), ("ps", 6)])
            if s_ == NS - 1:
                S.op("dve", lambda e, h=h: e.tensor_tensor(out=fo_, in0=PS(4)[:, 0:32], in1=onew[:, h, :], op=ALU.add), r=[("ps", 4), ("onew", h)], w=["fo_"])
                S.op("dve", lambda e, h=h: e.tensor_tensor(out=fl_, in0=PS(6)[:, 0:32], in1=lnew[:, h, :], op=ALU.add), r=[("ps", 6), ("lnew", h)], w=["fl_"])
                S.op("act", lambda e: e.activation(out=fl_, in_=fl_, func=AF.Ln), r=["fl_"], w=["fl_"])
                S.op("act", lambda e: e.activation(out=fl_, in_=fl_, func=AF.Exp, scale=-1.0), r=["fl_"], w=["fl_"])
                S.op("dve", lambda e: e.tensor_tensor(out=fo_, in0=fo_, in1=fl_, op=ALU.mult), r=["fo_", "fl_"], w=["fo_"])
                S.op("dve", lambda e, h=h: e.tensor_tensor(out=ogT[:, h, L:TT], in0=fo_, in1=zs_all[:, h, :], op=ALU.mult),
                     r=["fo_", ("zs", h)], w=[("ogT", h, L + q_ * 8) for q_ in range(NS)])
        S.barrier()
        R3.reset()
        ada_phase(1, gate_p, gate_s, R3, parts=("gate",))
        S.barrier()

        R4.reset()
        xss = R4.alloc([D], F32, parts=32)
        fng_bc = R4.alloc([D], F32)
        S.dma("sp", fng_bc, I["fng"].partition_broadcast(128), w=["fng"])
        wout_phase(1, I["bwout"], lambda t: (x1[:, t, :], [("x1", t, 0), ("x1", t, 1)]), lambda t: x1[:, t, :])
        ssq2 = R4.alloc([17], F32)
        junk3 = R4.alloc([D], BF16)
        ost = [R4.alloc([D], F32) for _ in range(2)]
        for t in range(17):
            p = 128 if t < 16 else 32
            xb, xk = x_src1(t)
            S.op("act", lambda e, xb=xb, p=p, t=t: e.activation(out=junk3[0:p], in_=xb[0:p], func=AF.Square, accum_out=ssq2[0:p, t:t + 1]),
                 r=xk, w=["junk3", ("ssq2", t)])
            S.op("pool", lambda e, p=p, t=t: e.tensor_scalar(out=ssq2[0:p, t:t + 1], in0=ssq2[0:p, t:t + 1], scalar1=1.0 / D, scalar2=EPS, op0=ALU.mult, op1=ALU.add),
                 r=[("ssq2", t)], w=[("ssq2", t)])
            S.op("pool", lambda e, p=p, t=t: e.tensor_tensor(out=ssq2[0:p, t:t + 1], in0=ssq2[0:p, t:t + 1], in1=nhalf_c[0:p], op=ALU.pow),
                 r=[("ssq2", t), "cf"], w=[("ssq2", t)])
            ob = ost[t % 2]
            okey = ("ost", t % 2)
            S.op("act", lambda e, xb=xb, p=p, t=t, ob=ob: e.activation(out=ob[0:p], in_=xb[0:p], func=AF.Identity, scale=ssq2[0:p, t:t + 1]),
                 r=xk + [("ssq2", t)], w=[okey])
            S.op("dve", lambda e, p=p, ob=ob: e.tensor_tensor(out=ob[0:p], in0=ob[0:p], in1=fng_bc[0:p], op=ALU.mult), r=[okey, "fng"], w=[okey])
            if t < 16:
                S.dma("sp", O["yp"][t * 128:(t + 1) * 128, :], ob, r=[okey])
            else:
                S.dma("sp", O["ys"], ob[0:32], r=[okey])
        S.finish()
        print("instructions:", S.ninst, "sems:", S.nsem)
    return nc


def prep_inputs(inp):
    f = lambda a: np.ascontiguousarray(np.asarray(a, dtype=np.float32))
    cf, cb = make_consts()
    ngc = f(np.asarray(inp["norm_g"]).reshape(2, 8, 128).transpose(0, 2, 1))
    adab = f(inp["ada_b"])
    adabc = f(adab[:, 0:2048].reshape(2, 16, 128).transpose(0, 2, 1))
    cw = f(np.asarray(inp["a_conv_w"])[0].reshape(4, 24, 128).transpose(2, 1, 0))
    hp = np.zeros((128, 17), np.float32)
    hp[:, 0:8] = np.asarray(inp["a_A_log"])[0][None, :]
    hp[:, 8:16] = np.asarray(inp["a_dt_bias"])[0][None, :]
    hp[:, 16] = np.asarray(inp["a_out_norm_g"])[0]
    shared = dict(ngc=ngc, adaw=f(inp["ada_w"]), adabc=adabc, adab=adab, awin=f(np.asarray(inp["a_w_in"])[0]), cw=cw, hp=hp,
                  awout=f(np.asarray(inp["a_w_out"])[0]), cf=cf, cb=cb, fng=f(inp["final_norm_g"]),
                  bwin=f(np.asarray(inp["b_w_in"])[0]), bwout=f(np.asarray(inp["b_w_out"])[0]))
    maps = []
    for c in range(NCORES):
        m = dict(shared)
        m["xp"] = f(np.asarray(inp["x_prompt"])[c])
        m["xs"] = f(np.asarray(inp["x_sample"])[4 * c:4 * c + 4].reshape(32, D))
        cc = np.concatenate([np.asarray(inp["c_prompt"])[c:c + 1], np.asarray(inp["c_sample"])[4 * c:4 * c + 4]], 0)
        m["cT"] = f(cc.T.reshape(8, 128, 5).transpose(1, 0, 2))
        sc = np.asarray(inp["state_conv"])[0, 4 * c:4 * c + 4]
        m["sconv"] = f(sc.reshape(4, 3, 24, 128).transpose(3, 2, 0, 1))
        m["sdelta"] = f(np.asarray(inp["state_delta"])[0, 4 * c:4 * c + 4])
        m["kc0"] = f(np.asarray(inp["cache_kv_w128"])[0, 4 * c:4 * c + 4])
        m["kc1"] = f(np.asarray(inp["cache_kv_w512"])[0, 4 * c:4 * c + 4])
        m["kc2"] = f(np.asarray(inp["cache_kv_w2048"])[0, 4 * c:4 * c + 4])
        maps.append(m)
    return maps


_NC_CACHE = {}


def kernel(**inp):
    if "nc" not in _NC_CACHE:
        _NC_CACHE["nc"] = build()
    nc = _NC_CACHE["nc"]
    maps = prep_inputs(inp)
    res = run_bass_kernel_spmd(nc, maps, core_ids=list(range(NCORES)))
    R = res.results
    g = lambda k: [np.asarray(R[c][k], dtype=np.float32) for c in range(NCORES)]
    y_prompt = np.stack(g("yp"), 0)
    y_sample = np.concatenate([a.reshape(NS, LS, D) for a in g("ys")], 0)
    delta_p = np.stack(g("dp"), 0)[None]
    delta_s = np.concatenate(g("ds"), 0)[None]
    conv_p = np.stack(g("cp"), 0)[None]
    conv_s = np.concatenate([a.reshape(NS, 3, 3072) for a in g("cs")], 0)[None]
    outs = [y_prompt, y_sample, delta_p, delta_s, conv_p, conv_s]
    for gi in range(3):
        outs.append(np.stack(g("kv%dp" % gi), 0)[None])
        outs.append(np.concatenate(g("kv%ds" % gi), 0)[None])
    return tuple(np.ascontiguousarray(o, dtype=np.float32) for o in outs)
```

```python
import numpy as np
import ml_dtypes
from contextlib import ExitStack
import concourse.bass as bass
import concourse.mybir as mybir
from concourse.bass_utils import run_bass_kernel_spmd

F32 = mybir.dt.float32
BF16 = mybir.dt.bfloat16
AF = mybir.ActivationFunctionType
ALU = mybir.AluOpType

NEG = -30000.0
NCORES = 8
L = 2048
NS = 4
LS = 8
TT = L + NS * LS
D = 1024
EPS = 1e-6


class Sched:
    ENG = ("pe", "act", "dve", "pool", "sp")

    def __init__(self, nc, stack, sempool):
        self.nc = nc
        self.stack = stack
        self.sempool = sempool
        self.eng = {"pe": nc.tensor, "act": nc.scalar, "dve": nc.vector, "pool": nc.gpsimd, "sp": nc.sync}
        self.gen = {e: 0 for e in self.ENG}
        self.cnt = {e: 0 for e in self.ENG}
        self.esem = {}
        self.seen = {e: {} for e in self.ENG}
        self.res = {}
        self.dsem = {}
        self.nsem = 0
        self.ninst = {e: 0 for e in self.ENG}

    def _newsem(self, name):
        self.nsem += 1
        return self.sempool.pop()

    def _R(self, k):
        r = self.res.get(k)
        if r is None:
            r = [None, {}]
            self.res[k] = r
        return r

    def _wait(self, en, events):
        need = {}
        for ev in events:
            if ev is None:
                continue
            if ev[0] == "E":
                if ev[1] == en and en == "pe":
                    continue
                k = ("E", ev[1])
                v = (ev[2], ev[3])
            else:
                k = ("D", ev[1])
                v = (0, ev[2])
            if need.get(k, (-1, -1)) < v:
                need[k] = v
        for k, v in need.items():
            if self.seen[en].get(k, (-1, -1)) >= v:
                continue
            self.seen[en][k] = v
            sem = self.esem[(k[1], v[0])] if k[0] == "E" else self.dsem[k[1]][0]
            self.eng[en].wait_ge(sem, v[1])

    def _deps(self, r, w):
        evs = []
        for k in r:
            R = self.res.get(k)
            if R is not None:
                evs.append(R[0])
                if isinstance(k, tuple) and k[0] == "ps":
                    evs.extend(R[1].values())
        for k in w:
            R = self.res.get(k)
            if R is not None:
                evs.append(R[0])
                evs.extend(R[1].values())
        return evs

    def _record(self, ev, semid, r, w):
        for k in r:
            self._R(k)[1][semid] = ev
        for k in w:
            R = self._R(k)
            R[0] = ev
            R[1] = {}

    def op(self, en, fn, r=(), w=()):
        self._wait(en, self._deps(r, w))
        inst = fn(self.eng[en])
        self.ninst[en] += 1
        if self.cnt[en] >= 12000:
            self.gen[en] += 1
            self.cnt[en] = 0
        self.cnt[en] += 1
        g = self.gen[en]
        if (en, g) not in self.esem:
            self.esem[(en, g)] = self._newsem(f"e_{en}_{g}")
        inst.then_inc(self.esem[(en, g)], 1)
        ev = ("E", en, g, self.cnt[en])
        self._record(ev, ("E", en), r, w)
        return inst

    def dma(self, q, out, in_, r=(), w=(), key=None, **kw):
        self._wait(q, self._deps(r, w))
        inst = self.eng[q].dma_start(out=out, in_=in_, **kw)
        if key is None:
            key = ("w", w[0]) if w else ("r", r[0])
        ds = self.dsem.get(key)
        if ds is None:
            ds = [self._newsem(f"d{len(self.dsem)}"), 0]
            self.dsem[key] = ds
        ds[1] += 16
        inst.then_inc(ds[0], 16)
        ev = ("D", key, ds[1])
        self._record(ev, ("D", key), r, w)
        return inst

    def _all_events(self):
        evs = []
        for e in self.ENG:
            if self.cnt[e] > 0 or self.gen[e] > 0:
                evs.append(("E", e, self.gen[e], self.cnt[e]))
        for k, ds in self.dsem.items():
            evs.append(("D", k, ds[1]))
        return evs

    def barrier(self):
        evs = self._all_events()
        for e in self.ENG:
            self._wait(e, evs)

    def finish(self):
        self._wait("sp", self._all_events())


WAIT = "WAIT"


def run_tasks(gens):
    act = list(gens)
    idle = 0
    i = 0
    while act:
        i %= len(act)
        g = act[i]
        try:
            v = next(g)
        except StopIteration:
            act.pop(i)
            idle = 0
            continue
        if v is WAIT:
            idle += 1
            assert idle <= 4 * len(act) + 4, "all tasks waiting"
        else:
            idle = 0
        i += 1


class Arena:
    def __init__(self, ap, nbytes):
        self.ap = ap
        self.nbytes = nbytes

    def view(self, off, parts, shape, dt):
        esz = 4 if dt == F32 else 2
        n = int(np.prod(shape))
        nb = n * esz
        assert off % 4 == 0 and off + nb <= self.nbytes, (off, nb, self.nbytes)
        nw = (nb + 3) // 4
        v = self.ap[0:parts, off // 4: off // 4 + nw]
        if dt != F32:
            v = v.bitcast(dt)
            if v.shape[1] != n:
                v = v[:, 0:n]
        if len(shape) == 2:
            return v.rearrange("p (a b) -> p a b", a=shape[0])
        if len(shape) == 3:
            return v.rearrange("p (a b c) -> p a b c", a=shape[0], b=shape[1])
        return v


class Region:
    def __init__(self, arena, lo, hi):
        self.arena, self.lo, self.hi = arena, lo, hi
        self.cur = lo

    def reset(self):
        self.cur = self.lo

    def alloc(self, shape, dt, parts=128):
        esz = 4 if dt == F32 else 2
        nb = (int(np.prod(shape)) * esz + 3) // 4 * 4
        off = self.cur
        assert off + nb <= self.hi, ("region overflow", off, nb, self.hi)
        self.cur += nb
        return self.arena.view(off, parts, shape, dt)


CF_ID, CF_U, CF_MASK, CF_EPS, CF_ONE, CF_NHALF, CF_LNSC, CF_ZERO, CF_N = 0, 128, 256, 512, 513, 514, 515, 516, 520
CB_ID, CB_ONES, CB_BD, CB_ML = 0, 128, 256, 384
CB_MC, CB_MP, CB_MN, CB_MK, CB_N = 1152, 1280, 1408, 1504, 1528


def make_consts():
    cf = np.zeros((128, CF_N), np.float32)
    cf[:, CF_ID:CF_ID + 128] = np.eye(128)
    k = np.arange(128)[:, None]
    i = np.arange(128)[None, :]
    cf[:, CF_U:CF_U + 128] = (k <= i)
    cf[:, CF_MASK:CF_MASK + 128] = np.where(i >= k, 0.0, NEG)
    cf[:, CF_MASK + 128:CF_MASK + 256] = np.where(i > k, 0.0, NEG)
    cf[:, CF_EPS] = EPS
    cf[:, CF_ONE] = 1.0
    cf[:, CF_NHALF] = -0.5
    cf[:, CF_LNSC] = np.log(128.0 ** -0.5)
    cb = np.zeros((128, CB_N), np.float32)
    cb[:, CB_ID:CB_ID + 128] = np.eye(128)
    cb[:, CB_ONES:CB_ONES + 128] = 1.0
    ii = np.arange(128)[:, None]
    jj = np.arange(128)[None, :]
    cb[:, CB_BD:CB_BD + 128] = (ii // 16 == jj // 16)
    for lv, b in enumerate((16, 32, 64)):
        off = (ii // (2 * b) == jj // (2 * b)) & (ii % (2 * b) >= b) & (jj % (2 * b) < b)
        cb[:, CB_ML + lv * 256:CB_ML + lv * 256 + 128] = off.T
        cb[:, CB_ML + lv * 256 + 128:CB_ML + lv * 256 + 256] = off
    cb[:, CB_MC:CB_MC + 128] = np.where(jj >= ii, 0.0, NEG)
    cb[:, CB_MP:CB_MP + 128] = np.where(jj <= ii, 0.0, NEG)
    for gi, d in enumerate((1, 4, 16)):
        a = np.arange(32)
        sk, jk = a[:, None] // 8, a[:, None] % 8
        sq, lq = a[None, :] // 8, a[None, :] % 8
        ok = (sk == sq) & (jk <= lq) & ((lq - jk) % d == 0)
        cb[0:32, CB_MN + gi * 32:CB_MN + (gi + 1) * 32] = np.where(ok, 0.0, NEG)
        cb[:, CB_MK + gi * 8:CB_MK + (gi + 1) * 8] = np.where(ii >= (np.arange(8)[None, :] // d), 0.0, NEG)
    return cf, cb.astype(ml_dtypes.bfloat16)


def build(dbg=False, nheads=8, do_l1=True, stop=None):
    nc = bass.Bass("TRN2", target_bir_lowering=False)

    def din(name, shape, dt=F32):
        return nc.dram_tensor(name, list(shape), dt, kind="ExternalInput").ap()

    def dout(name, shape, dt=F32):
        return nc.dram_tensor(name, list(shape), dt, kind="ExternalOutput").ap()

    I = dict(
        xp=din("xp", [L, D]), xs=din("xs", [NS * LS, D]),
        cT=din("cT", [128, 8, 5]), ngc=din("ngc", [2, 128, 8]),
        adaw=din("adaw", [2, D, 3 * D]), adabc=din("adabc", [2, 128, 16]), adab=din("adab", [2, 3 * D]),
        awin=din("awin", [D, 4112]), cw=din("cw", [128, 24, 4]), hp=din("hp", [128, 17]),
        sconv=din("sconv", [128, 24, NS, 3]), sdelta=din("sdelta", [NS, 8, 128, 128]),
        awout=din("awout", [D, D]),
        cf=din("cf", [128, CF_N]), cb=din("cb", [128, CB_N], BF16),
        fng=din("fng", [D]),
        bwin=din("bwin", [D, 10240]), bwout=din("bwout", [D, D]),
        kc0=din("kc0", [NS, 128, 2, 8, 128]), kc1=din("kc1", [NS, 512, 2, 8, 128]), kc2=din("kc2", [NS, 2048, 2, 8, 128]),
    )
    O = dict(
        yp=dout("yp", [L, D]), ys=dout("ys", [NS * LS, D]),
        dp=dout("dp", [8, 128, 128]), ds=dout("ds", [NS, 8, 128, 128]),
        cp=dout("cp", [3, 3072]), cs=dout("cs", [NS * 3, 3072]),
        kv0p=dout("kv0p", [128, 2, 8, 128]), kv1p=dout("kv1p", [512, 2, 8, 128]), kv2p=dout("kv2p", [2048, 2, 8, 128]),
        kv0s=dout("kv0s", [NS, LS, 2, 8, 128]), kv1s=dout("kv1s", [NS, LS, 2, 8, 128]), kv2s=dout("kv2s", [NS, LS, 2, 8, 128]),
    )
    if dbg:
        O["dbg_x1p"] = dout("dbg_x1p", [L, D])
        O["dbg_x1s"] = dout("dbg_x1s", [NS * LS, D])
        O["dbg_og"] = dout("dbg_og", [128, 8, TT], BF16)
        O["dbg_hT"] = dout("dbg_hT", [128, 8, TT], BF16)

    stack = ExitStack()
    with stack:
        NB = 212000
        arena_t = stack.enter_context(nc.sbuf_tensor("arena", [128, NB // 4], F32))
        banks = [stack.enter_context(nc.psum_tensor(f"bank{i}", [128, 512], F32)) for i in range(8)]
        sempool = [stack.enter_context(nc.semaphore(f"s{i}")) for i in range(96)]
        stack.enter_context(nc.Block())
        S = Sched(nc, stack, sempool)
        A = Arena(arena_t, NB)

        def PS(b):
            return banks[b][:, :]

        def PSB(b):
            return banks[b][:, :].bitcast(BF16)

        RC = Region(A, 0, 8192)
        R1 = Region(A, 8192, 73728)
        R2 = Region(A, 73728, 107008)
        R3 = Region(A, 107008, 140288)
        R4 = Region(A, 140288, NB)

        cf = RC.alloc([CF_N], F32)
        cbf = RC.alloc([CB_N], BF16)
        hpar = RC.alloc([17], F32)
        S.dma("sp", cf, I["cf"], w=["cf"])
        S.dma("sp", cbf, I["cb"], w=["cb"])
        S.dma("sp", hpar, I["hp"], w=["hpar"])
        ident_f = cf[:, CF_ID:CF_ID + 128]
        U_f = cf[:, CF_U:CF_U + 128]
        mask2 = cf[:, CF_MASK:CF_MASK + 256]
        eps_c = cf[:, CF_EPS:CF_EPS + 1]
        one_c = cf[:, CF_ONE:CF_ONE + 1]
        nhalf_c = cf[:, CF_NHALF:CF_NHALF + 1]
        lnsc_c = cf[:, CF_LNSC:CF_LNSC + 1]
        ident_b = cbf[:, CB_ID:CB_ID + 128]
        ones_b = cbf[:, CB_ONES:CB_ONES + 128]
        bd_b = cbf[:, CB_BD:CB_BD + 128]
        ml_b = [cbf[:, CB_ML + lv * 256:CB_ML + (lv + 1) * 256] for lv in range(3)]
        mT_cur = cbf[:, CB_MC:CB_MC + 128]
        mT_prev = cbf[:, CB_MP:CB_MP + 128]
        CK = ["cf", "cb", "hpar"]

        modc = RC.alloc([16, 5], F32)
        gmod = RC.alloc([8, 5], F32)
        ngc = RC.alloc([8], F32)
        adabc = RC.alloc([16], F32)
        cT = RC.alloc([8, 5], F32)
        scb = RC.alloc([8, 5], BF16)

        def ada_phase(l, gate_p, gate_s, reg, parts=("col", "gate")):
            S.dma("sp", ngc, I["ngc"][l], w=["ngc"])
            S.dma("sp", adabc, I["adabc"][l], w=["adabc"])
            if l == 0:
                S.dma("sp", cT, I["cT"], w=["cT"])
                S.op("act", lambda e: e.activation(out=scb, in_=cT, func=AF.Silu), r=["cT"], w=["scb"])
            scp = reg.alloc([8, 128], BF16)
            scs = reg.alloc([8, 32], BF16)
            S.op("act", lambda e: e.activation(out=scp, in_=cT[:, :, 0:1].broadcast_to([128, 8, 128]), func=AF.Silu),
                 r=["cT"], w=["scp"])
            for s in range(NS):
                S.op("act", lambda e: e.activation(out=scs[:, :, 8 * s:8 * s + 8],
                                                   in_=cT[:, :, 1 + s:2 + s].broadcast_to([128, 8, 8]), func=AF.Silu),
                     r=["cT"], w=[("scs", s)])
            gb = reg.alloc([D], F32)
            S.dma("sp", gb, I["adab"][l, 2 * D:3 * D].partition_broadcast(128), w=["gb"])
            wb = [reg.alloc([8, 512], BF16) for _ in range(2)]
            for blk in range(6):
                if (blk < 4 and "col" not in parts) or (blk >= 4 and "gate" not in parts):
                    continue
                buf = wb[blk % 2]
                key = ("adaw", blk % 2)
                S.dma("pool", buf, I["adaw"][l][:, blk * 512:(blk + 1) * 512].rearrange("(c p) n -> p c n", p=128),
                      w=[key])
                if blk < 4:
                    def f(e, blk=blk, buf=buf):
                        last = None
                        for ecl in range(4):
                            ec = blk * 4 + ecl
                            for kc in range(8):
                                last = e.matmul(PS(0)[:, ec * 5:ec * 5 + 5], buf[:, kc, ecl * 128:(ecl + 1) * 128],
                                                scb[:, kc, :], start=(kc == 0), stop=(kc == 7))
                        return last
                    S.op("pe", f, r=[key, "scb"], w=[("ps", 0)])
                else:
                    hb = blk - 4
                    bk = 1 + (hb % 2)
                    def f(e, buf=buf, bk=bk):
                        last = None
                        for kc in range(8):
                            last = e.matmul(PS(bk), scp[:, kc, :], buf[:, kc, :], start=(kc == 0), stop=(kc == 7))
                        return last
                    S.op("pe", f, r=[key, "scp"], w=[("ps", bk)])
                    S.op("dve", lambda e, bk=bk, hb=hb: e.tensor_tensor(out=gate_p[:, hb * 512:(hb + 1) * 512], in0=PS(bk),
                                                                         in1=gb[:, hb * 512:(hb + 1) * 512], op=ALU.add),
                         r=[("ps", bk), "gb"], w=[("gate_p", l)])
                    def f2(e, buf=buf, bk=bk):
                        last = None
                        for kc in range(8):
                            last = e.matmul(PS(bk)[0:32, :], scs[:, kc, :], buf[:, kc, :], start=(kc == 0), stop=(kc == 7))
                        return last
                    S.op("pe", f2, r=[key] + [("scs", s) for s in range(NS)], w=[("ps", bk)])
                    S.op("dve", lambda e, bk=bk, hb=hb: e.tensor_tensor(out=gate_s[:, hb * 512:(hb + 1) * 512], in0=PS(bk)[0:32, :],
                                                                         in1=gb[0:32, hb * 512:(hb + 1) * 512], op=ALU.add),
                         r=[("ps", bk), "gb"], w=[("gate_s", l)])
            if "col" not in parts:
                return
            S.op("dve", lambda e: e.tensor_tensor(out=modc, in0=PS(0)[:, 0:80].rearrange("p (a b) -> p a b", b=5),
                                                  in1=adabc.unsqueeze(2).broadcast_to([128, 16, 5]), op=ALU.add),
                 r=[("ps", 0), "adabc"], w=["modc"])
            S.op("dve", lambda e: e.scalar_tensor_tensor(out=gmod, in0=modc[:, 8:16, :], scalar=1.0,
                                                         in1=ngc.unsqueeze(2).broadcast_to([128, 8, 5]),
                                                         op0=ALU.add, op1=ALU.mult),
                 r=["modc", "ngc"], w=["gmod"])

        def norm_phase(l, hT, x_src, reg):
            ssq = reg.alloc([17], F32)
            rstd = reg.alloc([17], F32)
            junk = reg.alloc([D], BF16)
            xn = [reg.alloc([D], BF16) for _ in range(2)]
            ntile = 17
            tiles = []
            if l == 0:
                xst = [reg.alloc([D], F32) for _ in range(3)]

            def xt(t):
                if l == 0:
                    return xst[t % 3], ("xst", t % 3)
                return x_src(t)

            def load(t):
                if l != 0:
                    return
                buf, key = xt(t)
                if t < 16:
                    S.dma("sp", buf, I["xp"][t * 128:(t + 1) * 128, :], w=[key])
                else:
                    S.dma("sp", buf[0:32], I["xs"], w=[key])

            def sq(t):
                buf, key = xt(t)
                p = 128 if t < 16 else 32
                keys = key if isinstance(key, list) else [key]
                S.op("act", lambda e: e.activation(out=junk[0:p], in_=buf[0:p], func=AF.Square, accum_out=ssq[0:p, t:t + 1]),
                     r=keys, w=["junk", ("ssq", t)])
                S.op("pool", lambda e: e.tensor_scalar(out=rstd[0:p, t:t + 1], in0=ssq[0:p, t:t + 1], scalar1=1.0 / D, scalar2=EPS,
                                                       op0=ALU.mult, op1=ALU.add), r=[("ssq", t)], w=[("rstd", t)])
                S.op("pool", lambda e: e.tensor_tensor(out=rstd[0:p, t:t + 1], in0=rstd[0:p, t:t + 1], in1=nhalf_c[0:p], op=ALU.pow),
                     r=[("rstd", t), "cf"], w=[("rstd", t)])

            def scale_T(t):
                buf, key = xt(t)
                p = 128 if t < 16 else 32
                xb = xn[t % 2]
                keys = key if isinstance(key, list) else [key]
                S.op("act", lambda e: e.activation(out=xb[0:p], in_=buf[0:p], func=AF.Identity, scale=rstd[0:p, t:t + 1]),
                     r=keys + [("rstd", t)], w=[("xn", t % 2)])
                bk = 2 + (t % 2)
                def f(e):
                    last = None
                    for kc in range(8):
                        last = e.transpose(PSB(bk)[:, kc * 128:kc * 128 + p], xb[0:p, kc * 128:(kc + 1) * 128], ident_b[0:p, 0:p])
                    return last
                S.op("pe", f, r=[("xn", t % 2), "cb"], w=[("ps", bk)])
                for kc in range(8):
                    if t < 16:
                        dst = hT[:, kc, t * 128:(t + 1) * 128]
                        src = PSB(bk)[:, kc * 128:(kc + 1) * 128]
                        if kc % 2 == 0:
                            S.op("dve", lambda e, dst=dst, src=src, kc=kc: e.tensor_scalar(
                                out=dst, in0=src, scalar1=gmod[:, kc, 0:1], scalar2=modc[:, kc, 0:1], op0=ALU.mult, op1=ALU.add),
                                 r=[("ps", bk), "gmod", "modc"], w=[("hT", t, kc)])
                        else:
                            S.op("act", lambda e, dst=dst, src=src, kc=kc: e.activation(
                                out=dst, in_=src, func=AF.Identity, scale=gmod[:, kc, 0:1], bias=modc[:, kc, 0:1]),
                                 r=[("ps", bk), "gmod", "modc"], w=[("hT", t, kc)])
                    else:
                        for s in range(NS):
                            dst = hT[:, kc, L + s * 8:L + s * 8 + 8]
                            src = PSB(bk)[:, kc * 128 + s * 8:kc * 128 + s * 8 + 8]
                            S.op("dve", lambda e, dst=dst, src=src, kc=kc, s=s: e.tensor_scalar(
                                out=dst, in0=src, scalar1=gmod[:, kc, 1 + s:2 + s], scalar2=modc[:, kc, 1 + s:2 + s],
                                op0=ALU.mult, op1=ALU.add), r=[("ps", bk), "gmod", "modc"], w=[("hT", t, kc)])

            load(0)
            load(1)
            sq(0)
            for t in range(ntile):
                if t + 2 < ntile:
                    load(t + 2)
                if t + 1 < ntile:
                    sq(t + 1)
                scale_T(t)

        HT_KEYS = [("hT", t, kc) for t in range(17) for kc in range(8)]

        R1.reset(); R2.reset(); R3.reset(); R4.reset()
        hT = R1.alloc([8, TT], BF16)
        ogT = R2.alloc([8, TT], BF16)
        R4g = Region(A, NB - 12288, NB)
        gate_p = R4g.alloc([D], F32)
        gate_s = R4g.alloc([D], F32, parts=32)
        x1s = R4g.alloc([D], F32, parts=32)
        R4 = Region(A, 140288, NB - 12288)

        class _Stop(Exception):
            pass

        def maybe_stop(tag):
            if stop == tag:
                S.finish()
                print("STOP at", tag, "instructions:", S.ninst, "sems:", S.nsem)
                raise _Stop()

        try:
            _build_rest = None
        finally:
            pass
        ada_phase(0, gate_p, gate_s, R3)
        R4.reset()
        if stop == "ada":
            S.finish(); print("STOP ada", S.ninst); return nc
        norm_phase(0, hT, None, R4)
        S.barrier()
        if stop == "norm":
            S.finish(); print("STOP norm", S.ninst); return nc
        R3.reset(); R4.reset()

        NCH = 16
        wab = R1.alloc([8, 16], BF16)
        S.dma("pool", wab, I["awin"][:, 4096:4112].rearrange("(c p) n -> p c n", p=128), w=["wab"])
        def fab(e):
            last = None
            for t in range(NCH):
                for kc in range(8):
                    last = e.matmul(PS(0)[:, t * 16:(t + 1) * 16], hT[:, kc, t * 128:(t + 1) * 128], wab[:, kc, :],
                                    start=(kc == 0), stop=(kc == 7))
            for s in range(NS):
                for kc in range(8):
                    last = e.matmul(PS(1)[0:8, s * 16:(s + 1) * 16], hT[:, kc, L + s * 8:L + s * 8 + 8], wab[:, kc, :],
                                    start=(kc == 0), stop=(kc == 7))
            return last
        S.op("pe", fab, r=["wab"] + HT_KEYS, w=[("ps", 0), ("ps", 1)])

        NCOL = NCH * 8 + NS * 8
        def galloc():
            return R1.alloc([NCOL], F32)
        xa, ax, ex, lx, g_t, beta_t, lbeta_t, gc_t, gcl_t, eg_t, gtot_t, ekd_t = [galloc() for _ in range(12)]
        nA = R1.alloc([8], F32)
        A_bc = hpar[:, 0:8]
        dt_bc = hpar[:, 8:16]
        outg_c = hpar[:, 16:17]
        def pv(tl):
            return tl[:, 0:128].rearrange("p (c h) -> p c h", h=8)
        def sv(tl):
            return tl[0:8, 128:160].rearrange("p (c h) -> p c h", h=8)
        abp = PS(0)[:, 0:256].rearrange("p (c k) -> p c k", k=16)
        abs_ = PS(1)[0:8, 0:64].rearrange("p (c k) -> p c k", k=16)
        S.op("dve", lambda e: e.tensor_tensor(out=pv(xa), in0=abp[:, :, 0:8], in1=dt_bc.unsqueeze(1).broadcast_to([128, 16, 8]), op=ALU.add),
             r=[("ps", 0), "hpar"], w=["xa_p"])
        S.op("dve", lambda e: e.tensor_tensor(out=sv(xa), in0=abs_[:, :, 0:8], in1=dt_bc[0:8].unsqueeze(1).broadcast_to([8, 4, 8]), op=ALU.add),
             r=[("ps", 1), "hpar"], w=["xa_s"])
        S.op("act", lambda e: e.activation(out=pv(ex), in_=abp[:, :, 8:16], func=AF.Exp, scale=-1.0), r=[("ps", 0)], w=["ex_p"])
        S.op("act", lambda e: e.activation(out=sv(ex), in_=abs_[:, :, 8:16], func=AF.Exp, scale=-1.0), r=[("ps", 1)], w=["ex_s"])
        GP = (128, slice(0, 128))
        GS = (8, slice(128, 160))
        for (p, cs_), tg in ((GP, "p"), (GS, "s")):
            def T(tl, p=p, cs_=cs_):
                return tl[0:p, cs_]
            S.op("act", lambda e, T=T: e.activation(out=T(lbeta_t), in_=T(ex), func=AF.Ln, bias=one_c[0:T(ex).shape[0]], scale=1.0),
                 r=["ex_" + tg, "cf"], w=["lbeta_" + tg])
            S.op("dve", lambda e, T=T: e.tensor_scalar(out=T(lbeta_t), in0=T(lbeta_t), scalar1=-1.0, scalar2=None, op0=ALU.mult),
                 r=["lbeta_" + tg], w=["lbeta_" + tg])
            S.op("act", lambda e, T=T: e.activation(out=T(beta_t), in_=T(lbeta_t), func=AF.Exp), r=["lbeta_" + tg], w=["beta_" + tg])
            S.op("dve", lambda e, T=T: e.tensor_scalar(out=T(ax), in0=T(xa), scalar1=-1.0, scalar2=None, op0=ALU.mult),
                 r=["xa_" + tg], w=["ax_" + tg])
            S.op("dve", lambda e, T=T: e.tensor_tensor(out=T(ax), in0=T(ax), in1=T(xa), op=ALU.max),
                 r=["xa_" + tg, "ax_" + tg], w=["ax_" + tg])
            S.op("act", lambda e, T=T: e.activation(out=T(ax), in_=T(ax), func=AF.Exp, scale=-1.0), r=["ax_" + tg], w=["ax_" + tg])
            S.op("act", lambda e, T=T: e.activation(out=T(lx), in_=T(ax), func=AF.Ln, bias=one_c[0:T(ax).shape[0]], scale=1.0),
                 r=["ax_" + tg, "cf"], w=["lx_" + tg])
            S.op("dve", lambda e, T=T: e.scalar_tensor_tensor(out=T(lx), in0=T(xa), scalar=0.0, in1=T(lx), op0=ALU.max, op1=ALU.add),
                 r=["xa_" + tg, "lx_" + tg], w=["lx_" + tg])
        S.op("act", lambda e: e.activation(out=nA, in_=A_bc, func=AF.Exp), r=["hpar"], w=["nA"])
        S.op("dve", lambda e: e.tensor_scalar(out=nA, in0=nA, scalar1=-1.0, scalar2=None, op0=ALU.mult), r=["nA"], w=["nA"])
        S.op("dve", lambda e: e.tensor_tensor(out=pv(g_t), in0=pv(lx), in1=nA.unsqueeze(1).broadcast_to([128, 16, 8]), op=ALU.mult),
             r=["lx_p", "nA"], w=["g_p"])
        S.op("dve", lambda e: e.tensor_tensor(out=sv(g_t), in0=sv(lx), in1=nA[0:8].unsqueeze(1).broadcast_to([8, 4, 8]), op=ALU.mult),
             r=["lx_s", "nA"], w=["g_s"])
        S.op("pe", lambda e: e.matmul(PS(2)[:, 0:128], U_f, g_t[:, 0:128], start=True, stop=True), r=["cf", "g_p"], w=[("ps", 2)])
        S.op("pe", lambda e: e.matmul(PS(3)[0:8, 0:32], U_f[0:8, 0:8], g_t[0:8, 128:160], start=True, stop=True), r=["cf", "g_s"], w=[("ps", 3)])
        S.op("act", lambda e: e.activation(out=gc_t[:, 0:128], in_=PS(2)[:, 0:128], func=AF.Copy), r=[("ps", 2)], w=["gc_p"])
        S.op("act", lambda e: e.activation(out=gc_t[0:8, 128:160], in_=PS(3)[0:8, 0:32], func=AF.Copy), r=[("ps", 3)], w=["gc_s"])
        S.op("pe", lambda e: e.matmul(PS(2)[:, 128:256], ident_f[:, 127:128].broadcast_to([128, 128]), gc_t[:, 0:128], start=True, stop=True),
             r=["cf", "gc_p"], w=[("ps", 2)])
        S.op("pe", lambda e: e.matmul(PS(3)[:, 128:160], ident_f[0:8, 7:8].broadcast_to([8, 128]), gc_t[0:8, 128:160], start=True, stop=True),
             r=["cf", "gc_s"], w=[("ps", 3)])
        S.op("act", lambda e: e.activation(out=gcl_t[:, 0:128], in_=PS(2)[:, 128:256], func=AF.Copy), r=[("ps", 2)], w=["gcl_p"])
        S.op("act", lambda e: e.activation(out=gcl_t[:, 128:160], in_=PS(3)[:, 128:160], func=AF.Copy), r=[("ps", 3)], w=["gcl_s"])
        for (p, cs_), tg in ((GP, "p"), (GS, "s")):
            def T(tl, p=p, cs_=cs_):
                return tl[0:p, cs_]
            S.op("act", lambda e, T=T: e.activation(out=T(eg_t), in_=T(gc_t), func=AF.Exp), r=["gc_" + tg], w=["eg_" + tg])
            S.op("act", lambda e, cs_=cs_: e.activation(out=gtot_t[:, cs_], in_=gcl_t[:, cs_], func=AF.Exp), r=["gcl_" + tg], w=["gtot_" + tg])
            S.op("dve", lambda e, T=T: e.tensor_tensor(out=T(ekd_t), in0=T(gcl_t), in1=T(gc_t), op=ALU.subtract),
                 r=["gcl_" + tg, "gc_" + tg], w=["ekd_" + tg])
        G_KEYS = [k + t for k in ("beta_", "lbeta_", "gc_", "gcl_", "eg_", "gtot_", "ekd_") for t in ("p", "s")]

        if stop == "G":
            S.finish(); print("STOP G", S.ninst); return nc
        NT = 16
        UW = 3 + L
        UT = UW + NS * (3 + LS)
        NUB = 3
        ubuf = [R4.alloc([UT], BF16) for _ in range(NUB)]
        Wh = [R1.alloc([8, 512], BF16) for _ in range(2)]
        diag = [R1.alloc([12, 128], BF16) for _ in range(2)]
        cwt = R1.alloc([24, 4], F32)
        S.dma("sp", cwt, I["cw"], w=["cwt"])
        sconv = R1.alloc([24, NS * 3], F32)
        S.dma("sp", sconv, I["sconv"].rearrange("p a s i -> p a (s i)"), w=["sconv"])
        cvo = [[R3.alloc([TT], BF16) for _ in range(4)] for _ in range(2)]
        sqb1 = R4.alloc([TT], BF16)
        sqb = [sqb1, sqb1]
        NHC = NT + NS
        def halloc(n=NHC):
            return [R4.alloc([n], F32) for _ in range(2)]
        ss_k, ss_q, lrnk, lrq, rows1, rows2, biasj, kbg_s, kdec_s, qdec_s = [halloc() for _ in range(10)]
        rowsT = [R4.alloc([256], F32, parts=16) for _ in range(2)]
        rowsTs = [R4.alloc([16], F32, parts=4) for _ in range(2)]
        cst1 = R4.alloc([384], F32, parts=3)
        css1 = R4.alloc([384], F32, parts=32)
        cst = [cst1, cst1]
        css = [css1, css1]
        thb = [R4.alloc([512], BF16) for _ in range(2)]
        thc = [0]
        outg_h = R4.alloc([1], F32)
        S.op("dve", lambda e: e.tensor_scalar(out=outg_h, in0=outg_c, scalar1=0.5, scalar2=None, op0=ALU.mult), r=["hpar"], w=["outg_h"])
        for ub in range(NUB):
            S.op("pool", lambda e, ub=ub: e.memset(ubuf[ub][:, 0:3], 0.0), w=[("u", ub, "h")])

        ucount = [0]

        def P_head(h):
            hs_ = h % 2
            W = Wh[hs_]
            wkey = [("Wh", hs_, j) for j in range(4)]
            for j in range(4):
                col = (j * 1024 + h * 128)
                S.dma("pool", W[:, :, j * 128:(j + 1) * 128],
                      I["awin"][:, col:col + 128].rearrange("(c p) n -> p c n", p=128), w=[wkey[j]])
            dg = diag[hs_]
            for j in range(3):
                for i in range(4):
                    S.op("pool", lambda e, j=j, i=i: e.tensor_scalar(out=dg[:, j * 4 + i, :], in0=ident_f, scalar1=cwt[:, j * 8 + h, i:i + 1],
                                                                      scalar2=0.5, op0=ALU.mult, op1=ALU.mult),
                         r=["cf", "cwt"], w=[("diag", hs_, j)])
            yield 0.02
            blocks = [(q * 512, 512) for q in range(4)] + [(L, NS * LS)]
            step = 0
            nstep = 4 * 5 * 2.0
            for j in range(4):
                if j < 3:
                    ui = ucount[0] % NUB
                    ucount[0] += 1
                    ub = ubuf[ui]
                    ukey = None
                    ukeys = [("u", ui, bi_) for bi_ in range(5)]
                    S.op("act", lambda e, ub=ub, j=j: e.activation(
                        out=ub[:, UW:UT].rearrange("p (s i) -> p s i", i=3 + LS)[:, :, 0:3],
                        in_=sconv[:, j * 8 + h, :].rearrange("p (s i) -> p s i", i=3), func=AF.Copy),
                         r=["sconv"], w=[("u", ui, "sh")])
                for bi, (t0, n) in enumerate(blocks):
                    bk = 0
                    step += 1
                    def f(e, t0=t0, n=n, bk=bk, j=j):
                        last = None
                        for kc in range(8):
                            last = e.matmul(PS(bk)[:, 0:n], W[:, kc, j * 128:(j + 1) * 128], hT[:, kc, t0:t0 + n],
                                            start=(kc == 0), stop=(kc == 7))
                        return last
                    S.op("pe", f, r=[wkey[j]] + HT_KEYS, w=[("ps", bk)])
                    if j == 3:
                        tb = thb[thc[0] % 2]
                        tk = ("thb", thc[0] % 2)
                        thc[0] += 1
                        S.op("act", lambda e, n=n, bk=bk, tb=tb: e.activation(out=tb[:, 0:n], in_=PS(bk)[:, 0:n], func=AF.Tanh, scale=0.5),
                             r=[("ps", bk)], w=[tk])
                        S.op("dve", lambda e, t0=t0, n=n, bk=bk, tb=tb: e.scalar_tensor_tensor(out=cvo[hs_][3][:, t0:t0 + n], in0=tb[:, 0:n], scalar=1.0,
                                                                                              in1=PS(bk)[:, 0:n], op0=ALU.add, op1=ALU.mult),
                             r=[("ps", bk), tk], w=[("cvo", hs_, 3, bi)])
                    else:
                        if bi < 4:
                            dst = ub[:, 3 + t0:3 + t0 + n]
                            src = PS(bk)[:, 0:n]
                        else:
                            dst = ub[:, UW:UT].rearrange("p (s i) -> p s i", i=3 + LS)[:, :, 3:3 + LS]
                            src = PS(bk)[:, 0:n].rearrange("p (s i) -> p s i", i=LS)
                        S.op("act", lambda e, dst=dst, src=src: e.activation(out=dst, in_=src, func=AF.Copy), r=[("ps", bk)], w=[ukeys[bi]])
                        ck = 1
                        def fc(e, t0=t0, n=n, ck=ck, bi=bi, j=j, ub=ub):
                            last = None
                            for i in range(4):
                                if bi < 4:
                                    rhs = ub[:, t0 + i:t0 + i + n]
                                    out = PS(ck)[:, 0:n]
                                else:
                                    rhs = ub[:, UW:UT].rearrange("p (s i) -> p s i", i=3 + LS)[:, :, i:i + LS]
                                    out = PS(ck)[:, 0:n].rearrange("p (s i) -> p s i", i=LS)
                                last = e.matmul(out, dg[:, j * 4 + i, :], rhs, start=(i == 0), stop=(i == 3))
                            return last
                        rk = [ukeys[bi], ("diag", hs_, j)] + ([ukeys[bi - 1]] if 0 < bi < 4 else []) + ([("u", ui, "h")] if bi == 0 else []) + ([("u", ui, "sh")] if bi == 4 else [])
                        yield 0.5
                        S.op("pe", fc, r=rk, w=[("ps", ck)])
                        tb = thb[thc[0] % 2]
                        tk = ("thb", thc[0] % 2)
                        thc[0] += 1
                        S.op("act", lambda e, n=n, ck=ck, tb=tb: e.activation(out=tb[:, 0:n], in_=PS(ck)[:, 0:n], func=AF.Tanh),
                             r=[("ps", ck)], w=[tk])
                        S.op("dve", lambda e, t0=t0, n=n, ck=ck, j=j, tb=tb: e.scalar_tensor_tensor(out=cvo[hs_][j][:, t0:t0 + n], in0=tb[:, 0:n], scalar=1.0,
                                                                                                   in1=PS(ck)[:, 0:n], op0=ALU.add, op1=ALU.mult),
                             r=[("ps", ck), tk], w=[("cvo", hs_, j, bi)])
                    yield 0.02 + 0.8 * step / 20.0
            def fcs(e):
                last = None
                for kc in range(8):
                    last = e.matmul(PS(0)[0:3, 0:384], hT[:, kc, L - 3:L], W[:, kc, 0:384], start=(kc == 0), stop=(kc == 7))
                for kc in range(8):
                    last = e.matmul(PS(1)[0:32, 0:384], hT[:, kc, L:TT], W[:, kc, 0:384], start=(kc == 0), stop=(kc == 7))
                return last
            S.op("pe", fcs, r=wkey + HT_KEYS, w=[("ps", 0), ("ps", 1)])
            S.op("act", lambda e: e.activation(out=cst[hs_], in_=PS(0)[0:3, 0:384], func=AF.Copy), r=[("ps", 0)], w=[("cst", 0)])
            S.op("act", lambda e: e.activation(out=css[hs_], in_=PS(1)[0:32, 0:384], func=AF.Copy), r=[("ps", 1)], w=[("css", 0)])
            S.dma("sp", O["cp"].rearrange("p (j c) -> p j c", j=3)[:, :, h * 128:(h + 1) * 128],
                  cst[hs_].rearrange("p (j c) -> p j c", j=3), r=[("cst", 0)])
            for s_ in range(NS):
                S.dma("sp", O["cs"].rearrange("p (j c) -> p j c", j=3)[3 * s_:3 * s_ + 3, :, h * 128:(h + 1) * 128],
                      css[hs_][8 * s_ + 5:8 * s_ + 8].rearrange("p (j c) -> p j c", j=3), r=[("css", 0)])
            for j, sst in ((1, ss_k[hs_]), (0, ss_q[hs_])):
                sb = sqb[j]
                S.op("act", lambda e, j=j, sb=sb: e.activation(out=sb, in_=cvo[hs_][j], func=AF.Square),
                     r=[("cvo", hs_, j, bi) for bi in range(5)], w=[("sqb", 0)])
                def fs(e, sb=sb, j=j):
                    last = None
                    for c in range(NT):
                        last = e.matmul(PS(j)[:, c:c + 1], sb[:, c * 128:(c + 1) * 128], ones_b[:, 0:1], start=True, stop=True)
                    for s in range(NS):
                        last = e.matmul(PS(j)[0:8, NT + s:NT + s + 1], sb[:, L + s * 8:L + s * 8 + 8], ones_b[:, 0:1], start=True, stop=True)
                    return last
                S.op("pe", fs, r=[("sqb", 0), "cb"], w=[("ps", j)])
                S.op("act", lambda e, j=j, sst=sst: e.activation(out=sst[:, 0:NT], in_=PS(j)[:, 0:NT], func=AF.Copy), r=[("ps", j)], w=[("ss", hs_, j)])
                S.op("act", lambda e, j=j, sst=sst: e.activation(out=sst[0:8, NT:NHC], in_=PS(j)[0:8, NT:NHC], func=AF.Copy), r=[("ps", j)], w=[("ss", hs_, j, "s")])
            yield 0.9
            def gcol(tl, which):
                if which == "p":
                    return tl[:, 0:128].rearrange("p (c hh) -> p c hh", hh=8)[:, :, h]
                return tl[0:8, 128:160].rearrange("p (c hh) -> p c hh", hh=8)[:, :, h]
            for which, p, cs_ in (("p", 128, slice(0, NT)), ("s", 8, slice(NT, NHC))):
                def T(tl, p=p, cs_=cs_):
                    return tl[hs_][0:p, cs_]
                hk = ("hs", hs_, which)
                S.op("act", lambda e, T=T, p=p: e.activation(out=T(lrnk), in_=T(ss_k), func=AF.Ln, bias=eps_c[0:p], scale=1.0),
                     r=[("ss", hs_, 1), ("ss", hs_, 1, "s"), "cf"], w=[hk + ("lrnk",)])
                S.op("act", lambda e, T=T, p=p: e.activation(out=T(lrq), in_=T(ss_q), func=AF.Ln, bias=eps_c[0:p], scale=1.0),
                     r=[("ss", hs_, 0), ("ss", hs_, 0, "s"), "cf"], w=[hk + ("lrq",)])
                S.op("dve", lambda e, T=T: e.tensor_scalar(out=T(lrnk), in0=T(lrnk), scalar1=-0.5, scalar2=None, op0=ALU.mult),
                     r=[hk + ("lrnk",)], w=[hk + ("lrnk",)])
                S.op("dve", lambda e, T=T, p=p: e.tensor_scalar(out=T(lrq), in0=T(lrq), scalar1=-0.5, scalar2=lnsc_c[0:p], op0=ALU.mult, op1=ALU.add),
                     r=[hk + ("lrq",), "cf"], w=[hk + ("lrq",)])
                S.op("dve", lambda e, T=T, which=which: e.tensor_tensor(out=T(rows1), in0=T(lrnk), in1=gcol(lbeta_t, which), op=ALU.add),
                     r=[hk + ("lrnk",), "lbeta_" + which], w=[hk + ("rows1",)])
                S.op("dve", lambda e, T=T, which=which: e.tensor_tensor(out=T(rows1), in0=T(rows1), in1=gcol(gc_t, which), op=ALU.add),
                     r=[hk + ("rows1",), "gc_" + which], w=[hk + ("rows1",)])
                S.op("dve", lambda e, T=T, which=which: e.tensor_tensor(out=T(rows2), in0=T(lrq), in1=gcol(gc_t, which), op=ALU.add),
                     r=[hk + ("lrq",), "gc_" + which], w=[hk + ("rows2",)])
                S.op("dve", lambda e, T=T, which=which: e.tensor_tensor(out=T(biasj), in0=T(lrnk), in1=gcol(gc_t, which), op=ALU.subtract),
                     r=[hk + ("lrnk",), "gc_" + which], w=[hk + ("biasj",)])
                S.op("act", lambda e, T=T: e.activation(out=T(kbg_s), in_=T(rows1), func=AF.Exp), r=[hk + ("rows1",)], w=[hk + ("kbg_s",)])
                S.op("act", lambda e, T=T: e.activation(out=T(qdec_s), in_=T(rows2), func=AF.Exp), r=[hk + ("rows2",)], w=[hk + ("qdec_s",)])
                S.op("dve", lambda e, T=T, which=which: e.tensor_tensor(out=T(kdec_s), in0=T(lrnk), in1=gcol(ekd_t, which), op=ALU.add),
                     r=[hk + ("lrnk",), "ekd_" + which], w=[hk + ("kdec_s",)])
                S.op("act", lambda e, T=T: e.activation(out=T(kdec_s), in_=T(kdec_s), func=AF.Exp), r=[hk + ("kdec_s",)], w=[hk + ("kdec_s",)])
            def ft(e):
                e.transpose(PS(0)[0:16, 0:128], rows2[hs_][:, 0:NT], ident_f)
                e.transpose(PS(0)[0:16, 128:256], rows1[hs_][:, 0:NT], ident_f)
                e.transpose(PS(1)[0:4, 0:8], rows2[hs_][0:8, NT:NHC], ident_f[0:8, 0:8])
                return e.transpose(PS(1)[0:4, 8:16], rows1[hs_][0:8, NT:NHC], ident_f[0:8, 0:8])
            S.op("pe", ft, r=[("hs", hs_, w_, n_) for w_ in ("p", "s") for n_ in ("rows1", "rows2")] + ["cf"], w=[("ps", 0), ("ps", 1)])
            S.op("act", lambda e: e.activation(out=rowsT[hs_], in_=PS(0)[0:16, 0:256], func=AF.Copy), r=[("ps", 0)], w=[("rowsT", hs_)])
            S.op("act", lambda e: e.activation(out=rowsTs[hs_], in_=PS(1)[0:4, 0:16], func=AF.Copy), r=[("ps", 1)], w=[("rowsTs", hs_)])
            yield 1.0

        NSET = 3
        import os as _os
        NLANE = int(_os.environ.get("K_NLANE", "3"))
        NHO = 6
        STAG = int(_os.environ.get("K_STAG", "5"))
        LANE_BANK = (6, 2, 3)

        def lane_ws():
            d = {}
            d["t"] = R4.alloc([256], F32)
            d["D"] = d["t"]
            d["W"] = [R4.alloc([512], BF16) for _ in range(2)]
            d["BN"] = [w_[:, 0:256] for w_ in d["W"]]
            d["Q"] = [w_[:, 256:384] for w_ in d["W"]]
            d["M"] = [R4.alloc([256], BF16) for _ in range(3)]
            d["XY"] = R4.alloc([256], BF16)
            d["QTs"] = R4.alloc([128], BF16)
            return d

        def handoff():
            d = {}
            d["ktok"] = R4.alloc([128], BF16)
            d["vb"] = R4.alloc([128], BF16)
            d["NTQK"] = R4.alloc([384], BF16)
            d["kcTn"] = R4.alloc([128], BF16)
            d["QT"] = R4.alloc([128], BF16)
            return d
        HO_NAMES = ("ktok", "vb", "NTQK", "kcTn", "QT")
        lanes = [lane_ws() for _ in range(NLANE)]
        hos = [handoff() for _ in range(NHO)]
        u_t = [R4.alloc([128], BF16) for _ in range(2)]
        us_t = [R4.alloc([128], BF16) for _ in range(2)]
        t2_t = [R4.alloc([128], F32) for _ in range(2)]
        o_t = [R4.alloc([128], F32) for _ in range(2)]
        on_t = [R4.alloc([128], BF16) for _ in range(2)]
        junk2 = R4.alloc([128], BF16)
        oss = R4.alloc([2], F32)
        Sf = [R4.alloc([128], F32) for _ in range(2)]
        Sb = [R4.alloc([128], BF16) for _ in range(2)]
        sidx = [0]
        cidx = [0]
        cp_done = set()
        cr_done = [0]
        cr_o = [0]
        cn_done = [0]
        stg = [0]

        def chunk_specs(h):
            sp = [(128, c * 128, c, False, 0) for c in range(NT)]
            sp += [(8, L + s * 8, NT + s, True, s) for s in range(NS)]
            return sp

        def CP_chunk(h, spec, cs, lane, slot):
            RB = LANE_BANK[lane]
            def KK(name, *rest):
                return ((("ho", slot) if name in HO_NAMES else ("lw", lane)), name) + rest
            C, t0, col, smp, s = spec
            hs_ = h % 2
            qT, kT, vT = cvo[hs_][0], cvo[hs_][1], cvo[hs_][2]
            cvk = [("cvo", hs_, j, bi) for j in range(3) for bi in range(5)]
            hk = lambda n: ("hs", hs_, "s" if smp else "p", n)
            ksl = kT[:, t0:t0 + C]
            def f1(e):
                e.transpose(PSB(4)[0:C, 0:128], ksl, ident_b)
                return e.transpose(PSB(4)[0:C, 128:256], vT[:, t0:t0 + C], ident_b)
            S.op("pe", f1, r=cvk + ["cb"], w=[("ps", 4)])
            S.op("act", lambda e: e.activation(out=cs["ktok"][0:C], in_=PSB(4)[0:C, 0:128], func=AF.Copy), r=[("ps", 4)], w=[KK("ktok")])
            bcol = (beta_t[0:C, 128 + s * 8 + h:128 + s * 8 + h + 1] if smp else beta_t[:, col * 8 + h:col * 8 + h + 1])
            S.op("dve", lambda e: e.tensor_scalar(out=cs["vb"][0:C], in0=PSB(4)[0:C, 128:256], scalar1=bcol, scalar2=None, op0=ALU.mult),
                 r=[("ps", 4), "beta_s" if smp else "beta_p"], w=[KK("vb")])
            def f2(e):
                e.matmul(PS(RB)[0:C, 0:C], ksl, qT[:, t0:t0 + C], start=True, stop=True)
                e.matmul(PS(RB)[0:C, 128:128 + C], ksl, ksl, start=True, stop=True)
                if smp:
                    e.matmul(PS(RB)[0:C, 256:256 + C], ident_f[0:4, s:s + 1].broadcast_to([4, C]), rowsTs[hs_][:, 0:8], start=True, stop=True)
                    return e.matmul(PS(RB)[0:C, 384:384 + C], ident_f[0:4, s:s + 1].broadcast_to([4, C]), rowsTs[hs_][:, 8:16], start=True, stop=True)
                return e.matmul(PS(RB)[:, 256:512], ident_f[0:16, col:col + 1].broadcast_to([16, 128]), rowsT[hs_], start=True, stop=True)
            S.op("pe", f2, r=cvk + ["cf", ("rowsTs" if smp else "rowsT", hs_)], w=[("ps", RB)])
            t3 = cs["t"][0:C, :].rearrange("p (a b) -> p a b", a=2)[:, :, 0:C]
            D3 = cs["D"][0:C, :].rearrange("p (a b) -> p a b", a=2)[:, :, 0:C]
            N3 = cs["NTQK"][0:C, 0:256].rearrange("p (a b) -> p a b", a=2)[:, :, 0:C]
            m3 = mask2[0:C, :].rearrange("p (a b) -> p a b", a=2)[:, :, 0:C]
            E3 = PS(RB)[0:C, 256:512].rearrange("p (a b) -> p a b", a=2)[:, :, 0:C]
            raw3 = PS(RB)[0:C, 0:256].rearrange("p (a b) -> p a b", a=2)[:, :, 0:C]
            yield
            S.op("dve", lambda e: e.tensor_tensor(out=t3, in0=E3, in1=m3, op=ALU.add), r=[("ps", RB), "cf"], w=[KK("t")])
            yield
            S.op("act", lambda e: e.activation(out=D3, in_=t3, func=AF.Exp, bias=biasj[hs_][0:C, col:col + 1], scale=1.0),
                 r=[KK("t"), hk("biasj")], w=[KK("t")])
            yield
            S.op("dve", lambda e: e.tensor_tensor(out=N3, in0=raw3, in1=D3, op=ALU.mult), r=[("ps", RB), KK("t")], w=[KK("NTQK")])
            yield
            Bm = cs["NTQK"][0:C, 128:128 + C]
            S.op("pe", lambda e: e.transpose(PSB(4)[0:C, 256:256 + C], Bm, ident_b[0:C, 0:C]), r=[KK("NTQK"), "cb"], w=[("ps", 4)])
            if C == 128:
                W = cs["W"]
                wk = lambda i: KK("W", i)
                S.op("act", lambda e: e.activation(out=cs["NTQK"][:, 256:384], in_=PSB(4)[:, 256:384], func=AF.Copy), r=[("ps", 4)], w=[KK("NTQK")])
                BNf = cs["NTQK"][:, 128:384]
                v2 = lambda ap: ap.rearrange("p (a b) -> p a b", a=2)
                bd2 = bd_b.unsqueeze(1).broadcast_to([128, 2, 128])
                id2 = ident_b.unsqueeze(1).broadcast_to([128, 2, 128])
                W0bn = W[0].rearrange("p (a b) -> p a b", a=4)[:, 1::2, :]
                W1qt = W[1].rearrange("p (a b) -> p a b", a=4)[:, 0::2, :]
                S.op("dve", lambda e: e.tensor_tensor(out=W0bn, in0=v2(BNf), in1=bd2, op=ALU.mult), r=[KK("NTQK"), "cb"], w=[wk(0)])
                S.op("dve", lambda e: e.tensor_tensor(out=W1qt, in0=id2, in1=W0bn, op=ALU.subtract), r=[wk(0), "cb"], w=[wk(1)])
                for lv in range(3):
                    S.op("dve", lambda e, lv=lv: e.tensor_tensor(out=cs["M"][lv], in0=BNf, in1=ml_b[lv], op=ALU.mult),
                         r=[KK("NTQK"), "cb"], w=[KK("M", lv)])
                yield
                cur = 0
                for k in range(4):
                    nxt = 1 - cur
                    last = (k == 3)
                    Wc = W[cur]
                    Bk = Wc[:, 128:256]
                    Nk = Wc[:, 384:512]
                    src = Wc if k > 0 else None
                    def fr(e, k=k, Wc=Wc, Bk=Bk, Nk=Nk, last=last):
                        if k == 0:
                            e.matmul(PS(RB)[:, 128:256], Nk, Bk, start=True, stop=True)
                            return e.matmul(PS(RB)[:, 384:512], Bk, Nk, start=True, stop=True)
                        if last:
                            e.matmul(PS(RB)[:, 0:128], Nk, Wc[:, 0:128], start=True, stop=True)
                            return e.matmul(PS(RB)[:, 256:384], Bk, Wc[:, 256:384], start=True, stop=True)
                        e.matmul(PS(RB)[:, 0:256], Nk, Wc[:, 0:256], start=True, stop=True)
                        return e.matmul(PS(RB)[:, 256:512], Bk, Wc[:, 256:512], start=True, stop=True)
                    S.op("pe", fr, r=[wk(cur)], w=[("ps", RB)])
                    P4 = PS(RB).rearrange("p (a b) -> p a b", a=4)
                    Wn4 = W[nxt].rearrange("p (a b) -> p a b", a=4)
                    Wc4 = Wc.rearrange("p (a b) -> p a b", a=4)
                    if k == 0:
                        S.op("act", lambda e, P4=P4, Wn4=Wn4: e.activation(out=Wn4[:, 1::2, :], in_=P4[:, 1::2, :], func=AF.Copy),
                             r=[("ps", RB)], w=[wk(nxt)])
                    else:
                        if not last:
                            S.op("act", lambda e, P4=P4, Wn4=Wn4: e.activation(out=Wn4[:, 1::2, :], in_=P4[:, 1::2, :], func=AF.Copy),
                                 r=[("ps", RB)], w=[wk(nxt)])
                        S.op("dve", lambda e, P4=P4, Wn4=Wn4, Wc4=Wc4: e.tensor_tensor(out=Wn4[:, 0::2, :], in0=P4[:, 0::2, :], in1=Wc4[:, 0::2, :], op=ALU.add),
                             r=[("ps", RB), wk(cur)], w=[wk(nxt)])
                    cur = nxt
                    yield
                Wc = W[cur]
                Qv = Wc[:, 0:128]
                Tv = Wc[:, 256:384]
                XY = cs["XY"]
                for lv in range(3):
                    lastl = (lv == 2)
                    Bm_l = cs["M"][lv][:, 0:128]
                    Nm_l = cs["M"][lv][:, 128:256]
                    def f1_(e, Bm_l=Bm_l, Nm_l=Nm_l, lastl=lastl):
                        r_ = e.matmul(PS(RB)[:, 0:128], Nm_l, Qv, start=True, stop=True)
                        if not lastl:
                            r_ = e.matmul(PS(RB)[:, 128:256], Bm_l, Tv, start=True, stop=True)
                        return r_
                    S.op("pe", f1_, r=[wk(cur), KK("M", lv)], w=[("ps", RB)])
                    nx = 128 if lastl else 256
                    S.op("act", lambda e, nx=nx: e.activation(out=XY[:, 0:nx], in_=PS(RB)[:, 0:nx], func=AF.Copy), r=[("ps", RB)], w=[KK("XY")])
                    yield
                    def f2_(e, lastl=lastl):
                        r_ = e.matmul(PS(RB)[:, 256:384], Tv, XY[:, 0:128], start=True, stop=True)
                        if not lastl:
                            r_ = e.matmul(PS(RB)[:, 384:512], Qv, XY[:, 128:256], start=True, stop=True)
                        return r_
                    S.op("pe", f2_, r=[wk(cur), KK("XY")], w=[("ps", RB)])
                    if lastl:
                        S.op("dve", lambda e: e.tensor_tensor(out=cs["QT"], in0=Qv, in1=PS(RB)[:, 256:384], op=ALU.subtract),
                             r=[("ps", RB), wk(cur)], w=[KK("QT")])
                    else:
                        Wq = Wc.rearrange("p (a b) -> p a b", a=4)[:, 0::2, :]
                        S.op("dve", lambda e, Wq=Wq: e.tensor_tensor(out=Wq, in0=Wq, in1=PS(RB)[:, 256:512].rearrange("p (a b) -> p a b", a=2), op=ALU.subtract),
                             r=[("ps", RB), wk(cur)], w=[wk(cur)])
                    yield
                QT = cs["QT"]
                qkey = KK("QT")
            else:
                BN = cs["BN"]
                Q = cs["Q"]
                S.op("act", lambda e: e.activation(out=BN[0][0:C, 128:128 + C], in_=PSB(4)[0:C, 256:256 + C], func=AF.Copy), r=[("ps", 4)], w=[KK("W", 0)])
                S.op("pool", lambda e: e.tensor_copy(out=BN[0][0:C, 0:C], in_=Bm), r=[KK("NTQK")], w=[KK("W", 0)])
                S.op("pool", lambda e: e.tensor_tensor(out=Q[0][0:C, 0:C], in0=ident_b[0:C, 0:C], in1=Bm, op=ALU.subtract),
                     r=[KK("NTQK"), "cb"], w=[KK("W", 0)])
                nr = 7 if C == 128 else 3
                cur = 0
                for k in range(nr):
                    nxt = 1 - cur
                    Bk = BN[cur][0:C, 0:C]
                    Nk = BN[cur][0:C, 128:128 + C]
                    last = (k == nr - 1)
                    def fr(e, k=k, Bk=Bk, Nk=Nk, cur=cur, last=last):
                        r_ = None
                        if k > 0:
                            r_ = e.matmul(PS(RB)[0:C, 0:C], Nk, Q[cur][0:C, 0:C], start=True, stop=True)
                        if not last:
                            r_ = e.matmul(PS(RB)[0:C, 128:128 + C], Nk, Bk, start=True, stop=True)
                            r_ = e.matmul(PS(RB)[0:C, 256:256 + C], Bk, Nk, start=True, stop=True)
                        return r_
                    S.op("pe", fr, r=[KK("W", cur)], w=[("ps", RB)])
                    if not last:
                        S.op("act", lambda e, nxt=nxt: e.activation(
                            out=BN[nxt][0:C, :].rearrange("p (a b) -> p a b", a=2)[:, :, 0:C],
                            in_=PS(RB)[0:C, 128:384].rearrange("p (a b) -> p a b", a=2)[:, :, 0:C], func=AF.Copy),
                             r=[("ps", RB)], w=[KK("W", nxt)])
                    if k > 0:
                        S.op("dve", lambda e, cur=cur, nxt=nxt: e.tensor_tensor(out=Q[nxt][0:C, 0:C], in0=PS(RB)[0:C, 0:C], in1=Q[cur][0:C, 0:C], op=ALU.add),
                             r=[("ps", RB), KK("W", cur)], w=[KK("W", nxt)])
                    else:
                        S.op("pool", lambda e, cur=cur, nxt=nxt: e.tensor_copy(out=Q[nxt][0:C, 0:C], in_=Q[cur][0:C, 0:C]),
                             r=[KK("W", cur)], w=[KK("W", nxt)])
                    cur = nxt
                    yield
                S.op("pool", lambda e, cur=cur: e.tensor_copy(out=cs["QT"][0:C, 0:C], in_=Q[cur][0:C, 0:C]), r=[KK("W", cur)], w=[KK("QT")])
                QT = cs["QT"][0:C, 0:C]
                qkey = KK("QT")
            S.op("dve", lambda e: e.tensor_scalar(out=cs["QTs"][0:C, 0:C], in0=QT, scalar1=kbg_s[hs_][0:C, col:col + 1], scalar2=None, op0=ALU.mult),
                 r=[qkey, hk("kbg_s")], w=[KK("QTs")])
            yield
            S.op("pe", lambda e: e.matmul(PS(4)[:, 256:256 + C], cs["ktok"][0:C, :], cs["QTs"][0:C, 0:C], start=True, stop=True),
                 r=[KK("ktok"), KK("QTs")], w=[("ps", 4)])
            S.op("act", lambda e: e.activation(out=cs["kcTn"][:, 0:C], in_=PS(4)[:, 256:256 + C], func=AF.Copy, scale=-1.0),
                 r=[("ps", 4)], w=[KK("kcTn")])
            yield

        def CR_chunk(h, spec, cs, slot, Sfl, Sbf, skey, n):
            def KK(name, *rest):
                return (("ho", slot), name) + rest
            C, t0, col, smp, s = spec
            hs_ = h % 2
            qT = cvo[hs_][0]
            zT = cvo[hs_][3]
            cvk = [("cvo", hs_, j, bi) for j in (0, 3) for bi in range(5)]
            hk = lambda nm: ("hs", hs_, "s" if smp else "p", nm)
            QT = cs["QT"][0:C, 0:C]
            qk = KK("QT")
            x = n % 2
            def fu(e):
                e.matmul(PS(7)[0:C, 0:128], QT, cs["vb"][0:C, :], start=True, stop=False)
                return e.matmul(PS(7)[0:C, 0:128], cs["kcTn"][:, 0:C], Sbf, start=False, stop=True)
            S.op("pe", fu, r=[qk, KK("vb"), KK("kcTn"), skey + ("b",)], w=[("ps", 7)])
            S.op("act", lambda e: e.activation(out=u_t[x][0:C], in_=PS(7)[0:C, 0:128], func=AF.Copy), r=[("ps", 7)], w=[("u_t", x)])
            S.op("dve", lambda e: e.tensor_scalar(out=us_t[x][0:C], in0=PS(7)[0:C, 0:128], scalar1=kdec_s[hs_][0:C, col:col + 1], scalar2=None, op0=ALU.mult),
                 r=[("ps", 7), hk("kdec_s")], w=[("us_t", x)])
            yield
            def fo(e):
                e.matmul(PS(7)[0:C, 128:256], qT[:, t0:t0 + C], Sbf, start=True, stop=True)
                e.matmul(PS(7)[0:C, 256:384], cs["NTQK"][0:C, 0:C], u_t[x][0:C], start=True, stop=True)
                return e.matmul(PS(7)[:, 384:512], cs["ktok"][0:C, :], us_t[x][0:C], start=True, stop=True)
            S.op("pe", fo, r=cvk + [skey + ("b",), KK("NTQK"), ("u_t", x), KK("ktok"), ("us_t", x)], w=[("ps", 7)])
            gcolumn = (gtot_t[:, 128 + s * 8 + h:128 + s * 8 + h + 1] if smp else gtot_t[:, col * 8 + h:col * 8 + h + 1])
            S.op("dve", lambda e: e.scalar_tensor_tensor(out=Sfl, in0=Sfl, scalar=gcolumn, in1=PS(7)[:, 384:512], op0=ALU.mult, op1=ALU.add),
                 r=[("ps", 7), skey + ("f",), "gtot_s" if smp else "gtot_p"], w=[skey + ("f",)])
            while n - cn_done[0] >= 2:
                yield WAIT
            S.op("act", lambda e: e.activation(out=t2_t[x][0:C], in_=PS(7)[0:C, 256:384], func=AF.Copy), r=[("ps", 7)], w=[("t2", x)])
            yield
            S.op("act", lambda e: e.activation(out=Sbf, in_=Sfl, func=AF.Copy), r=[skey + ("f",)], w=[skey + ("b",)])
            S.op("dve", lambda e: e.scalar_tensor_tensor(out=o_t[x][0:C], in0=PS(7)[0:C, 128:256], scalar=qdec_s[hs_][0:C, col:col + 1],
                                                         in1=t2_t[x][0:C], op0=ALU.mult, op1=ALU.add),
                 r=[("ps", 7), ("t2", x), hk("qdec_s")], w=[("o_t", x)])
            yield
        def CN_chunk(h, spec, n):
            C, t0, col, smp, s = spec
            hs_ = h % 2
            zT = cvo[hs_][3]
            cvk = [("cvo", hs_, j, bi) for j in (0, 3) for bi in range(5)]
            x = n % 2
            S.op("act", lambda e: e.activation(out=junk2[0:C], in_=o_t[x][0:C], func=AF.Square, accum_out=oss[0:C, x:x + 1]),
                 r=[("o_t", x)], w=["junk2", ("oss", x)])
            yield
            S.op("pool", lambda e: e.tensor_scalar(out=oss[0:C, x:x + 1], in0=oss[0:C, x:x + 1], scalar1=1.0 / 128, scalar2=EPS, op0=ALU.mult, op1=ALU.add),
                 r=[("oss", x)], w=[("oss", x)])
            S.op("pool", lambda e: e.tensor_tensor(out=oss[0:C, x:x + 1], in0=oss[0:C, x:x + 1], in1=nhalf_c[0:C], op=ALU.pow),
                 r=[("oss", x), "cf"], w=[("oss", x)])
            yield
            S.op("act", lambda e: e.activation(out=on_t[x][0:C], in_=o_t[x][0:C], func=AF.Identity, scale=oss[0:C, x:x + 1]),
                 r=[("o_t", x), ("oss", x)], w=[("on_t", x)])
            yield
            S.op("pe", lambda e: e.transpose(PSB(5)[:, 0:C], on_t[x][0:C, :], ident_b[0:C, 0:C]), r=[("on_t", x), "cb"], w=[("ps", 5)])
            S.op("dve", lambda e: e.scalar_tensor_tensor(out=ogT[:, h, t0:t0 + C], in0=PSB(5)[:, 0:C], scalar=outg_h,
                                                         in1=zT[:, t0:t0 + C], op0=ALU.mult, op1=ALU.mult),
                 r=[("ps", 5), "outg_h"] + cvk, w=[("ogT", h, t0)])
            yield

        def CN_head(h):
            specs = chunk_specs(h)
            for n, spec in enumerate(specs):
                gi = cidx[0] + n
                while cr_o[0] <= gi:
                    yield WAIT
                for _ in CN_chunk(h, spec, gi):
                    yield 0.5
                cn_done[0] = gi + 1
                yield 0.5

        def CP_lane(h, lane):
            specs = chunk_specs(h)
            mine = list(range(lane, len(specs), NLANE))
            for _ in range(lane * STAG):
                yield 0.0
            for k_, n in enumerate(mine):
                spec = specs[n]
                gi = cidx[0] + n
                while gi - cr_done[0] >= NHO:
                    yield WAIT
                slot = gi % NHO
                cs = dict(lanes[lane])
                cs.update(hos[slot])
                for yi, _ in enumerate(CP_chunk(h, spec, cs, lane, slot)):
                    yield (k_ + min(0.95, (yi + 1) / 14.0)) / len(mine)
                cp_done.add(gi)
                yield (k_ + 1.0) / len(mine)

        def CR_head(h):
            specs = chunk_specs(h)
            yield 0.0
            for n, spec in enumerate(specs):
                C, t0, col, smp, s = spec
                gi = cidx[0] + n
                while gi not in cp_done:
                    yield WAIT
                slot = gi % NHO
                cs = hos[slot]
                if n == 0 or smp:
                    si = sidx[0] % 2
                    sidx[0] += 1
                    Sfl, Sbf, skey = Sf[si], Sb[si], ("S", si)
                    if smp:
                        S.dma("sp", Sfl, I["sdelta"][s, h], w=[skey + ("f",)])
                        S.op("act", lambda e, Sbf=Sbf, Sfl=Sfl: e.activation(out=Sbf, in_=Sfl, func=AF.Copy), r=[skey + ("f",)], w=[skey + ("b",)])
                    else:
                        S.op("pool", lambda e, Sfl=Sfl: e.memset(Sfl, 0.0), w=[skey + ("f",)])
                        S.op("pool", lambda e, Sbf=Sbf: e.memset(Sbf, 0.0), w=[skey + ("b",)])
                for v_ in CR_chunk(h, spec, cs, slot, Sfl, Sbf, skey, gi):
                    yield (WAIT if v_ is WAIT else 0.5)
                cr_o[0] = gi + 1
                if n == NT - 1 or smp:
                    dst = O["ds"][s, h] if smp else O["dp"][h]
                    S.dma("sp", dst, Sfl, r=[skey + ("f",)])
                cr_done[0] = gi + 1
                yield (n + 1.0) / len(specs) - 0.08

        S.barrier()
        run_tasks([P_head(0)])
        if stop == "P0":
            S.finish(); print("STOP P0", S.ninst); return nc
        for h in range(nheads):
            gens = [CP_lane(h, ln) for ln in range(NLANE)] + [CR_head(h), CN_head(h)]
            if h + 1 < nheads:
                gens.append(P_head(h + 1))
            run_tasks(gens)
            cidx[0] += NT + NS
            if stop == ("H", h):
                S.finish(); print("STOP H", h, S.ninst); return nc
        S.barrier()

        if dbg:
            S.dma("sp", O["dbg_og"], ogT, r=[("ogT", h, t * 128) for h in range(8) for t in range(16)] + [("ogT", h, L + s * 8) for h in range(8) for s in range(NS)])
            S.dma("sp", O["dbg_hT"], hT, r=HT_KEYS)
            S.barrier()
        R1.reset(); R3.reset(); R4.reset()
        x1 = R1.alloc([16, D], F32)
        wo = R3.alloc([8, D], BF16)
        xst2 = [R3.alloc([D], F32) for _ in range(2)]
        xss = R4.alloc([D], F32, parts=32)
        ysb0 = R4.alloc([D], F32, parts=32)

        def wout_phase(l, wsrc, x_in, x_out_fn):
            S.dma("pool", wo, wsrc.rearrange("(c p) n -> p c n", p=128), w=["wo"])
            wos = wo
            if l == 0:
                S.dma("sp", xss, I["xs"], w=["xss"])
            for hb in range(2):
                def f(e, hb=hb):
                    last = None
                    for ec in range(8):
                        last = e.matmul(PS(hb)[0:32, :], ogT[:, ec, L:TT], wo[:, ec, hb * 512:(hb + 1) * 512], start=(ec == 0), stop=(ec == 7))
                    return last
                S.op("pe", f, r=["wo"] + [("ogT", h, L + s * 8) for h in range(8) for s in range(NS)], w=[("ps", hb)])
                ysb = xss if l == 1 else ysb0
                S.op("dve", lambda e, hb=hb, ysb=ysb: e.tensor_tensor(out=ysb[:, hb * 512:(hb + 1) * 512], in0=PS(hb)[0:32, :], in1=gate_s[:, hb * 512:(hb + 1) * 512], op=ALU.mult),
                     r=[("ps", hb), ("gate_s", l)], w=[("ysb", hb)])
                src = xss if l == 0 else x1s
                S.op("dve", lambda e, hb=hb, src=src, ysb=ysb: e.tensor_tensor(out=x1s[:, hb * 512:(hb + 1) * 512], in0=ysb[:, hb * 512:(hb + 1) * 512],
                                                                       in1=src[:, hb * 512:(hb + 1) * 512], op=ALU.add),
                     r=[("ysb", hb), "xss", ("x1s", hb)], w=[("x1s", hb)])
            for ec in range(8):
                S.op("pool", lambda e, ec=ec: e.tensor_tensor(out=wo[:, ec, :], in0=wo[:, ec, :], in1=gate_p, op=ALU.mult),
                     r=["wo", ("gate_p", l)], w=["wo"])
            for t in range(16):
                xb, xkey = x_in(t)
                for hb in range(2):
                    bk = (2 * t + hb) % 4
                    def f(e, hb=hb, bk=bk, t=t):
                        last = None
                        for ec in range(8):
                            last = e.matmul(PS(bk), ogT[:, ec, t * 128:(t + 1) * 128], wo[:, ec, hb * 512:(hb + 1) * 512], start=(ec == 0), stop=(ec == 7))
                        return last
                    S.op("pe", f, r=["wo"] + [("ogT", h, t * 128) for h in range(8)], w=[("ps", bk)])
                    S.op("dve", lambda e, hb=hb, bk=bk, t=t, xb=xb: e.tensor_tensor(out=x_out_fn(t)[:, hb * 512:(hb + 1) * 512], in0=PS(bk),
                                                                                     in1=xb[:, hb * 512:(hb + 1) * 512], op=ALU.add),
                         r=[("ps", bk)] + (xkey if isinstance(xkey, list) else [xkey]), w=[("x1", t, hb)])

        def x_in0(t):
            buf = xst2[t % 2]
            key = ("xst2", t % 2)
            S.dma("sp", buf, I["xp"][t * 128:(t + 1) * 128, :], w=[key])
            return buf, key

        wout_phase(0, I["awout"], x_in0, lambda t: x1[:, t, :])
        if dbg:
            for t in range(16):
                S.dma("sp", O["dbg_x1p"][t * 128:(t + 1) * 128, :], x1[:, t, :], r=[("x1", t, 0), ("x1", t, 1)])
            S.dma("sp", O["dbg_x1s"], x1s, r=[("x1s", 0), ("x1s", 1)])
        if not do_l1:
            S.finish()
            print("instructions:", S.ninst, "sems:", S.nsem)
            return nc
        ada_phase(1, gate_p, gate_s, R4, parts=("col",))
        S.barrier()
        R3.reset(); R4.reset()
        hT1 = R3.alloc([8, TT], BF16)
        RE = Region(A, NB - 12288, NB - 4096)
        qs_all = RE.alloc([8, 3, 32], BF16)
        zs_all = RE.alloc([8, 32], BF16)
        onew = RE.alloc([8, 32], F32)
        lnew = RE.alloc([8, 32], F32)
        ptn = RE.alloc([32], BF16, parts=32)

        def x_src1(t):
            if t < 16:
                return x1[:, t, :], [("x1", t, 0), ("x1", t, 1)]
            return x1s, [("x1s", 0), ("x1s", 1)]

        norm_phase(1, hT1, x_src1, R4)
        S.barrier()
        R4.reset()
        GRP = ((128, 1), (512, 4), (2048, 16))
        SCALE = 128.0 ** -0.5
        Wg = [R4.alloc([8, 384], BF16) for _ in range(2)]
        Wz = RC.alloc([8, 128], BF16)
        QK_ = [R4.alloc([2, TT], BF16) for _ in range(3)]
        QT_ = [qk_[:, 0, :] for qk_ in QK_]
        KT_ = [qk_[:, 1, :] for qk_ in QK_]
        qkb = [R4.alloc([256], BF16) for _ in range(2)]
        Vt_ = [R4.alloc([17, 128], BF16) for _ in range(3)]
        zTh = R4.alloc([1024], BF16)
        PTb = [R4.alloc([512], BF16) for _ in range(2)]
        stg1_ = R4.alloc([256], F32)
        stg_ = [stg1_, stg1_]
        fin1_ = R4.alloc([512], F32)
        fin_ = [fin1_, fin1_]
        print("L1 R4 used", R4.cur - R4.lo, "of", R4.hi - R4.lo)
        KVO = [O["kv0p"], O["kv1p"], O["kv2p"]]
        wcount = [0]
        stc = [0]
        ptc = [0]
        fnc = [0]

        def tok_ap(g, ti):
            d = GRP[g][1]
            if d == 1:
                return lambda kc: hT1[:, kc, ti * 128:(ti + 1) * 128]
            if d == 4:
                r, nb = ti // 4, ti % 4
                return lambda kc: hT1[:, kc, 0:L].rearrange("p (j r) -> p r j", r=4)[:, r, nb * 128:(nb + 1) * 128]
            return lambda kc: hT1[:, kc, 0:L].rearrange("p (j r) -> p r j", r=16)[:, ti, :]

        def blk_ap(g, blk):
            d = GRP[g][1]
            if d == 1:
                return lambda kc: hT1[:, kc, blk * 512:(blk + 1) * 512]
            if d == 4:
                return lambda kc: hT1[:, kc, 0:L].rearrange("p (j r) -> p r j", r=4)[:, blk, :]
            return lambda kc: hT1[:, kc, 0:L].rearrange("p (j r) -> p r j", r=16)[:, 4 * blk:4 * blk + 4, :]

        def kept_rows(g, ti):
            w_, d = GRP[g]
            if d == 1:
                return (0, 1) if ti == 15 else None
            if d == 4:
                r, nb = ti // 4, ti % 4
                return (r, 4) if nb == 3 else None
            return (ti, 16)

        def L1_proj(h, g):
            d = GRP[g][1]
            wb = Wg[wcount[0] % 2]
            wkey = ("Wg", wcount[0] % 2)
            wcount[0] += 1
            for t_ in range(3):
                col = t_ * 3072 + g * 1024 + h * 128
                S.dma("pool", wb[:, :, t_ * 128:(t_ + 1) * 128], I["bwin"][:, col:col + 128].rearrange("(c p) n -> p c n", p=128),
                      w=[wkey + (t_,)])
            wk = [wkey + (t_,) for t_ in range(3)]
            pend = []

            def emit_tr(ti, p, qb, qbk):
                tb_ = 2 + (ti % 2)
                def ftp(e):
                    e.transpose(PSB(tb_)[:, 0:p], qb[0:p, 0:128], ident_b[0:p, 0:p])
                    return e.transpose(PSB(tb_)[:, 128:128 + p], qb[0:p, 128:256], ident_b[0:p, 0:p])
                S.op("pe", ftp, r=[qbk, "cb"], w=[("ps", tb_)])
                t0 = ti * 128 if ti < 16 else L
                dst = QK_[g][:, :, t0:t0 + p]
                src = PSB(tb_)[:, 0:256].rearrange("p (a b) -> p a b", a=2)[:, :, 0:p]
                if ti % 2 == 1:
                    S.op("act", lambda e: e.activation(out=dst, in_=src, func=AF.Copy), r=[("ps", tb_)], w=[("qkT", g, ti)])
                else:
                    S.op("dve", lambda e: e.tensor_copy(out=dst, in_=src), r=[("ps", tb_)], w=[("qkT", g, ti)])

            for ti in range(17):
                bk = ti % 2
                p = 128 if ti < 16 else 32
                ap = tok_ap(g, ti) if ti < 16 else (lambda kc: hT1[:, kc, L:TT])
                def f(e, ap=ap, bk=bk, p=p):
                    last = None
                    for kc in range(8):
                        last = e.matmul(PS(bk)[0:p, 0:384], ap(kc), wb[:, kc, 0:384], start=(kc == 0), stop=(kc == 7))
                    return last
                S.op("pe", f, r=wk + HT_KEYS, w=[("ps", bk)])
                qb = qkb[ti % 2]
                qbk = ("qkb", ti % 2)
                S.op("act", lambda e, bk=bk, p=p, qb=qb: e.activation(out=qb[0:p], in_=PS(bk)[0:p, 0:256], func=AF.Copy), r=[("ps", bk)], w=[qbk])
                S.op("dve", lambda e, bk=bk, p=p, ti=ti: e.tensor_copy(out=Vt_[g][0:p, ti, :], in_=PS(bk)[0:p, 256:384]),
                     r=[("ps", bk)], w=[("vt", g, ti)])
                kr = kept_rows(g, ti) if ti < 16 else None
                if kr is not None or ti == 16:
                    sb = stg_[stc[0] % 2]
                    skey = ("stg", 0)
                    stc[0] += 1
                    S.op("dve", lambda e, sb=sb, bk=bk, p=p: e.tensor_copy(out=sb[0:p], in_=PS(bk)[0:p, 128:384]), r=[("ps", bk)], w=[skey])
                    if ti < 16:
                        r0, st = kr
                        dst = KVO[g].rearrange("(q s) k hh e -> s q k hh e", s=st)[r0, :, :, h, :]
                        S.dma("sp", dst, sb.rearrange("p (k e) -> p k e", k=2), r=[skey])
                    else:
                        dst = O["kv%ds" % g].rearrange("s l k hh e -> (s l) k hh e")[:, :, h, :]
                        S.dma("sp", dst, sb[0:32].rearrange("p (k e) -> p k e", k=2), r=[skey])
                pend.append((ti, p, qb, qbk))
                if len(pend) > 1:
                    emit_tr(*pend.pop(0))
                yield
            while pend:
                emit_tr(*pend.pop(0))
            yield

        def L1_units(g, hf):
            d = GRP[g][1]
            units = []
            if d == 1:
                for nb in range(8 * hf, 8 * hf + 8):
                    col0 = 128 * (nb - 8 * hf)
                    outs = [(col0 // 512, (lambda B, c=col0 % 512: B[:, c:c + 128]), 0, 128)]
                    q = QT_[g][:, nb * 128:(nb + 1) * 128]
                    for pv, kt in ((True, nb - 1), (False, nb)):
                        if kt < 0:
                            continue
                        units.append((KT_[g][:, kt * 128:(kt + 1) * 128], Vt_[g][:, kt, :], 128, mT_prev if pv else mT_cur, q, 128, outs, ("vt", g, kt)))
            elif d == 4:
                for r in range(4):
                    for nb in range(2 * hf, 2 * hf + 2):
                        outs = [(nb - 2 * hf, (lambda B, r=r: B.rearrange("p (q r) -> p r q", r=4)[:, r, :]), 0, 128)]
                        q = QT_[g][:, r * 512 + nb * 128:r * 512 + (nb + 1) * 128]
                        for pv, kb in ((True, nb - 1), (False, nb)):
                            if kb < 0:
                                continue
                            kt = r * 4 + kb
                            units.append((KT_[g][:, kt * 128:(kt + 1) * 128], Vt_[g][:, kt, :], 128, mT_prev if pv else mT_cur, q, 128, outs, ("vt", g, kt)))
            else:
                for r in range(16):
                    outs = [(m, (lambda B, r=r: B.rearrange("p (j r) -> p r j", r=16)[:, r, :]), 32 * m, 32) for m in range(2)]
                    q = QT_[g][:, r * 128 + 64 * hf:r * 128 + 64 * hf + 64]
                    if hf == 0:
                        units.append((KT_[g][:, r * 128:r * 128 + 64], Vt_[g][0:64, r, :], 64, mT_cur[0:64, 0:64], q, 64, outs, ("vt", g, r)))
                    else:
                        units.append((KT_[g][:, r * 128:(r + 1) * 128], Vt_[g][:, r, :], 128, mT_cur[:, 64:128], q, 64, outs, ("vt", g, r)))
            return units

        OB = (4, 5)
        LB = (6, 7)

        def L1_pass(h, hf):
            for b_ in OB + LB:
                S.op("dve", lambda e, b_=b_: e.memset(PS(b_), 0.0), w=[("ps", b_)])
            for m in range(2):
                def f(e, m=m):
                    last = None
                    for kc in range(8):
                        last = e.matmul(PS(m), Wz[:, kc, :], hT1[:, kc, 1024 * hf + 512 * m:1024 * hf + 512 * (m + 1)], start=(kc == 0), stop=(kc == 7))
                    return last
                S.op("pe", f, r=["Wz"] + HT_KEYS, w=[("ps", m)])
                S.op("act", lambda e, m=m: e.activation(out=zTh[:, 512 * m:512 * (m + 1)], in_=PS(m), func=AF.Silu), r=[("ps", m)], w=[("zTh", m)])
            yield
            allu = []
            pend_pv = []
            for g in range(3):
                allu += [(g, u) for u in L1_units(g, hf)]
            for i0 in range(0, len(allu), 4):
                grp = allu[i0:i0 + 4]
                sb_ = 2 + (ptc[0] % 2)
                pt = PTb[ptc[0] % 2]
                pkey = ("PT", ptc[0] % 2)
                ptc[0] += 1
                rk = []
                def fs(e, grp=grp, sb_=sb_):
                    last = None
                    for ui, (g, (kT, v, nk, mk, q, nq, outs, vkey)) in enumerate(grp):
                        o_ = PS(sb_)[0:nk, ui * 128:ui * 128 + nq]
                        e.matmul(o_, kT, q, start=True, stop=False)
                        last = e.matmul(o_, ident_b[0:nk, 0:nk], mk, start=False, stop=True)
                    return last
                for (g, u) in grp:
                    rk += [("qkT", g, ti_) for ti_ in range(16)]
                S.op("pe", fs, r=list(set(rk)) + ["cb"], w=[("ps", sb_)])
                S.op("act", lambda e, sb_=sb_, pt=pt: e.activation(out=pt, in_=PS(sb_), func=AF.Exp, scale=SCALE), r=[("ps", sb_)], w=[pkey])
                def fp(e, grp=grp, pt=pt):
                    last = None
                    for ui, (g, (kT, v, nk, mk, q, nq, outs, vkey)) in enumerate(grp):
                        for (bi_, ofn, c0, n_) in outs:
                            rhs = pt[0:nk, ui * 128 + c0:ui * 128 + c0 + n_]
                            e.matmul(ofn(PS(OB[bi_])), v, rhs, start=False, stop=False, skip_group_check=True)
                            last = e.matmul(ofn(PS(LB[bi_])), ones_b[0:nk, :], rhs, start=False, stop=False, skip_group_check=True)
                    return last
                pend_pv.append((fp, [pkey, "cb"] + [u[7] for (_, u) in grp]))
                if len(pend_pv) > 1:
                    fq, rq_ = pend_pv.pop(0)
                    S.op("pe", fq, r=rq_, w=[("ps", b_) for b_ in OB + LB])
                yield
            while pend_pv:
                fq, rq_ = pend_pv.pop(0)
                S.op("pe", fq, r=rq_, w=[("ps", b_) for b_ in OB + LB])
            for m in range(2):
                fb = fin_[fnc[0] % 2]
                fkey = ("fin", 0)
                fnc[0] += 1
                S.op("act", lambda e, fb=fb, m=m: e.activation(out=fb, in_=PS(LB[m]), func=AF.Ln), r=[("ps", LB[m])], w=[fkey])
                S.op("act", lambda e, fb=fb: e.activation(out=fb, in_=fb, func=AF.Exp, scale=-1.0), r=[fkey], w=[fkey])
                S.op("dve", lambda e, fb=fb, m=m: e.tensor_tensor(out=fb, in0=PS(OB[m]), in1=fb, op=ALU.mult), r=[("ps", OB[m]), fkey], w=[fkey])
                p0 = 1024 * hf + 512 * m
                S.op("dve", lambda e, fb=fb, m=m, p0=p0: e.tensor_tensor(out=ogT[:, h, p0:p0 + 512], in0=fb, in1=zTh[:, 512 * m:512 * (m + 1)], op=ALU.mult),
                     r=[fkey, ("zTh", m)], w=[("ogT", h, p0 // 128 * 128 + i * 128) for i in range(4)])
            yield

        def L1_newrows(h):
            for g in range(3):
                S.op("pool", lambda e, g=g: e.tensor_copy(out=qs_all[:, h, g, :], in_=QT_[g][:, L:TT]), r=[("qkT", g, 16)], w=[("qs", h, g)])
                def fsn(e, g=g):
                    e.matmul(PS(2)[0:32, 0:32], KT_[g][:, L:TT], QT_[g][:, L:TT], start=True, stop=False)
                    return e.matmul(PS(2)[0:32, 0:32], ident_b[0:32, 0:32], cbf[0:32, CB_MN + g * 32:CB_MN + (g + 1) * 32], start=False, stop=True)
                S.op("pe", fsn, r=[("qkT", g, 16), "cb"], w=[("ps", 2)])
                S.op("act", lambda e: e.activation(out=ptn, in_=PS(2)[0:32, 0:32], func=AF.Exp, scale=SCALE), r=[("ps", 2)], w=["ptn"])
                def fpn(e, g=g):
                    e.matmul(PS(4)[:, 0:32], Vt_[g][0:32, 16, :], ptn, start=(g == 0), stop=(g == 2))
                    return e.matmul(PS(6)[:, 0:32], ones_b[0:32, :], ptn, start=(g == 0), stop=(g == 2))
                S.op("pe", fpn, r=["ptn", ("vt", g, 16), "cb"], w=[("ps", 4), ("ps", 6)])
            S.op("act", lambda e: e.activation(out=onew[:, h, :], in_=PS(4)[:, 0:32], func=AF.Copy), r=[("ps", 4)], w=[("onew", h)])
            S.op("dve", lambda e: e.tensor_copy(out=lnew[:, h, :], in_=PS(6)[:, 0:32]), r=[("ps", 6)], w=[("lnew", h)])
            def fz(e):
                last = None
                for kc in range(8):
                    last = e.matmul(PS(0)[:, 0:32], Wz[:, kc, :], hT1[:, kc, L:TT], start=(kc == 0), stop=(kc == 7))
                return last
            S.op("pe", fz, r=["Wz"] + HT_KEYS, w=[("ps", 0)])
            S.op("act", lambda e: e.activation(out=zs_all[:, h, :], in_=PS(0)[:, 0:32], func=AF.Silu), r=[("ps", 0)], w=[("zs", h)])

        def L1_head(h):
            S.dma("pool", Wz, I["bwin"][:, 9216 + h * 128:9216 + (h + 1) * 128].rearrange("(c p) n -> p c n", p=128), w=["Wz"])
            for g in range(3):
                for _ in L1_proj(h, g):
                    yield
            L1_newrows(h)
            yield
            for hf in range(2):
                for _ in L1_pass(h, hf):
                    yield

        for h in range(nheads):
            for _ in L1_head(h):
                pass
        S.barrier()

        R4.reset()
        NPF = 3
        NCL = (1, 4, 8)
        ckb = [[R4.alloc([NCL[g], 2, 128], BF16) for g in range(3)] for _ in range(NPF)]
        kTc = [R4.alloc([13, 128], BF16) for _ in range(2)]
        pts = [R4.alloc([24], BF16) for _ in range(2)]
        fo_ = R4.alloc([32], F32)
        fl_ = R4.alloc([32], F32)
        items = [(h, s_) for h in range(nheads) for s_ in range(NS)]

        def e2_load(i):
            h, s_ = items[i]
            for g in range(3):
                d = GRP[g][1]
                src = I["kc%d" % g][s_].rearrange("(k r) kv hh e -> k r kv hh e", r=d)[:, 0:NCL[g], :, h, :]
                S.dma("pool", ckb[i % NPF][g], src, w=[("ck", i % NPF, g)])

        for i in range(min(NPF - 1, len(items))):
            e2_load(i)
        for i, (h, s_) in enumerate(items):
            if i + NPF - 1 < len(items):
                e2_load(i + NPF - 1)
            buf = ckb[i % NPF]
            ckeys = [("ck", i % NPF, g) for g in range(3)]
            kt = kTc[i % 2]
            ktk = ("kTc", i % 2)
            pt = pts[i % 2]
            ptk = ("pts", i % 2)
            sb_ = 2 + (i % 2)
            tiles = [(0, 0)] + [(1, r) for r in range(4)] + [(2, r) for r in range(8)]
            if s_ == 0:
                S.op("dve", lambda e: e.memset(PS(4)[:, 0:32], 0.0), w=[("ps", 4)])
                S.op("dve", lambda e: e.memset(PS(6)[:, 0:32], 0.0), w=[("ps", 6)])
            def ftr(e, buf=buf):
                last = None
                for ti, (g, r) in enumerate(tiles):
                    last = e.transpose(PSB(ti // 8)[:, (ti % 8) * 128:(ti % 8 + 1) * 128], buf[g][:, r, 0, :], ident_b)
                return last
            S.op("pe", ftr, r=ckeys + ["cb"], w=[("ps", 0), ("ps", 1)])
            S.op("act", lambda e, kt=kt: e.activation(out=kt[:, 0:8, :], in_=PSB(0).rearrange("p (a b) -> p a b", a=8), func=AF.Copy), r=[("ps", 0)], w=[ktk + (0,)])
            S.op("dve", lambda e, kt=kt: e.tensor_copy(out=kt[:, 8:13, :], in_=PSB(1)[:, 0:640].rearrange("p (a b) -> p a b", a=5)), r=[("ps", 1)], w=[ktk + (1,)])
            def cls(g, r):
                d = GRP[g][1]
                nq = 8 // d if d <= 8 else 1
                c0 = (0, 8 + 2 * r, 16 + r)[g]
                if d == 1:
                    q = qs_all[:, h, g, 8 * s_:8 * s_ + 8]
                    mk = cbf[:, CB_MK:CB_MK + 8]
                    oc = lambda B: B[:, 8 * s_:8 * s_ + 8]
                elif d == 4:
                    q = qs_all[:, h, g, 8 * s_:8 * s_ + 8].rearrange("p (i r) -> p r i", r=4)[:, r, :]
                    mk = cbf[:, CB_MK + 8:CB_MK + 16].rearrange("p (i r) -> p r i", r=4)[:, r, :]
                    oc = lambda B: B[:, 8 * s_:8 * s_ + 8].rearrange("p (i r) -> p r i", r=4)[:, r, :]
                else:
                    q = qs_all[:, h, g, 8 * s_ + r:8 * s_ + r + 1]
                    mk = None
                    oc = lambda B: B[:, 8 * s_ + r:8 * s_ + r + 1]
                return nq, c0, q, mk, oc
            def fsc(e, kt=kt, sb_=sb_):
                last = None
                for ti, (g, r) in enumerate(tiles):
                    nq, c0, q, mk, oc = cls(g, r)
                    o_ = PS(sb_)[:, c0:c0 + nq]
                    last = e.matmul(o_, kt[:, ti, :], q, start=True, stop=(mk is None))
                    if mk is not None:
                        last = e.matmul(o_, ident_b, mk, start=False, stop=True)
                return last
            S.op("pe", fsc, r=[ktk + (0,), ktk + (1,), "cb"] + [("qs", h, g) for g in range(3)], w=[("ps", sb_)])
            S.op("act", lambda e, pt=pt, sb_=sb_: e.activation(out=pt, in_=PS(sb_)[:, 0:24], func=AF.Exp, scale=SCALE), r=[("ps", sb_)], w=[ptk])
            def fpv(e, buf=buf, pt=pt):
                last = None
                for ti, (g, r) in enumerate(tiles):
                    nq, c0, q, mk, oc = cls(g, r)
                    e.matmul(oc(PS(4)), buf[g][:, r, 1, :], pt[:, c0:c0 + nq], start=False, stop=False, skip_group_check=True)
                    last = e.matmul(oc(PS(6)), ones_b, pt[:, c0:c0 + nq], start=False, stop=False, skip_group_check=True)
                return last
            S.op("pe", fpv, r=ckeys + [ptk, "cb"], w=[("ps", 4), ("ps", 6)])
            if s_ == NS - 1:
                S.op("dve", lambda e, h=h: e.tensor_tensor(out=fo_, in0=PS(4)[:, 0:32], in1=onew[:, h, :], op=ALU.add), r=[("ps", 4), ("onew", h)], w=["fo_"])
                S.op("dve", lambda e, h=h: e.tensor_tensor(out=fl_, in0=PS(6)[:, 0:32], in1=lnew[:, h, :], op=ALU.add), r=[("ps", 6), ("lnew", h)], w=["fl_"])
                S.op("act", lambda e: e.activation(out=fl_, in_=fl_, func=AF.Ln), r=["fl_"], w=["fl_"])
                S.op("act", lambda e: e.activation(out=fl_, in_=fl_, func=AF.Exp, scale=-1.0), r=["fl_"], w=["fl_"])
                S.op("dve", lambda e: e.tensor_tensor(out=fo_, in0=fo_, in1=fl_, op=ALU.mult), r=["fo_", "fl_"], w=["fo_"])
                S.op("dve", lambda e, h=h: e.tensor_tensor(out=ogT[:, h, L:TT], in0=fo_, in1=zs_all[:, h, :], op=ALU.mult),
                     r=["fo_", ("zs", h)], w=[("ogT", h, L + q_ * 8) for q_ in range(NS)])
        S.barrier()
        R3.reset()
        ada_phase(1, gate_p, gate_s, R3, parts=("gate",))
        S.barrier()

        R4.reset()
        xss = R4.alloc([D], F32, parts=32)
        fng_bc = R4.alloc([D], F32)
        S.dma("sp", fng_bc, I["fng"].partition_broadcast(128), w=["fng"])
        wout_phase(1, I["bwout"], lambda t: (x1[:, t, :], [("x1", t, 0), ("x1", t, 1)]), lambda t: x1[:, t, :])
        ssq2 = R4.alloc([17], F32)
        junk3 = R4.alloc([D], BF16)
        ost = [R4.alloc([D], F32) for _ in range(2)]
        for t in range(17):
            p = 128 if t < 16 else 32
            xb, xk = x_src1(t)
            S.op("act", lambda e, xb=xb, p=p, t=t: e.activation(out=junk3[0:p], in_=xb[0:p], func=AF.Square, accum_out=ssq2[0:p, t:t + 1]),
                 r=xk, w=["junk3", ("ssq2", t)])
            S.op("pool", lambda e, p=p, t=t: e.tensor_scalar(out=ssq2[0:p, t:t + 1], in0=ssq2[0:p, t:t + 1], scalar1=1.0 / D, scalar2=EPS, op0=ALU.mult, op1=ALU.add),
                 r=[("ssq2", t)], w=[("ssq2", t)])
            S.op("pool", lambda e, p=p, t=t: e.tensor_tensor(out=ssq2[0:p, t:t + 1], in0=ssq2[0:p, t:t + 1], in1=nhalf_c[0:p], op=ALU.pow),
                 r=[("ssq2", t), "cf"], w=[("ssq2", t)])
            ob = ost[t % 2]
            okey = ("ost", t % 2)
            S.op("act", lambda e, xb=xb, p=p, t=t, ob=ob: e.activation(out=ob[0:p], in_=xb[0:p], func=AF.Identity, scale=ssq2[0:p, t:t + 1]),
                 r=xk + [("ssq2", t)], w=[okey])
            S.op("dve", lambda e, p=p, ob=ob: e.tensor_tensor(out=ob[0:p], in0=ob[0:p], in1=fng_bc[0:p], op=ALU.mult), r=[okey, "fng"], w=[okey])
            if t < 16:
                S.dma("sp", O["yp"][t * 128:(t + 1) * 128, :], ob, r=[okey])
            else:
                S.dma("sp", O["ys"], ob[0:32], r=[okey])
        S.finish()
        print("instructions:", S.ninst, "sems:", S.nsem)
    return nc


def prep_inputs(inp):
    f = lambda a: np.ascontiguousarray(np.asarray(a, dtype=np.float32))
    cf, cb = make_consts()
    ngc = f(np.asarray(inp["norm_g"]).reshape(2, 8, 128).transpose(0, 2, 1))
    adab = f(inp["ada_b"])
    adabc = f(adab[:, 0:2048].reshape(2, 16, 128).transpose(0, 2, 1))
    cw = f(np.asarray(inp["a_conv_w"])[0].reshape(4, 24, 128).transpose(2, 1, 0))
    hp = np.zeros((128, 17), np.float32)
    hp[:, 0:8] = np.asarray(inp["a_A_log"])[0][None, :]
    hp[:, 8:16] = np.asarray(inp["a_dt_bias"])[0][None, :]
    hp[:, 16] = np.asarray(inp["a_out_norm_g"])[0]
    shared = dict(ngc=ngc, adaw=f(inp["ada_w"]), adabc=adabc, adab=adab, awin=f(np.asarray(inp["a_w_in"])[0]), cw=cw, hp=hp,
                  awout=f(np.asarray(inp["a_w_out"])[0]), cf=cf, cb=cb, fng=f(inp["final_norm_g"]),
                  bwin=f(np.asarray(inp["b_w_in"])[0]), bwout=f(np.asarray(inp["b_w_out"])[0]))
    maps = []
    for c in range(NCORES):
        m = dict(shared)
        m["xp"] = f(np.asarray(inp["x_prompt"])[c])
        m["xs"] = f(np.asarray(inp["x_sample"])[4 * c:4 * c + 4].reshape(32, D))
        cc = np.concatenate([np.asarray(inp["c_prompt"])[c:c + 1], np.asarray(inp["c_sample"])[4 * c:4 * c + 4]], 0)
        m["cT"] = f(cc.T.reshape(8, 128, 5).transpose(1, 0, 2))
        sc = np.asarray(inp["state_conv"])[0, 4 * c:4 * c + 4]
        m["sconv"] = f(sc.reshape(4, 3, 24, 128).transpose(3, 2, 0, 1))
        m["sdelta"] = f(np.asarray(inp["state_delta"])[0, 4 * c:4 * c + 4])
        m["kc0"] = f(np.asarray(inp["cache_kv_w128"])[0, 4 * c:4 * c + 4])
        m["kc1"] = f(np.asarray(inp["cache_kv_w512"])[0, 4 * c:4 * c + 4])
        m["kc2"] = f(np.asarray(inp["cache_kv_w2048"])[0, 4 * c:4 * c + 4])
        maps.append(m)
    return maps


_NC_CACHE = {}


def kernel(**inp):
    if "nc" not in _NC_CACHE:
        _NC_CACHE["nc"] = build()
    nc = _NC_CACHE["nc"]
    maps = prep_inputs(inp)
    res = run_bass_kernel_spmd(nc, maps, core_ids=list(range(NCORES)))
    R = res.results
    g = lambda k: [np.asarray(R[c][k], dtype=np.float32) for c in range(NCORES)]
    y_prompt = np.stack(g("yp"), 0)
    y_sample = np.concatenate([a.reshape(NS, LS, D) for a in g("ys")], 0)
    delta_p = np.stack(g("dp"), 0)[None]
    delta_s = np.concatenate(g("ds"), 0)[None]
    conv_p = np.stack(g("cp"), 0)[None]
    conv_s = np.concatenate([a.reshape(NS, 3, 3072) for a in g("cs")], 0)[None]
    outs = [y_prompt, y_sample, delta_p, delta_s, conv_p, conv_s]
    for gi in range(3):
        outs.append(np.stack(g("kv%dp" % gi), 0)[None])
        outs.append(np.concatenate(g("kv%ds" % gi), 0)[None])
    return tuple(np.ascontiguousarray(o, dtype=np.float32) for o in outs)
```

```python
import numpy as np
import ml_dtypes
from contextlib import ExitStack
import concourse.bass as bass
import concourse.mybir as mybir
from concourse.bass_utils import run_bass_kernel_spmd

F32 = mybir.dt.float32
BF16 = mybir.dt.bfloat16
AF = mybir.ActivationFunctionType
ALU = mybir.AluOpType

NEG = -30000.0
NCORES = 8
L = 2048
NS = 4
LS = 8
TT = L + NS * LS
D = 1024
EPS = 1e-6


class Sched:
    ENG = ("pe", "act", "dve", "pool", "sp")

    def __init__(self, nc, stack, sempool):
        self.nc = nc
        self.stack = stack
        self.sempool = sempool
        self.eng = {"pe": nc.tensor, "act": nc.scalar, "dve": nc.vector, "pool": nc.gpsimd, "sp": nc.sync}
        self.gen = {e: 0 for e in self.ENG}
        self.cnt = {e: 0 for e in self.ENG}
        self.esem = {}
        self.seen = {e: {} for e in self.ENG}
        self.res = {}
        self.dsem = {}
        self.nsem = 0
        self.ninst = {e: 0 for e in self.ENG}

    def _newsem(self, name):
        self.nsem += 1
        return self.sempool.pop()

    def _R(self, k):
        r = self.res.get(k)
        if r is None:
            r = [None, {}]
            self.res[k] = r
        return r

    def _wait(self, en, events):
        need = {}
        for ev in events:
            if ev is None:
                continue
            if ev[0] == "E":
                if ev[1] == en and en == "pe":
                    continue
                k = ("E", ev[1])
                v = (ev[2], ev[3])
            else:
                k = ("D", ev[1])
                v = (0, ev[2])
            if need.get(k, (-1, -1)) < v:
                need[k] = v
        for k, v in need.items():
            if self.seen[en].get(k, (-1, -1)) >= v:
                continue
            self.seen[en][k] = v
            sem = self.esem[(k[1], v[0])] if k[0] == "E" else self.dsem[k[1]][0]
            self.eng[en].wait_ge(sem, v[1])

    def _deps(self, r, w):
        evs = []
        for k in r:
            R = self.res.get(k)
            if R is not None:
                evs.append(R[0])
                if isinstance(k, tuple) and k[0] == "ps":
                    evs.extend(R[1].values())
        for k in w:
            R = self.res.get(k)
            if R is not None:
                evs.append(R[0])
                evs.extend(R[1].values())
        return evs

    def _record(self, ev, semid, r, w):
        for k in r:
            self._R(k)[1][semid] = ev
        for k in w:
            R = self._R(k)
            R[0] = ev
            R[1] = {}

    def op(self, en, fn, r=(), w=()):
        self._wait(en, self._deps(r, w))
        inst = fn(self.eng[en])
        self.ninst[en] += 1
        if self.cnt[en] >= 12000:
            self.gen[en] += 1
            self.cnt[en] = 0
        self.cnt[en] += 1
        g = self.gen[en]
        if (en, g) not in self.esem:
            self.esem[(en, g)] = self._newsem(f"e_{en}_{g}")
        inst.then_inc(self.esem[(en, g)], 1)
        ev = ("E", en, g, self.cnt[en])
        self._record(ev, ("E", en), r, w)
        return inst

    def dma(self, q, out, in_, r=(), w=(), key=None, **kw):
        self._wait(q, self._deps(r, w))
        inst = self.eng[q].dma_start(out=out, in_=in_, **kw)
        if key is None:
            key = ("w", w[0]) if w else ("r", r[0])
        ds = self.dsem.get(key)
        if ds is None:
            ds = [self._newsem(f"d{len(self.dsem)}"), 0]
            self.dsem[key] = ds
        ds[1] += 16
        inst.then_inc(ds[0], 16)
        ev = ("D", key, ds[1])
        self._record(ev, ("D", key), r, w)
        return inst

    def _all_events(self):
        evs = []
        for e in self.ENG:
            if self.cnt[e] > 0 or self.gen[e] > 0:
                evs.append(("E", e, self.gen[e], self.cnt[e]))
        for k, ds in self.dsem.items():
            evs.append(("D", k, ds[1]))
        return evs

    def barrier(self):
        evs = self._all_events()
        for e in self.ENG:
            self._wait(e, evs)

    def finish(self):
        self._wait("sp", self._all_events())


WAIT = "WAIT"


def run_tasks(gens):
    act = list(gens)
    idle = 0
    i = 0
    while act:
        i %= len(act)
        g = act[i]
        try:
            v = next(g)
        except StopIteration:
            act.pop(i)
            idle = 0
            continue
        if v is WAIT:
            idle += 1
            assert idle <= 4 * len(act) + 4, "all tasks waiting"
        else:
            idle = 0
        i += 1


class Arena:
    def __init__(self, ap, nbytes):
        self.ap = ap
        self.nbytes = nbytes

    def view(self, off, parts, shape, dt):
        esz = 4 if dt == F32 else 2
        n = int(np.prod(shape))
        nb = n * esz
        assert off % 4 == 0 and off + nb <= self.nbytes, (off, nb, self.nbytes)
        nw = (nb + 3) // 4
        v = self.ap[0:parts, off // 4: off // 4 + nw]
        if dt != F32:
            v = v.bitcast(dt)
            if v.shape[1] != n:
                v = v[:, 0:n]
        if len(shape) == 2:
            return v.rearrange("p (a b) -> p a b", a=shape[0])
        if len(shape) == 3:
            return v.rearrange("p (a b c) -> p a b c", a=shape[0], b=shape[1])
        return v


class Region:
    def __init__(self, arena, lo, hi):
        self.arena, self.lo, self.hi = arena, lo, hi
        self.cur = lo

    def reset(self):
        self.cur = self.lo

    def alloc(self, shape, dt, parts=128):
        esz = 4 if dt == F32 else 2
        nb = (int(np.prod(shape)) * esz + 3) // 4 * 4
        off = self.cur
        assert off + nb <= self.hi, ("region overflow", off, nb, self.hi)
        self.cur += nb
        return self.arena.view(off, parts, shape, dt)


CF_ID, CF_U, CF_MASK, CF_EPS, CF_ONE, CF_NHALF, CF_LNSC, CF_ZERO, CF_N = 0, 128, 256, 512, 513, 514, 515, 516, 520
CB_ID, CB_ONES, CB_BD, CB_ML = 0, 128, 256, 384
CB_MC, CB_MP, CB_MN, CB_MK, CB_N = 1152, 1280, 1408, 1504, 1528


def make_consts():
    cf = np.zeros((128, CF_N), np.float32)
    cf[:, CF_ID:CF_ID + 128] = np.eye(128)
    k = np.arange(128)[:, None]
    i = np.arange(128)[None, :]
    cf[:, CF_U:CF_U + 128] = (k <= i)
    cf[:, CF_MASK:CF_MASK + 128] = np.where(i >= k, 0.0, NEG)
    cf[:, CF_MASK + 128:CF_MASK + 256] = np.where(i > k, 0.0, NEG)
    cf[:, CF_EPS] = EPS
    cf[:, CF_ONE] = 1.0
    cf[:, CF_NHALF] = -0.5
    cf[:, CF_LNSC] = np.log(128.0 ** -0.5)
    cb = np.zeros((128, CB_N), np.float32)
    cb[:, CB_ID:CB_ID + 128] = np.eye(128)
    cb[:, CB_ONES:CB_ONES + 128] = 1.0
    ii = np.arange(128)[:, None]
    jj = np.arange(128)[None, :]
    cb[:, CB_BD:CB_BD + 128] = (ii // 16 == jj // 16)
    for lv, b in enumerate((16, 32, 64)):
        off = (ii // (2 * b) == jj // (2 * b)) & (ii % (2 * b) >= b) & (jj % (2 * b) < b)
        cb[:, CB_ML + lv * 256:CB_ML + lv * 256 + 128] = off.T
        cb[:, CB_ML + lv * 256 + 128:CB_ML + lv * 256 + 256] = off
    cb[:, CB_MC:CB_MC + 128] = np.where(jj >= ii, 0.0, NEG)
    cb[:, CB_MP:CB_MP + 128] = np.where(jj <= ii, 0.0, NEG)
    for gi, d in enumerate((1, 4, 16)):
        a = np.arange(32)
        sk, jk = a[:, None] // 8, a[:, None] % 8
        sq, lq = a[None, :] // 8, a[None, :] % 8
        ok = (sk == sq) & (jk <= lq) & ((lq - jk) % d == 0)
        cb[0:32, CB_MN + gi * 32:CB_MN + (gi + 1) * 32] = np.where(ok, 0.0, NEG)
        cb[:, CB_MK + gi * 8:CB_MK + (gi + 1) * 8] = np.where(ii >= (np.arange(8)[None, :] // d), 0.0, NEG)
    return cf, cb.astype(ml_dtypes.bfloat16)


def build(dbg=False, nheads=8, do_l1=True, stop=None):
    nc = bass.Bass("TRN2", target_bir_lowering=False)

    def din(name, shape, dt=F32):
        return nc.dram_tensor(name, list(shape), dt, kind="ExternalInput").ap()

    def dout(name, shape, dt=F32):
        return nc.dram_tensor(name, list(shape), dt, kind="ExternalOutput").ap()

    I = dict(
        xp=din("xp", [L, D]), xs=din("xs", [NS * LS, D]),
        cT=din("cT", [128, 8, 5]), ngc=din("ngc", [2, 128, 8]),
        adaw=din("adaw", [2, D, 3 * D]), adabc=din("adabc", [2, 128, 16]), adab=din("adab", [2, 3 * D]),
        awin=din("awin", [D, 4112]), cw=din("cw", [128, 24, 4]), hp=din("hp", [128, 17]),
        sconv=din("sconv", [128, 24, NS, 3]), sdelta=din("sdelta", [NS, 8, 128, 128]),
        awout=din("awout", [D, D]),
        cf=din("cf", [128, CF_N]), cb=din("cb", [128, CB_N], BF16),
        fng=din("fng", [D]),
        bwin=din("bwin", [D, 10240]), bwout=din("bwout", [D, D]),
        kc0=din("kc0", [NS, 128, 2, 8, 128]), kc1=din("kc1", [NS, 512, 2, 8, 128]), kc2=din("kc2", [NS, 2048, 2, 8, 128]),
    )
    O = dict(
        yp=dout("yp", [L, D]), ys=dout("ys", [NS * LS, D]),
        dp=dout("dp", [8, 128, 128]), ds=dout("ds", [NS, 8, 128, 128]),
        cp=dout("cp", [3, 3072]), cs=dout("cs", [NS * 3, 3072]),
        kv0p=dout("kv0p", [128, 2, 8, 128]), kv1p=dout("kv1p", [512, 2, 8, 128]), kv2p=dout("kv2p", [2048, 2, 8, 128]),
        kv0s=dout("kv0s", [NS, LS, 2, 8, 128]), kv1s=dout("kv1s", [NS, LS, 2, 8, 128]), kv2s=dout("kv2s", [NS, LS, 2, 8, 128]),
    )
    if dbg:
        O["dbg_x1p"] = dout("dbg_x1p", [L, D])
        O["dbg_x1s"] = dout("dbg_x1s", [NS * LS, D])
        O["dbg_og"] = dout("dbg_og", [128, 8, TT], BF16)
        O["dbg_hT"] = dout("dbg_hT", [128, 8, TT], BF16)

    stack = ExitStack()
    with stack:
        NB = 212000
        arena_t = stack.enter_context(nc.sbuf_tensor("arena", [128, NB // 4], F32))
        banks = [stack.enter_context(nc.psum_tensor(f"bank{i}", [128, 512], F32)) for i in range(8)]
        sempool = [stack.enter_context(nc.semaphore(f"s{i}")) for i in range(96)]
        stack.enter_context(nc.Block())
        S = Sched(nc, stack, sempool)
        A = Arena(arena_t, NB)

        def PS(b):
            return banks[b][:, :]

        def PSB(b):
            return banks[b][:, :].bitcast(BF16)

        RC = Region(A, 0, 8192)
        R1 = Region(A, 8192, 73728)
        R2 = Region(A, 73728, 107008)
        R3 = Region(A, 107008, 140288)
        R4 = Region(A, 140288, NB)

        cf = RC.alloc([CF_N], F32)
        cbf = RC.alloc([CB_N], BF16)
        hpar = RC.alloc([17], F32)
        S.dma("sp", cf, I["cf"], w=["cf"])
        S.dma("sp", cbf, I["cb"], w=["cb"])
        S.dma("sp", hpar, I["hp"], w=["hpar"])
        ident_f = cf[:, CF_ID:CF_ID + 128]
        U_f = cf[:, CF_U:CF_U + 128]
        mask2 = cf[:, CF_MASK:CF_MASK + 256]
        eps_c = cf[:, CF_EPS:CF_EPS + 1]
        one_c = cf[:, CF_ONE:CF_ONE + 1]
        nhalf_c = cf[:, CF_NHALF:CF_NHALF + 1]
        lnsc_c = cf[:, CF_LNSC:CF_LNSC + 1]
        ident_b = cbf[:, CB_ID:CB_ID + 128]
        ones_b = cbf[:, CB_ONES:CB_ONES + 128]
        bd_b = cbf[:, CB_BD:CB_BD + 128]
        ml_b = [cbf[:, CB_ML + lv * 256:CB_ML + (lv + 1) * 256] for lv in range(3)]
        mT_cur = cbf[:, CB_MC:CB_MC + 128]
        mT_prev = cbf[:, CB_MP:CB_MP + 128]
        CK = ["cf", "cb", "hpar"]

        modc = RC.alloc([16, 5], F32)
        gmod = RC.alloc([8, 5], F32)
        ngc = RC.alloc([8], F32)
        adabc = RC.alloc([16], F32)
        cT = RC.alloc([8, 5], F32)
        scb = RC.alloc([8, 5], BF16)

        def ada_phase(l, gate_p, gate_s, reg, parts=("col", "gate")):
            S.dma("sp", ngc, I["ngc"][l], w=["ngc"])
            S.dma("sp", adabc, I["adabc"][l], w=["adabc"])
            if l == 0:
                S.dma("sp", cT, I["cT"], w=["cT"])
                S.op("act", lambda e: e.activation(out=scb, in_=cT, func=AF.Silu), r=["cT"], w=["scb"])
            scp = reg.alloc([8, 128], BF16)
            scs = reg.alloc([8, 32], BF16)
            S.op("act", lambda e: e.activation(out=scp, in_=cT[:, :, 0:1].broadcast_to([128, 8, 128]), func=AF.Silu),
                 r=["cT"], w=["scp"])
            for s in range(NS):
                S.op("act", lambda e: e.activation(out=scs[:, :, 8 * s:8 * s + 8],
                                                   in_=cT[:, :, 1 + s:2 + s].broadcast_to([128, 8, 8]), func=AF.Silu),
                     r=["cT"], w=[("scs", s)])
            gb = reg.alloc([D], F32)
            S.dma("sp", gb, I["adab"][l, 2 * D:3 * D].partition_broadcast(128), w=["gb"])
            wb = [reg.alloc([8, 512], BF16) for _ in range(2)]
            for blk in range(6):
                if (blk < 4 and "col" not in parts) or (blk >= 4 and "gate" not in parts):
                    continue
                buf = wb[blk % 2]
                key = ("adaw", blk % 2)
                S.dma("pool", buf, I["adaw"][l][:, blk * 512:(blk + 1) * 512].rearrange("(c p) n -> p c n", p=128),
                      w=[key])
                if blk < 4:
                    def f(e, blk=blk, buf=buf):
                        last = None
                        for ecl in range(4):
                            ec = blk * 4 + ecl
                            for kc in range(8):
                                last = e.matmul(PS(0)[:, ec * 5:ec * 5 + 5], buf[:, kc, ecl * 128:(ecl + 1) * 128],
                                                scb[:, kc, :], start=(kc == 0), stop=(kc == 7))
                        return last
                    S.op("pe", f, r=[key, "scb"], w=[("ps", 0)])
                else:
                    hb = blk - 4
                    bk = 1 + (hb % 2)
                    def f(e, buf=buf, bk=bk):
                        last = None
                        for kc in range(8):
                            last = e.matmul(PS(bk), scp[:, kc, :], buf[:, kc, :], start=(kc == 0), stop=(kc == 7))
                        return last
                    S.op("pe", f, r=[key, "scp"], w=[("ps", bk)])
                    S.op("dve", lambda e, bk=bk, hb=hb: e.tensor_tensor(out=gate_p[:, hb * 512:(hb + 1) * 512], in0=PS(bk),
                                                                         in1=gb[:, hb * 512:(hb + 1) * 512], op=ALU.add),
                         r=[("ps", bk), "gb"], w=[("gate_p", l)])
                    def f2(e, buf=buf, bk=bk):
                        last = None
                        for kc in range(8):
                            last = e.matmul(PS(bk)[0:32, :], scs[:, kc, :], buf[:, kc, :], start=(kc == 0), stop=(kc == 7))
                        return last
                    S.op("pe", f2, r=[key] + [("scs", s) for s in range(NS)], w=[("ps", bk)])
                    S.op("dve", lambda e, bk=bk, hb=hb: e.tensor_tensor(out=gate_s[:, hb * 512:(hb + 1) * 512], in0=PS(bk)[0:32, :],
                                                                         in1=gb[0:32, hb * 512:(hb + 1) * 512], op=ALU.add),
                         r=[("ps", bk), "gb"], w=[("gate_s", l)])
            if "col" not in parts:
                return
            S.op("dve", lambda e: e.tensor_tensor(out=modc, in0=PS(0)[:, 0:80].rearrange("p (a b) -> p a b", b=5),
                                                  in1=adabc.unsqueeze(2).broadcast_to([128, 16, 5]), op=ALU.add),
                 r=[("ps", 0), "adabc"], w=["modc"])
            S.op("dve", lambda e: e.scalar_tensor_tensor(out=gmod, in0=modc[:, 8:16, :], scalar=1.0,
                                                         in1=ngc.unsqueeze(2).broadcast_to([128, 8, 5]),
                                                         op0=ALU.add, op1=ALU.mult),
                 r=["modc", "ngc"], w=["gmod"])

        def norm_phase(l, hT, x_src, reg):
            ssq = reg.alloc([17], F32)
            rstd = reg.alloc([17], F32)
            junk = reg.alloc([D], BF16)
            xn = [reg.alloc([D], BF16) for _ in range(2)]
            ntile = 17
            tiles = []
            if l == 0:
                xst = [reg.alloc([D], F32) for _ in range(3)]

            def xt(t):
                if l == 0:
                    return xst[t % 3], ("xst", t % 3)
                return x_src(t)

            def load(t):
                if l != 0:
                    return
                buf, key = xt(t)
                if t < 16:
                    S.dma("sp", buf, I["xp"][t * 128:(t + 1) * 128, :], w=[key])
                else:
                    S.dma("sp", buf[0:32], I["xs"], w=[key])

            def sq(t):
                buf, key = xt(t)
                p = 128 if t < 16 else 32
                keys = key if isinstance(key, list) else [key]
                S.op("act", lambda e: e.activation(out=junk[0:p], in_=buf[0:p], func=AF.Square, accum_out=ssq[0:p, t:t + 1]),
                     r=keys, w=["junk", ("ssq", t)])
                S.op("pool", lambda e: e.tensor_scalar(out=rstd[0:p, t:t + 1], in0=ssq[0:p, t:t + 1], scalar1=1.0 / D, scalar2=EPS,
                                                       op0=ALU.mult, op1=ALU.add), r=[("ssq", t)], w=[("rstd", t)])
                S.op("pool", lambda e: e.tensor_tensor(out=rstd[0:p, t:t + 1], in0=rstd[0:p, t:t + 1], in1=nhalf_c[0:p], op=ALU.pow),
                     r=[("rstd", t), "cf"], w=[("rstd", t)])

            def scale_T(t):
                buf, key = xt(t)
                p = 128 if t < 16 else 32
                xb = xn[t % 2]
                keys = key if isinstance(key, list) else [key]
                S.op("act", lambda e: e.activation(out=xb[0:p], in_=buf[0:p], func=AF.Identity, scale=rstd[0:p, t:t + 1]),
                     r=keys + [("rstd", t)], w=[("xn", t % 2)])
                bk = 2 + (t % 2)
                def f(e):
                    last = None
                    for kc in range(8):
                        last = e.transpose(PSB(bk)[:, kc * 128:kc * 128 + p], xb[0:p, kc * 128:(kc + 1) * 128], ident_b[0:p, 0:p])
                    return last
                S.op("pe", f, r=[("xn", t % 2), "cb"], w=[("ps", bk)])
                for kc in range(8):
                    if t < 16:
                        dst = hT[:, kc, t * 128:(t + 1) * 128]
                        src = PSB(bk)[:, kc * 128:(kc + 1) * 128]
                        if kc % 2 == 0:
                            S.op("dve", lambda e, dst=dst, src=src, kc=kc: e.tensor_scalar(
                                out=dst, in0=src, scalar1=gmod[:, kc, 0:1], scalar2=modc[:, kc, 0:1], op0=ALU.mult, op1=ALU.add),
                                 r=[("ps", bk), "gmod", "modc"], w=[("hT", t, kc)])
                        else:
                            S.op("act", lambda e, dst=dst, src=src, kc=kc: e.activation(
                                out=dst, in_=src, func=AF.Identity, scale=gmod[:, kc, 0:1], bias=modc[:, kc, 0:1]),
                                 r=[("ps", bk), "gmod", "modc"], w=[("hT", t, kc)])
                    else:
                        for s in range(NS):
                            dst = hT[:, kc, L + s * 8:L + s * 8 + 8]
                            src = PSB(bk)[:, kc * 128 + s * 8:kc * 128 + s * 8 + 8]
                            S.op("dve", lambda e, dst=dst, src=src, kc=kc, s=s: e.tensor_scalar(
                                out=dst, in0=src, scalar1=gmod[:, kc, 1 + s:2 + s], scalar2=modc[:, kc, 1 + s:2 + s],
                                op0=ALU.mult, op1=ALU.add), r=[("ps", bk), "gmod", "modc"], w=[("hT", t, kc)])

            load(0)
            load(1)
            sq(0)
            for t in range(ntile):
                if t + 2 < ntile:
                    load(t + 2)
                if t + 1 < ntile:
                    sq(t + 1)
                scale_T(t)

        HT_KEYS = [("hT", t, kc) for t in range(17) for kc in range(8)]

        R1.reset(); R2.reset(); R3.reset(); R4.reset()
        hT = R1.alloc([8, TT], BF16)
        ogT = R2.alloc([8, TT], BF16)
        R4g = Region(A, NB - 12288, NB)
        gate_p = R4g.alloc([D], F32)
        gate_s = R4g.alloc([D], F32, parts=32)
        x1s = R4g.alloc([D], F32, parts=32)
        R4 = Region(A, 140288, NB - 12288)

        class _Stop(Exception):
            pass

        def maybe_stop(tag):
            if stop == tag:
                S.finish()
                print("STOP at", tag, "instructions:", S.ninst, "sems:", S.nsem)
                raise _Stop()

        try:
            _build_rest = None
        finally:
            pass
        ada_phase(0, gate_p, gate_s, R3)
        R4.reset()
        if stop == "ada":
            S.finish(); print("STOP ada", S.ninst); return nc
        norm_phase(0, hT, None, R4)
        S.barrier()
        if stop == "norm":
            S.finish(); print("STOP norm", S.ninst); return nc
        R3.reset(); R4.reset()

        NCH = 16
        wab = R1.alloc([8, 16], BF16)
        S.dma("pool", wab, I["awin"][:, 4096:4112].rearrange("(c p) n -> p c n", p=128), w=["wab"])
        def fab(e):
            last = None
            for t in range(NCH):
                for kc in range(8):
                    last = e.matmul(PS(0)[:, t * 16:(t + 1) * 16], hT[:, kc, t * 128:(t + 1) * 128], wab[:, kc, :],
                                    start=(kc == 0), stop=(kc == 7))
            for s in range(NS):
                for kc in range(8):
                    last = e.matmul(PS(1)[0:8, s * 16:(s + 1) * 16], hT[:, kc, L + s * 8:L + s * 8 + 8], wab[:, kc, :],
                                    start=(kc == 0), stop=(kc == 7))
            return last
        S.op("pe", fab, r=["wab"] + HT_KEYS, w=[("ps", 0), ("ps", 1)])

        NCOL = NCH * 8 + NS * 8
        def galloc():
            return R1.alloc([NCOL], F32)
        xa, ax, ex, lx, g_t, beta_t, lbeta_t, gc_t, gcl_t, eg_t, gtot_t, ekd_t = [galloc() for _ in range(12)]
        nA = R1.alloc([8], F32)
        A_bc = hpar[:, 0:8]
        dt_bc = hpar[:, 8:16]
        outg_c = hpar[:, 16:17]
        def pv(tl):
            return tl[:, 0:128].rearrange("p (c h) -> p c h", h=8)
        def sv(tl):
            return tl[0:8, 128:160].rearrange("p (c h) -> p c h", h=8)
        abp = PS(0)[:, 0:256].rearrange("p (c k) -> p c k", k=16)
        abs_ = PS(1)[0:8, 0:64].rearrange("p (c k) -> p c k", k=16)
        S.op("dve", lambda e: e.tensor_tensor(out=pv(xa), in0=abp[:, :, 0:8], in1=dt_bc.unsqueeze(1).broadcast_to([128, 16, 8]), op=ALU.add),
             r=[("ps", 0), "hpar"], w=["xa_p"])
        S.op("dve", lambda e: e.tensor_tensor(out=sv(xa), in0=abs_[:, :, 0:8], in1=dt_bc[0:8].unsqueeze(1).broadcast_to([8, 4, 8]), op=ALU.add),
             r=[("ps", 1), "hpar"], w=["xa_s"])
        S.op("act", lambda e: e.activation(out=pv(ex), in_=abp[:, :, 8:16], func=AF.Exp, scale=-1.0), r=[("ps", 0)], w=["ex_p"])
        S.op("act", lambda e: e.activation(out=sv(ex), in_=abs_[:, :, 8:16], func=AF.Exp, scale=-1.0), r=[("ps", 1)], w=["ex_s"])
        GP = (128, slice(0, 128))
        GS = (8, slice(128, 160))
        for (p, cs_), tg in ((GP, "p"), (GS, "s")):
            def T(tl, p=p, cs_=cs_):
                return tl[0:p, cs_]
            S.op("act", lambda e, T=T: e.activation(out=T(lbeta_t), in_=T(ex), func=AF.Ln, bias=one_c[0:T(ex).shape[0]], scale=1.0),
                 r=["ex_" + tg, "cf"], w=["lbeta_" + tg])
            S.op("dve", lambda e, T=T: e.tensor_scalar(out=T(lbeta_t), in0=T(lbeta_t), scalar1=-1.0, scalar2=None, op0=ALU.mult),
                 r=["lbeta_" + tg], w=["lbeta_" + tg])
            S.op("act", lambda e, T=T: e.activation(out=T(beta_t), in_=T(lbeta_t), func=AF.Exp), r=["lbeta_" + tg], w=["beta_" + tg])
            S.op("dve", lambda e, T=T: e.tensor_scalar(out=T(ax), in0=T(xa), scalar1=-1.0, scalar2=None, op0=ALU.mult),
                 r=["xa_" + tg], w=["ax_" + tg])
            S.op("dve", lambda e, T=T: e.tensor_tensor(out=T(ax), in0=T(ax), in1=T(xa), op=ALU.max),
                 r=["xa_" + tg, "ax_" + tg], w=["ax_" + tg])
            S.op("act", lambda e, T=T: e.activation(out=T(ax), in_=T(ax), func=AF.Exp, scale=-1.0), r=["ax_" + tg], w=["ax_" + tg])
            S.op("act", lambda e, T=T: e.activation(out=T(lx), in_=T(ax), func=AF.Ln, bias=one_c[0:T(ax).shape[0]], scale=1.0),
                 r=["ax_" + tg, "cf"], w=["lx_" + tg])
            S.op("dve", lambda e, T=T: e.scalar_tensor_tensor(out=T(lx), in0=T(xa), scalar=0.0, in1=T(lx), op0=ALU.max, op1=ALU.add),
                 r=["xa_" + tg, "lx_" + tg], w=["lx_" + tg])
        S.op("act", lambda e: e.activation(out=nA, in_=A_bc, func=AF.Exp), r=["hpar"], w=["nA"])
        S.op("dve", lambda e: e.tensor_scalar(out=nA, in0=nA, scalar1=-1.0, scalar2=None, op0=ALU.mult), r=["nA"], w=["nA"])
        S.op("dve", lambda e: e.tensor_tensor(out=pv(g_t), in0=pv(lx), in1=nA.unsqueeze(1).broadcast_to([128, 16, 8]), op=ALU.mult),
             r=["lx_p", "nA"], w=["g_p"])
        S.op("dve", lambda e: e.tensor_tensor(out=sv(g_t), in0=sv(lx), in1=nA[0:8].unsqueeze(1).broadcast_to([8, 4, 8]), op=ALU.mult),
             r=["lx_s", "nA"], w=["g_s"])
        S.op("pe", lambda e: e.matmul(PS(2)[:, 0:128], U_f, g_t[:, 0:128], start=True, stop=True), r=["cf", "g_p"], w=[("ps", 2)])
        S.op("pe", lambda e: e.matmul(PS(3)[0:8, 0:32], U_f[0:8, 0:8], g_t[0:8, 128:160], start=True, stop=True), r=["cf", "g_s"], w=[("ps", 3)])
        S.op("act", lambda e: e.activation(out=gc_t[:, 0:128], in_=PS(2)[:, 0:128], func=AF.Copy), r=[("ps", 2)], w=["gc_p"])
        S.op("act", lambda e: e.activation(out=gc_t[0:8, 128:160], in_=PS(3)[0:8, 0:32], func=AF.Copy), r=[("ps", 3)], w=["gc_s"])
        S.op("pe", lambda e: e.matmul(PS(2)[:, 128:256], ident_f[:, 127:128].broadcast_to([128, 128]), gc_t[:, 0:128], start=True, stop=True),
             r=["cf", "gc_p"], w=[("ps", 2)])
        S.op("pe", lambda e: e.matmul(PS(3)[:, 128:160], ident_f[0:8, 7:8].broadcast_to([8, 128]), gc_t[0:8, 128:160], start=True, stop=True),
             r=["cf", "gc_s"], w=[("ps", 3)])
        S.op("act", lambda e: e.activation(out=gcl_t[:, 0:128], in_=PS(2)[:, 128:256], func=AF.Copy), r=[("ps", 2)], w=["gcl_p"])
        S.op("act", lambda e: e.activation(out=gcl_t[:, 128:160], in_=PS(3)[:, 128:160], func=AF.Copy), r=[("ps", 3)], w=["gcl_s"])
        for (p, cs_), tg in ((GP, "p"), (GS, "s")):
            def T(tl, p=p, cs_=cs_):
                return tl[0:p, cs_]
            S.op("act", lambda e, T=T: e.activation(out=T(eg_t), in_=T(gc_t), func=AF.Exp), r=["gc_" + tg], w=["eg_" + tg])
            S.op("act", lambda e, cs_=cs_: e.activation(out=gtot_t[:, cs_], in_=gcl_t[:, cs_], func=AF.Exp), r=["gcl_" + tg], w=["gtot_" + tg])
            S.op("dve", lambda e, T=T: e.tensor_tensor(out=T(ekd_t), in0=T(gcl_t), in1=T(gc_t), op=ALU.subtract),
                 r=["gcl_" + tg, "gc_" + tg], w=["ekd_" + tg])
        G_KEYS = [k + t for k in ("beta_", "lbeta_", "gc_", "gcl_", "eg_", "gtot_", "ekd_") for t in ("p", "s")]

        if stop == "G":
            S.finish(); print("STOP G", S.ninst); return nc
        NT = 16
        UW = 3 + L
        UT = UW + NS * (3 + LS)
        NUB = 2
        ubuf = [R4.alloc([UT], BF16) for _ in range(NUB)]
        Wh = [R1.alloc([8, 512], BF16) for _ in range(2)]
        diag = [R1.alloc([12, 128], BF16) for _ in range(2)]
        cwt = R1.alloc([24, 4], F32)
        S.dma("sp", cwt, I["cw"], w=["cwt"])
        sconv = R1.alloc([24, NS * 3], F32)
        S.dma("sp", sconv, I["sconv"].rearrange("p a s i -> p a (s i)"), w=["sconv"])
        cvo = [[R3.alloc([TT], BF16) for _ in range(4)] for _ in range(2)]
        sqb1 = R4.alloc([TT], BF16)
        sqb = [sqb1, sqb1]
        NHC = NT + NS
        def halloc(n=NHC):
            return [R4.alloc([n], F32) for _ in range(2)]
        ss_k, ss_q, lrnk, lrq, rows1, rows2, biasj, kbg_s, kdec_s, qdec_s = [halloc() for _ in range(10)]
        rowsT = [R4.alloc([256], F32, parts=16) for _ in range(2)]
        rowsTs = [R4.alloc([16], F32, parts=4) for _ in range(2)]
        cst1 = R4.alloc([384], F32, parts=3)
        css1 = R4.alloc([384], F32, parts=32)
        cst = [cst1, cst1]
        css = [css1, css1]
        thb = [R4.alloc([512], BF16) for _ in range(2)]
        thc = [0]
        outg_h = R4.alloc([1], F32)
        S.op("dve", lambda e: e.tensor_scalar(out=outg_h, in0=outg_c, scalar1=0.5, scalar2=None, op0=ALU.mult), r=["hpar"], w=["outg_h"])
        for ub in range(NUB):
            S.op("pool", lambda e, ub=ub: e.memset(ubuf[ub][:, 0:3], 0.0), w=[("u", ub, "h")])

        ucount = [0]

        def P_head(h):
            hs_ = h % 2
            W = Wh[hs_]
            wkey = [("Wh", hs_, j) for j in range(4)]
            for j in range(4):
                col = (j * 1024 + h * 128)
                S.dma("pool", W[:, :, j * 128:(j + 1) * 128],
                      I["awin"][:, col:col + 128].rearrange("(c p) n -> p c n", p=128), w=[wkey[j]])
            dg = diag[hs_]
            for j in range(3):
                for i in range(4):
                    S.op("pool", lambda e, j=j, i=i: e.tensor_scalar(out=dg[:, j * 4 + i, :], in0=ident_f, scalar1=cwt[:, j * 8 + h, i:i + 1],
                                                                      scalar2=0.5, op0=ALU.mult, op1=ALU.mult),
                         r=["cf", "cwt"], w=[("diag", hs_, j)])
            yield 0.02
            blocks = [(q * 512, 512) for q in range(4)] + [(L, NS * LS)]
            step = 0
            nstep = 4 * 5 * 2.0
            for j in range(4):
                if j < 3:
                    ui = ucount[0] % NUB
                    ucount[0] += 1
                    ub = ubuf[ui]
                    ukey = None
                    ukeys = [("u", ui, bi_) for bi_ in range(5)]
                    S.op("act", lambda e, ub=ub, j=j: e.activation(
                        out=ub[:, UW:UT].rearrange("p (s i) -> p s i", i=3 + LS)[:, :, 0:3],
                        in_=sconv[:, j * 8 + h, :].rearrange("p (s i) -> p s i", i=3), func=AF.Copy),
                         r=["sconv"], w=[("u", ui, "sh")])
                for bi, (t0, n) in enumerate(blocks):
                    bk = 0
                    step += 1
                    def f(e, t0=t0, n=n, bk=bk, j=j):
                        last = None
                        for kc in range(8):
                            last = e.matmul(PS(bk)[:, 0:n], W[:, kc, j * 128:(j + 1) * 128], hT[:, kc, t0:t0 + n],
                                            start=(kc == 0), stop=(kc == 7))
                        return last
                    S.op("pe", f, r=[wkey[j]] + HT_KEYS, w=[("ps", bk)])
                    if j == 3:
                        tb = thb[thc[0] % 2]
                        tk = ("thb", thc[0] % 2)
                        thc[0] += 1
                        S.op("act", lambda e, n=n, bk=bk, tb=tb: e.activation(out=tb[:, 0:n], in_=PS(bk)[:, 0:n], func=AF.Tanh, scale=0.5),
                             r=[("ps", bk)], w=[tk])
                        S.op("dve", lambda e, t0=t0, n=n, bk=bk, tb=tb: e.scalar_tensor_tensor(out=cvo[hs_][3][:, t0:t0 + n], in0=tb[:, 0:n], scalar=1.0,
                                                                                              in1=PS(bk)[:, 0:n], op0=ALU.add, op1=ALU.mult),
                             r=[("ps", bk), tk], w=[("cvo", hs_, 3, bi)])
                    else:
                        if bi < 4:
                            dst = ub[:, 3 + t0:3 + t0 + n]
                            src = PS(bk)[:, 0:n]
                        else:
                            dst = ub[:, UW:UT].rearrange("p (s i) -> p s i", i=3 + LS)[:, :, 3:3 + LS]
                            src = PS(bk)[:, 0:n].rearrange("p (s i) -> p s i", i=LS)
                        S.op("act", lambda e, dst=dst, src=src: e.activation(out=dst, in_=src, func=AF.Copy), r=[("ps", bk)], w=[ukeys[bi]])
                        ck = 1
                        def fc(e, t0=t0, n=n, ck=ck, bi=bi, j=j, ub=ub):
                            last = None
                            for i in range(4):
                                if bi < 4:
                                    rhs = ub[:, t0 + i:t0 + i + n]
                                    out = PS(ck)[:, 0:n]
                                else:
                                    rhs = ub[:, UW:UT].rearrange("p (s i) -> p s i", i=3 + LS)[:, :, i:i + LS]
                                    out = PS(ck)[:, 0:n].rearrange("p (s i) -> p s i", i=LS)
                                last = e.matmul(out, dg[:, j * 4 + i, :], rhs, start=(i == 0), stop=(i == 3))
                            return last
                        rk = [ukeys[bi], ("diag", hs_, j)] + ([ukeys[bi - 1]] if 0 < bi < 4 else []) + ([("u", ui, "h")] if bi == 0 else []) + ([("u", ui, "sh")] if bi == 4 else [])
                        yield 0.5
                        S.op("pe", fc, r=rk, w=[("ps", ck)])
                        tb = thb[thc[0] % 2]
                        tk = ("thb", thc[0] % 2)
                        thc[0] += 1
                        S.op("act", lambda e, n=n, ck=ck, tb=tb: e.activation(out=tb[:, 0:n], in_=PS(ck)[:, 0:n], func=AF.Tanh),
                             r=[("ps", ck)], w=[tk])
                        S.op("dve", lambda e, t0=t0, n=n, ck=ck, j=j, tb=tb: e.scalar_tensor_tensor(out=cvo[hs_][j][:, t0:t0 + n], in0=tb[:, 0:n], scalar=1.0,
                                                                                                   in1=PS(ck)[:, 0:n], op0=ALU.add, op1=ALU.mult),
                             r=[("ps", ck), tk], w=[("cvo", hs_, j, bi)])
                    yield 0.02 + 0.8 * step / 20.0
            def fcs(e):
                last = None
                for kc in range(8):
                    last = e.matmul(PS(0)[0:3, 0:384], hT[:, kc, L - 3:L], W[:, kc, 0:384], start=(kc == 0), stop=(kc == 7))
                for kc in range(8):
                    last = e.matmul(PS(1)[0:32, 0:384], hT[:, kc, L:TT], W[:, kc, 0:384], start=(kc == 0), stop=(kc == 7))
                return last
            S.op("pe", fcs, r=wkey + HT_KEYS, w=[("ps", 0), ("ps", 1)])
            S.op("act", lambda e: e.activation(out=cst[hs_], in_=PS(0)[0:3, 0:384], func=AF.Copy), r=[("ps", 0)], w=[("cst", 0)])
            S.op("act", lambda e: e.activation(out=css[hs_], in_=PS(1)[0:32, 0:384], func=AF.Copy), r=[("ps", 1)], w=[("css", 0)])
            S.dma("sp", O["cp"].rearrange("p (j c) -> p j c", j=3)[:, :, h * 128:(h + 1) * 128],
                  cst[hs_].rearrange("p (j c) -> p j c", j=3), r=[("cst", 0)])
            for s_ in range(NS):
                S.dma("sp", O["cs"].rearrange("p (j c) -> p j c", j=3)[3 * s_:3 * s_ + 3, :, h * 128:(h + 1) * 128],
                      css[hs_][8 * s_ + 5:8 * s_ + 8].rearrange("p (j c) -> p j c", j=3), r=[("css", 0)])
            for j, sst in ((1, ss_k[hs_]), (0, ss_q[hs_])):
                sb = sqb[j]
                S.op("act", lambda e, j=j, sb=sb: e.activation(out=sb, in_=cvo[hs_][j], func=AF.Square),
                     r=[("cvo", hs_, j, bi) for bi in range(5)], w=[("sqb", 0)])
                def fs(e, sb=sb, j=j):
                    last = None
                    for c in range(NT):
                        last = e.matmul(PS(j)[:, c:c + 1], sb[:, c * 128:(c + 1) * 128], ones_b[:, 0:1], start=True, stop=True)
                    for s in range(NS):
                        last = e.matmul(PS(j)[0:8, NT + s:NT + s + 1], sb[:, L + s * 8:L + s * 8 + 8], ones_b[:, 0:1], start=True, stop=True)
                    return last
                S.op("pe", fs, r=[("sqb", 0), "cb"], w=[("ps", j)])
                S.op("act", lambda e, j=j, sst=sst: e.activation(out=sst[:, 0:NT], in_=PS(j)[:, 0:NT], func=AF.Copy), r=[("ps", j)], w=[("ss", hs_, j)])
                S.op("act", lambda e, j=j, sst=sst: e.activation(out=sst[0:8, NT:NHC], in_=PS(j)[0:8, NT:NHC], func=AF.Copy), r=[("ps", j)], w=[("ss", hs_, j, "s")])
            yield 0.9
            def gcol(tl, which):
                if which == "p":
                    return tl[:, 0:128].rearrange("p (c hh) -> p c hh", hh=8)[:, :, h]
                return tl[0:8, 128:160].rearrange("p (c hh) -> p c hh", hh=8)[:, :, h]
            for which, p, cs_ in (("p", 128, slice(0, NT)), ("s", 8, slice(NT, NHC))):
                def T(tl, p=p, cs_=cs_):
                    return tl[hs_][0:p, cs_]
                hk = ("hs", hs_, which)
                S.op("act", lambda e, T=T, p=p: e.activation(out=T(lrnk), in_=T(ss_k), func=AF.Ln, bias=eps_c[0:p], scale=1.0),
                     r=[("ss", hs_, 1), ("ss", hs_, 1, "s"), "cf"], w=[hk + ("lrnk",)])
                S.op("act", lambda e, T=T, p=p: e.activation(out=T(lrq), in_=T(ss_q), func=AF.Ln, bias=eps_c[0:p], scale=1.0),
                     r=[("ss", hs_, 0), ("ss", hs_, 0, "s"), "cf"], w=[hk + ("lrq",)])
                S.op("dve", lambda e, T=T: e.tensor_scalar(out=T(lrnk), in0=T(lrnk), scalar1=-0.5, scalar2=None, op0=ALU.mult),
                     r=[hk + ("lrnk",)], w=[hk + ("lrnk",)])
                S.op("dve", lambda e, T=T, p=p: e.tensor_scalar(out=T(lrq), in0=T(lrq), scalar1=-0.5, scalar2=lnsc_c[0:p], op0=ALU.mult, op1=ALU.add),
                     r=[hk + ("lrq",), "cf"], w=[hk + ("lrq",)])
                S.op("dve", lambda e, T=T, which=which: e.tensor_tensor(out=T(rows1), in0=T(lrnk), in1=gcol(lbeta_t, which), op=ALU.add),
                     r=[hk + ("lrnk",), "lbeta_" + which], w=[hk + ("rows1",)])
                S.op("dve", lambda e, T=T, which=which: e.tensor_tensor(out=T(rows1), in0=T(rows1), in1=gcol(gc_t, which), op=ALU.add),
                     r=[hk + ("rows1",), "gc_" + which], w=[hk + ("rows1",)])
                S.op("dve", lambda e, T=T, which=which: e.tensor_tensor(out=T(rows2), in0=T(lrq), in1=gcol(gc_t, which), op=ALU.add),
                     r=[hk + ("lrq",), "gc_" + which], w=[hk + ("rows2",)])
                S.op("dve", lambda e, T=T, which=which: e.tensor_tensor(out=T(biasj), in0=T(lrnk), in1=gcol(gc_t, which), op=ALU.subtract),
                     r=[hk + ("lrnk",), "gc_" + which], w=[hk + ("biasj",)])
                S.op("act", lambda e, T=T: e.activation(out=T(kbg_s), in_=T(rows1), func=AF.Exp), r=[hk + ("rows1",)], w=[hk + ("kbg_s",)])
                S.op("act", lambda e, T=T: e.activation(out=T(qdec_s), in_=T(rows2), func=AF.Exp), r=[hk + ("rows2",)], w=[hk + ("qdec_s",)])
                S.op("dve", lambda e, T=T, which=which: e.tensor_tensor(out=T(kdec_s), in0=T(lrnk), in1=gcol(ekd_t, which), op=ALU.add),
                     r=[hk + ("lrnk",), "ekd_" + which], w=[hk + ("kdec_s",)])
                S.op("act", lambda e, T=T: e.activation(out=T(kdec_s), in_=T(kdec_s), func=AF.Exp), r=[hk + ("kdec_s",)], w=[hk + ("kdec_s",)])
            def ft(e):
                e.transpose(PS(0)[0:16, 0:128], rows2[hs_][:, 0:NT], ident_f)
                e.transpose(PS(0)[0:16, 128:256], rows1[hs_][:, 0:NT], ident_f)
                e.transpose(PS(1)[0:4, 0:8], rows2[hs_][0:8, NT:NHC], ident_f[0:8, 0:8])
                return e.transpose(PS(1)[0:4, 8:16], rows1[hs_][0:8, NT:NHC], ident_f[0:8, 0:8])
            S.op("pe", ft, r=[("hs", hs_, w_, n_) for w_ in ("p", "s") for n_ in ("rows1", "rows2")] + ["cf"], w=[("ps", 0), ("ps", 1)])
            S.op("act", lambda e: e.activation(out=rowsT[hs_], in_=PS(0)[0:16, 0:256], func=AF.Copy), r=[("ps", 0)], w=[("rowsT", hs_)])
            S.op("act", lambda e: e.activation(out=rowsTs[hs_], in_=PS(1)[0:4, 0:16], func=AF.Copy), r=[("ps", 1)], w=[("rowsTs", hs_)])
            yield 1.0

        NSET = 3
        import os as _os
        NLANE = int(_os.environ.get("K_NLANE", "4"))
        NHO = 5
        STAG = int(_os.environ.get("K_STAG", "5"))
        LANE_BANK = (6, 2, 3, 5)

        def lane_ws():
            d = {}
            d["t"] = R4.alloc([256], F32)
            d["D"] = d["t"]
            d["W"] = [R4.alloc([512], BF16) for _ in range(2)]
            d["BN"] = [w_[:, 0:256] for w_ in d["W"]]
            d["Q"] = [w_[:, 256:384] for w_ in d["W"]]
            d["M"] = [R4.alloc([256], BF16) for _ in range(3)]
            d["XY"] = R4.alloc([256], BF16)
            d["QTs"] = R4.alloc([128], BF16)
            return d

        def handoff():
            d = {}
            d["ktok"] = R4.alloc([128], BF16)
            d["vb"] = R4.alloc([128], BF16)
            d["NTQK"] = R4.alloc([384], BF16)
            d["kcTn"] = R4.alloc([128], BF16)
            d["QT"] = R4.alloc([128], BF16)
            return d
        HO_NAMES = ("ktok", "vb", "NTQK", "kcTn", "QT")
        lanes = [lane_ws() for _ in range(NLANE)]
        hos = [handoff() for _ in range(NHO)]
        u_t = [R4.alloc([128], BF16) for _ in range(2)]
        us_t = [R4.alloc([128], BF16) for _ in range(2)]
        t2_t = [R4.alloc([128], F32) for _ in range(2)]
        o_t = [R4.alloc([128], F32) for _ in range(2)]
        on_t = [R4.alloc([128], BF16) for _ in range(2)]
        junk2 = R4.alloc([128], BF16)
        oss = R4.alloc([2], F32)
        Sf = [R4.alloc([128], F32) for _ in range(2)]
        Sb = [R4.alloc([128], BF16) for _ in range(2)]
        sidx = [0]
        cidx = [0]
        cp_done = set()
        cr_done = [0]
        cr_o = [0]
        cn_done = [0]
        stg = [0]

        def chunk_specs(h):
            sp = [(128, c * 128, c, False, 0) for c in range(NT)]
            sp += [(8, L + s * 8, NT + s, True, s) for s in range(NS)]
            return sp

        def CP_chunk(h, spec, cs, lane, slot):
            RB = LANE_BANK[lane]
            def KK(name, *rest):
                return ((("ho", slot) if name in HO_NAMES else ("lw", lane)), name) + rest
            C, t0, col, smp, s = spec
            hs_ = h % 2
            qT, kT, vT = cvo[hs_][0], cvo[hs_][1], cvo[hs_][2]
            cvk = [("cvo", hs_, j, bi) for j in range(3) for bi in range(5)]
            hk = lambda n: ("hs", hs_, "s" if smp else "p", n)
            ksl = kT[:, t0:t0 + C]
            def f1(e):
                e.transpose(PSB(4)[0:C, 0:128], ksl, ident_b)
                return e.transpose(PSB(4)[0:C, 128:256], vT[:, t0:t0 + C], ident_b)
            S.op("pe", f1, r=cvk + ["cb"], w=[("ps", 4)])
            S.op("act", lambda e: e.activation(out=cs["ktok"][0:C], in_=PSB(4)[0:C, 0:128], func=AF.Copy), r=[("ps", 4)], w=[KK("ktok")])
            bcol = (beta_t[0:C, 128 + s * 8 + h:128 + s * 8 + h + 1] if smp else beta_t[:, col * 8 + h:col * 8 + h + 1])
            S.op("dve", lambda e: e.tensor_scalar(out=cs["vb"][0:C], in0=PSB(4)[0:C, 128:256], scalar1=bcol, scalar2=None, op0=ALU.mult),
                 r=[("ps", 4), "beta_s" if smp else "beta_p"], w=[KK("vb")])
            def f2(e):
                e.matmul(PS(RB)[0:C, 0:C], ksl, qT[:, t0:t0 + C], start=True, stop=True)
                e.matmul(PS(RB)[0:C, 128:128 + C], ksl, ksl, start=True, stop=True)
                if smp:
                    e.matmul(PS(RB)[0:C, 256:256 + C], ident_f[0:4, s:s + 1].broadcast_to([4, C]), rowsTs[hs_][:, 0:8], start=True, stop=True)
                    return e.matmul(PS(RB)[0:C, 384:384 + C], ident_f[0:4, s:s + 1].broadcast_to([4, C]), rowsTs[hs_][:, 8:16], start=True, stop=True)
                return e.matmul(PS(RB)[:, 256:512], ident_f[0:16, col:col + 1].broadcast_to([16, 128]), rowsT[hs_], start=True, stop=True)
            S.op("pe", f2, r=cvk + ["cf", ("rowsTs" if smp else "rowsT", hs_)], w=[("ps", RB)])
            t3 = cs["t"][0:C, :].rearrange("p (a b) -> p a b", a=2)[:, :, 0:C]
            D3 = cs["D"][0:C, :].rearrange("p (a b) -> p a b", a=2)[:, :, 0:C]
            N3 = cs["NTQK"][0:C, 0:256].rearrange("p (a b) -> p a b", a=2)[:, :, 0:C]
            m3 = mask2[0:C, :].rearrange("p (a b) -> p a b", a=2)[:, :, 0:C]
            E3 = PS(RB)[0:C, 256:512].rearrange("p (a b) -> p a b", a=2)[:, :, 0:C]
            raw3 = PS(RB)[0:C, 0:256].rearrange("p (a b) -> p a b", a=2)[:, :, 0:C]
            yield
            S.op("dve", lambda e: e.tensor_tensor(out=t3, in0=E3, in1=m3, op=ALU.add), r=[("ps", RB), "cf"], w=[KK("t")])
            yield
            S.op("act", lambda e: e.activation(out=D3, in_=t3, func=AF.Exp, bias=biasj[hs_][0:C, col:col + 1], scale=1.0),
                 r=[KK("t"), hk("biasj")], w=[KK("t")])
            yield
            S.op("dve", lambda e: e.tensor_tensor(out=N3, in0=raw3, in1=D3, op=ALU.mult), r=[("ps", RB), KK("t")], w=[KK("NTQK")])
            yield
            Bm = cs["NTQK"][0:C, 128:128 + C]
            S.op("pe", lambda e: e.transpose(PSB(4)[0:C, 256:256 + C], Bm, ident_b[0:C, 0:C]), r=[KK("NTQK"), "cb"], w=[("ps", 4)])
            if C == 128:
                W = cs["W"]
                wk = lambda i: KK("W", i)
                S.op("act", lambda e: e.activation(out=cs["NTQK"][:, 256:384], in_=PSB(4)[:, 256:384], func=AF.Copy), r=[("ps", 4)], w=[KK("NTQK")])
                BNf = cs["NTQK"][:, 128:384]
                v2 = lambda ap: ap.rearrange("p (a b) -> p a b", a=2)
                bd2 = bd_b.unsqueeze(1).broadcast_to([128, 2, 128])
                id2 = ident_b.unsqueeze(1).broadcast_to([128, 2, 128])
                W0bn = W[0].rearrange("p (a b) -> p a b", a=4)[:, 1::2, :]
                W1qt = W[1].rearrange("p (a b) -> p a b", a=4)[:, 0::2, :]
                S.op("dve", lambda e: e.tensor_tensor(out=W0bn, in0=v2(BNf), in1=bd2, op=ALU.mult), r=[KK("NTQK"), "cb"], w=[wk(0)])
                S.op("dve", lambda e: e.tensor_tensor(out=W1qt, in0=id2, in1=W0bn, op=ALU.subtract), r=[wk(0), "cb"], w=[wk(1)])
                for lv in range(3):
                    S.op("dve", lambda e, lv=lv: e.tensor_tensor(out=cs["M"][lv], in0=BNf, in1=ml_b[lv], op=ALU.mult),
                         r=[KK("NTQK"), "cb"], w=[KK("M", lv)])
                yield
                cur = 0
                for k in range(4):
                    nxt = 1 - cur
                    last = (k == 3)
                    Wc = W[cur]
                    Bk = Wc[:, 128:256]
                    Nk = Wc[:, 384:512]
                    src = Wc if k > 0 else None
                    def fr(e, k=k, Wc=Wc, Bk=Bk, Nk=Nk, last=last):
                        if k == 0:
                            e.matmul(PS(RB)[:, 128:256], Nk, Bk, start=True, stop=True)
                            return e.matmul(PS(RB)[:, 384:512], Bk, Nk, start=True, stop=True)
                        if last:
                            e.matmul(PS(RB)[:, 0:128], Nk, Wc[:, 0:128], start=True, stop=True)
                            return e.matmul(PS(RB)[:, 256:384], Bk, Wc[:, 256:384], start=True, stop=True)
                        e.matmul(PS(RB)[:, 0:256], Nk, Wc[:, 0:256], start=True, stop=True)
                        return e.matmul(PS(RB)[:, 256:512], Bk, Wc[:, 256:512], start=True, stop=True)
                    S.op("pe", fr, r=[wk(cur)], w=[("ps", RB)])
                    P4 = PS(RB).rearrange("p (a b) -> p a b", a=4)
                    Wn4 = W[nxt].rearrange("p (a b) -> p a b", a=4)
                    Wc4 = Wc.rearrange("p (a b) -> p a b", a=4)
                    if k == 0:
                        S.op("act", lambda e, P4=P4, Wn4=Wn4: e.activation(out=Wn4[:, 1::2, :], in_=P4[:, 1::2, :], func=AF.Copy),
                             r=[("ps", RB)], w=[wk(nxt)])
                    else:
                        if not last:
                            S.op("act", lambda e, P4=P4, Wn4=Wn4: e.activation(out=Wn4[:, 1::2, :], in_=P4[:, 1::2, :], func=AF.Copy),
                                 r=[("ps", RB)], w=[wk(nxt)])
                        S.op("dve", lambda e, P4=P4, Wn4=Wn4, Wc4=Wc4: e.tensor_tensor(out=Wn4[:, 0::2, :], in0=P4[:, 0::2, :], in1=Wc4[:, 0::2, :], op=ALU.add),
                             r=[("ps", RB), wk(cur)], w=[wk(nxt)])
                    cur = nxt
                    yield
                Wc = W[cur]
                Qv = Wc[:, 0:128]
                Tv = Wc[:, 256:384]
                XY = cs["XY"]
                for lv in range(3):
                    lastl = (lv == 2)
                    Bm_l = cs["M"][lv][:, 0:128]
                    Nm_l = cs["M"][lv][:, 128:256]
                    def f1_(e, Bm_l=Bm_l, Nm_l=Nm_l, lastl=lastl):
                        r_ = e.matmul(PS(RB)[:, 0:128], Nm_l, Qv, start=True, stop=True)
                        if not lastl:
                            r_ = e.matmul(PS(RB)[:, 128:256], Bm_l, Tv, start=True, stop=True)
                        return r_
                    S.op("pe", f1_, r=[wk(cur), KK("M", lv)], w=[("ps", RB)])
                    nx = 128 if lastl else 256
                    S.op("act", lambda e, nx=nx: e.activation(out=XY[:, 0:nx], in_=PS(RB)[:, 0:nx], func=AF.Copy), r=[("ps", RB)], w=[KK("XY")])
                    yield
                    def f2_(e, lastl=lastl):
                        r_ = e.matmul(PS(RB)[:, 256:384], Tv, XY[:, 0:128], start=True, stop=True)
                        if not lastl:
                            r_ = e.matmul(PS(RB)[:, 384:512], Qv, XY[:, 128:256], start=True, stop=True)
                        return r_
                    S.op("pe", f2_, r=[wk(cur), KK("XY")], w=[("ps", RB)])
                    if lastl:
                        S.op("dve", lambda e: e.tensor_tensor(out=cs["QT"], in0=Qv, in1=PS(RB)[:, 256:384], op=ALU.subtract),
                             r=[("ps", RB), wk(cur)], w=[KK("QT")])
                    else:
                        Wq = Wc.rearrange("p (a b) -> p a b", a=4)[:, 0::2, :]
                        S.op("dve", lambda e, Wq=Wq: e.tensor_tensor(out=Wq, in0=Wq, in1=PS(RB)[:, 256:512].rearrange("p (a b) -> p a b", a=2), op=ALU.subtract),
                             r=[("ps", RB), wk(cur)], w=[wk(cur)])
                    yield
                QT = cs["QT"]
                qkey = KK("QT")
            else:
                BN = cs["BN"]
                Q = cs["Q"]
                S.op("act", lambda e: e.activation(out=BN[0][0:C, 128:128 + C], in_=PSB(4)[0:C, 256:256 + C], func=AF.Copy), r=[("ps", 4)], w=[KK("W", 0)])
                S.op("pool", lambda e: e.tensor_copy(out=BN[0][0:C, 0:C], in_=Bm), r=[KK("NTQK")], w=[KK("W", 0)])
                S.op("pool", lambda e: e.tensor_tensor(out=Q[0][0:C, 0:C], in0=ident_b[0:C, 0:C], in1=Bm, op=ALU.subtract),
                     r=[KK("NTQK"), "cb"], w=[KK("W", 0)])
                nr = 7 if C == 128 else 3
                cur = 0
                for k in range(nr):
                    nxt = 1 - cur
                    Bk = BN[cur][0:C, 0:C]
                    Nk = BN[cur][0:C, 128:128 + C]
                    last = (k == nr - 1)
                    def fr(e, k=k, Bk=Bk, Nk=Nk, cur=cur, last=last):
                        r_ = None
                        if k > 0:
                            r_ = e.matmul(PS(RB)[0:C, 0:C], Nk, Q[cur][0:C, 0:C], start=True, stop=True)
                        if not last:
                            r_ = e.matmul(PS(RB)[0:C, 128:128 + C], Nk, Bk, start=True, stop=True)
                            r_ = e.matmul(PS(RB)[0:C, 256:256 + C], Bk, Nk, start=True, stop=True)
                        return r_
                    S.op("pe", fr, r=[KK("W", cur)], w=[("ps", RB)])
                    if not last:
                        S.op("act", lambda e, nxt=nxt: e.activation(
                            out=BN[nxt][0:C, :].rearrange("p (a b) -> p a b", a=2)[:, :, 0:C],
                            in_=PS(RB)[0:C, 128:384].rearrange("p (a b) -> p a b", a=2)[:, :, 0:C], func=AF.Copy),
                             r=[("ps", RB)], w=[KK("W", nxt)])
                    if k > 0:
                        S.op("dve", lambda e, cur=cur, nxt=nxt: e.tensor_tensor(out=Q[nxt][0:C, 0:C], in0=PS(RB)[0:C, 0:C], in1=Q[cur][0:C, 0:C], op=ALU.add),
                             r=[("ps", RB), KK("W", cur)], w=[KK("W", nxt)])
                    else:
                        S.op("pool", lambda e, cur=cur, nxt=nxt: e.tensor_copy(out=Q[nxt][0:C, 0:C], in_=Q[cur][0:C, 0:C]),
                             r=[KK("W", cur)], w=[KK("W", nxt)])
                    cur = nxt
                    yield
                S.op("pool", lambda e, cur=cur: e.tensor_copy(out=cs["QT"][0:C, 0:C], in_=Q[cur][0:C, 0:C]), r=[KK("W", cur)], w=[KK("QT")])
                QT = cs["QT"][0:C, 0:C]
                qkey = KK("QT")
            S.op("dve", lambda e: e.tensor_scalar(out=cs["QTs"][0:C, 0:C], in0=QT, scalar1=kbg_s[hs_][0:C, col:col + 1], scalar2=None, op0=ALU.mult),
                 r=[qkey, hk("kbg_s")], w=[KK("QTs")])
            yield
            S.op("pe", lambda e: e.matmul(PS(4)[:, 256:256 + C], cs["ktok"][0:C, :], cs["QTs"][0:C, 0:C], start=True, stop=True),
                 r=[KK("ktok"), KK("QTs")], w=[("ps", 4)])
            S.op("act", lambda e: e.activation(out=cs["kcTn"][:, 0:C], in_=PS(4)[:, 256:256 + C], func=AF.Copy, scale=-1.0),
                 r=[("ps", 4)], w=[KK("kcTn")])
            yield

        def CR_chunk(h, spec, cs, slot, Sfl, Sbf, skey, n):
            def KK(name, *rest):
                return (("ho", slot), name) + rest
            C, t0, col, smp, s = spec
            hs_ = h % 2
            qT = cvo[hs_][0]
            zT = cvo[hs_][3]
            cvk = [("cvo", hs_, j, bi) for j in (0, 3) for bi in range(5)]
            hk = lambda nm: ("hs", hs_, "s" if smp else "p", nm)
            QT = cs["QT"][0:C, 0:C]
            qk = KK("QT")
            x = n % 2
            def fu(e):
                e.matmul(PS(7)[0:C, 0:128], QT, cs["vb"][0:C, :], start=True, stop=False)
                return e.matmul(PS(7)[0:C, 0:128], cs["kcTn"][:, 0:C], Sbf, start=False, stop=True)
            S.op("pe", fu, r=[qk, KK("vb"), KK("kcTn"), skey + ("b",)], w=[("ps", 7)])
            S.op("act", lambda e: e.activation(out=u_t[x][0:C], in_=PS(7)[0:C, 0:128], func=AF.Copy), r=[("ps", 7)], w=[("u_t", x)])
            S.op("dve", lambda e: e.tensor_scalar(out=us_t[x][0:C], in0=PS(7)[0:C, 0:128], scalar1=kdec_s[hs_][0:C, col:col + 1], scalar2=None, op0=ALU.mult),
                 r=[("ps", 7), hk("kdec_s")], w=[("us_t", x)])
            yield
            def fo(e):
                e.matmul(PS(7)[0:C, 128:256], qT[:, t0:t0 + C], Sbf, start=True, stop=True)
                e.matmul(PS(7)[0:C, 256:384], cs["NTQK"][0:C, 0:C], u_t[x][0:C], start=True, stop=True)
                return e.matmul(PS(7)[:, 384:512], cs["ktok"][0:C, :], us_t[x][0:C], start=True, stop=True)
            S.op("pe", fo, r=cvk + [skey + ("b",), KK("NTQK"), ("u_t", x), KK("ktok"), ("us_t", x)], w=[("ps", 7)])
            gcolumn = (gtot_t[:, 128 + s * 8 + h:128 + s * 8 + h + 1] if smp else gtot_t[:, col * 8 + h:col * 8 + h + 1])
            S.op("dve", lambda e: e.scalar_tensor_tensor(out=Sfl, in0=Sfl, scalar=gcolumn, in1=PS(7)[:, 384:512], op0=ALU.mult, op1=ALU.add),
                 r=[("ps", 7), skey + ("f",), "gtot_s" if smp else "gtot_p"], w=[skey + ("f",)])
            while n - cn_done[0] >= 2:
                yield WAIT
            S.op("act", lambda e: e.activation(out=t2_t[x][0:C], in_=PS(7)[0:C, 256:384], func=AF.Copy), r=[("ps", 7)], w=[("t2", x)])
            yield
            S.op("act", lambda e: e.activation(out=Sbf, in_=Sfl, func=AF.Copy), r=[skey + ("f",)], w=[skey + ("b",)])
            S.op("dve", lambda e: e.scalar_tensor_tensor(out=o_t[x][0:C], in0=PS(7)[0:C, 128:256], scalar=qdec_s[hs_][0:C, col:col + 1],
                                                         in1=t2_t[x][0:C], op0=ALU.mult, op1=ALU.add),
                 r=[("ps", 7), ("t2", x), hk("qdec_s")], w=[("o_t", x)])
            yield
        def CN_chunk(h, spec, n):
            C, t0, col, smp, s = spec
            hs_ = h % 2
            zT = cvo[hs_][3]
            cvk = [("cvo", hs_, j, bi) for j in (0, 3) for bi in range(5)]
            x = n % 2
            S.op("act", lambda e: e.activation(out=junk2[0:C], in_=o_t[x][0:C], func=AF.Square, accum_out=oss[0:C, x:x + 1]),
                 r=[("o_t", x)], w=["junk2", ("oss", x)])
            yield
            S.op("pool", lambda e: e.tensor_scalar(out=oss[0:C, x:x + 1], in0=oss[0:C, x:x + 1], scalar1=1.0 / 128, scalar2=EPS, op0=ALU.mult, op1=ALU.add),
                 r=[("oss", x)], w=[("oss", x)])
            S.op("pool", lambda e: e.tensor_tensor(out=oss[0:C, x:x + 1], in0=oss[0:C, x:x + 1], in1=nhalf_c[0:C], op=ALU.pow),
                 r=[("oss", x), "cf"], w=[("oss", x)])
            yield
            S.op("act", lambda e: e.activation(out=on_t[x][0:C], in_=o_t[x][0:C], func=AF.Identity, scale=oss[0:C, x:x + 1]),
                 r=[("o_t", x), ("oss", x)], w=[("on_t", x)])
            yield
            S.op("pe", lambda e: e.transpose(PSB(4)[:, 768:768 + C], on_t[x][0:C, :], ident_b[0:C, 0:C]), r=[("on_t", x), "cb"], w=[("ps", 4)])
            S.op("dve", lambda e: e.scalar_tensor_tensor(out=ogT[:, h, t0:t0 + C], in0=PSB(4)[:, 768:768 + C], scalar=outg_h,
                                                         in1=zT[:, t0:t0 + C], op0=ALU.mult, op1=ALU.mult),
                 r=[("ps", 4), "outg_h"] + cvk, w=[("ogT", h, t0)])
            yield

        def CN_head(h):
            specs = chunk_specs(h)
            for n, spec in enumerate(specs):
                gi = cidx[0] + n
                while cr_o[0] <= gi:
                    yield WAIT
                for _ in CN_chunk(h, spec, gi):
                    yield 0.5
                cn_done[0] = gi + 1
                yield 0.5

        def CP_lane(h, lane):
            specs = chunk_specs(h)
            mine = list(range(lane, len(specs), NLANE))
            for _ in range(lane * STAG):
                yield 0.0
            for k_, n in enumerate(mine):
                spec = specs[n]
                gi = cidx[0] + n
                while gi - cr_done[0] >= NHO:
                    yield WAIT
                slot = gi % NHO
                cs = dict(lanes[lane])
                cs.update(hos[slot])
                for yi, _ in enumerate(CP_chunk(h, spec, cs, lane, slot)):
                    yield (k_ + min(0.95, (yi + 1) / 14.0)) / len(mine)
                cp_done.add(gi)
                yield (k_ + 1.0) / len(mine)

        def CR_head(h):
            specs = chunk_specs(h)
            yield 0.0
            for n, spec in enumerate(specs):
                C, t0, col, smp, s = spec
                gi = cidx[0] + n
                while gi not in cp_done:
                    yield WAIT
                slot = gi % NHO
                cs = hos[slot]
                if n == 0 or smp:
                    si = sidx[0] % 2
                    sidx[0] += 1
                    Sfl, Sbf, skey = Sf[si], Sb[si], ("S", si)
                    if smp:
                        S.dma("sp", Sfl, I["sdelta"][s, h], w=[skey + ("f",)])
                        S.op("act", lambda e, Sbf=Sbf, Sfl=Sfl: e.activation(out=Sbf, in_=Sfl, func=AF.Copy), r=[skey + ("f",)], w=[skey + ("b",)])
                    else:
                        S.op("pool", lambda e, Sfl=Sfl: e.memset(Sfl, 0.0), w=[skey + ("f",)])
                        S.op("pool", lambda e, Sbf=Sbf: e.memset(Sbf, 0.0), w=[skey + ("b",)])
                for v_ in CR_chunk(h, spec, cs, slot, Sfl, Sbf, skey, gi):
                    yield (WAIT if v_ is WAIT else 0.5)
                cr_o[0] = gi + 1
                if n == NT - 1 or smp:
                    dst = O["ds"][s, h] if smp else O["dp"][h]
                    S.dma("sp", dst, Sfl, r=[skey + ("f",)])
                cr_done[0] = gi + 1
                yield (n + 1.0) / len(specs) - 0.08

        S.barrier()
        run_tasks([P_head(0)])
        if stop == "P0":
            S.finish(); print("STOP P0", S.ninst); return nc
        for h in range(nheads):
            gens = [CP_lane(h, ln) for ln in range(NLANE)] + [CR_head(h), CN_head(h)]
            if h + 1 < nheads:
                gens.append(P_head(h + 1))
            run_tasks(gens)
            cidx[0] += NT + NS
            if stop == ("H", h):
                S.finish(); print("STOP H", h, S.ninst); return nc
        S.barrier()

        if dbg:
            S.dma("sp", O["dbg_og"], ogT, r=[("ogT", h, t * 128) for h in range(8) for t in range(16)] + [("ogT", h, L + s * 8) for h in range(8) for s in range(NS)])
            S.dma("sp", O["dbg_hT"], hT, r=HT_KEYS)
            S.barrier()
        R1.reset(); R3.reset(); R4.reset()
        x1 = R1.alloc([16, D], F32)
        wo = R3.alloc([8, D], BF16)
        xst2 = [R3.alloc([D], F32) for _ in range(2)]
        xss = R4.alloc([D], F32, parts=32)
        ysb0 = R4.alloc([D], F32, parts=32)

        def wout_phase(l, wsrc, x_in, x_out_fn):
            S.dma("pool", wo, wsrc.rearrange("(c p) n -> p c n", p=128), w=["wo"])
            wos = wo
            if l == 0:
                S.dma("sp", xss, I["xs"], w=["xss"])
            for hb in range(2):
                def f(e, hb=hb):
                    last = None
                    for ec in range(8):
                        last = e.matmul(PS(hb)[0:32, :], ogT[:, ec, L:TT], wo[:, ec, hb * 512:(hb + 1) * 512], start=(ec == 0), stop=(ec == 7))
                    return last
                S.op("pe", f, r=["wo"] + [("ogT", h, L + s * 8) for h in range(8) for s in range(NS)], w=[("ps", hb)])
                ysb = xss if l == 1 else ysb0
                S.op("dve", lambda e, hb=hb, ysb=ysb: e.tensor_tensor(out=ysb[:, hb * 512:(hb + 1) * 512], in0=PS(hb)[0:32, :], in1=gate_s[:, hb * 512:(hb + 1) * 512], op=ALU.mult),
                     r=[("ps", hb), ("gate_s", l)], w=[("ysb", hb)])
                src = xss if l == 0 else x1s
                S.op("dve", lambda e, hb=hb, src=src, ysb=ysb: e.tensor_tensor(out=x1s[:, hb * 512:(hb + 1) * 512], in0=ysb[:, hb * 512:(hb + 1) * 512],
                                                                       in1=src[:, hb * 512:(hb + 1) * 512], op=ALU.add),
                     r=[("ysb", hb), "xss", ("x1s", hb)], w=[("x1s", hb)])
            for ec in range(8):
                S.op("pool", lambda e, ec=ec: e.tensor_tensor(out=wo[:, ec, :], in0=wo[:, ec, :], in1=gate_p, op=ALU.mult),
                     r=["wo", ("gate_p", l)], w=["wo"])
            for t in range(16):
                xb, xkey = x_in(t)
                for hb in range(2):
                    bk = (2 * t + hb) % 4
                    def f(e, hb=hb, bk=bk, t=t):
                        last = None
                        for ec in range(8):
                            last = e.matmul(PS(bk), ogT[:, ec, t * 128:(t + 1) * 128], wo[:, ec, hb * 512:(hb + 1) * 512], start=(ec == 0), stop=(ec == 7))
                        return last
                    S.op("pe", f, r=["wo"] + [("ogT", h, t * 128) for h in range(8)], w=[("ps", bk)])
                    S.op("dve", lambda e, hb=hb, bk=bk, t=t, xb=xb: e.tensor_tensor(out=x_out_fn(t)[:, hb * 512:(hb + 1) * 512], in0=PS(bk),
                                                                                     in1=xb[:, hb * 512:(hb + 1) * 512], op=ALU.add),
                         r=[("ps", bk)] + (xkey if isinstance(xkey, list) else [xkey]), w=[("x1", t, hb)])

        def x_in0(t):
            buf = xst2[t % 2]
            key = ("xst2", t % 2)
            S.dma("sp", buf, I["xp"][t * 128:(t + 1) * 128, :], w=[key])
            return buf, key

        wout_phase(0, I["awout"], x_in0, lambda t: x1[:, t, :])
        if dbg:
            for t in range(16):
                S.dma("sp", O["dbg_x1p"][t * 128:(t + 1) * 128, :], x1[:, t, :], r=[("x1", t, 0), ("x1", t, 1)])
            S.dma("sp", O["dbg_x1s"], x1s, r=[("x1s", 0), ("x1s", 1)])
        if not do_l1:
            S.finish()
            print("instructions:", S.ninst, "sems:", S.nsem)
            return nc
        ada_phase(1, gate_p, gate_s, R4, parts=("col",))
        S.barrier()
        R3.reset(); R4.reset()
        hT1 = R3.alloc([8, TT], BF16)
        RE = Region(A, NB - 12288, NB - 4096)
        qs_all = RE.alloc([8, 3, 32], BF16)
        zs_all = RE.alloc([8, 32], BF16)
        onew = RE.alloc([8, 32], F32)
        lnew = RE.alloc([8, 32], F32)
        ptn = RE.alloc([32], BF16, parts=32)

        def x_src1(t):
            if t < 16:
                return x1[:, t, :], [("x1", t, 0), ("x1", t, 1)]
            return x1s, [("x1s", 0), ("x1s", 1)]

        norm_phase(1, hT1, x_src1, R4)
        S.barrier()
        R4.reset()
        GRP = ((128, 1), (512, 4), (2048, 16))
        SCALE = 128.0 ** -0.5
        Wg = [R4.alloc([8, 384], BF16) for _ in range(2)]
        Wz = RC.alloc([8, 128], BF16)
        QK_ = [R4.alloc([2, TT], BF16) for _ in range(3)]
        QT_ = [qk_[:, 0, :] for qk_ in QK_]
        KT_ = [qk_[:, 1, :] for qk_ in QK_]
        qkb = [R4.alloc([256], BF16) for _ in range(2)]
        Vt_ = [R4.alloc([17, 128], BF16) for _ in range(3)]
        zTh = R4.alloc([1024], BF16)
        PTb = [R4.alloc([512], BF16) for _ in range(2)]
        stg1_ = R4.alloc([256], F32)
        stg_ = [stg1_, stg1_]
        fin1_ = R4.alloc([512], F32)
        fin_ = [fin1_, fin1_]
        print("L1 R4 used", R4.cur - R4.lo, "of", R4.hi - R4.lo)
        KVO = [O["kv0p"], O["kv1p"], O["kv2p"]]
        wcount = [0]
        stc = [0]
        ptc = [0]
        fnc = [0]

        def tok_ap(g, ti):
            d = GRP[g][1]
            if d == 1:
                return lambda kc: hT1[:, kc, ti * 128:(ti + 1) * 128]
            if d == 4:
                r, nb = ti // 4, ti % 4
                return lambda kc: hT1[:, kc, 0:L].rearrange("p (j r) -> p r j", r=4)[:, r, nb * 128:(nb + 1) * 128]
            return lambda kc: hT1[:, kc, 0:L].rearrange("p (j r) -> p r j", r=16)[:, ti, :]

        def blk_ap(g, blk):
            d = GRP[g][1]
            if d == 1:
                return lambda kc: hT1[:, kc, blk * 512:(blk + 1) * 512]
            if d == 4:
                return lambda kc: hT1[:, kc, 0:L].rearrange("p (j r) -> p r j", r=4)[:, blk, :]
            return lambda kc: hT1[:, kc, 0:L].rearrange("p (j r) -> p r j", r=16)[:, 4 * blk:4 * blk + 4, :]

        def kept_rows(g, ti):
            w_, d = GRP[g]
            if d == 1:
                return (0, 1) if ti == 15 else None
            if d == 4:
                r, nb = ti // 4, ti % 4
                return (r, 4) if nb == 3 else None
            return (ti, 16)

        def L1_proj(h, g):
            d = GRP[g][1]
            wb = Wg[wcount[0] % 2]
            wkey = ("Wg", wcount[0] % 2)
            wcount[0] += 1
            for t_ in range(3):
                col = t_ * 3072 + g * 1024 + h * 128
                S.dma("pool", wb[:, :, t_ * 128:(t_ + 1) * 128], I["bwin"][:, col:col + 128].rearrange("(c p) n -> p c n", p=128),
                      w=[wkey + (t_,)])
            wk = [wkey + (t_,) for t_ in range(3)]
            pend = []

            def emit_tr(ti, p, qb, qbk):
                tb_ = 2 + (ti % 2)
                def ftp(e):
                    e.transpose(PSB(tb_)[:, 0:p], qb[0:p, 0:128], ident_b[0:p, 0:p])
                    return e.transpose(PSB(tb_)[:, 128:128 + p], qb[0:p, 128:256], ident_b[0:p, 0:p])
                S.op("pe", ftp, r=[qbk, "cb"], w=[("ps", tb_)])
                t0 = ti * 128 if ti < 16 else L
                dst = QK_[g][:, :, t0:t0 + p]
                src = PSB(tb_)[:, 0:256].rearrange("p (a b) -> p a b", a=2)[:, :, 0:p]
                if ti % 2 == 1:
                    S.op("act", lambda e: e.activation(out=dst, in_=src, func=AF.Copy), r=[("ps", tb_)], w=[("qkT", g, ti)])
                else:
                    S.op("dve", lambda e: e.tensor_copy(out=dst, in_=src), r=[("ps", tb_)], w=[("qkT", g, ti)])

            for ti in range(17):
                bk = ti % 2
                p = 128 if ti < 16 else 32
                ap = tok_ap(g, ti) if ti < 16 else (lambda kc: hT1[:, kc, L:TT])
                def f(e, ap=ap, bk=bk, p=p):
                    last = None
                    for kc in range(8):
                        last = e.matmul(PS(bk)[0:p, 0:384], ap(kc), wb[:, kc, 0:384], start=(kc == 0), stop=(kc == 7))
                    return last
                S.op("pe", f, r=wk + HT_KEYS, w=[("ps", bk)])
                qb = qkb[ti % 2]
                qbk = ("qkb", ti % 2)
                S.op("act", lambda e, bk=bk, p=p, qb=qb: e.activation(out=qb[0:p], in_=PS(bk)[0:p, 0:256], func=AF.Copy), r=[("ps", bk)], w=[qbk])
                S.op("dve", lambda e, bk=bk, p=p, ti=ti: e.tensor_copy(out=Vt_[g][0:p, ti, :], in_=PS(bk)[0:p, 256:384]),
                     r=[("ps", bk)], w=[("vt", g, ti)])
                kr = kept_rows(g, ti) if ti < 16 else None
                if kr is not None or ti == 16:
                    sb = stg_[stc[0] % 2]
                    skey = ("stg", 0)
                    stc[0] += 1
                    S.op("dve", lambda e, sb=sb, bk=bk, p=p: e.tensor_copy(out=sb[0:p], in_=PS(bk)[0:p, 128:384]), r=[("ps", bk)], w=[skey])
                    if ti < 16:
                        r0, st = kr
                        dst = KVO[g].rearrange("(q s) k hh e -> s q k hh e", s=st)[r0, :, :, h, :]
                        S.dma("sp", dst, sb.rearrange("p (k e) -> p k e", k=2), r=[skey])
                    else:
                        dst = O["kv%ds" % g].rearrange("s l k hh e -> (s l) k hh e")[:, :, h, :]
                        S.dma("sp", dst, sb[0:32].rearrange("p (k e) -> p k e", k=2), r=[skey])
                pend.append((ti, p, qb, qbk))
                if len(pend) > 1:
                    emit_tr(*pend.pop(0))
                yield
            while pend:
                emit_tr(*pend.pop(0))
            yield

        def L1_units(g, hf):
            d = GRP[g][1]
            units = []
            if d == 1:
                for nb in range(8 * hf, 8 * hf + 8):
                    col0 = 128 * (nb - 8 * hf)
                    outs = [(col0 // 512, (lambda B, c=col0 % 512: B[:, c:c + 128]), 0, 128)]
                    q = QT_[g][:, nb * 128:(nb + 1) * 128]
                    for pv, kt in ((True, nb - 1), (False, nb)):
                        if kt < 0:
                            continue
                        units.append((KT_[g][:, kt * 128:(kt + 1) * 128], Vt_[g][:, kt, :], 128, mT_prev if pv else mT_cur, q, 128, outs, ("vt", g, kt)))
            elif d == 4:
                for r in range(4):
                    for nb in range(2 * hf, 2 * hf + 2):
                        outs = [(nb - 2 * hf, (lambda B, r=r: B.rearrange("p (q r) -> p r q", r=4)[:, r, :]), 0, 128)]
                        q = QT_[g][:, r * 512 + nb * 128:r * 512 + (nb + 1) * 128]
                        for pv, kb in ((True, nb - 1), (False, nb)):
                            if kb < 0:
                                continue
                            kt = r * 4 + kb
                            units.append((KT_[g][:, kt * 128:(kt + 1) * 128], Vt_[g][:, kt, :], 128, mT_prev if pv else mT_cur, q, 128, outs, ("vt", g, kt)))
            else:
                for r in range(16):
                    outs = [(m, (lambda B, r=r: B.rearrange("p (j r) -> p r j", r=16)[:, r, :]), 32 * m, 32) for m in range(2)]
                    q = QT_[g][:, r * 128 + 64 * hf:r * 128 + 64 * hf + 64]
                    if hf == 0:
                        units.append((KT_[g][:, r * 128:r * 128 + 64], Vt_[g][0:64, r, :], 64, mT_cur[0:64, 0:64], q, 64, outs, ("vt", g, r)))
                    else:
                        units.append((KT_[g][:, r * 128:(r + 1) * 128], Vt_[g][:, r, :], 128, mT_cur[:, 64:128], q, 64, outs, ("vt", g, r)))
            return units

        OB = (4, 5)
        LB = (6, 7)

        def L1_pass(h, hf):
            for b_ in OB + LB:
                S.op("dve", lambda e, b_=b_: e.memset(PS(b_), 0.0), w=[("ps", b_)])
            for m in range(2):
                def f(e, m=m):
                    last = None
                    for kc in range(8):
                        last = e.matmul(PS(m), Wz[:, kc, :], hT1[:, kc, 1024 * hf + 512 * m:1024 * hf + 512 * (m + 1)], start=(kc == 0), stop=(kc == 7))
                    return last
                S.op("pe", f, r=["Wz"] + HT_KEYS, w=[("ps", m)])
                S.op("act", lambda e, m=m: e.activation(out=zTh[:, 512 * m:512 * (m + 1)], in_=PS(m), func=AF.Silu), r=[("ps", m)], w=[("zTh", m)])
            yield
            allu = []
            pend_pv = []
            for g in range(3):
                allu += [(g, u) for u in L1_units(g, hf)]
            for i0 in range(0, len(allu), 4):
                grp = allu[i0:i0 + 4]
                sb_ = 2 + (ptc[0] % 2)
                pt = PTb[ptc[0] % 2]
                pkey = ("PT", ptc[0] % 2)
                ptc[0] += 1
                rk = []
                def fs(e, grp=grp, sb_=sb_):
                    last = None
                    for ui, (g, (kT, v, nk, mk, q, nq, outs, vkey)) in enumerate(grp):
                        o_ = PS(sb_)[0:nk, ui * 128:ui * 128 + nq]
                        e.matmul(o_, kT, q, start=True, stop=False)
                        last = e.matmul(o_, ident_b[0:nk, 0:nk], mk, start=False, stop=True)
                    return last
                for (g, u) in grp:
                    rk += [("qkT", g, ti_) for ti_ in range(16)]
                S.op("pe", fs, r=list(set(rk)) + ["cb"], w=[("ps", sb_)])
                S.op("act", lambda e, sb_=sb_, pt=pt: e.activation(out=pt, in_=PS(sb_), func=AF.Exp, scale=SCALE), r=[("ps", sb_)], w=[pkey])
                def fp(e, grp=grp, pt=pt):
                    last = None
                    for ui, (g, (kT, v, nk, mk, q, nq, outs, vkey)) in enumerate(grp):
                        for (bi_, ofn, c0, n_) in outs:
                            rhs = pt[0:nk, ui * 128 + c0:ui * 128 + c0 + n_]
                            e.matmul(ofn(PS(OB[bi_])), v, rhs, start=False, stop=False, skip_group_check=True)
                            last = e.matmul(ofn(PS(LB[bi_])), ones_b[0:nk, :], rhs, start=False, stop=False, skip_group_check=True)
                    return last
                pend_pv.append((fp, [pkey, "cb"] + [u[7] for (_, u) in grp]))
                if len(pend_pv) > 1:
                    fq, rq_ = pend_pv.pop(0)
                    S.op("pe", fq, r=rq_, w=[("ps", b_) for b_ in OB + LB])
                yield
            while pend_pv:
                fq, rq_ = pend_pv.pop(0)
                S.op("pe", fq, r=rq_, w=[("ps", b_) for b_ in OB + LB])
            for m in range(2):
                fb = fin_[fnc[0] % 2]
                fkey = ("fin", 0)
                fnc[0] += 1
                S.op("act", lambda e, fb=fb, m=m: e.activation(out=fb, in_=PS(LB[m]), func=AF.Ln), r=[("ps", LB[m])], w=[fkey])
                S.op("act", lambda e, fb=fb: e.activation(out=fb, in_=fb, func=AF.Exp, scale=-1.0), r=[fkey], w=[fkey])
                S.op("dve", lambda e, fb=fb, m=m: e.tensor_tensor(out=fb, in0=PS(OB[m]), in1=fb, op=ALU.mult), r=[("ps", OB[m]), fkey], w=[fkey])
                p0 = 1024 * hf + 512 * m
                S.op("dve", lambda e, fb=fb, m=m, p0=p0: e.tensor_tensor(out=ogT[:, h, p0:p0 + 512], in0=fb, in1=zTh[:, 512 * m:512 * (m + 1)], op=ALU.mult),
                     r=[fkey, ("zTh", m)], w=[("ogT", h, p0 // 128 * 128 + i * 128) for i in range(4)])
            yield

        def L1_newrows(h):
            for g in range(3):
                S.op("pool", lambda e, g=g: e.tensor_copy(out=qs_all[:, h, g, :], in_=QT_[g][:, L:TT]), r=[("qkT", g, 16)], w=[("qs", h, g)])
                def fsn(e, g=g):
                    e.matmul(PS(2)[0:32, 0:32], KT_[g][:, L:TT], QT_[g][:, L:TT], start=True, stop=False)
                    return e.matmul(PS(2)[0:32, 0:32], ident_b[0:32, 0:32], cbf[0:32, CB_MN + g * 32:CB_MN + (g + 1) * 32], start=False, stop=True)
                S.op("pe", fsn, r=[("qkT", g, 16), "cb"], w=[("ps", 2)])
                S.op("act", lambda e: e.activation(out=ptn, in_=PS(2)[0:32, 0:32], func=AF.Exp, scale=SCALE), r=[("ps", 2)], w=["ptn"])
                def fpn(e, g=g):
                    e.matmul(PS(4)[:, 0:32], Vt_[g][0:32, 16, :], ptn, start=(g == 0), stop=(g == 2))
                    return e.matmul(PS(6)[:, 0:32], ones_b[0:32, :], ptn, start=(g == 0), stop=(g == 2))
                S.op("pe", fpn, r=["ptn", ("vt", g, 16), "cb"], w=[("ps", 4), ("ps", 6)])
            S.op("act", lambda e: e.activation(out=onew[:, h, :], in_=PS(4)[:, 0:32], func=AF.Copy), r=[("ps", 4)], w=[("onew", h)])
            S.op("dve", lambda e: e.tensor_copy(out=lnew[:, h, :], in_=PS(6)[:, 0:32]), r=[("ps", 6)], w=[("lnew", h)])
            def fz(e):
                last = None
                for kc in range(8):
                    last = e.matmul(PS(0)[:, 0:32], Wz[:, kc, :], hT1[:, kc, L:TT], start=(kc == 0), stop=(kc == 7))
                return last
            S.op("pe", fz, r=["Wz"] + HT_KEYS, w=[("ps", 0)])
            S.op("act", lambda e: e.activation(out=zs_all[:, h, :], in_=PS(0)[:, 0:32], func=AF.Silu), r=[("ps", 0)], w=[("zs", h)])

        def L1_head(h):
            S.dma("pool", Wz, I["bwin"][:, 9216 + h * 128:9216 + (h + 1) * 128].rearrange("(c p) n -> p c n", p=128), w=["Wz"])
            for g in range(3):
                for _ in L1_proj(h, g):
                    yield
            L1_newrows(h)
            yield
            for hf in range(2):
                for _ in L1_pass(h, hf):
                    yield

        for h in range(nheads):
            for _ in L1_head(h):
                pass
        S.barrier()

        R4.reset()
        NPF = 3
        NCL = (1, 4, 8)
        ckb = [[R4.alloc([NCL[g], 2, 128], BF16) for g in range(3)] for _ in range(NPF)]
        kTc = [R4.alloc([13, 128], BF16) for _ in range(2)]
        pts = [R4.alloc([24], BF16) for _ in range(2)]
        fo_ = R4.alloc([32], F32)
        fl_ = R4.alloc([32], F32)
        items = [(h, s_) for h in range(nheads) for s_ in range(NS)]

        def e2_load(i):
            h, s_ = items[i]
            for g in range(3):
                d = GRP[g][1]
                src = I["kc%d" % g][s_].rearrange("(k r) kv hh e -> k r kv hh e", r=d)[:, 0:NCL[g], :, h, :]
                S.dma("pool", ckb[i % NPF][g], src, w=[("ck", i % NPF, g)])

        for i in range(min(NPF - 1, len(items))):
            e2_load(i)
        for i, (h, s_) in enumerate(items):
            if i + NPF - 1 < len(items):
                e2_load(i + NPF - 1)
            buf = ckb[i % NPF]
            ckeys = [("ck", i % NPF, g) for g in range(3)]
            kt = kTc[i % 2]
            ktk = ("kTc", i % 2)
            pt = pts[i % 2]
            ptk = ("pts", i % 2)
            sb_ = 2 + (i % 2)
            tiles = [(0, 0)] + [(1, r) for r in range(4)] + [(2, r) for r in range(8)]
            if s_ == 0:
                S.op("dve", lambda e: e.memset(PS(4)[:, 0:32], 0.0), w=[("ps", 4)])
                S.op("dve", lambda e: e.memset(PS(6)[:, 0:32], 0.0), w=[("ps", 6)])
            def ftr(e, buf=buf):
                last = None
                for ti, (g, r) in enumerate(tiles):
                    last = e.transpose(PSB(ti // 8)[:, (ti % 8) * 128:(ti % 8 + 1) * 128], buf[g][:, r, 0, :], ident_b)
                return last
            S.op("pe", ftr, r=ckeys + ["cb"], w=[("ps", 0), ("ps", 1)])
            S.op("act", lambda e, kt=kt: e.activation(out=kt[:, 0:8, :], in_=PSB(0).rearrange("p (a b) -> p a b", a=8), func=AF.Copy), r=[("ps", 0)], w=[ktk + (0,)])
            S.op("dve", lambda e, kt=kt: e.tensor_copy(out=kt[:, 8:13, :], in_=PSB(1)[:, 0:640].rearrange("p (a b) -> p a b", a=5)), r=[("ps", 1)], w=[ktk + (1,)])
            def cls(g, r):
                d = GRP[g][1]
                nq = 8 // d if d <= 8 else 1
                c0 = (0, 8 + 2 * r, 16 + r)[g]
                if d == 1:
                    q = qs_all[:, h, g, 8 * s_:8 * s_ + 8]
                    mk = cbf[:, CB_MK:CB_MK + 8]
                    oc = lambda B: B[:, 8 * s_:8 * s_ + 8]
                elif d == 4:
                    q = qs_all[:, h, g, 8 * s_:8 * s_ + 8].rearrange("p (i r) -> p r i", r=4)[:, r, :]
                    mk = cbf[:, CB_MK + 8:CB_MK + 16].rearrange("p (i r) -> p r i", r=4)[:, r, :]
                    oc = lambda B: B[:, 8 * s_:8 * s_ + 8].rearrange("p (i r) -> p r i", r=4)[:, r, :]
                else:
                    q = qs_all[:, h, g, 8 * s_ + r:8 * s_ + r + 1]
                    mk = None
                    oc = lambda B: B[:, 8 * s_ + r:8 * s_ + r + 1]
                return nq, c0, q, mk, oc
            def fsc(e, kt=kt, sb_=sb_):
                last = None
                for ti, (g, r) in enumerate(tiles):
                    nq, c0, q, mk, oc = cls(g, r)
                    o_ = PS(sb_)[:, c0:c0 + nq]
                    last = e.matmul(o_, kt[:, ti, :], q, start=True, stop=(mk is None))
                    if mk is not None:
                        last = e.matmul(o_, ident_b, mk, start=False, stop=True)
                return last
            S.op("pe", fsc, r=[ktk + (0,), ktk + (1,), "cb"] + [("qs", h, g) for g in range(3)], w=[("ps", sb_)])
            S.op("act", lambda e, pt=pt, sb_=sb_: e.activation(out=pt, in_=PS(sb_)[:, 0:24], func=AF.Exp, scale=SCALE), r=[("ps", sb_)], w=[ptk])
            def fpv(e, buf=buf, pt=pt):
                last = None
                for ti, (g, r) in enumerate(tiles):
                    nq, c0, q, mk, oc = cls(g, r)
                    e.matmul(oc(PS(4)), buf[g][:, r, 1, :], pt[:, c0:c0 + nq], start=False, stop=False, skip_group_check=True)
                    last = e.matmul(oc(PS(6)), ones_b, pt[:, c0:c0 + nq], start=False, stop=False, skip_group_check=True)
                return last
            S.op("pe", fpv, r=ckeys + [ptk, "cb"], w=[("ps", 4), ("ps", 6)])
            if s_ == NS - 1:
                S.op("dve", lambda e, h=h: e.tensor_tensor(out=fo_, in0=PS(4)[:, 0:32], in1=onew[:, h, :], op=ALU.add), r=[("ps", 4), ("onew", h)], w=["fo_"])
                S.op("dve", lambda e, h=h: e.tensor_tensor(out=fl_, in0=PS(6)[:, 0:32], in1=lnew[:, h, :], op=ALU.add), r=[("ps", 6), ("lnew", h)], w=["fl_"])
                S.op("act", lambda e: e.activation(out=fl_, in_=fl_, func=AF.Ln), r=["fl_"], w=["fl_"])
                S.op("act", lambda e: e.activation(out=fl_, in_=fl_, func=AF.Exp, scale=-1.0), r=["fl_"], w=["fl_"])
                S.op("dve", lambda e: e.tensor_tensor(out=fo_, in0=fo_, in1=fl_, op=ALU.mult), r=["fo_", "fl_"], w=["fo_"])
                S.op("dve", lambda e, h=h: e.tensor_tensor(out=ogT[:, h, L:TT], in0=fo_, in1=zs_all[:, h, :], op=ALU.mult),
                     r=["fo_", ("zs", h)], w=[("ogT", h, L + q_ * 8) for q_ in range(NS)])
        S.barrier()
        R3.reset()
        ada_phase(1, gate_p, gate_s, R3, parts=("gate",))
        S.barrier()

        R4.reset()
        xss = R4.alloc([D], F32, parts=32)
        fng_bc = R4.alloc([D], F32)
        S.dma("sp", fng_bc, I["fng"].partition_broadcast(128), w=["fng"])
        wout_phase(1, I["bwout"], lambda t: (x1[:, t, :], [("x1", t, 0), ("x1", t, 1)]), lambda t: x1[:, t, :])
        ssq2 = R4.alloc([17], F32)
        junk3 = R4.alloc([D], BF16)
        ost = [R4.alloc([D], F32) for _ in range(2)]
        for t in range(17):
            p = 128 if t < 16 else 32
            xb, xk = x_src1(t)
            S.op("act", lambda e, xb=xb, p=p, t=t: e.activation(out=junk3[0:p], in_=xb[0:p], func=AF.Square, accum_out=ssq2[0:p, t:t + 1]),
                 r=xk, w=["junk3", ("ssq2", t)])
            S.op("pool", lambda e, p=p, t=t: e.tensor_scalar(out=ssq2[0:p, t:t + 1], in0=ssq2[0:p, t:t + 1], scalar1=1.0 / D, scalar2=EPS, op0=ALU.mult, op1=ALU.add),
                 r=[("ssq2", t)], w=[("ssq2", t)])
            S.op("pool", lambda e, p=p, t=t: e.tensor_tensor(out=ssq2[0:p, t:t + 1], in0=ssq2[0:p, t:t + 1], in1=nhalf_c[0:p], op=ALU.pow),
                 r=[("ssq2", t), "cf"], w=[("ssq2", t)])
            ob = ost[t % 2]
            okey = ("ost", t % 2)
            S.op("act", lambda e, xb=xb, p=p, t=t, ob=ob: e.activation(out=ob[0:p], in_=xb[0:p], func=AF.Identity, scale=ssq2[0:p, t:t + 1]),
                 r=xk + [("ssq2", t)], w=[okey])
            S.op("dve", lambda e, p=p, ob=ob: e.tensor_tensor(out=ob[0:p], in0=ob[0:p], in1=fng_bc[0:p], op=ALU.mult), r=[okey, "fng"], w=[okey])
            if t < 16:
                S.dma("sp", O["yp"][t * 128:(t + 1) * 128, :], ob, r=[okey])
            else:
                S.dma("sp", O["ys"], ob[0:32], r=[okey])
        S.finish()
        print("instructions:", S.ninst, "sems:", S.nsem)
    return nc


def prep_inputs(inp):
    f = lambda a: np.ascontiguousarray(np.asarray(a, dtype=np.float32))
    cf, cb = make_consts()
    ngc = f(np.asarray(inp["norm_g"]).reshape(2, 8, 128).transpose(0, 2, 1))
    adab = f(inp["ada_b"])
    adabc = f(adab[:, 0:2048].reshape(2, 16, 128).transpose(0, 2, 1))
    cw = f(np.asarray(inp["a_conv_w"])[0].reshape(4, 24, 128).transpose(2, 1, 0))
    hp = np.zeros((128, 17), np.float32)
    hp[:, 0:8] = np.asarray(inp["a_A_log"])[0][None, :]
    hp[:, 8:16] = np.asarray(inp["a_dt_bias"])[0][None, :]
    hp[:, 16] = np.asarray(inp["a_out_norm_g"])[0]
    shared = dict(ngc=ngc, adaw=f(inp["ada_w"]), adabc=adabc, adab=adab, awin=f(np.asarray(inp["a_w_in"])[0]), cw=cw, hp=hp,
                  awout=f(np.asarray(inp["a_w_out"])[0]), cf=cf, cb=cb, fng=f(inp["final_norm_g"]),
                  bwin=f(np.asarray(inp["b_w_in"])[0]), bwout=f(np.asarray(inp["b_w_out"])[0]))
    maps = []
    for c in range(NCORES):
        m = dict(shared)
        m["xp"] = f(np.asarray(inp["x_prompt"])[c])
        m["xs"] = f(np.asarray(inp["x_sample"])[4 * c:4 * c + 4].reshape(32, D))
        cc = np.concatenate([np.asarray(inp["c_prompt"])[c:c + 1], np.asarray(inp["c_sample"])[4 * c:4 * c + 4]], 0)
        m["cT"] = f(cc.T.reshape(8, 128, 5).transpose(1, 0, 2))
        sc = np.asarray(inp["state_conv"])[0, 4 * c:4 * c + 4]
        m["sconv"] = f(sc.reshape(4, 3, 24, 128).transpose(3, 2, 0, 1))
        m["sdelta"] = f(np.asarray(inp["state_delta"])[0, 4 * c:4 * c + 4])
        m["kc0"] = f(np.asarray(inp["cache_kv_w128"])[0, 4 * c:4 * c + 4])
        m["kc1"] = f(np.asarray(inp["cache_kv_w512"])[0, 4 * c:4 * c + 4])
        m["kc2"] = f(np.asarray(inp["cache_kv_w2048"])[0, 4 * c:4 * c + 4])
        maps.append(m)
    return maps


_NC_CACHE = {}


def kernel(**inp):
    if "nc" not in _NC_CACHE:
        _NC_CACHE["nc"] = build()
    nc = _NC_CACHE["nc"]
    maps = prep_inputs(inp)
    res = run_bass_kernel_spmd(nc, maps, core_ids=list(range(NCORES)))
    R = res.results
    g = lambda k: [np.asarray(R[c][k], dtype=np.float32) for c in range(NCORES)]
    y_prompt = np.stack(g("yp"), 0)
    y_sample = np.concatenate([a.reshape(NS, LS, D) for a in g("ys")], 0)
    delta_p = np.stack(g("dp"), 0)[None]
    delta_s = np.concatenate(g("ds"), 0)[None]
    conv_p = np.stack(g("cp"), 0)[None]
    conv_s = np.concatenate([a.reshape(NS, 3, 3072) for a in g("cs")], 0)[None]
    outs = [y_prompt, y_sample, delta_p, delta_s, conv_p, conv_s]
    for gi in range(3):
        outs.append(np.stack(g("kv%dp" % gi), 0)[None])
        outs.append(np.concatenate(g("kv%ds" % gi), 0)[None])
    return tuple(np.ascontiguousarray(o, dtype=np.float32) for o in outs)
```
